# Optimizing a Trainium2 kernel written in Bass

```python
import jax
import jax.numpy as jnp
from jax import lax
import numpy as np

D_MODEL = 1024
BATCH = 2
SEQ = 8192
DEPTH = 4

EPS = 1e-6
N_EVEN = (DEPTH + 1) // 2
N_ODD = DEPTH // 2

RET_HEADS = 4
RET_DK = D_MODEL // 8
RET_DV = D_MODEL // 4
RET_CHUNK = 128
ROPE_BASE = 10000.0

SSD_D_INNER = D_MODEL
SSD_HEADDIM = 64
SSD_HEADS = SSD_D_INNER // SSD_HEADDIM
SSD_GROUPS = 2
SSD_STATE = 128
SSD_CONV = 4
SSD_CHUNK = 128
SSD_CONV_DIM = SSD_D_INNER + 2 * SSD_GROUPS * SSD_STATE

LRU_WIDTH = D_MODEL
LRU_BLOCKS = 8
LRU_BLOCK = LRU_WIDTH // LRU_BLOCKS
LRU_C = 8.0
LRU_CONV = 4

SB_HEADS = 8
SB_HEAD_DIM = 64
SB_BLOCK = 128

D_FF = 4 * D_MODEL

EVEN_SPLITS = (RET_HEADS * RET_DK, RET_HEADS * RET_DK, RET_HEADS * RET_DV, RET_HEADS * RET_DV, SSD_D_INNER, SSD_CONV_DIM, SSD_HEADS)
EVEN_IN = sum(EVEN_SPLITS)
EVEN_OUT = RET_HEADS * RET_DV + SSD_D_INNER
ODD_SPLITS = (LRU_WIDTH, LRU_WIDTH, SB_HEADS * SB_HEAD_DIM, SB_HEADS * SB_HEAD_DIM, SB_HEADS * SB_HEAD_DIM)
ODD_IN = sum(ODD_SPLITS)
ODD_OUT = LRU_WIDTH + SB_HEADS * SB_HEAD_DIM

kernel_name = 'hybrid_retention_ssd_rglru_stickbreak'


def _split(a, sizes):
    return jnp.split(a, [int(s) for s in np.cumsum(sizes)[:-1]], axis=-1)


def rmsnorm(x, g):
    xf = x.astype(jnp.float32)
    y = xf * lax.rsqrt(jnp.mean(xf * xf, axis=-1, keepdims=True) + EPS)
    return (y * g.astype(jnp.float32)).astype(x.dtype)


def head_groupnorm(y, g):
    B, T, H, dv = y.shape
    mu = jnp.mean(y, axis=-1, keepdims=True)
    yc = y - mu
    yn = yc * lax.rsqrt(jnp.mean(yc * yc, axis=-1, keepdims=True) + EPS)
    return (yn * g.astype(jnp.float32).reshape(H, dv)).reshape(B, T, H * dv)


def causal_dwconv(x, w, b):
    K, C = w.shape
    y = lax.conv_general_dilated(x, w.astype(x.dtype)[:, None, :], window_strides=(1,), padding=[(K - 1, 0)], dimension_numbers=('NWC', 'WIO', 'NWC'), feature_group_count=C)
    return y + b.astype(x.dtype)


def rotary(x, pos):
    half = x.shape[-1] // 2
    inv = ROPE_BASE ** (-jnp.arange(half, dtype=jnp.float32) / half)
    ang = pos.astype(jnp.float32)[:, None] * inv[None, :]
    cos = jnp.cos(ang)[None, :, None, :]
    sin = jnp.sin(ang)[None, :, None, :]
    x1, x2 = x[..., :half], x[..., half:]
    return jnp.concatenate([x1 * cos - x2 * sin, x1 * sin + x2 * cos], axis=-1)


def retention_chunkwise(q, k, v, log_gamma):
    B, T, H, dk = q.shape
    dv = v.shape[-1]
    C = RET_CHUNK
    n = T // C
    q = q.reshape(B, n, C, H, dk)
    k = k.reshape(B, n, C, H, dk)
    v = v.reshape(B, n, C, H, dv)
    idx = jnp.arange(C, dtype=jnp.float32)
    rel = idx[:, None] - idx[None, :]
    causal = rel >= 0
    decay = jnp.where(causal[None], jnp.exp(log_gamma[:, None, None] * jnp.maximum(rel, 0.0)[None]), 0.0)
    scores = jnp.einsum('bcihd,bcjhd->bchij', q, k) * decay[None, None]
    inner = jnp.einsum('bchij,bcjhe->bcihe', scores, v)
    k_dec = jnp.exp(log_gamma[None, :] * (C - 1 - idx)[:, None])
    chunk_kv = jnp.einsum('bcjhd,jh,bcjhe->bchde', k, k_dec, v)
    chunk_decay = jnp.exp(log_gamma * C)

    def step(S, kv):
        return S * chunk_decay[None, :, None, None] + kv, S

    _, S_prev = lax.scan(step, jnp.zeros((B, H, dk, dv), jnp.float32), jnp.moveaxis(chunk_kv, 1, 0))
    S_prev = jnp.moveaxis(S_prev, 0, 1)
    q_dec = jnp.exp(log_gamma[None, :] * (idx + 1.0)[:, None])
    cross = jnp.einsum('bcihd,ih,bchde->bcihe', q, q_dec, S_prev)
    return (inner + cross).reshape(B, T, H, dv)


def ssd_chunked(x, dt, A, Bm, Cm):
    Bsz, T, H, P = x.shape
    G, N = Bm.shape[2], Bm.shape[3]
    R = H // G
    Q = SSD_CHUNK
    n = T // Q
    xr = (x * dt[..., None]).reshape(Bsz, n, Q, G, R, P)
    dA = (dt * A).reshape(Bsz, n, Q, G, R)
    Acum = jnp.cumsum(dA, axis=2)
    Br = Bm.reshape(Bsz, n, Q, G, N)
    Cr = Cm.reshape(Bsz, n, Q, G, N)
    seg = Acum[:, :, :, None] - Acum[:, :, None, :]
    causal = jnp.tril(jnp.ones((Q, Q), dtype=bool))[None, None, :, :, None, None]
    L = jnp.exp(jnp.where(causal, seg, -jnp.inf))
    CB = jnp.einsum('bcigs,bcjgs->bcijg', Cr, Br)
    y_diag = jnp.einsum('bcijg,bcijgr,bcjgrp->bcigrp', CB, L, xr)
    decay_states = jnp.exp(Acum[:, :, -1:] - Acum)
    states = jnp.einsum('bcjgs,bcjgr,bcjgrp->bcgrps', Br, decay_states, xr)
    chunk_decay = jnp.exp(Acum[:, :, -1])

    def step(S, inp):
        st, dec = inp
        return S * dec[..., None, None] + st, S

    _, S_prev = lax.scan(step, jnp.zeros((Bsz, G, R, P, N), jnp.float32), (jnp.moveaxis(states, 1, 0), jnp.moveaxis(chunk_decay, 1, 0)))
    S_prev = jnp.moveaxis(S_prev, 0, 1)
    y_off = jnp.einsum('bcigs,bcgrps,bcigr->bcigrp', Cr, S_prev, jnp.exp(Acum))
    return (y_diag + y_off).reshape(Bsz, T, H, P)


def rg_lru(x, wa, ba, wx, bx, lam):
    B, T, W = x.shape
    xb = x.reshape(B, T, LRU_BLOCKS, LRU_BLOCK)
    r = jax.nn.sigmoid(jnp.einsum('btki,kij->btkj', xb, wa.astype(x.dtype)) + ba.astype(x.dtype)).reshape(B, T, W)
    i = jax.nn.sigmoid(jnp.einsum('btki,kij->btkj', xb, wx.astype(x.dtype)) + bx.astype(x.dtype)).reshape(B, T, W)
    log_a = -LRU_C * r * jax.nn.softplus(-lam.astype(x.dtype))
    a = jnp.exp(log_a)
    u = jnp.sqrt(-jnp.expm1(2.0 * log_a)) * (i * x)

    def combine(c1, c2):
        a1, b1 = c1
        a2, b2 = c2
        return a1 * a2, a2 * b1 + b2

    _, h = lax.associative_scan(combine, (a, u), axis=1)
    return h


def stick_breaking(q, k, v):
    B, T, H, d = q.shape
    nb = T // SB_BLOCK
    scale = d ** -0.5
    qb = q.reshape(B, nb, SB_BLOCK, H, d).transpose(1, 0, 3, 2, 4)
    kt = k.transpose(0, 2, 1, 3)
    vt = v.transpose(0, 2, 1, 3)
    kpos = jnp.arange(T)

    def block(args):
        qblk, bi = args
        qpos = bi * SB_BLOCK + jnp.arange(SB_BLOCK)
        z = jnp.einsum('bhqd,bhkd->bhqk', qblk, kt) * scale
        mask = (kpos[None, :] < qpos[:, None])[None, None]
        log_beta = jax.nn.log_sigmoid(z)
        log_1mb = jnp.where(mask, jax.nn.log_sigmoid(-z), 0.0)
        suffix = jnp.flip(jnp.cumsum(jnp.flip(log_1mb, -1), axis=-1), -1) - log_1mb
        w = jnp.where(mask, jnp.exp(log_beta + suffix), 0.0)
        return jnp.einsum('bhqk,bhkd->bhqd', w, vt)

    out = lax.map(block, (qb, jnp.arange(nb)))
    return out.transpose(1, 0, 3, 2, 4).reshape(B, T, H * d)


def even_mixer(h, w_in, w_out, qn, kn, gn, conv_w, conv_b, dt_bias, a_log, d_skip, ssd_g):
    B, T, _ = h.shape
    f32 = jnp.float32
    q, k, v, g, z, xbc, dt_raw = _split(h @ w_in, EVEN_SPLITS)
    pos = jnp.arange(T)
    q = rotary(rmsnorm(q.reshape(B, T, RET_HEADS, RET_DK).astype(f32), qn), pos)
    k = rotary(rmsnorm(k.reshape(B, T, RET_HEADS, RET_DK).astype(f32), kn), pos) * (RET_DK ** -0.5)
    v = v.reshape(B, T, RET_HEADS, RET_DV).astype(f32)
    log_gamma = jnp.log1p(-jnp.exp2(-5.0 - jnp.arange(RET_HEADS, dtype=f32)))
    ya = retention_chunkwise(q, k, v, log_gamma)
    ya = jax.nn.silu(g.astype(f32)) * head_groupnorm(ya, gn)
    xbc = jax.nn.silu(causal_dwconv(xbc, conv_w, conv_b).astype(f32))
    xs, bm, cm = _split(xbc, (SSD_D_INNER, SSD_GROUPS * SSD_STATE, SSD_GROUPS * SSD_STATE))
    dt = jax.nn.softplus(dt_raw.astype(f32) + dt_bias.astype(f32))
    A = -jnp.exp(a_log.astype(f32))
    xs = xs.reshape(B, T, SSD_HEADS, SSD_HEADDIM)
    yb = ssd_chunked(xs, dt, A, bm.reshape(B, T, SSD_GROUPS, SSD_STATE), cm.reshape(B, T, SSD_GROUPS, SSD_STATE))
    yb = yb + xs * d_skip.astype(f32)[:, None]
    yb = yb.reshape(B, T, SSD_D_INNER) * jax.nn.silu(z.astype(f32))
    yb = rmsnorm(yb.reshape(B, T, SSD_GROUPS, SSD_D_INNER // SSD_GROUPS), ssd_g.reshape(SSD_GROUPS, SSD_D_INNER // SSD_GROUPS)).reshape(B, T, SSD_D_INNER)
    y = jnp.concatenate([ya, yb], axis=-1).astype(h.dtype)
    return y @ w_out


def odd_mixer(h, w_in, w_out, conv_w, conv_b, wa, ba, wx, bx, lam, qn, kn):
    B, T, _ = h.shape
    f32 = jnp.float32
    gate, xc, q, k, v = _split(h @ w_in, ODD_SPLITS)
    xc = causal_dwconv(xc, conv_w, conv_b).astype(f32)
    yc = rg_lru(xc, wa, ba, wx, bx, lam) * jax.nn.gelu(gate.astype(f32))
    shp = (B, T, SB_HEADS, SB_HEAD_DIM)
    q = rmsnorm(q.reshape(shp).astype(f32), qn)
    k = rmsnorm(k.reshape(shp).astype(f32), kn)
    yd = stick_breaking(q, k, v.reshape(shp).astype(f32))
    y = jnp.concatenate([yc, yd], axis=-1).astype(h.dtype)
    return y @ w_out


def setup_inputs(seed: int = 0) -> dict:
    key = jax.random.key(seed)
    keys = iter(jax.random.split(key, 40))
    f32 = jnp.float32

    def nrm(shape, scale):
        return jax.random.normal(next(keys), shape, f32) * scale

    def gain(shape):
        return 1.0 + nrm(shape, 0.05)

    def unif(shape, lo, hi):
        return jax.random.uniform(next(keys), shape, f32, lo, hi)

    x = nrm((BATCH, SEQ, D_MODEL), 1.0)
    norm_mix = gain((DEPTH, D_MODEL))
    norm_mlp = gain((DEPTH, D_MODEL))
    mlp_w1 = nrm((DEPTH, D_MODEL, D_FF), D_MODEL ** -0.5)
    mlp_w2 = nrm((DEPTH, D_FF, D_MODEL), D_FF ** -0.5)
    ev_w_in = nrm((N_EVEN, D_MODEL, EVEN_IN), D_MODEL ** -0.5)
    ev_w_out = nrm((N_EVEN, EVEN_OUT, D_MODEL), EVEN_OUT ** -0.5)
    ret_qn = gain((N_EVEN, RET_DK))
    ret_kn = gain((N_EVEN, RET_DK))
    ret_gn = gain((N_EVEN, RET_HEADS * RET_DV))
    ssd_conv_w = nrm((N_EVEN, SSD_CONV, SSD_CONV_DIM), SSD_CONV ** -0.5)
    ssd_conv_b = nrm((N_EVEN, SSD_CONV_DIM), 0.02)
    dt0 = jnp.exp(unif((N_EVEN, SSD_HEADS), float(np.log(1e-3)), float(np.log(1e-1))))
    ssd_dt_bias = dt0 + jnp.log(-jnp.expm1(-dt0))
    ssd_a_log = jnp.log(unif((N_EVEN, SSD_HEADS), 1.0, 16.0))
    ssd_d = gain((N_EVEN, SSD_HEADS))
    ssd_norm = gain((N_EVEN, SSD_D_INNER))
    od_w_in = nrm((N_ODD, D_MODEL, ODD_IN), D_MODEL ** -0.5)
    od_w_out = nrm((N_ODD, ODD_OUT, D_MODEL), ODD_OUT ** -0.5)
    lru_conv_w = nrm((N_ODD, LRU_CONV, LRU_WIDTH), LRU_CONV ** -0.5)
    lru_conv_b = nrm((N_ODD, LRU_WIDTH), 0.02)
    lru_wa = nrm((N_ODD, LRU_BLOCKS, LRU_BLOCK, LRU_BLOCK), LRU_BLOCK ** -0.5)
    lru_ba = nrm((N_ODD, LRU_BLOCKS, LRU_BLOCK), 0.1)
    lru_wx = nrm((N_ODD, LRU_BLOCKS, LRU_BLOCK, LRU_BLOCK), LRU_BLOCK ** -0.5)
    lru_bx = nrm((N_ODD, LRU_BLOCKS, LRU_BLOCK), 0.1)
    a_c = unif((N_ODD, LRU_WIDTH), 0.81, 0.998)
    s = a_c ** (1.0 / LRU_C)
    lru_lam = jnp.log(s) - jnp.log1p(-s)
    sb_qn = gain((N_ODD, SB_HEAD_DIM))
    sb_kn = gain((N_ODD, SB_HEAD_DIM))
    return {'x': x, 'norm_mix': norm_mix, 'norm_mlp': norm_mlp, 'mlp_w1': mlp_w1, 'mlp_w2': mlp_w2,
            'ev_w_in': ev_w_in, 'ev_w_out': ev_w_out, 'ret_qn': ret_qn, 'ret_kn': ret_kn, 'ret_gn': ret_gn,
            'ssd_conv_w': ssd_conv_w, 'ssd_conv_b': ssd_conv_b, 'ssd_dt_bias': ssd_dt_bias, 'ssd_a_log': ssd_a_log,
            'ssd_d': ssd_d, 'ssd_norm': ssd_norm, 'od_w_in': od_w_in, 'od_w_out': od_w_out,
            'lru_conv_w': lru_conv_w, 'lru_conv_b': lru_conv_b, 'lru_wa': lru_wa, 'lru_ba': lru_ba,
            'lru_wx': lru_wx, 'lru_bx': lru_bx, 'lru_lam': lru_lam, 'sb_qn': sb_qn, 'sb_kn': sb_kn}


def reference(x, norm_mix, norm_mlp, mlp_w1, mlp_w2, ev_w_in, ev_w_out, ret_qn, ret_kn, ret_gn, ssd_conv_w, ssd_conv_b, ssd_dt_bias, ssd_a_log, ssd_d, ssd_norm, od_w_in, od_w_out, lru_conv_w, lru_conv_b, lru_wa, lru_ba, lru_wx, lru_bx, lru_lam, sb_qn, sb_kn):
    for l in range(DEPTH):
        h = rmsnorm(x, norm_mix[l])
        if l % 2 == 0:
            e = l // 2
            x = x + even_mixer(h, ev_w_in[e], ev_w_out[e], ret_qn[e], ret_kn[e], ret_gn[e], ssd_conv_w[e], ssd_conv_b[e], ssd_dt_bias[e], ssd_a_log[e], ssd_d[e], ssd_norm[e])
        else:
            o = l // 2
            x = x + odd_mixer(h, od_w_in[o], od_w_out[o], lru_conv_w[o], lru_conv_b[o], lru_wa[o], lru_ba[o], lru_wx[o], lru_bx[o], lru_lam[o], sb_qn[o], sb_kn[o])
        h = rmsnorm(x, norm_mlp[l])
        x = x + jnp.square(jax.nn.relu(h @ mlp_w1[l])) @ mlp_w2[l]
    return x
```

```python
import contextlib
import math
import numpy as np
import concourse.bass as bass
import concourse.mybir as mybir
from concourse.bass_utils import run_bass_kernel_spmd

F32 = mybir.dt.float32
BF16 = mybir.dt.bfloat16
AF = mybir.ActivationFunctionType
ALU = mybir.AluOpType
AX = mybir.AxisListType

D = 1024
KC = 8
TB = 512
EPS = 1e-6


class Buf:
    __slots__ = ("name", "t", "lws", "rd")

    def __init__(self, name, t=None):
        self.name = name
        self.t = t
        self.lws = []
        self.rd = {}

    def __getitem__(self, k):
        return self.t[k]


class Rot:
    def __init__(self, bufs):
        self.bufs = list(bufs)
        self.i = 0

    def next(self):
        b = self.bufs[self.i % len(self.bufs)]
        self.i += 1
        return b


class Sched:
    ENG = ("pe", "act", "dve", "pool", "sp")

    def __init__(self, nc, es, n_dma_sems=32):
        self.nc = nc
        self.es = es
        self.q = {e: [] for e in self.ENG}
        self.cnt = {e: 0 for e in self.ENG}
        self.sem = {e: es.enter_context(nc.semaphore("c_" + e)) for e in ("pe", "act", "dve", "pool")}
        self.dsem = [es.enter_context(nc.semaphore("d%d" % i)) for i in range(n_dma_sems)]
        self.dcnt = [0] * n_dma_sems
        self.dnext = 0
        self.waited = {e: {} for e in self.ENG}
        self.ndma = 0

    def sb(self, name, shape, dt, es=None):
        self.uid = getattr(self, "uid", 0) + 1
        name = "%s_u%d" % (name, self.uid)
        return Buf(name, (es or self.es).enter_context(self.nc.sbuf_tensor(name, list(shape), dt)))

    def ps(self, name, shape, dt=F32):
        return Buf(name, self.es.enter_context(self.nc.psum_tensor(name, list(shape), dt)))

    def dram(self, name, shape, dt, kind="Internal"):
        return Buf(name, self.nc.dram_tensor(name, list(shape), dt, kind=kind).ap())

    def _need(self, eng, dep, waits):
        if dep is None:
            return
        key, val, semh = dep
        w = self.waited[eng]
        if w.get(key, 0) >= val:
            return
        w[key] = val
        waits.append((semh, val))

    def _deps(self, eng, reads, writes, same, par=False):
        waits = []
        for b in reads:
            for lw in b.lws:
                if same or lw[0] != eng:
                    self._need(eng, lw, waits)
        for b in writes:
            if not par:
                for lw in b.lws:
                    if same or lw[0] != eng:
                        self._need(eng, lw, waits)
            for k, d in b.rd.items():
                if same or k != eng:
                    self._need(eng, d, waits)
        return waits

    def op(self, eng, fn, reads=(), writes=()):
        same = eng != "pe"
        waits = self._deps(eng, reads, writes, same)
        self.cnt[eng] += 1
        tok = (eng, self.cnt[eng], self.sem[eng])
        self.q[eng].append((waits, fn, (self.sem[eng], 1)))
        for b in reads:
            b.rd[eng] = tok
        for b in writes:
            b.lws = [tok]
            b.rd = {}
        return tok

    def dma(self, qeng, out_ap, in_ap, reads=(), writes=(), par=False, **kw):
        waits = self._deps(qeng, reads, writes, True, par)
        s = self.dnext
        self.dnext = (self.dnext + 1) % len(self.dsem)
        if self.dcnt[s] > 0:
            self._need(qeng, ("d%d" % s, self.dcnt[s], self.dsem[s]), waits)
        self.dcnt[s] += 16
        tok = ("d%d" % s, self.dcnt[s], self.dsem[s])
        self.q[qeng].append((waits, lambda e: e.dma_start(out=out_ap, in_=in_ap, **kw), (self.dsem[s], 16)))
        for b in reads:
            b.rd["dma%d" % self.ndma] = tok
        for b in writes:
            if par:
                if b.rd:
                    b.lws = []
                    b.rd = {}
                b.lws.append(tok)
            else:
                b.lws = [tok]
                b.rd = {}
        self.ndma += 1
        return tok

    def barrier(self):
        for e in self.ENG:
            waits = []
            for e2 in ("pe", "act", "dve", "pool"):
                if e2 != e and self.cnt[e2] > 0:
                    self._need(e, (e2, self.cnt[e2], self.sem[e2]), waits)
            for s in range(len(self.dsem)):
                if self.dcnt[s] > 0:
                    self._need(e, ("d%d" % s, self.dcnt[s], self.dsem[s]), waits)
            if waits:
                self.q[e].append((waits, None, None))

    def finish(self):
        self.barrier()
        nc = self.nc
        q = self.q

        def replay(eh, items):
            for waits, fn, inc in items:
                for semh, val in waits:
                    eh.wait_ge(semh, val)
                if fn is not None:
                    ins = fn(eh)
                    if inc is not None:
                        ins.then_inc(inc[0], inc[1])

        with nc.Block() as block:
            @block.sync
            def _(e):
                replay(e, q["sp"])

            @block.tensor
            def _(e):
                replay(e, q["pe"])

            @block.scalar
            def _(e):
                replay(e, q["act"])

            @block.vector
            def _(e):
                replay(e, q["dve"])

            @block.gpsimd
            def _(e):
                replay(e, q["pool"])


class DT:
    def __init__(self, S, name, shape, dt, nblk, kind="Internal"):
        self.ap = S.nc.dram_tensor(name, list(shape), dt, kind=kind).ap()
        self.blk = [Buf("%s_b%d" % (name, i)) for i in range(nblk)]
        self.all = self.blk


class K:
    def __init__(self, T, layers):
        self.T = T
        self.NB = T // TB
        self.layers = layers
        self.nc = bass.Bass("TRN2", target_bir_lowering=False)
        self.es = contextlib.ExitStack()

    def mm(self, P, out_ap, lhsT, rhs, start, stop, reads):
        self.S.op("pe", lambda e: e.matmul(out_ap, lhsT=lhsT, rhs=rhs, start=start, stop=stop), reads=reads, writes=[P])

    def act(self, out_ap, in_ap, func, reads, writes, **kw):
        self.S.op("act", lambda e: e.activation(out=out_ap, in_=in_ap, func=func, **kw), reads=reads, writes=writes)

    def tt(self, eng, out_ap, a, b, op, reads, writes):
        self.S.op(eng, lambda e: e.tensor_tensor(out=out_ap, in0=a, in1=b, op=op), reads=reads, writes=writes)

    def ts(self, eng, out_ap, a, s1, op0, reads, writes, s2=None, op1=None):
        if op1 is None:
            self.S.op(eng, lambda e: e.tensor_scalar(out=out_ap, in0=a, scalar1=s1, scalar2=None, op0=op0), reads=reads, writes=writes)
        else:
            self.S.op(eng, lambda e: e.tensor_scalar(out=out_ap, in0=a, scalar1=s1, scalar2=s2, op0=op0, op1=op1), reads=reads, writes=writes)

    def stt(self, out_ap, a, s, b, op0, op1, reads, writes):
        self.S.op("dve", lambda e: e.scalar_tensor_tensor(out=out_ap, in0=a, scalar=s, in1=b, op0=op0, op1=op1), reads=reads, writes=writes)

    def copy(self, eng, out_ap, in_ap, reads, writes):
        if eng == "act":
            self.S.op("act", lambda e: e.copy(out=out_ap, in_=in_ap), reads=reads, writes=writes)
        else:
            self.S.op(eng, lambda e: e.tensor_copy(out=out_ap, in_=in_ap), reads=reads, writes=writes)

    def xview(self, xdt, b):
        return xdt.ap.rearrange("(kc p) t -> p kc t", p=128)[:, :, b * TB:(b + 1) * TB]

    def load_wcast(self, wbuf, w_ap, kc_n, ncols, grp=1):
        S = self.S
        wv = w_ap.rearrange("(kc p) f -> p kc f", p=128)
        for k0 in range(0, kc_n, grp):
            S.dma("pool", wbuf[:, k0:k0 + grp, :], wv[:, k0:k0 + grp, :], reads=[], writes=[wbuf], max_dma_last_dim=4096)

    def norm(self, xs, g_ap, hb, es_bufs):
        S = self.S
        sq, rst = es_bufs
        sqv = sq[:, 0:KC, :]
        P = self.PS.next()
        self.act(sqv, xs[:], AF.Square, [xs], [sq])
        for kc in range(KC):
            self.mm(P, P[:], self.ones[:], sqv[:, kc, :], kc == 0, kc == KC - 1, [self.ones, sq])
        self.act(rst[:], P[:], AF.Ln, [P], [rst], scale=1.0 / D, bias=self.epsb[:])
        self.act(rst[:], rst[:], AF.Exp, [rst], [rst], scale=-0.5)
        for kc in range(KC):
            self.stt(hb[:, kc, :], xs[:, kc, :], g_ap[:, kc:kc + 1], rst[:], ALU.mult, ALU.mult, [xs, rst, self.gains], [hb])

    def phase_mlp(self, l, xin, xout):
        S = self.S
        with contextlib.ExitStack() as es:
            w1b = S.sb("w1b", [128, KC, 4096], BF16, es)
            w2b = S.sb("w2b", [128, 32, D], BF16, es)
            xs = S.sb("m_xs", [128, KC, TB], F32, es)
            hb = S.sb("m_hb", [128, KC, TB], BF16, es)
            ab = S.sb("m_ab", [128, 32, TB], BF16, es)
            sq = ab
            rst = S.sb("m_rst", [128, TB], F32, es)
            tmps = Rot([S.sb("m_tmp%d" % i, [128, TB], F32, es) for i in range(2)])
            self.load_wcast(w1b, self.w["mlp_w1"][l], KC, 4096)
            self.load_wcast(w2b, self.w["mlp_w2"][l], 32, D, grp=4)
            for b in range(self.NB):
                S.dma("sp", xs[:], self.xview(xin, b), reads=[xin.blk[b]], writes=[xs])
                self.norm(xs, self.g_mlp[l], hb, (sq, rst))
                for fc in range(32):
                    P = self.PS.next()
                    for kc in range(KC):
                        self.mm(P, P[:], w1b[:, kc, fc * 128:(fc + 1) * 128], hb[:, kc, :], kc == 0, kc == KC - 1, [w1b, hb])
                    tmp = tmps.next()
                    self.act(tmp[:], P[:], AF.Relu, [P], [tmp])
                    self.tt("dve" if fc % 2 == 0 else "pool", ab[:, fc, :], tmp[:], tmp[:], ALU.mult, [tmp], [ab])
                for oc in range(KC):
                    P = self.PS.next()
                    for fc in range(32):
                        self.mm(P, P[:], w2b[:, fc, oc * 128:(oc + 1) * 128], ab[:, fc, :], fc == 0, fc == 31, [w2b, ab])
                    self.tt("dve", xs[:, oc, :], P[:], xs[:, oc, :], ALU.add, [P, xs], [xs])
                S.dma("pool", self.xview(xout, b), xs[:], reads=[xs], writes=[xout.blk[b]])
            S.barrier()

    def phase_outproj(self, wname, widx, nkc, yT, xin, xout):
        S = self.S
        with contextlib.ExitStack() as es:
            wob = S.sb("wob", [128, nkc, D], BF16, es)
            xs = S.sb("o_xs", [128, KC, TB], F32, es)
            ybs = Rot([S.sb("o_yb%d" % i, [128, nkc, TB], BF16, es) for i in range(2)])
            self.load_wcast(wob, self.w[wname][widx], nkc, D, grp=4)
            yv = yT.ap.rearrange("(kc p) t -> p kc t", p=128)
            for b in range(self.NB):
                yb = ybs.next()
                S.dma("sp", yb[:], yv[:, 0:nkc, b * TB:(b + 1) * TB], reads=[yT.blk[b]], writes=[yb])
                S.dma("sp", xs[:], self.xview(xin, b), reads=[xin.blk[b]], writes=[xs])
                for oc in range(KC):
                    P = self.PS.next()
                    for kc in range(nkc):
                        self.mm(P, P[:], wob[:, kc, oc * 128:(oc + 1) * 128], yb[:, kc, :], kc == 0, kc == nkc - 1, [wob, yb])
                    self.tt("dve", xs[:, oc, :], P[:], xs[:, oc, :], ALU.add, [P, xs], [xs])
                S.dma("pool", self.xview(xout, b), xs[:], reads=[xs], writes=[xout.blk[b]])
            S.barrier()

    def phase_odd_a(self, o, l, xin, yT, qT, kT, vtm):
        S = self.S
        T = self.T
        with contextlib.ExitStack() as es:
            wib = S.sb("oa_wi", [128, KC, 3584], BF16, es)
            wab = S.sb("oa_wa", [128, 8, 128], BF16, es)
            wxb = S.sb("oa_wx", [128, 8, 128], BF16, es)
            xs = S.sb("oa_xs", [128, KC, TB], F32, es)
            hb = S.sb("oa_hb", [128, KC, TB], BF16, es)
            sq = S.sb("oa_sq", [128, KC, TB], BF16, es)
            rst = S.sb("oa_rst", [128, TB], F32, es)
            xr = [[S.sb("oa_xr%d_%d" % (c, i), [128, TB + 3], F32, es) for i in range(2)] for c in range(8)]
            hst = S.sb("oa_hst", [128, 8], F32, es)
            f32t = Rot([S.sb("oa_f%d" % i, [128, TB], F32, es) for i in range(12)])
            b16t = Rot([S.sb("oa_b%d" % i, [128, TB], BF16, es) for i in range(6)])
            vb = Rot([S.sb("oa_vb%d" % i, [128, 512], BF16, es) for i in range(2)])
            self.load_wcast(wib, self.w["od_w_in"][o], KC, 3584)
            S.dma("pool", wab[:], self.w["lru_wa"][o].rearrange("k i j -> i k j"), writes=[wab])
            S.dma("pool", wxb[:], self.w["lru_wx"][o].rearrange("k i j -> i k j"), writes=[wxb])
            S.op("dve", lambda e: e.memset(hst[:], 0.0), writes=[hst])
            for c in range(8):
                S.op("pool", lambda e, c=c: e.memset(xr[c][1][:, TB:TB + 3], 0.0), writes=[xr[c][1]])
            cw = self.lru_cw[o]
            cb = self.lru_cb[o]
            ba = self.lru_ba[o]
            bx = self.lru_bx[o]
            cl = self.lru_cl[o]
            cst = self.cst
            for b in range(self.NB):
                S.dma("sp", xs[:], self.xview(xin, b), reads=[xin.blk[b]], writes=[xs])
                self.norm(xs, self.g_mix[l], hb, (sq, rst))
                for c in range(8):
                    cur = xr[c][b % 2]
                    prv = xr[c][(b + 1) % 2]
                    Pg = self.PS.next()
                    for kc in range(KC):
                        self.mm(Pg, Pg[:], wib[:, kc, c * 128:(c + 1) * 128], hb[:, kc, :], kc == 0, kc == KC - 1, [wib, hb])
                    gl = f32t.next()
                    self.act(gl[:], Pg[:], AF.Gelu_apprx_tanh, [Pg], [gl])
                    Px = self.PS.next()
                    for kc in range(KC):
                        self.mm(Px, Px[:], wib[:, kc, 1024 + c * 128:1024 + (c + 1) * 128], hb[:, kc, :], kc == 0, kc == KC - 1, [wib, hb])
                    self.copy("act", cur[:, 3:TB + 3], Px[:], [Px], [cur])
                    self.copy("pool", cur[:, 0:3], prv[:, TB:TB + 3], [prv], [cur])
                    xcv = f32t.next()
                    self.ts("dve", xcv[:], cur[:, 3:TB + 3], cw[:, c, 3:4], ALU.mult, [cur, cst], [xcv], s2=cb[:, c:c + 1], op1=ALU.add)
                    for k in range(3):
                        self.stt(xcv[:], cur[:, k:k + TB], cw[:, c, k:k + 1], xcv[:], ALU.mult, ALU.add, [cur, xcv, cst], [xcv])
                    xcb = b16t.next()
                    self.copy("pool", xcb[:], xcv[:], [xcv], [xcb])
                    Pr = self.PS.next()
                    self.mm(Pr, Pr[:], wab[:, c, :], xcb[:], True, True, [wab, xcb])
                    Pi = self.PS.next()
                    self.mm(Pi, Pi[:], wxb[:, c, :], xcb[:], True, True, [wxb, xcb])
                    rr = f32t.next()
                    self.act(rr[:], Pr[:], AF.Sigmoid, [Pr, cst], [rr], bias=ba[:, c:c + 1])
                    ii = f32t.next()
                    self.act(ii[:], Pi[:], AF.Sigmoid, [Pi, cst], [ii], bias=bx[:, c:c + 1])
                    aa = f32t.next()
                    self.act(aa[:], rr[:], AF.Exp, [rr, cst], [aa], scale=cl[:, c:c + 1])
                    a2 = f32t.next()
                    self.tt("pool", a2[:], aa[:], aa[:], ALU.mult, [aa], [a2])
                    self.act(a2[:], a2[:], AF.Sqrt, [a2], [a2], scale=-1.0, bias=self.oneb[:])
                    self.tt("pool", ii[:], ii[:], xcv[:], ALU.mult, [ii, xcv], [ii])
                    self.tt("dve", ii[:], ii[:], a2[:], ALU.mult, [ii, a2], [ii])
                    hh = f32t.next()
                    S.op("dve", lambda e, hh=hh, aa=aa, ii=ii, c=c: e.tensor_tensor_scan(out=hh[:], data0=aa[:], data1=ii[:], initial=hst[:, c:c + 1], op0=ALU.mult, op1=ALU.add),
                         reads=[aa, ii, hst], writes=[hh])
                    self.copy("pool", hst[:, c:c + 1], hh[:, TB - 1:TB], [hh], [hst])
                    yc = b16t.next()
                    self.tt("dve", yc[:], hh[:], gl[:], ALU.mult, [hh, gl], [yc])
                    S.dma("pool", yT.ap[c * 128:(c + 1) * 128, b * TB:(b + 1) * TB], yc[:], reads=[yc], writes=[yT.blk[b]], par=True)
                for qi in range(8):
                    isq = qi < 4
                    col0 = 2048 + qi * 128
                    Pq = self.PS.next()
                    for kc in range(KC):
                        self.mm(Pq, Pq[:], wib[:, kc, col0:col0 + 128], hb[:, kc, :], kc == 0, kc == KC - 1, [wib, hb])
                    s2 = b16t.next()
                    self.act(s2[:], Pq[:], AF.Square, [Pq], [s2])
                    Ps = self.PS.next()
                    self.mm(Ps, Ps[:], self.bones[:], s2[:], True, True, [self.bones, s2])
                    r2 = f32t.next()
                    self.act(r2[:], Ps[:], AF.Ln, [Ps], [r2], scale=1.0 / 64, bias=self.epsb[:])
                    self.act(r2[:], r2[:], AF.Exp, [r2], [r2], scale=-0.5)
                    qo = b16t.next()
                    gn = (self.sb_qn if isq else self.sb_kn)[o]
                    self.stt(qo[:], Pq[:], gn[:, 0:1], r2[:], ALU.mult, ALU.mult, [Pq, r2, cst], [qo])
                    dst = qT if isq else kT
                    r0 = (qi % 4) * 128
                    S.dma("pool", dst.ap[r0:r0 + 128, b * TB:(b + 1) * TB], qo[:], reads=[qo], writes=[dst.blk[b]], par=True)
                for tt_ in range(4):
                    Pv = self.PS.next()
                    for kc in range(KC):
                        self.mm(Pv, Pv[:], hb[:, kc, tt_ * 128:(tt_ + 1) * 128], wib[:, kc, 3072:3584], kc == 0, kc == KC - 1, [wib, hb])
                    vv = vb.next()
                    self.copy("act", vv[:], Pv[:], [Pv], [vv])
                    t0 = b * TB + tt_ * 128
                    S.dma("pool", vtm.ap[t0:t0 + 128, 0:512], vv[:], reads=[vv], writes=[vtm.blk[b]], par=True)
            S.barrier()

    def phase_sb(self, qT, kT, vtm, yT):
        S = self.S
        T = self.T
        NKB = T // 128
        scale = 64 ** -0.5
        with contextlib.ExitStack() as es:
            qh = [S.sb("sb_q%d" % i, [64, T], BF16, es) for i in range(2)]
            kh = [S.sb("sb_k%d" % i, [64, T], BF16, es) for i in range(2)]
            vh = [S.sb("sb_v%d" % i, [128, NKB, 64], BF16, es) for i in range(2)]
            carry = S.sb("sb_carry", [128, TB], F32, es)
            f32t = Rot([S.sb("sb_f%d" % i, [128, TB], F32, es) for i in range(8)])
            b16t = Rot([S.sb("sb_b%d" % i, [128, TB], BF16, es) for i in range(6)])
            ot = Rot([S.sb("sb_o%d" % i, [64, TB], BF16, es) for i in range(2)])
            Pacc = S_ps = self.PSacc
            for h in range(8):
                q, k, v = qh[h % 2], kh[h % 2], vh[h % 2]
                S.dma("sp", q[:], qT.ap[h * 64:(h + 1) * 64, :], reads=qT.all, writes=[q])
                S.dma("sp", k[:], kT.ap[h * 64:(h + 1) * 64, :], reads=kT.all, writes=[k])
                S.dma("sp", v[:], vtm.ap[:, h * 64:(h + 1) * 64].rearrange("(n p) d -> p n d", p=128), reads=vtm.all, writes=[v])
                for Q in range(self.NB):
                    first = True
                    for kb in range(4 * Q + 3, -1, -1):
                        loc = kb - 4 * Q
                        c0 = max(loc, 0) * 128
                        n = TB - c0
                        diag = loc >= 0
                        Pz = self.PS.next()
                        self.mm(Pz, Pz[:, c0:TB], k[:, kb * 128:(kb + 1) * 128], q[:, Q * TB + c0:(Q + 1) * TB], True, True, [k, q])
                        ee = f32t.next()
                        self.act(ee[:, c0:TB], Pz[:, c0:TB], AF.Exp, [Pz], [ee], scale=-scale)
                        sp_ = f32t.next()
                        self.act(sp_[:, c0:TB], ee[:, c0:TB], AF.Ln, [ee], [sp_], bias=self.oneb[:])
                        l1 = b16t.next()
                        self.stt(l1[:, c0:TB], Pz[:, c0:TB], -scale, sp_[:, c0:TB], ALU.mult, ALU.subtract, [Pz, sp_], [l1])
                        if diag:
                            self.tt("pool", l1[:, c0:c0 + 128], l1[:, c0:c0 + 128], self.mask_lt[:], ALU.mult, [l1, self.cst], [l1])
                        Psf = self.PS.next()
                        self.mm(Psf, Psf[:, c0:TB], self.umat[:], l1[:, c0:TB], True, True, [self.umat, l1])
                        arg = f32t.next()
                        if first:
                            self.tt("dve", arg[:, c0:TB], Psf[:, c0:TB], sp_[:, c0:TB], ALU.subtract, [Psf, sp_], [arg])
                        else:
                            a_ = f32t.next()
                            self.tt("pool", a_[:, c0:TB], carry[:, c0:TB], sp_[:, c0:TB], ALU.subtract, [carry, sp_], [a_])
                            self.tt("dve", arg[:, c0:TB], Psf[:, c0:TB], a_[:, c0:TB], ALU.add, [Psf, a_], [arg])
                        ww = b16t.next()
                        self.act(ww[:, c0:TB], arg[:, c0:TB], AF.Exp, [arg], [ww])
                        if diag:
                            self.tt("pool", ww[:, c0:c0 + 128], ww[:, c0:c0 + 128], self.mask_lt[:], ALU.mult, [ww, self.cst], [ww])
                        self.mm(Pacc, Pacc[0:64, c0:TB], v[:, kb, :], ww[:, c0:TB], first, kb == 0, [v, ww])
                        if kb > 0:
                            Pc = self.PS.next()
                            self.mm(Pc, Pc[:, c0:TB], self.ones[:], l1[:, c0:TB], True, True, [self.ones, l1])
                            if first:
                                if c0 > 0:
                                    S.op("pool", lambda e, c0=c0: e.memset(carry[:, 0:c0], 0.0), writes=[carry])
                                self.copy("dve", carry[:, c0:TB], Pc[:, c0:TB], [Pc], [carry])
                            else:
                                self.tt("dve", carry[:, c0:TB], Pc[:, c0:TB], carry[:, c0:TB], ALU.add, [Pc, carry], [carry])
                        first = False
                    oo = ot.next()
                    self.copy("act", oo[:], Pacc[0:64, 0:TB], [Pacc], [oo])
                    S.dma("pool", yT.ap[1024 + h * 64:1024 + (h + 1) * 64, Q * TB:(Q + 1) * TB], oo[:], reads=[oo], writes=[yT.blk[Q]], par=True)
            S.barrier()


    def transpose_to(self, P, out_cols, src_buf, src_ap):
        pv = P[:].bitcast(BF16)
        self.S.op("pe", lambda e: e.transpose(out=pv[:, out_cols:out_cols + 128], in_=src_ap, identity=self.ident[:]),
                  reads=[src_buf, self.ident], writes=[P])

    def phase_even_a(self, ei, l, xin, sc):
        S = self.S
        T = self.T
        w = self.w
        with contextlib.ExitStack() as es:
            wib = S.sb("ea_wi", [128, KC, 5648], BF16, es)
            xs = S.sb("ea_xs", [128, KC, TB], F32, es)
            hb = S.sb("ea_hb", [128, KC, TB], BF16, es)
            sq = S.sb("ea_sq", [128, KC, TB], BF16, es)
            rst = S.sb("ea_rst", [128, TB], F32, es)
            cosb = S.sb("ea_cos", [128, TB], F32, es)
            sinb = S.sb("ea_sin", [128, TB], F32, es)
            halo = S.sb("ea_halo", [128, 12, 3], F32, es)
            work = Rot([S.sb("ea_wk%d" % i, [128, TB + 3], F32, es) for i in range(2)])
            f32t = Rot([S.sb("ea_f%d" % i, [128, TB], F32, es) for i in range(8)])
            b16t = Rot([S.sb("ea_b%d" % i, [128, TB], BF16, es) for i in range(6)])
            xct = [S.sb("ea_xc%d" % i, [128, TB], BF16, es) for i in range(12)]
            tkb = Rot([S.sb("ea_tk%d" % i, [128, 1024], BF16, es) for i in range(3)])
            dts = Rot([S.sb("ea_dt%d" % i, [128, 32], F32, es) for i in range(2)])
            self.load_wcast(wib, w["ev_w_in"][ei], KC, 5648)
            S.op("dve", lambda e: e.memset(halo[:], 0.0), writes=[halo])
            cst = self.cst
            cw = self.ssd_cw[ei]; cb = self.ssd_cb[ei]
            qn = self.ret_qn[ei]; kn = self.ret_kn[ei]
            for b in range(self.NB):
                bs = slice(b * TB, (b + 1) * TB)
                S.dma("sp", xs[:], self.xview(xin, b), reads=[xin.blk[b]], writes=[xs])
                S.dma("sp", cosb[:], w["c_cos"][:, bs], writes=[cosb])
                S.dma("sp", sinb[:], w["c_sin"][:, bs], writes=[sinb])
                self.norm(xs, self.g_mix[l], hb, (sq, rst))
                for qi in range(8):
                    isq = qi < 4
                    col0 = qi * 128
                    Pq = self.PS.next()
                    for kc in range(KC):
                        self.mm(Pq, Pq[:], wib[:, kc, col0:col0 + 128], hb[:, kc, :], kc == 0, kc == KC - 1, [wib, hb])
                    qraw = f32t.next()
                    self.copy("act", qraw[:], Pq[:], [Pq], [qraw])
                    s2 = b16t.next()
                    self.act(s2[:], Pq[:], AF.Square, [Pq], [s2])
                    Ps = self.PS.next()
                    self.mm(Ps, Ps[:], self.ones[:], s2[:], True, True, [self.ones, s2])
                    r2 = f32t.next()
                    self.act(r2[:], Ps[:], AF.Ln, [Ps], [r2], scale=1.0 / 128, bias=self.epsb[:])
                    self.act(r2[:], r2[:], AF.Exp, [r2], [r2], scale=-0.5)
                    Pw = self.PS.next()
                    self.mm(Pw, Pw[:], self.swapm[:], qraw[:], True, True, [self.swapm, qraw])
                    gn = qn if isq else kn
                    t1 = f32t.next()
                    self.stt(t1[:], qraw[:], gn[:, 0:1], cosb[:], ALU.mult, ALU.mult, [qraw, cosb, cst], [t1])
                    t2 = f32t.next()
                    self.stt(t2[:], Pw[:], gn[:, 1:2], sinb[:], ALU.mult, ALU.mult, [Pw, sinb, cst], [t2])
                    self.tt("pool", t1[:], t1[:], t2[:], ALU.add, [t1, t2], [t1])
                    qo = b16t.next()
                    self.stt(qo[:], t1[:], 1.0 if isq else 128 ** -0.5, r2[:], ALU.mult, ALU.mult, [t1, r2], [qo])
                    dst = sc["qr"] if isq else sc["kr"]
                    r0 = (qi % 4) * 128
                    S.dma("pool", dst.ap[r0:r0 + 128, bs], qo[:], reads=[qo], writes=[dst.blk[b]], par=True)
                for (c0, dstn, fn) in ((1024, "v", None), (2048, "g", AF.Silu), (3072, "z", AF.Silu)):
                    for tt_ in range(4):
                        tk = tkb.next()
                        for hf in range(2):
                            Pv = self.PS.next()
                            for kc in range(KC):
                                self.mm(Pv, Pv[:], hb[:, kc, tt_ * 128:(tt_ + 1) * 128], wib[:, kc, c0 + hf * 512:c0 + (hf + 1) * 512],
                                        kc == 0, kc == KC - 1, [wib, hb])
                            if fn is None:
                                self.copy("act", tk[:, hf * 512:(hf + 1) * 512], Pv[:], [Pv], [tk])
                            else:
                                self.act(tk[:, hf * 512:(hf + 1) * 512], Pv[:], fn, [Pv], [tk])
                        t0 = b * TB + tt_ * 128
                        S.dma("pool", sc[dstn].ap[t0:t0 + 128, :], tk[:], reads=[tk], writes=[sc[dstn].blk[b]], par=True)
                for tt_ in range(4):
                    Pd = self.PS.next()
                    for kc in range(KC):
                        self.mm(Pd, Pd[:, 0:16], hb[:, kc, tt_ * 128:(tt_ + 1) * 128], wib[:, kc, 5632:5648], kc == 0, kc == KC - 1, [wib, hb])
                    dd = dts.next()
                    self.tt("dve", dd[:, 0:16], Pd[:, 0:16], self.ssd_dtb[ei], ALU.add, [Pd, cst], [dd])
                    self.act(dd[:, 0:16], dd[:, 0:16], AF.Exp, [dd], [dd])
                    self.act(dd[:, 0:16], dd[:, 0:16], AF.Ln, [dd], [dd], bias=self.oneb[:])
                    self.tt("dve", dd[:, 16:32], dd[:, 0:16], self.ssd_A[ei], ALU.mult, [dd, cst], [dd])
                    t0 = b * TB + tt_ * 128
                    S.dma("pool", sc["dtA"].ap[t0:t0 + 128, :], dd[:], reads=[dd], writes=[sc["dtA"].blk[b]], par=True)
                for c in range(12):
                    Px = self.PS.next()
                    col0 = 4096 + c * 128
                    for kc in range(KC):
                        self.mm(Px, Px[:], wib[:, kc, col0:col0 + 128], hb[:, kc, :], kc == 0, kc == KC - 1, [wib, hb])
                    wk = work.next()
                    self.copy("act", wk[:, 3:TB + 3], Px[:], [Px], [wk])
                    self.copy("pool", wk[:, 0:3], halo[:, c, :], [halo], [wk])
                    self.copy("pool", halo[:, c, :], wk[:, TB:TB + 3], [wk], [halo])
                    xcv = f32t.next()
                    self.ts("dve", xcv[:], wk[:, 3:TB + 3], cw[:, c, 3:4], ALU.mult, [wk, cst], [xcv], s2=cb[:, c:c + 1], op1=ALU.add)
                    for k in range(3):
                        self.stt(xcv[:], wk[:, k:k + TB], cw[:, c, k:k + 1], xcv[:], ALU.mult, ALU.add, [wk, xcv, cst], [xcv])
                    self.act(xct[c][:], xcv[:], AF.Silu, [xcv], [xct[c]])
                    if c >= 8:
                        dstn = "BT" if c < 10 else "CT"
                        r0 = (c % 2) * 128
                        S.dma("pool", sc[dstn].ap[r0:r0 + 128, bs], xct[c][:], reads=[xct[c]], writes=[sc[dstn].blk[b]], par=True)
                for tt_ in range(4):
                    tk = tkb.next()
                    Pt = self.PS.next()
                    for c in range(8):
                        self.transpose_to(Pt, c * 128, xct[c], xct[c][:, tt_ * 128:(tt_ + 1) * 128])
                    self.copy("act", tk[:], Pt[:].bitcast(BF16), [Pt], [tk])
                    t0 = b * TB + tt_ * 128
                    S.dma("pool", sc["xs"].ap[t0:t0 + 128, :], tk[:], reads=[tk], writes=[sc["xs"].blk[b]], par=True)
                    tk2 = tkb.next()
                    Pt2 = self.PS.next()
                    for c in range(2):
                        self.transpose_to(Pt2, c * 128, xct[8 + c], xct[8 + c][:, tt_ * 128:(tt_ + 1) * 128])
                    self.copy("act", tk2[:, 0:256], Pt2[:].bitcast(BF16)[:, 0:256], [Pt2], [tk2])
                    S.dma("pool", sc["Btm"].ap[t0:t0 + 128, :], tk2[:, 0:256], reads=[tk2], writes=[sc["Btm"].blk[b]], par=True)
            S.barrier()

    def phase_even_b(self, ei, sc, yT):
        S = self.S
        T = self.T
        NCH = T // 128
        cst = self.cst
        gam = [1.0 - 2.0 ** (-5.0 - h) for h in range(4)]
        with contextlib.ExitStack() as es:
            def rot(name, shape, dt, n=2):
                return Rot([S.sb("%s%d" % (name, i), shape, dt, es) for i in range(n)])
            qrb = rot("eb_q", [128, 4, 128], BF16); krb = rot("eb_k", [128, 4, 128], BF16)
            vb = rot("eb_v", [128, 1024], BF16); gb = rot("eb_g", [128, 1024], BF16); zb = rot("eb_z", [128, 1024], BF16)
            xsb = rot("eb_xs", [128, 16, 64], BF16); btmb = rot("eb_bt", [128, 2, 128], BF16)
            BTb = rot("eb_BT", [128, 2, 128], BF16); CTb = rot("eb_CT", [128, 2, 128], BF16)
            dtb = rot("eb_dt", [128, 32], F32)
            Sst = S.sb("eb_S", [128, 4, 256], F32, es); Sbf = S.sb("eb_Sbf", [128, 4, 256], BF16, es)
            STs = S.sb("eb_ST", [128, 16, 64], F32, es); STbf = S.sb("eb_STbf", [128, 16, 64], BF16, es)
            smb = rot("eb_sm", [128, 4, 128], BF16); ktmb = rot("eb_ktm", [128, 4, 128], BF16); q2b = rot("eb_q2", [128, 4, 128], BF16)
            Dm = S.sb("eb_D", [128, 16, 128], F32, es)
            acol = rot("eb_acol", [128, 48], F32)
            segb = rot("eb_seg", [128, 4, 128], F32, 3); ltb = rot("eb_lt", [128, 4, 128], F32, 2); erb = rot("eb_er", [128, 4, 128], F32, 2)
            cbm = rot("eb_cbm", [128, 2, 128], F32)
            MTb = S.sb("eb_MT", [128, 16, 128], BF16, es); CsTb = S.sb("eb_CsT", [128, 16, 128], BF16, es)
            xdtb = rot("eb_xdt", [128, 16, 64], BF16); xdt2b = rot("eb_xdt2", [128, 16, 64], BF16)
            yf = rot("eb_yf", [128, 1024], F32, 2)
            yab = rot("eb_ya", [128, 1024], BF16, 2); ybb = rot("eb_yb", [128, 1024], BF16, 2)
            stat = rot("eb_stat", [128, 4, 6], F32); mv = rot("eb_mv", [128, 4, 2], F32); rs4 = rot("eb_rs4", [128, 8], F32)
            yTo = rot("eb_yTo", [128, 16, 128], BF16, 2)
            junk = S.sb("eb_junk", [128, 512], BF16, es)
            S.op("dve", lambda e: e.memset(Sst[:], 0.0), writes=[Sst])
            S.op("pool", lambda e: e.memset(Sbf[:], 0.0), writes=[Sbf])
            S.op("dve", lambda e: e.memset(STs[:], 0.0), writes=[STs])
            S.op("pool", lambda e: e.memset(STbf[:], 0.0), writes=[STbf])
            gnr_b = S.sb("eb_gnr", [128, 1024], F32, es); nrr_b = S.sb("eb_nrr", [128, 1024], F32, es)
            S.dma("sp", gnr_b[:], self.w["ret_gn"][ei].partition_broadcast(128), writes=[gnr_b])
            S.dma("sp", nrr_b[:], self.w["ssd_norm"][ei].partition_broadcast(128), writes=[nrr_b])
            gnrow = gnr_b[:]; nrow = nrr_b[:]; dsk = self.ssd_D[ei]
            for c in range(NCH):
                b = c // 4
                ts_ = slice(c * 128, (c + 1) * 128)
                qr = qrb.next(); kr = krb.next(); v = vb.next(); g = gb.next(); z = zb.next(); xs = xsb.next()
                btm = btmb.next(); BT = BTb.next(); CT = CTb.next(); dt = dtb.next()
                S.dma("sp", qr[:], sc["qr"].ap[:, ts_].rearrange("(h d) t -> d h t", d=128), reads=[sc["qr"].blk[b]], writes=[qr])
                S.dma("sp", kr[:], sc["kr"].ap[:, ts_].rearrange("(h d) t -> d h t", d=128), reads=[sc["kr"].blk[b]], writes=[kr])
                S.dma("sp", v[:], sc["v"].ap[ts_, :], reads=[sc["v"].blk[b]], writes=[v])
                S.dma("sp", g[:], sc["g"].ap[ts_, :], reads=[sc["g"].blk[b]], writes=[g])
                S.dma("sp", z[:], sc["z"].ap[ts_, :], reads=[sc["z"].blk[b]], writes=[z])
                S.dma("sp", xs[:], sc["xs"].ap[ts_, :].rearrange("t (h p) -> t h p", p=64), reads=[sc["xs"].blk[b]], writes=[xs])
                S.dma("sp", btm[:], sc["Btm"].ap[ts_, :].rearrange("t (g s) -> t g s", s=128), reads=[sc["Btm"].blk[b]], writes=[btm])
                S.dma("sp", BT[:], sc["BT"].ap[:, ts_].rearrange("(g s) t -> s g t", s=128), reads=[sc["BT"].blk[b]], writes=[BT])
                S.dma("sp", CT[:], sc["CT"].ap[:, ts_].rearrange("(g s) t -> s g t", s=128), reads=[sc["CT"].blk[b]], writes=[CT])
                S.dma("sp", dt[:], sc["dtA"].ap[ts_, :], reads=[sc["dtA"].blk[b]], writes=[dt])

                Psc = self.PS.next()
                for h in range(4):
                    self.mm(Psc, Psc[:, h * 128:(h + 1) * 128], kr[:, h, :], qr[:, h, :], True, True, [kr, qr])
                sm = smb.next()
                self.tt("dve", sm[:].rearrange("p h i -> p (h i)"), Psc[:], self.ret_decay[:], ALU.mult, [Psc, cst], [sm])
                Pkt = self.PS.next()
                for h in range(4):
                    self.transpose_to(Pkt, h * 128, kr, kr[:, h, :])
                ktm = ktmb.next()
                self.tt("dve", ktm[:], Pkt[:].bitcast(BF16)[:, 0:512].rearrange("p (h d) -> p h d", d=128),
                        self.ret_kdec[:].unsqueeze(2).broadcast_to([128, 4, 128]), ALU.mult, [Pkt, cst], [ktm])
                q2 = q2b.next()
                self.tt("pool", q2[:].rearrange("p h i -> p (h i)"), qr[:].rearrange("p h i -> p (h i)"), self.ret_qdec[:], ALU.mult, [qr, cst], [q2])
                PO = self.PW.next()
                for h in range(4):
                    self.mm(PO, PO[:, h * 256:(h + 1) * 256], sm[:, h, :], v[:, h * 256:(h + 1) * 256], True, False, [sm, v])
                    self.mm(PO, PO[:, h * 256:(h + 1) * 256], q2[:, h, :], Sbf[:, h, :], False, True, [q2, Sbf])
                Pkv = self.PW.next()
                for h in range(4):
                    self.mm(Pkv, Pkv[:, h * 256:(h + 1) * 256], ktm[:, h, :], v[:, h * 256:(h + 1) * 256], True, True, [ktm, v])
                for h in range(4):
                    self.stt(Sst[:, h, :], Sst[:, h, :], gam[h] ** 128, Pkv[:, h * 256:(h + 1) * 256], ALU.mult, ALU.add, [Sst, Pkv], [Sst])
                self.copy("act", Sbf[:], Sst[:], [Sst], [Sbf])
                st = stat.next(); m2 = mv.next(); r4 = rs4.next()
                for h in range(4):
                    S.op("dve", lambda e, h=h, st=st, PO=PO: e.bn_stats(out=st[:, h, :], in_=PO[:, h * 256:(h + 1) * 256]), reads=[PO], writes=[st])
                for h in range(4):
                    S.op("dve", lambda e, h=h, st=st, m2=m2: e.bn_aggr(out=m2[:, h, :], in_=st[:, h, :]), reads=[st], writes=[m2])
                self.act(r4[:, 0:4], m2[:, :, 1], AF.Ln, [m2], [r4], bias=self.epsb[:])
                self.act(r4[:, 0:4], r4[:, 0:4], AF.Exp, [r4], [r4], scale=-0.5)
                y1 = yf.next()
                for h in range(4):
                    self.ts("dve", y1[:, h * 256:(h + 1) * 256], PO[:, h * 256:(h + 1) * 256], m2[:, h, 0:1], ALU.subtract, [PO, m2, r4], [y1],
                            s2=r4[:, h:h + 1], op1=ALU.mult)
                self.tt("pool", y1[:], y1[:], gnrow, ALU.mult, [y1, gnr_b], [y1])
                ya = yab.next()
                self.tt("pool", ya[:], y1[:], g[:], ALU.mult, [y1, g], [ya])

                PA = self.PS.next()
                self.mm(PA, PA[:, 0:16], self.tri_f[:], dt[:, 16:32], True, True, [self.tri_f, dt])
                self.mm(PA, PA[:, 16:32], self.ones_f[:], dt[:, 16:32], True, True, [self.ones_f, dt])
                ac = acol.next()
                self.copy("act", ac[:, 0:32], PA[:, 0:32], [PA], [ac])
                self.tt("dve", ac[:, 32:48], ac[:, 16:32], ac[:, 0:16], ALU.subtract, [ac], [ac])
                self.act(ac[:, 32:48], ac[:, 32:48], AF.Exp, [ac], [ac])
                self.act(ac[:, 16:32], ac[:, 16:32], AF.Exp, [ac], [ac])
                self.tt("dve", Dm[:], dt[:, 16:32].unsqueeze(2).broadcast_to([128, 16, 128]),
                        self.tri_f[:].unsqueeze(1).broadcast_to([128, 16, 128]), ALU.mult, [dt, self.tri_f], [Dm])
                PCB = self.PS.next()
                for gi in range(2):
                    self.mm(PCB, PCB[:, gi * 128:(gi + 1) * 128], BT[:, gi, :], CT[:, gi, :], True, True, [BT, CT])
                cb_ = cbm.next()
                self.tt("dve", cb_[:].rearrange("p g i -> p (g i)"), PCB[:, 0:256], self.tri2[:], ALU.mult, [PCB, cst], [cb_])
                for q4 in range(4):
                    gi = q4 // 2
                    hs = slice(q4 * 4, q4 * 4 + 4)
                    Prb = self.PS.next()
                    self.mm(Prb, Prb[:], self.ones_f[:], Dm[:, hs, :].rearrange("p h i -> p (h i)"), True, True, [self.ones_f, Dm])
                    sg = segb.next()
                    self.tt("dve", sg[:], Prb[:].rearrange("p (h i) -> p h i", i=128), ac[:, q4 * 4:q4 * 4 + 4].unsqueeze(2).broadcast_to([128, 4, 128]),
                            ALU.subtract, [Prb, ac], [sg])
                    self.ts("pool", sg[:], sg[:], 0.0, ALU.min, [sg], [sg])
                    lt = ltb.next()
                    self.act(lt[:], sg[:], AF.Exp, [sg], [lt])
                    er = erb.next()
                    self.act(er[:].rearrange("p h i -> p (h i)"), Prb[:], AF.Exp, [Prb], [er])
                    self.tt("dve", MTb[:, hs, :], lt[:], cb_[:, gi, :].unsqueeze(1).broadcast_to([128, 4, 128]), ALU.mult, [lt, cb_], [MTb])
                    self.tt("pool", CsTb[:, hs, :], er[:], CT[:, gi, :].unsqueeze(1).broadcast_to([128, 4, 128]), ALU.mult, [er, CT], [CsTb])
                xdt = xdtb.next(); xdt2 = xdt2b.next()
                self.tt("dve", xdt[:], xs[:], dt[:, 0:16].unsqueeze(2).broadcast_to([128, 16, 64]), ALU.mult, [xs, dt], [xdt])
                self.tt("pool", xdt2[:], xdt[:], ac[:, 32:48].unsqueeze(2).broadcast_to([128, 16, 64]), ALU.mult, [xdt, ac], [xdt2])
                PY = self.PW.next()
                for h in range(16):
                    self.mm(PY, PY[:, h * 64:(h + 1) * 64], MTb[:, h, :], xdt[:, h, :], True, False, [MTb, xdt])
                    self.mm(PY, PY[:, h * 64:(h + 1) * 64], CsTb[:, h, :], STbf[:, h, :], False, True, [CsTb, STbf])
                PSt = self.PW.next()
                for h in range(16):
                    self.mm(PSt, PSt[:, h * 64:(h + 1) * 64], btm[:, h // 8, :], xdt2[:, h, :], True, True, [btm, xdt2])
                self.tt("dve", STs[:], STs[:], ac[:, 16:32].unsqueeze(2).broadcast_to([128, 16, 64]), ALU.mult, [STs, ac], [STs])
                self.tt("dve", STs[:].rearrange("p h d -> p (h d)"), PSt[:], STs[:].rearrange("p h d -> p (h d)"), ALU.add, [PSt, STs], [STs])
                self.copy("act", STbf[:], STs[:], [STs], [STbf])
                y2 = yf.next()
                self.tt("pool", y2[:].rearrange("p (h d) -> p h d", d=64), xs[:], dsk.unsqueeze(2).broadcast_to([128, 16, 64]), ALU.mult, [xs, cst], [y2])
                self.tt("dve", y2[:], PY[:], y2[:], ALU.add, [PY, y2], [y2])
                self.tt("pool", y2[:], y2[:], z[:], ALU.mult, [y2, z], [y2])
                r8 = rs4.next()
                for gi in range(2):
                    S.op("act", lambda e, gi=gi, y2=y2, r8=r8: e.activation(out=junk[:], in_=y2[:, gi * 512:(gi + 1) * 512], func=AF.Square, accum_out=r8[:, gi:gi + 1]),
                         reads=[y2], writes=[junk, r8])
                self.act(r8[:, 0:2], r8[:, 0:2], AF.Ln, [r8], [r8], scale=1.0 / 512, bias=self.epsb[:])
                self.act(r8[:, 0:2], r8[:, 0:2], AF.Exp, [r8], [r8], scale=-0.5)
                yb = ybb.next()
                for gi in range(2):
                    self.stt(yb[:, gi * 512:(gi + 1) * 512], y2[:, gi * 512:(gi + 1) * 512], r8[:, gi:gi + 1], nrow[:, gi * 512:(gi + 1) * 512],
                             ALU.mult, ALU.mult, [y2, r8, nrr_b], [yb])
                yo = yTo.next()
                for half, src in ((0, ya), (1, yb)):
                    Pt = self.PS.next()
                    for cc in range(8):
                        self.transpose_to(Pt, cc * 128, src, src[:, cc * 128:(cc + 1) * 128])
                    self.copy("act", yo[:, half * 8:(half + 1) * 8, :].rearrange("p c i -> p (c i)"), Pt[:].bitcast(BF16), [Pt], [yo])
                S.dma("pool", yT.ap[:, ts_].rearrange("(c p) t -> p c t", p=128), yo[:], reads=[yo], writes=[yT.blk[b]], par=True)
            S.barrier()

    def build(self):
        nc, es = self.nc, self.es
        T, NB = self.T, self.NB
        S = self.S = Sched(nc, es)
        ne = sum(1 for x in self.layers if x == "e")
        no = sum(1 for x in self.layers if x == "o")
        L = len(self.layers)
        w = self.w = {}

        def inp(name, shape):
            w[name] = nc.dram_tensor(name, list(shape), F32, kind="ExternalInput").ap()

        self.xin = DT(S, "xT", [D, T], F32, NB, kind="ExternalInput")
        self.xout = DT(S, "outT", [D, T], F32, NB, kind="ExternalOutput")
        inp("norm_mix", [L, D]); inp("norm_mlp", [L, D])
        inp("mlp_w1", [L, D, 4096]); inp("mlp_w2", [L, 4096, D])
        if no:
            inp("od_w_in", [no, D, 3584]); inp("od_w_out", [no, 1536, D])
            inp("lru_conv_w", [no, 4, D]); inp("lru_conv_b", [no, D])
            inp("lru_wa", [no, 8, 128, 128]); inp("lru_ba", [no, 8, 128])
            inp("lru_wx", [no, 8, 128, 128]); inp("lru_bx", [no, 8, 128])
            inp("lru_lam", [no, D]); inp("sb_qn", [no, 64]); inp("sb_kn", [no, 64])
        if ne:
            inp("ev_w_in", [ne, D, 5648]); inp("ev_w_out", [ne, 2048, D])
            inp("ret_qn", [ne, 128]); inp("ret_kn", [ne, 128]); inp("ret_gn", [ne, 1024])
            inp("ssd_conv_w", [ne, 4, 1536]); inp("ssd_conv_b", [ne, 1536])
            inp("ssd_dt_bias", [ne, 16]); inp("ssd_a_log", [ne, 16]); inp("ssd_d", [ne, 16]); inp("ssd_norm", [ne, 1024])
            inp("c_cos", [128, T]); inp("c_sin", [128, T])
            inp("c_ident", [128, 128]); inp("c_swap", [128, 128]); inp("c_tri", [128, 128]); inp("c_tri2", [128, 256])
            inp("c_rdecay", [128, 512]); inp("c_rqdec", [128, 512]); inp("c_rkdec", [128, 4])
        inp("c_masklt", [128, 128]); inp("c_umat", [128, 128]); inp("c_bones", [128, 128])

        self.PS = Rot([S.ps("ps%d" % i, [128, 512]) for i in range(4)])
        pw = [S.ps("pw%d" % i, [128, 1024]) for i in range(2)]
        self.PW = Rot(pw)
        self.PSacc = pw[0]

        cst = self.cst = Buf("cst")

        def csb(name, shape, dt=F32):
            return es.enter_context(nc.sbuf_tensor(name, list(shape), dt))

        self.ones = S.sb("ones", [128, 128], BF16)
        S.op("dve", lambda e: e.memset(self.ones[:], 1.0), writes=[self.ones])
        self.epsb = csb("epsb", [128, 1]); self.oneb = csb("oneb", [128, 1])
        S.op("dve", lambda e: e.memset(self.epsb[:], EPS), writes=[cst])
        S.op("dve", lambda e: e.memset(self.oneb[:], 1.0), writes=[cst])
        self.mask_lt = csb("mask_lt", [128, 128], BF16)
        self.umat = S.sb("umat", [128, 128], BF16)
        self.bones = S.sb("bones", [128, 128], BF16)
        S.dma("pool", self.mask_lt[:], w["c_masklt"], writes=[cst])
        S.dma("pool", self.umat[:], w["c_umat"], writes=[self.umat])
        S.dma("pool", self.bones[:], w["c_bones"], writes=[self.bones])
        self.gains = Buf("gains")
        gmix = csb("gmix", [128, L, KC]); gmlp = csb("gmlp", [128, L, KC])
        S.dma("sp", gmix[:], w["norm_mix"].rearrange("l (kc p) -> p l kc", p=128), writes=[self.gains], allow_slow_non_contiguous=True)
        S.dma("sp", gmlp[:], w["norm_mlp"].rearrange("l (kc p) -> p l kc", p=128), writes=[self.gains], allow_slow_non_contiguous=True)
        self.g_mix = [gmix[:, l, :] for l in range(L)]
        self.g_mlp = [gmlp[:, l, :] for l in range(L)]
        if no:
            cw = csb("lru_cw", [128, no, 8, 4]); cb = csb("lru_cb", [128, no, 8])
            ba = csb("lru_ba_s", [128, no, 8]); bx = csb("lru_bx_s", [128, no, 8])
            cl = csb("lru_cl", [128, no, 8])
            qn = csb("sbqn", [128, no]); kn = csb("sbkn", [128, no])
            for o_ in range(no):
                for k_ in range(4):
                    S.dma("sp", cw[:, o_, :, k_], w["lru_conv_w"][o_, k_].rearrange("(c p) -> p c", p=128), writes=[cst], allow_slow_non_contiguous=True)
            S.dma("sp", cb[:], w["lru_conv_b"].rearrange("o (c p) -> p o c", p=128), writes=[cst], allow_slow_non_contiguous=True)
            S.dma("sp", ba[:], w["lru_ba"].rearrange("o c p -> p o c"), writes=[cst], allow_slow_non_contiguous=True)
            S.dma("sp", bx[:], w["lru_bx"].rearrange("o c p -> p o c"), writes=[cst], allow_slow_non_contiguous=True)
            S.dma("sp", cl[:], w["lru_lam"].rearrange("o (c p) -> p o c", p=128), writes=[cst], allow_slow_non_contiguous=True)
            for half in range(2):
                S.dma("sp", qn[half * 64:(half + 1) * 64, :], w["sb_qn"].rearrange("o d -> d o"), writes=[cst], allow_slow_non_contiguous=True)
                S.dma("sp", kn[half * 64:(half + 1) * 64, :], w["sb_kn"].rearrange("o d -> d o"), writes=[cst], allow_slow_non_contiguous=True)
            S.op("act", lambda e: e.activation(out=cl[:], in_=cl[:], func=AF.Exp, scale=-1.0), reads=[cst], writes=[cst])
            S.op("act", lambda e: e.activation(out=cl[:], in_=cl[:], func=AF.Ln, bias=self.oneb[:]), reads=[cst], writes=[cst])
            S.op("dve", lambda e: e.tensor_scalar(out=cl[:], in0=cl[:], scalar1=-8.0, scalar2=None, op0=ALU.mult), reads=[cst], writes=[cst])
            self.lru_cw = [cw[:, o] for o in range(no)]
            self.lru_cb = [cb[:, o] for o in range(no)]
            self.lru_ba = [ba[:, o] for o in range(no)]
            self.lru_bx = [bx[:, o] for o in range(no)]
            self.lru_cl = [cl[:, o] for o in range(no)]
            self.sb_qn = [qn[:, o:o + 1] for o in range(no)]
            self.sb_kn = [kn[:, o:o + 1] for o in range(no)]
        if ne:
            self.ident = S.sb("ident", [128, 128], BF16)
            S.dma("pool", self.ident[:], w["c_ident"], writes=[self.ident])
            self.swapm = S.sb("swapm", [128, 128], F32)
            S.dma("sp", self.swapm[:], w["c_swap"], writes=[self.swapm])
            self.tri_f = S.sb("tri_f", [128, 128], F32)
            S.dma("sp", self.tri_f[:], w["c_tri"], writes=[self.tri_f])
            self.ones_f = S.sb("ones_f", [128, 128], F32)
            S.op("dve", lambda e: e.memset(self.ones_f[:], 1.0), writes=[self.ones_f])
            self.tri2 = csb("tri2", [128, 256]); self.ret_decay = csb("rdecay", [128, 512]); self.ret_qdec = csb("rqdec", [128, 512])
            self.ret_kdec = csb("rkdec", [128, 4])
            S.dma("sp", self.tri2[:], w["c_tri2"], writes=[cst])
            S.dma("sp", self.ret_decay[:], w["c_rdecay"], writes=[cst])
            S.dma("sp", self.ret_qdec[:], w["c_rqdec"], writes=[cst])
            S.dma("sp", self.ret_kdec[:], w["c_rkdec"], writes=[cst])
            scw = csb("ssd_cw", [128, ne, 12, 4]); scb = csb("ssd_cb", [128, ne, 12])
            rqn = csb("ret_qn_s", [128, ne, 2]); rkn = csb("ret_kn_s", [128, ne, 2])
            dsk = csb("ssd_D_r", [128, ne, 16]); dtbr = csb("ssd_dtb_r", [128, ne, 16]); Ar = csb("ssd_A_r", [128, ne, 16])
            for e_ in range(ne):
                for k_ in range(4):
                    S.dma("sp", scw[:, e_, :, k_], w["ssd_conv_w"][e_, k_].rearrange("(c p) -> p c", p=128), writes=[cst], allow_slow_non_contiguous=True)
                S.dma("sp", scb[:, e_, :], w["ssd_conv_b"][e_].rearrange("(c p) -> p c", p=128), writes=[cst], allow_slow_non_contiguous=True)
                for nm, tl in (("ret_qn", rqn), ("ret_kn", rkn)):
                    col = w[nm][e_].rearrange("(d o) -> d o", o=1)
                    S.dma("sp", tl[:, e_, 0:1], col, writes=[cst], allow_slow_non_contiguous=True)
                    S.dma("sp", tl[0:64, e_, 1:2], col[64:128], writes=[cst], allow_slow_non_contiguous=True)
                    S.dma("sp", tl[64:128, e_, 1:2], col[0:64], writes=[cst], allow_slow_non_contiguous=True)
                S.dma("sp", dsk[:, e_, :], w["ssd_d"][e_].partition_broadcast(128), writes=[cst])
                S.dma("sp", dtbr[:, e_, :], w["ssd_dt_bias"][e_].partition_broadcast(128), writes=[cst])
                S.dma("sp", Ar[:, e_, :], w["ssd_a_log"][e_].partition_broadcast(128), writes=[cst])
            S.op("act", lambda e: e.activation(out=Ar[:], in_=Ar[:], func=AF.Exp), reads=[cst], writes=[cst])
            S.op("dve", lambda e: e.tensor_scalar(out=Ar[:], in0=Ar[:], scalar1=-1.0, scalar2=None, op0=ALU.mult), reads=[cst], writes=[cst])
            self.ssd_cw = [scw[:, e_] for e_ in range(ne)]; self.ssd_cb = [scb[:, e_] for e_ in range(ne)]
            self.ret_qn = [rqn[:, e_] for e_ in range(ne)]; self.ret_kn = [rkn[:, e_] for e_ in range(ne)]
            self.ssd_D = [dsk[:, e_] for e_ in range(ne)]; self.ssd_dtb = [dtbr[:, e_] for e_ in range(ne)]; self.ssd_A = [Ar[:, e_] for e_ in range(ne)]
        S.barrier()

        xa = DT(S, "xa", [D, T], F32, NB)
        xb_ = DT(S, "xb", [D, T], F32, NB)
        yT = DT(S, "yT", [2048, T], BF16, NB)
        qT = DT(S, "qT", [512, T], BF16, NB)
        kT = DT(S, "kT", [512, T], BF16, NB)
        vtm = DT(S, "vtm", [T, 1024], BF16, NB)
        sc = {"qr": qT, "kr": kT, "v": vtm}
        if ne:
            for nm in ("g", "z", "xs"):
                sc[nm] = DT(S, "sc_" + nm, [T, 1024], BF16, NB)
            sc["Btm"] = DT(S, "sc_Btm", [T, 256], BF16, NB)
            sc["BT"] = DT(S, "sc_BT", [256, T], BF16, NB)
            sc["CT"] = DT(S, "sc_CT", [256, T], BF16, NB)
            sc["dtA"] = DT(S, "sc_dtA", [T, 32], F32, NB)

        cur = self.xin
        ie = io = 0
        for l, kind in enumerate(self.layers):
            last = l == L - 1
            if kind == "o":
                self.phase_odd_a(io, l, cur, yT, qT, kT, vtm)
                self.phase_sb(qT, kT, vtm, yT)
                self.phase_outproj("od_w_out", io, 12, yT, cur, xa)
                io += 1
            else:
                self.phase_even_a(ie, l, cur, sc)
                self.phase_even_b(ie, sc, yT)
                self.phase_outproj("ev_w_out", ie, 16, yT, cur, xa)
                ie += 1
            self.phase_mlp(l, xa, self.xout if last else xb_)
            cur = xb_
        S.finish()
        es.close()
        return nc


def consts(T=None, even=False):
    i = np.arange(128)
    c = _consts_base(i)
    if even:
        f32 = np.float32
        inv = (f32(10000.0) ** (-(np.arange(64, dtype=f32)) / f32(64))).astype(f32)
        ang = (np.arange(T, dtype=f32)[None, :] * inv[:, None]).astype(f32).astype(np.float64)
        cos = np.cos(ang); sin = np.sin(ang)
        c["c_cos"] = np.concatenate([cos, cos], 0).astype(f32)
        c["c_sin"] = np.concatenate([-sin, sin], 0).astype(f32)
        c["c_ident"] = np.eye(128, dtype=f32)
        c["c_swap"] = (i[:, None] == ((i[None, :] + 64) % 128)).astype(f32)
        tri = (i[:, None] <= i[None, :]).astype(f32)
        c["c_tri"] = tri
        c["c_tri2"] = np.concatenate([tri, tri], 1)
        lg = np.log1p(-np.exp2(-5.0 - np.arange(4, dtype=np.float64)))
        rel = (i[None, :] - i[:, None]).astype(np.float64)
        dec = np.where(rel[:, None, :] >= 0, np.exp(lg[None, :, None] * np.maximum(rel, 0)[:, None, :]), 0.0)
        c["c_rdecay"] = dec.reshape(128, 512).astype(f32)
        qd = np.exp(lg[:, None] * (i[None, :] + 1.0))
        c["c_rqdec"] = np.broadcast_to(qd.reshape(1, 512), (128, 512)).astype(f32).copy()
        c["c_rkdec"] = np.exp(lg[None, :] * (127.0 - i[:, None])).astype(f32)
    return c


def _consts_base(i):
    return {
        "c_masklt": (i[:, None] < i[None, :]).astype(np.float32),
        "c_umat": (i[:, None] > i[None, :]).astype(np.float32),
        "c_bones": ((i[:, None] // 64) == (i[None, :] // 64)).astype(np.float32),
    }


def make_inputs(inputs, b, kinds):
    m = {"xT": np.ascontiguousarray(np.asarray(inputs["x"])[b].T)}
    L = len(kinds)
    for n in ("norm_mix", "norm_mlp", "mlp_w1", "mlp_w2"):
        m[n] = np.ascontiguousarray(np.asarray(inputs[n])[:L])
    if "e" in kinds:
        for n in ("ev_w_in", "ev_w_out", "ret_qn", "ret_kn", "ret_gn", "ssd_conv_w", "ssd_conv_b", "ssd_dt_bias", "ssd_a_log", "ssd_d", "ssd_norm"):
            m[n] = np.ascontiguousarray(np.asarray(inputs[n]))
    if "o" in kinds:
        for n in ("od_w_in", "od_w_out", "lru_conv_w", "lru_conv_b", "lru_wa", "lru_ba", "lru_wx", "lru_bx", "lru_lam", "sb_qn", "sb_kn"):
            m[n] = np.ascontiguousarray(np.asarray(inputs[n]))
    m.update(consts(m["xT"].shape[1], "e" in kinds))
    return m


KINDS = "eoeo"
SEQ = 8192
_NC_CACHE = {}


def kernel(**inputs):
    if "nc" not in _NC_CACHE:
        _NC_CACHE["nc"] = K(SEQ, list(KINDS)).build()
    nc = _NC_CACHE["nc"]
    nb = np.asarray(inputs["x"]).shape[0]
    in_maps = [make_inputs(inputs, b, KINDS) for b in range(nb)]
    res = run_bass_kernel_spmd(nc, in_maps, core_ids=list(range(nb)))
    out = np.stack([np.asarray(res.results[b]["outT"]).T for b in range(nb)])
    return np.ascontiguousarray(out.astype(np.float32))
```

```python
import contextlib
import math
import numpy as np
import concourse.bass as bass
import concourse.mybir as mybir
from concourse.bass_utils import run_bass_kernel_spmd

F32 = mybir.dt.float32
BF16 = mybir.dt.bfloat16
AF = mybir.ActivationFunctionType
ALU = mybir.AluOpType
AX = mybir.AxisListType

D = 1024
KC = 8
TB = 512
EPS = 1e-6


class Buf:
    __slots__ = ("name", "t", "lws", "rd")

    def __init__(self, name, t=None):
        self.name = name
        self.t = t
        self.lws = []
        self.rd = {}

    def __getitem__(self, k):
        return self.t[k]


class Rot:
    def __init__(self, bufs):
        self.bufs = list(bufs)
        self.i = 0

    def next(self):
        b = self.bufs[self.i % len(self.bufs)]
        self.i += 1
        return b


class Sched:
    ENG = ("pe", "act", "dve", "pool", "sp")

    def __init__(self, nc, es, n_dma_sems=32):
        self.nc = nc
        self.es = es
        self.q = {e: [] for e in self.ENG}
        self.cnt = {e: 0 for e in self.ENG}
        self.sem = {e: es.enter_context(nc.semaphore("c_" + e)) for e in ("pe", "act", "dve", "pool")}
        self.dsem = [es.enter_context(nc.semaphore("d%d" % i)) for i in range(n_dma_sems)]
        self.dcnt = [0] * n_dma_sems
        self.dnext = 0
        self.pnext = 0
        self.waited = {e: {} for e in self.ENG}
        self.ndma = 0

    def sb(self, name, shape, dt, es=None):
        self.uid = getattr(self, "uid", 0) + 1
        name = "%s_u%d" % (name, self.uid)
        return Buf(name, (es or self.es).enter_context(self.nc.sbuf_tensor(name, list(shape), dt)))

    def ps(self, name, shape, dt=F32):
        return Buf(name, self.es.enter_context(self.nc.psum_tensor(name, list(shape), dt)))

    def dram(self, name, shape, dt, kind="Internal"):
        return Buf(name, self.nc.dram_tensor(name, list(shape), dt, kind=kind).ap())

    def _need(self, eng, dep, waits):
        if dep is None:
            return
        key, val, semh = dep
        w = self.waited[eng]
        if w.get(key, 0) >= val:
            return
        w[key] = val
        waits.append((semh, val))

    def _deps(self, eng, reads, writes, same, par=False):
        waits = []
        for b in reads:
            for lw in b.lws:
                if same or lw[0] != eng:
                    self._need(eng, lw, waits)
        for b in writes:
            if not par:
                for lw in b.lws:
                    if same or lw[0] != eng:
                        self._need(eng, lw, waits)
            for k, d in b.rd.items():
                if same or k != eng:
                    self._need(eng, d, waits)
        return waits

    def op(self, eng, fn, reads=(), writes=()):
        same = eng != "pe"
        waits = self._deps(eng, reads, writes, same)
        self.cnt[eng] += 1
        tok = (eng, self.cnt[eng], self.sem[eng])
        self.q[eng].append((waits, fn, (self.sem[eng], 1)))
        for b in reads:
            b.rd[eng] = tok
        for b in writes:
            b.lws = [tok]
            b.rd = {}
        return tok

    def dma(self, qeng, out_ap, in_ap, reads=(), writes=(), par=False, **kw):
        waits = self._deps(qeng, reads, writes, True, par)
        if qeng == "pool":
            s = self.pnext
            self.pnext = (self.pnext + 1) % 4
        else:
            s = 4 + self.dnext
            self.dnext = (self.dnext + 1) % (len(self.dsem) - 4)
        if self.dcnt[s] > 0:
            self._need(qeng, ("d%d" % s, self.dcnt[s], self.dsem[s]), waits)
        self.dcnt[s] += 16
        tok = ("d%d" % s, self.dcnt[s], self.dsem[s])
        self.q[qeng].append((waits, lambda e: e.dma_start(out=out_ap, in_=in_ap, **kw), (self.dsem[s], 16)))
        for b in reads:
            b.rd["dma%d" % self.ndma] = tok
        for b in writes:
            if par:
                if b.rd:
                    b.lws = []
                    b.rd = {}
                b.lws.append(tok)
            else:
                b.lws = [tok]
                b.rd = {}
        self.ndma += 1
        return tok

    def barrier(self):
        for e in self.ENG:
            waits = []
            for e2 in ("pe", "act", "dve", "pool"):
                if e2 != e and self.cnt[e2] > 0:
                    self._need(e, (e2, self.cnt[e2], self.sem[e2]), waits)
            for s in range(len(self.dsem)):
                if self.dcnt[s] > 0:
                    self._need(e, ("d%d" % s, self.dcnt[s], self.dsem[s]), waits)
            if waits:
                self.q[e].append((waits, None, None))

    def finish(self):
        self.barrier()
        nc = self.nc
        q = self.q

        def replay(eh, items):
            for waits, fn, inc in items:
                for semh, val in waits:
                    eh.wait_ge(semh, val)
                if fn is not None:
                    ins = fn(eh)
                    if inc is not None:
                        ins.then_inc(inc[0], inc[1])

        with nc.Block() as block:
            @block.sync
            def _(e):
                replay(e, q["sp"])

            @block.tensor
            def _(e):
                replay(e, q["pe"])

            @block.scalar
            def _(e):
                replay(e, q["act"])

            @block.vector
            def _(e):
                replay(e, q["dve"])

            @block.gpsimd
            def _(e):
                replay(e, q["pool"])


class DT:
    def __init__(self, S, name, shape, dt, nblk, kind="Internal"):
        self.ap = S.nc.dram_tensor(name, list(shape), dt, kind=kind).ap()
        self.blk = [Buf("%s_b%d" % (name, i)) for i in range(nblk)]
        self.all = self.blk


class K:
    def __init__(self, T, layers):
        self.T = T
        self.NB = T // TB
        self.layers = layers
        self.nc = bass.Bass("TRN2", target_bir_lowering=False)
        self.es = contextlib.ExitStack()

    def mm(self, P, out_ap, lhsT, rhs, start, stop, reads):
        self.S.op("pe", lambda e: e.matmul(out_ap, lhsT=lhsT, rhs=rhs, start=start, stop=stop), reads=reads, writes=[P])

    def act(self, out_ap, in_ap, func, reads, writes, **kw):
        self.S.op("act", lambda e: e.activation(out=out_ap, in_=in_ap, func=func, **kw), reads=reads, writes=writes)

    def tt(self, eng, out_ap, a, b, op, reads, writes):
        self.S.op(eng, lambda e: e.tensor_tensor(out=out_ap, in0=a, in1=b, op=op), reads=reads, writes=writes)

    def ts(self, eng, out_ap, a, s1, op0, reads, writes, s2=None, op1=None):
        if op1 is None:
            self.S.op(eng, lambda e: e.tensor_scalar(out=out_ap, in0=a, scalar1=s1, scalar2=None, op0=op0), reads=reads, writes=writes)
        else:
            self.S.op(eng, lambda e: e.tensor_scalar(out=out_ap, in0=a, scalar1=s1, scalar2=s2, op0=op0, op1=op1), reads=reads, writes=writes)

    def stt(self, out_ap, a, s, b, op0, op1, reads, writes):
        self.S.op("dve", lambda e: e.scalar_tensor_tensor(out=out_ap, in0=a, scalar=s, in1=b, op0=op0, op1=op1), reads=reads, writes=writes)

    def copy(self, eng, out_ap, in_ap, reads, writes):
        if eng == "act":
            self.S.op("act", lambda e: e.copy(out=out_ap, in_=in_ap), reads=reads, writes=writes)
        else:
            self.S.op(eng, lambda e: e.tensor_copy(out=out_ap, in_=in_ap), reads=reads, writes=writes)

    def xview(self, xdt, b):
        return xdt.ap.rearrange("(kc p) t -> p kc t", p=128)[:, :, b * TB:(b + 1) * TB]

    def load_wcast(self, wbuf, w_ap, kc_n, ncols, grp=1):
        S = self.S
        wv = w_ap.rearrange("(kc p) f -> p kc f", p=128)
        for k0 in range(0, kc_n, grp):
            S.dma("pool", wbuf[:, k0:k0 + grp, :], wv[:, k0:k0 + grp, :], reads=[], writes=[wbuf], max_dma_last_dim=4096)

    def norm(self, xs, g_ap, hb, es_bufs):
        S = self.S
        sq, rst = es_bufs
        sqv = sq[:, 0:KC, :]
        P = self.PS.next()
        self.act(sqv, xs[:], AF.Square, [xs], [sq])
        for kc in range(KC):
            self.mm(P, P[:], self.ones[:], sqv[:, kc, :], kc == 0, kc == KC - 1, [self.ones, sq])
        self.act(rst[:], P[:], AF.Ln, [P], [rst], scale=1.0 / D, bias=self.epsb[:])
        self.act(rst[:], rst[:], AF.Exp, [rst], [rst], scale=-0.5)
        for kc in range(KC):
            self.stt(hb[:, kc, :], xs[:, kc, :], g_ap[:, kc:kc + 1], rst[:], ALU.mult, ALU.mult, [xs, rst, self.gains], [hb])

    def phase_mlp(self, l, xin, xout):
        S = self.S
        with contextlib.ExitStack() as es:
            w1b = S.sb("w1b", [128, KC, 4096], BF16, es)
            w2b = S.sb("w2b", [128, 32, D], BF16, es)
            xs = S.sb("m_xs", [128, KC, TB], F32, es)
            hb = S.sb("m_hb", [128, KC, TB], BF16, es)
            ab = S.sb("m_ab", [128, 32, TB], BF16, es)
            sq = ab
            rst = S.sb("m_rst", [128, TB], F32, es)
            tmps = Rot([S.sb("m_tmp%d" % i, [128, TB], F32, es) for i in range(2)])
            self.load_wcast(w1b, self.w["mlp_w1"][l], KC, 4096)
            self.load_wcast(w2b, self.w["mlp_w2"][l], 32, D, grp=4)
            for b in range(self.NB):
                S.dma("sp", xs[:], self.xview(xin, b), reads=[xin.blk[b]], writes=[xs])
                self.norm(xs, self.g_mlp[l], hb, (sq, rst))
                for fc in range(32):
                    P = self.PS.next()
                    for kc in range(KC):
                        self.mm(P, P[:], w1b[:, kc, fc * 128:(fc + 1) * 128], hb[:, kc, :], kc == 0, kc == KC - 1, [w1b, hb])
                    tmp = tmps.next()
                    self.act(tmp[:], P[:], AF.Relu, [P], [tmp])
                    self.tt("dve" if fc % 2 == 0 else "pool", ab[:, fc, :], tmp[:], tmp[:], ALU.mult, [tmp], [ab])
                for oc in range(KC):
                    P = self.PS.next()
                    for fc in range(32):
                        self.mm(P, P[:], w2b[:, fc, oc * 128:(oc + 1) * 128], ab[:, fc, :], fc == 0, fc == 31, [w2b, ab])
                    self.tt("dve", xs[:, oc, :], P[:], xs[:, oc, :], ALU.add, [P, xs], [xs])
                S.dma("pool", self.xview(xout, b), xs[:], reads=[xs], writes=[xout.blk[b]])
            S.barrier()

    def phase_outproj(self, wname, widx, nkc, yT, xin, xout):
        S = self.S
        with contextlib.ExitStack() as es:
            wob = S.sb("wob", [128, nkc, D], BF16, es)
            xs = S.sb("o_xs", [128, KC, TB], F32, es)
            ybs = Rot([S.sb("o_yb%d" % i, [128, nkc, TB], BF16, es) for i in range(2)])
            self.load_wcast(wob, self.w[wname][widx], nkc, D, grp=4)
            yv = yT.ap.rearrange("(kc p) t -> p kc t", p=128)
            for b in range(self.NB):
                yb = ybs.next()
                S.dma("sp", yb[:], yv[:, 0:nkc, b * TB:(b + 1) * TB], reads=[yT.blk[b]], writes=[yb])
                S.dma("sp", xs[:], self.xview(xin, b), reads=[xin.blk[b]], writes=[xs])
                for oc in range(KC):
                    P = self.PS.next()
                    for kc in range(nkc):
                        self.mm(P, P[:], wob[:, kc, oc * 128:(oc + 1) * 128], yb[:, kc, :], kc == 0, kc == nkc - 1, [wob, yb])
                    self.tt("dve", xs[:, oc, :], P[:], xs[:, oc, :], ALU.add, [P, xs], [xs])
                S.dma("pool", self.xview(xout, b), xs[:], reads=[xs], writes=[xout.blk[b]])
            S.barrier()

    def phase_odd_a(self, o, l, xin, yT, qT, kT, vtm):
        S = self.S
        T = self.T
        with contextlib.ExitStack() as es:
            wib = S.sb("oa_wi", [128, KC, 3584], BF16, es)
            wab = S.sb("oa_wa", [128, 8, 128], BF16, es)
            wxb = S.sb("oa_wx", [128, 8, 128], BF16, es)
            xs = S.sb("oa_xs", [128, KC, TB], F32, es)
            hb = S.sb("oa_hb", [128, KC, TB], BF16, es)
            sq = S.sb("oa_sq", [128, KC, TB], BF16, es)
            rst = S.sb("oa_rst", [128, TB], F32, es)
            xr = [[S.sb("oa_xr%d_%d" % (c, i), [128, TB + 3], F32, es) for i in range(2)] for c in range(8)]
            hst = S.sb("oa_hst", [128, 8], F32, es)
            f32t = Rot([S.sb("oa_f%d" % i, [128, TB], F32, es) for i in range(12)])
            b16t = Rot([S.sb("oa_b%d" % i, [128, TB], BF16, es) for i in range(6)])
            vb = Rot([S.sb("oa_vb%d" % i, [128, 512], BF16, es) for i in range(2)])
            self.load_wcast(wib, self.w["od_w_in"][o], KC, 3584)
            S.dma("pool", wab[:], self.w["lru_wa"][o].rearrange("k i j -> i k j"), writes=[wab])
            S.dma("pool", wxb[:], self.w["lru_wx"][o].rearrange("k i j -> i k j"), writes=[wxb])
            S.op("dve", lambda e: e.memset(hst[:], 0.0), writes=[hst])
            for c in range(8):
                S.op("pool", lambda e, c=c: e.memset(xr[c][1][:, TB:TB + 3], 0.0), writes=[xr[c][1]])
            cw = self.lru_cw[o]
            cb = self.lru_cb[o]
            ba = self.lru_ba[o]
            bx = self.lru_bx[o]
            cl = self.lru_cl[o]
            cst = self.cst
            for b in range(self.NB):
                S.dma("sp", xs[:], self.xview(xin, b), reads=[xin.blk[b]], writes=[xs])
                self.norm(xs, self.g_mix[l], hb, (sq, rst))
                for c in range(8):
                    cur = xr[c][b % 2]
                    prv = xr[c][(b + 1) % 2]
                    Pg = self.PS.next()
                    for kc in range(KC):
                        self.mm(Pg, Pg[:], wib[:, kc, c * 128:(c + 1) * 128], hb[:, kc, :], kc == 0, kc == KC - 1, [wib, hb])
                    gl = f32t.next()
                    self.act(gl[:], Pg[:], AF.Gelu_apprx_tanh, [Pg], [gl])
                    Px = self.PS.next()
                    for kc in range(KC):
                        self.mm(Px, Px[:], wib[:, kc, 1024 + c * 128:1024 + (c + 1) * 128], hb[:, kc, :], kc == 0, kc == KC - 1, [wib, hb])
                    self.copy("act", cur[:, 3:TB + 3], Px[:], [Px], [cur])
                    self.copy("pool", cur[:, 0:3], prv[:, TB:TB + 3], [prv], [cur])
                    xcv = f32t.next()
                    self.ts("dve", xcv[:], cur[:, 3:TB + 3], cw[:, c, 3:4], ALU.mult, [cur, cst], [xcv], s2=cb[:, c:c + 1], op1=ALU.add)
                    for k in range(3):
                        self.stt(xcv[:], cur[:, k:k + TB], cw[:, c, k:k + 1], xcv[:], ALU.mult, ALU.add, [cur, xcv, cst], [xcv])
                    xcb = b16t.next()
                    self.copy("pool", xcb[:], xcv[:], [xcv], [xcb])
                    Pr = self.PS.next()
                    self.mm(Pr, Pr[:], wab[:, c, :], xcb[:], True, True, [wab, xcb])
                    Pi = self.PS.next()
                    self.mm(Pi, Pi[:], wxb[:, c, :], xcb[:], True, True, [wxb, xcb])
                    rr = f32t.next()
                    self.act(rr[:], Pr[:], AF.Sigmoid, [Pr, cst], [rr], bias=ba[:, c:c + 1])
                    ii = f32t.next()
                    self.act(ii[:], Pi[:], AF.Sigmoid, [Pi, cst], [ii], bias=bx[:, c:c + 1])
                    aa = f32t.next()
                    self.act(aa[:], rr[:], AF.Exp, [rr, cst], [aa], scale=cl[:, c:c + 1])
                    a2 = f32t.next()
                    self.tt("pool", a2[:], aa[:], aa[:], ALU.mult, [aa], [a2])
                    self.act(a2[:], a2[:], AF.Sqrt, [a2], [a2], scale=-1.0, bias=self.oneb[:])
                    self.tt("pool", ii[:], ii[:], xcv[:], ALU.mult, [ii, xcv], [ii])
                    self.tt("dve", ii[:], ii[:], a2[:], ALU.mult, [ii, a2], [ii])
                    hh = f32t.next()
                    S.op("dve", lambda e, hh=hh, aa=aa, ii=ii, c=c: e.tensor_tensor_scan(out=hh[:], data0=aa[:], data1=ii[:], initial=hst[:, c:c + 1], op0=ALU.mult, op1=ALU.add),
                         reads=[aa, ii, hst], writes=[hh])
                    self.copy("pool", hst[:, c:c + 1], hh[:, TB - 1:TB], [hh], [hst])
                    yc = b16t.next()
                    self.tt("dve", yc[:], hh[:], gl[:], ALU.mult, [hh, gl], [yc])
                    S.dma("pool", yT.ap[c * 128:(c + 1) * 128, b * TB:(b + 1) * TB], yc[:], reads=[yc], writes=[yT.blk[b]], par=True)
                for qi in range(8):
                    isq = qi < 4
                    col0 = 2048 + qi * 128
                    Pq = self.PS.next()
                    for kc in range(KC):
                        self.mm(Pq, Pq[:], wib[:, kc, col0:col0 + 128], hb[:, kc, :], kc == 0, kc == KC - 1, [wib, hb])
                    s2 = b16t.next()
                    self.act(s2[:], Pq[:], AF.Square, [Pq], [s2])
                    Ps = self.PS.next()
                    self.mm(Ps, Ps[:], self.bones[:], s2[:], True, True, [self.bones, s2])
                    r2 = f32t.next()
                    self.act(r2[:], Ps[:], AF.Ln, [Ps], [r2], scale=1.0 / 64, bias=self.epsb[:])
                    self.act(r2[:], r2[:], AF.Exp, [r2], [r2], scale=-0.5)
                    qo = b16t.next()
                    gn = (self.sb_qn if isq else self.sb_kn)[o]
                    self.stt(qo[:], Pq[:], gn[:, 0:1], r2[:], ALU.mult, ALU.mult, [Pq, r2, cst], [qo])
                    dst = qT if isq else kT
                    r0 = (qi % 4) * 128
                    S.dma("pool", dst.ap[r0:r0 + 128, b * TB:(b + 1) * TB], qo[:], reads=[qo], writes=[dst.blk[b]], par=True)
                for tt_ in range(4):
                    Pv = self.PS.next()
                    for kc in range(KC):
                        self.mm(Pv, Pv[:], hb[:, kc, tt_ * 128:(tt_ + 1) * 128], wib[:, kc, 3072:3584], kc == 0, kc == KC - 1, [wib, hb])
                    vv = vb.next()
                    self.copy("act", vv[:], Pv[:], [Pv], [vv])
                    t0 = b * TB + tt_ * 128
                    S.dma("pool", vtm.ap[t0:t0 + 128, 0:512], vv[:], reads=[vv], writes=[vtm.blk[b]], par=True)
            S.barrier()

    def phase_sb(self, qT, kT, vtm, yT):
        S = self.S
        T = self.T
        NKB = T // 128
        scale = 64 ** -0.5
        with contextlib.ExitStack() as es:
            def rot(name, shape, dt, n):
                return Rot([S.sb("%s%d" % (name, i), shape, dt, es) for i in range(n)])
            qh = [S.sb("sb_q%d" % i, [64, T], BF16, es) for i in range(2)]
            kh = [S.sb("sb_k%d" % i, [64, T], BF16, es) for i in range(2)]
            vh = [S.sb("sb_v%d" % i, [128, NKB, 64], BF16, es) for i in range(2)]
            eeR = rot("sb_ee", [128, TB], F32, 2); spR = rot("sb_sp", [128, TB], F32, 7)
            l1R = rot("sb_l1", [128, TB], BF16, 7)
            argR = rot("sb_arg", [128, TB], F32, 3); wwR = rot("sb_ww", [128, TB], BF16, 4)
            ot = rot("sb_o", [64, TB], BF16, 2)
            PzR = Rot([self.psb[0], self.psb[1], self.psb[2], self.psb[3]]); PsfR = Rot([self.psh[0], self.psh[1]])
            PcR = Rot([self.psh[2]]); PaccR = Rot([self.psh[3]])

            def load_head(h):
                q, k, v = qh[h % 2], kh[h % 2], vh[h % 2]
                S.dma("sp", q[:], qT.ap[h * 64:(h + 1) * 64, :], reads=qT.all, writes=[q])
                S.dma("sp", k[:], kT.ap[h * 64:(h + 1) * 64, :], reads=kT.all, writes=[k])
                S.dma("sp", v[:], vtm.ap[:, h * 64:(h + 1) * 64].rearrange("(n p) d -> p n d", p=128), reads=vtm.all, writes=[v])

            units = []
            for h in range(8):
                for Q in range(self.NB):
                    top = 4 * Q + 3
                    pc0 = None
                    for kb in range(top, -1, -1):
                        loc = kb - 4 * Q
                        c0 = max(loc, 0) * 128
                        units.append(dict(h=h, Q=Q, kb=kb, first=(kb == top), last=(kb == 0), newhead=(Q == 0 and kb == top),
                                          c0=c0, pc0=pc0, diag=(loc >= 0)))
                        pc0 = c0
            st = {}

            def pe_z(u):
                h, Q, kb, c0 = u["h"], u["Q"], u["kb"], u["c0"]
                if u["newhead"] and h == 0:
                    load_head(0)
                q, k = qh[h % 2], kh[h % 2]
                Pz = PzR.next()
                self.mm(Pz, Pz[:, c0:TB], k[:, kb * 128:(kb + 1) * 128], q[:, Q * TB + c0:(Q + 1) * TB], True, True, [k, q])
                u["Pz"] = Pz

            def act_sp(u):
                c0, Pz = u["c0"], u["Pz"]
                ee = eeR.next()
                self.act(ee[:, c0:TB], Pz[:, c0:TB], AF.Exp, [Pz], [ee], scale=-scale)
                sp_ = spR.next()
                self.act(sp_[:, c0:TB], ee[:, c0:TB], AF.Ln, [ee], [sp_], bias=self.oneb[:])
                u["sp"] = sp_

            def dve_l1(u):
                c0 = u["c0"]
                Pz, sp_ = u["Pz"], u["sp"]
                l1 = l1R.next()
                self.stt(l1[:, c0:TB], Pz[:, c0:TB], -scale, sp_[:, c0:TB], ALU.mult, ALU.subtract, [Pz, sp_], [l1])
                if u["diag"]:
                    self.tt("pool", l1[:, c0:c0 + 128], l1[:, c0:c0 + 128], self.mask_lt[:], ALU.mult, [l1, self.cst], [l1])
                    if c0 > 0:
                        S.op("pool", lambda e, l1=l1, c0=c0: e.memset(l1[:, 0:c0], 0.0), writes=[l1])
                u["l1"] = l1

            def pe_suffix(u):
                c0 = u["c0"]
                Psf = PsfR.next()
                self.mm(Psf, Psf[:, c0:TB], self.umat[:], u["l1"][:, c0:TB], True, True, [self.umat, u["l1"]])
                u["Psf"] = Psf

            def dve_arg(u):
                c0, pc0 = u["c0"], u["pc0"]
                sp_, Psf = u["sp"], u["Psf"]
                if u["first"]:
                    st["Pc_r"] = PcR.next()
                Pc = u["Pc"] = st["Pc_r"]
                arg = argR.next()
                self.tt("dve", arg[:, c0:TB], Psf[:, c0:TB], sp_[:, c0:TB], ALU.subtract, [Psf, sp_], [arg])
                if not u["first"]:
                    self.tt("dve", arg[:, c0:TB], Pc[:, c0:TB], arg[:, c0:TB], ALU.add, [Pc, arg], [arg])
                u["arg"] = arg

            def pe_colsum(u):
                c0 = u["c0"]
                if not u["last"]:
                    self.mm(u["Pc"], u["Pc"][:, 0:TB], self.ones[:], u["l1"][:, 0:TB], u["first"], False, [self.ones, u["l1"]])

            def act_w(u):
                c0 = u["c0"]
                ww = wwR.next()
                self.act(ww[:, c0:TB], u["arg"][:, c0:TB], AF.Exp, [u["arg"]], [ww])
                if u["diag"]:
                    self.tt("pool", ww[:, c0:c0 + 128], ww[:, c0:c0 + 128], self.mask_lt[:], ALU.mult, [ww, self.cst], [ww])
                    if c0 > 0:
                        S.op("pool", lambda e, ww=ww, c0=c0: e.memset(ww[:, 0:c0], 0.0), writes=[ww])
                u["ww"] = ww

            def pe_wv(u):
                h, Q, kb, c0 = u["h"], u["Q"], u["kb"], u["c0"]
                v = vh[h % 2]
                if u["newhead"] and h + 1 < 8:
                    load_head(h + 1)
                if u["first"]:
                    st["Pacc"] = PaccR.next()
                Pacc = st["Pacc"]
                self.mm(Pacc, Pacc[0:64, 0:TB], v[:, kb, :], u["ww"][:, 0:TB], u["first"], u["last"], [v, u["ww"]])
                if u["last"]:
                    oo = ot.next()
                    self.copy("act", oo[:], Pacc[0:64, 0:TB], [Pacc], [oo])
                    S.dma("pool", yT.ap[1024 + h * 64:1024 + (h + 1) * 64, Q * TB:(Q + 1) * TB], oo[:], reads=[oo], writes=[yT.blk[Q]], par=True)
                u.clear()

            n = len(units)
            import os
            fmap = dict(pe_z=pe_z, pe_suffix=pe_suffix, pe_wv=pe_wv, pe_colsum=pe_colsum, act_sp=act_sp, act_w=act_w, dve_arg=dve_arg, dve_l1=dve_l1)
            if os.environ.get("SB_SCHED"):
                sched = tuple((fmap[x.split(":")[0]], int(x.split(":")[1])) for x in os.environ["SB_SCHED"].split(","))
            else:
                sched = ((pe_z, 0), (pe_wv, 7), (pe_suffix, 4), (pe_colsum, 6), (act_sp, 1), (act_w, 6), (dve_arg, 5), (dve_l1, 2))
            for i in range(n + 8):
                for fn, off in sched:
                    j = i - off
                    if 0 <= j < n:
                        fn(units[j])
            S.barrier()

    def transpose_to(self, P, out_cols, src_buf, src_ap):
        pv = P[:].bitcast(BF16)
        self.S.op("pe", lambda e: e.transpose(out=pv[:, out_cols:out_cols + 128], in_=src_ap, identity=self.ident[:]),
                  reads=[src_buf, self.ident], writes=[P])

    def phase_even_a(self, ei, l, xin, sc):
        S = self.S
        T = self.T
        w = self.w
        with contextlib.ExitStack() as es:
            wib = S.sb("ea_wi", [128, KC, 5648], BF16, es)
            xs = S.sb("ea_xs", [128, KC, TB], F32, es)
            hb = S.sb("ea_hb", [128, KC, TB], BF16, es)
            sq = S.sb("ea_sq", [128, KC, TB], BF16, es)
            rst = S.sb("ea_rst", [128, TB], F32, es)
            cosb = S.sb("ea_cos", [128, TB], F32, es)
            sinb = S.sb("ea_sin", [128, TB], F32, es)
            halo = S.sb("ea_halo", [128, 12, 3], F32, es)
            work = Rot([S.sb("ea_wk%d" % i, [128, TB + 3], F32, es) for i in range(2)])
            f32t = Rot([S.sb("ea_f%d" % i, [128, TB], F32, es) for i in range(8)])
            b16t = Rot([S.sb("ea_b%d" % i, [128, TB], BF16, es) for i in range(6)])
            xct = [S.sb("ea_xc%d" % i, [128, TB], BF16, es) for i in range(12)]
            tkb = Rot([S.sb("ea_tk%d" % i, [128, 1024], BF16, es) for i in range(3)])
            dts = Rot([S.sb("ea_dt%d" % i, [128, 32], F32, es) for i in range(2)])
            self.load_wcast(wib, w["ev_w_in"][ei], KC, 5648)
            S.op("dve", lambda e: e.memset(halo[:], 0.0), writes=[halo])
            cst = self.cst
            cw = self.ssd_cw[ei]; cb = self.ssd_cb[ei]
            qn = self.ret_qn[ei]; kn = self.ret_kn[ei]
            for b in range(self.NB):
                bs = slice(b * TB, (b + 1) * TB)
                S.dma("sp", xs[:], self.xview(xin, b), reads=[xin.blk[b]], writes=[xs])
                S.dma("sp", cosb[:], w["c_cos"][:, bs], writes=[cosb])
                S.dma("sp", sinb[:], w["c_sin"][:, bs], writes=[sinb])
                self.norm(xs, self.g_mix[l], hb, (sq, rst))
                for qi in range(8):
                    isq = qi < 4
                    col0 = qi * 128
                    Pq = self.PS.next()
                    for kc in range(KC):
                        self.mm(Pq, Pq[:], wib[:, kc, col0:col0 + 128], hb[:, kc, :], kc == 0, kc == KC - 1, [wib, hb])
                    qraw = f32t.next()
                    self.copy("act", qraw[:], Pq[:], [Pq], [qraw])
                    s2 = b16t.next()
                    self.act(s2[:], Pq[:], AF.Square, [Pq], [s2])
                    Ps = self.PS.next()
                    self.mm(Ps, Ps[:], self.ones[:], s2[:], True, True, [self.ones, s2])
                    r2 = f32t.next()
                    self.act(r2[:], Ps[:], AF.Ln, [Ps], [r2], scale=1.0 / 128, bias=self.epsb[:])
                    self.act(r2[:], r2[:], AF.Exp, [r2], [r2], scale=-0.5)
                    Pw = self.PS.next()
                    self.mm(Pw, Pw[:], self.swapm[:], qraw[:], True, True, [self.swapm, qraw])
                    gn = qn if isq else kn
                    t1 = f32t.next()
                    self.stt(t1[:], qraw[:], gn[:, 0:1], cosb[:], ALU.mult, ALU.mult, [qraw, cosb, cst], [t1])
                    t2 = f32t.next()
                    self.stt(t2[:], Pw[:], gn[:, 1:2], sinb[:], ALU.mult, ALU.mult, [Pw, sinb, cst], [t2])
                    self.tt("pool", t1[:], t1[:], t2[:], ALU.add, [t1, t2], [t1])
                    qo = b16t.next()
                    self.stt(qo[:], t1[:], 1.0 if isq else 128 ** -0.5, r2[:], ALU.mult, ALU.mult, [t1, r2], [qo])
                    dst = sc["qr"] if isq else sc["kr"]
                    r0 = (qi % 4) * 128
                    S.dma("pool", dst.ap[r0:r0 + 128, bs], qo[:], reads=[qo], writes=[dst.blk[b]], par=True)
                for (c0, dstn, fn) in ((1024, "v", None), (2048, "g", AF.Silu), (3072, "z", AF.Silu)):
                    for tt_ in range(4):
                        tk = tkb.next()
                        for hf in range(2):
                            Pv = self.PS.next()
                            for kc in range(KC):
                                self.mm(Pv, Pv[:], hb[:, kc, tt_ * 128:(tt_ + 1) * 128], wib[:, kc, c0 + hf * 512:c0 + (hf + 1) * 512],
                                        kc == 0, kc == KC - 1, [wib, hb])
                            if fn is None:
                                self.copy("act", tk[:, hf * 512:(hf + 1) * 512], Pv[:], [Pv], [tk])
                            else:
                                self.act(tk[:, hf * 512:(hf + 1) * 512], Pv[:], fn, [Pv], [tk])
                        t0 = b * TB + tt_ * 128
                        S.dma("pool", sc[dstn].ap[t0:t0 + 128, :], tk[:], reads=[tk], writes=[sc[dstn].blk[b]], par=True)
                for tt_ in range(4):
                    Pd = self.PS.next()
                    for kc in range(KC):
                        self.mm(Pd, Pd[:, 0:16], hb[:, kc, tt_ * 128:(tt_ + 1) * 128], wib[:, kc, 5632:5648], kc == 0, kc == KC - 1, [wib, hb])
                    dd = dts.next()
                    self.tt("dve", dd[:, 0:16], Pd[:, 0:16], self.ssd_dtb[ei], ALU.add, [Pd, cst], [dd])
                    self.act(dd[:, 0:16], dd[:, 0:16], AF.Exp, [dd], [dd])
                    self.act(dd[:, 0:16], dd[:, 0:16], AF.Ln, [dd], [dd], bias=self.oneb[:])
                    self.tt("dve", dd[:, 16:32], dd[:, 0:16], self.ssd_A[ei], ALU.mult, [dd, cst], [dd])
                    t0 = b * TB + tt_ * 128
                    S.dma("pool", sc["dtA"].ap[t0:t0 + 128, :], dd[:], reads=[dd], writes=[sc["dtA"].blk[b]], par=True)
                for c in range(12):
                    Px = self.PS.next()
                    col0 = 4096 + c * 128
                    for kc in range(KC):
                        self.mm(Px, Px[:], wib[:, kc, col0:col0 + 128], hb[:, kc, :], kc == 0, kc == KC - 1, [wib, hb])
                    wk = work.next()
                    self.copy("act", wk[:, 3:TB + 3], Px[:], [Px], [wk])
                    self.copy("pool", wk[:, 0:3], halo[:, c, :], [halo], [wk])
                    self.copy("pool", halo[:, c, :], wk[:, TB:TB + 3], [wk], [halo])
                    xcv = f32t.next()
                    self.ts("dve", xcv[:], wk[:, 3:TB + 3], cw[:, c, 3:4], ALU.mult, [wk, cst], [xcv], s2=cb[:, c:c + 1], op1=ALU.add)
                    for k in range(3):
                        self.stt(xcv[:], wk[:, k:k + TB], cw[:, c, k:k + 1], xcv[:], ALU.mult, ALU.add, [wk, xcv, cst], [xcv])
                    self.act(xct[c][:], xcv[:], AF.Silu, [xcv], [xct[c]])
                    if c >= 8:
                        dstn = "BT" if c < 10 else "CT"
                        r0 = (c % 2) * 128
                        S.dma("pool", sc[dstn].ap[r0:r0 + 128, bs], xct[c][:], reads=[xct[c]], writes=[sc[dstn].blk[b]], par=True)
                for tt_ in range(4):
                    tk = tkb.next()
                    Pt = self.PS.next()
                    for c in range(8):
                        self.transpose_to(Pt, c * 128, xct[c], xct[c][:, tt_ * 128:(tt_ + 1) * 128])
                    self.copy("act", tk[:], Pt[:].bitcast(BF16), [Pt], [tk])
                    t0 = b * TB + tt_ * 128
                    S.dma("pool", sc["xs"].ap[t0:t0 + 128, :], tk[:], reads=[tk], writes=[sc["xs"].blk[b]], par=True)
                    tk2 = tkb.next()
                    Pt2 = self.PS.next()
                    for c in range(2):
                        self.transpose_to(Pt2, c * 128, xct[8 + c], xct[8 + c][:, tt_ * 128:(tt_ + 1) * 128])
                    self.copy("act", tk2[:, 0:256], Pt2[:].bitcast(BF16)[:, 0:256], [Pt2], [tk2])
                    S.dma("pool", sc["Btm"].ap[t0:t0 + 128, :], tk2[:, 0:256], reads=[tk2], writes=[sc["Btm"].blk[b]], par=True)
            S.barrier()

    def phase_even_b(self, ei, sc, yT):
        S = self.S
        T = self.T
        NCH = T // 128
        cst = self.cst
        gam = [1.0 - 2.0 ** (-5.0 - h) for h in range(4)]
        with contextlib.ExitStack() as es:
            def rot(name, shape, dt, n=2):
                return Rot([S.sb("%s%d" % (name, i), shape, dt, es) for i in range(n)])
            qrb = rot("eb_q", [128, 4, 128], BF16); krb = rot("eb_k", [128, 4, 128], BF16)
            vb = rot("eb_v", [128, 1024], BF16); gb = rot("eb_g", [128, 1024], BF16); zb = rot("eb_z", [128, 1024], BF16)
            xsb = rot("eb_xs", [128, 16, 64], BF16); btmb = rot("eb_bt", [128, 2, 128], BF16)
            BTb = rot("eb_BT", [128, 2, 128], BF16); CTb = rot("eb_CT", [128, 2, 128], BF16)
            dtb = rot("eb_dt", [128, 32], F32)
            Sst = S.sb("eb_S", [128, 4, 256], F32, es); Sbf = S.sb("eb_Sbf", [128, 4, 256], BF16, es)
            STs = S.sb("eb_ST", [128, 16, 64], F32, es); STbf = S.sb("eb_STbf", [128, 16, 64], BF16, es)
            smb = rot("eb_sm", [128, 4, 128], BF16); ktmb = rot("eb_ktm", [128, 4, 128], BF16); q2b = rot("eb_q2", [128, 4, 128], BF16)
            Dm = S.sb("eb_D", [128, 16, 128], F32, es)
            acol = rot("eb_acol", [128, 48], F32)
            segb = rot("eb_seg", [128, 4, 128], F32, 3); ltb = rot("eb_lt", [128, 4, 128], F32, 2); erb = rot("eb_er", [128, 4, 128], F32, 2)
            cbm = rot("eb_cbm", [128, 2, 128], F32)
            MTb = S.sb("eb_MT", [128, 16, 128], BF16, es); CsTb = S.sb("eb_CsT", [128, 16, 128], BF16, es)
            xdtb = rot("eb_xdt", [128, 16, 64], BF16); xdt2b = rot("eb_xdt2", [128, 16, 64], BF16)
            yf = rot("eb_yf", [128, 1024], F32, 2)
            yab = rot("eb_ya", [128, 1024], BF16, 2); ybb = rot("eb_yb", [128, 1024], BF16, 2)
            stat = rot("eb_stat", [128, 4, 6], F32); mv = rot("eb_mv", [128, 4, 2], F32); rs4 = rot("eb_rs4", [128, 8], F32)
            yTo = rot("eb_yTo", [128, 16, 128], BF16, 2)
            junk = S.sb("eb_junk", [128, 512], BF16, es)
            S.op("dve", lambda e: e.memset(Sst[:], 0.0), writes=[Sst])
            S.op("pool", lambda e: e.memset(Sbf[:], 0.0), writes=[Sbf])
            S.op("dve", lambda e: e.memset(STs[:], 0.0), writes=[STs])
            S.op("pool", lambda e: e.memset(STbf[:], 0.0), writes=[STbf])
            gnr_b = S.sb("eb_gnr", [128, 1024], F32, es); nrr_b = S.sb("eb_nrr", [128, 1024], F32, es)
            S.dma("sp", gnr_b[:], self.w["ret_gn"][ei].partition_broadcast(128), writes=[gnr_b])
            S.dma("sp", nrr_b[:], self.w["ssd_norm"][ei].partition_broadcast(128), writes=[nrr_b])
            gnrow = gnr_b[:]; nrow = nrr_b[:]; dsk = self.ssd_D[ei]
            for c in range(NCH):
                b = c // 4
                ts_ = slice(c * 128, (c + 1) * 128)
                qr = qrb.next(); kr = krb.next(); v = vb.next(); g = gb.next(); z = zb.next(); xs = xsb.next()
                btm = btmb.next(); BT = BTb.next(); CT = CTb.next(); dt = dtb.next()
                S.dma("sp", qr[:], sc["qr"].ap[:, ts_].rearrange("(h d) t -> d h t", d=128), reads=[sc["qr"].blk[b]], writes=[qr])
                S.dma("sp", kr[:], sc["kr"].ap[:, ts_].rearrange("(h d) t -> d h t", d=128), reads=[sc["kr"].blk[b]], writes=[kr])
                S.dma("sp", v[:], sc["v"].ap[ts_, :], reads=[sc["v"].blk[b]], writes=[v])
                S.dma("sp", g[:], sc["g"].ap[ts_, :], reads=[sc["g"].blk[b]], writes=[g])
                S.dma("sp", z[:], sc["z"].ap[ts_, :], reads=[sc["z"].blk[b]], writes=[z])
                S.dma("sp", xs[:], sc["xs"].ap[ts_, :].rearrange("t (h p) -> t h p", p=64), reads=[sc["xs"].blk[b]], writes=[xs])
                S.dma("sp", btm[:], sc["Btm"].ap[ts_, :].rearrange("t (g s) -> t g s", s=128), reads=[sc["Btm"].blk[b]], writes=[btm])
                S.dma("sp", BT[:], sc["BT"].ap[:, ts_].rearrange("(g s) t -> s g t", s=128), reads=[sc["BT"].blk[b]], writes=[BT])
                S.dma("sp", CT[:], sc["CT"].ap[:, ts_].rearrange("(g s) t -> s g t", s=128), reads=[sc["CT"].blk[b]], writes=[CT])
                S.dma("sp", dt[:], sc["dtA"].ap[ts_, :], reads=[sc["dtA"].blk[b]], writes=[dt])

                Psc = self.PS.next()
                for h in range(4):
                    self.mm(Psc, Psc[:, h * 128:(h + 1) * 128], kr[:, h, :], qr[:, h, :], True, True, [kr, qr])
                sm = smb.next()
                self.tt("dve", sm[:].rearrange("p h i -> p (h i)"), Psc[:], self.ret_decay[:], ALU.mult, [Psc, cst], [sm])
                Pkt = self.PS.next()
                for h in range(4):
                    self.transpose_to(Pkt, h * 128, kr, kr[:, h, :])
                ktm = ktmb.next()
                self.tt("dve", ktm[:], Pkt[:].bitcast(BF16)[:, 0:512].rearrange("p (h d) -> p h d", d=128),
                        self.ret_kdec[:].unsqueeze(2).broadcast_to([128, 4, 128]), ALU.mult, [Pkt, cst], [ktm])
                q2 = q2b.next()
                self.tt("pool", q2[:].rearrange("p h i -> p (h i)"), qr[:].rearrange("p h i -> p (h i)"), self.ret_qdec[:], ALU.mult, [qr, cst], [q2])
                PO = self.PW.next()
                for h in range(4):
                    self.mm(PO, PO[:, h * 256:(h + 1) * 256], sm[:, h, :], v[:, h * 256:(h + 1) * 256], True, False, [sm, v])
                    self.mm(PO, PO[:, h * 256:(h + 1) * 256], q2[:, h, :], Sbf[:, h, :], False, True, [q2, Sbf])
                Pkv = self.PW.next()
                for h in range(4):
                    self.mm(Pkv, Pkv[:, h * 256:(h + 1) * 256], ktm[:, h, :], v[:, h * 256:(h + 1) * 256], True, True, [ktm, v])
                for h in range(4):
                    self.stt(Sst[:, h, :], Sst[:, h, :], gam[h] ** 128, Pkv[:, h * 256:(h + 1) * 256], ALU.mult, ALU.add, [Sst, Pkv], [Sst])
                self.copy("act", Sbf[:], Sst[:], [Sst], [Sbf])
                st = stat.next(); m2 = mv.next(); r4 = rs4.next()
                for h in range(4):
                    S.op("dve", lambda e, h=h, st=st, PO=PO: e.bn_stats(out=st[:, h, :], in_=PO[:, h * 256:(h + 1) * 256]), reads=[PO], writes=[st])
                for h in range(4):
                    S.op("dve", lambda e, h=h, st=st, m2=m2: e.bn_aggr(out=m2[:, h, :], in_=st[:, h, :]), reads=[st], writes=[m2])
                self.act(r4[:, 0:4], m2[:, :, 1], AF.Ln, [m2], [r4], bias=self.epsb[:])
                self.act(r4[:, 0:4], r4[:, 0:4], AF.Exp, [r4], [r4], scale=-0.5)
                y1 = yf.next()
                for h in range(4):
                    self.ts("dve", y1[:, h * 256:(h + 1) * 256], PO[:, h * 256:(h + 1) * 256], m2[:, h, 0:1], ALU.subtract, [PO, m2, r4], [y1],
                            s2=r4[:, h:h + 1], op1=ALU.mult)
                self.tt("pool", y1[:], y1[:], gnrow, ALU.mult, [y1, gnr_b], [y1])
                ya = yab.next()
                self.tt("pool", ya[:], y1[:], g[:], ALU.mult, [y1, g], [ya])

                PA = self.PS.next()
                self.mm(PA, PA[:, 0:16], self.tri_f[:], dt[:, 16:32], True, True, [self.tri_f, dt])
                self.mm(PA, PA[:, 16:32], self.ones_f[:], dt[:, 16:32], True, True, [self.ones_f, dt])
                ac = acol.next()
                self.copy("act", ac[:, 0:32], PA[:, 0:32], [PA], [ac])
                self.tt("dve", ac[:, 32:48], ac[:, 16:32], ac[:, 0:16], ALU.subtract, [ac], [ac])
                self.act(ac[:, 32:48], ac[:, 32:48], AF.Exp, [ac], [ac])
                self.act(ac[:, 16:32], ac[:, 16:32], AF.Exp, [ac], [ac])
                self.tt("dve", Dm[:], dt[:, 16:32].unsqueeze(2).broadcast_to([128, 16, 128]),
                        self.tri_f[:].unsqueeze(1).broadcast_to([128, 16, 128]), ALU.mult, [dt, self.tri_f], [Dm])
                PCB = self.PS.next()
                for gi in range(2):
                    self.mm(PCB, PCB[:, gi * 128:(gi + 1) * 128], BT[:, gi, :], CT[:, gi, :], True, True, [BT, CT])
                cb_ = cbm.next()
                self.tt("dve", cb_[:].rearrange("p g i -> p (g i)"), PCB[:, 0:256], self.tri2[:], ALU.mult, [PCB, cst], [cb_])
                for q4 in range(4):
                    gi = q4 // 2
                    hs = slice(q4 * 4, q4 * 4 + 4)
                    Prb = self.PS.next()
                    self.mm(Prb, Prb[:], self.ones_f[:], Dm[:, hs, :].rearrange("p h i -> p (h i)"), True, True, [self.ones_f, Dm])
                    sg = segb.next()
                    self.tt("dve", sg[:], Prb[:].rearrange("p (h i) -> p h i", i=128), ac[:, q4 * 4:q4 * 4 + 4].unsqueeze(2).broadcast_to([128, 4, 128]),
                            ALU.subtract, [Prb, ac], [sg])
                    self.ts("pool", sg[:], sg[:], 0.0, ALU.min, [sg], [sg])
                    lt = ltb.next()
                    self.act(lt[:], sg[:], AF.Exp, [sg], [lt])
                    er = erb.next()
                    self.act(er[:].rearrange("p h i -> p (h i)"), Prb[:], AF.Exp, [Prb], [er])
                    self.tt("dve", MTb[:, hs, :], lt[:], cb_[:, gi, :].unsqueeze(1).broadcast_to([128, 4, 128]), ALU.mult, [lt, cb_], [MTb])
                    self.tt("pool", CsTb[:, hs, :], er[:], CT[:, gi, :].unsqueeze(1).broadcast_to([128, 4, 128]), ALU.mult, [er, CT], [CsTb])
                xdt = xdtb.next(); xdt2 = xdt2b.next()
                self.tt("dve", xdt[:], xs[:], dt[:, 0:16].unsqueeze(2).broadcast_to([128, 16, 64]), ALU.mult, [xs, dt], [xdt])
                self.tt("pool", xdt2[:], xdt[:], ac[:, 32:48].unsqueeze(2).broadcast_to([128, 16, 64]), ALU.mult, [xdt, ac], [xdt2])
                PY = self.PW.next()
                for h in range(16):
                    self.mm(PY, PY[:, h * 64:(h + 1) * 64], MTb[:, h, :], xdt[:, h, :], True, False, [MTb, xdt])
                    self.mm(PY, PY[:, h * 64:(h + 1) * 64], CsTb[:, h, :], STbf[:, h, :], False, True, [CsTb, STbf])
                PSt = self.PW.next()
                for h in range(16):
                    self.mm(PSt, PSt[:, h * 64:(h + 1) * 64], btm[:, h // 8, :], xdt2[:, h, :], True, True, [btm, xdt2])
                self.tt("dve", STs[:], STs[:], ac[:, 16:32].unsqueeze(2).broadcast_to([128, 16, 64]), ALU.mult, [STs, ac], [STs])
                self.tt("dve", STs[:].rearrange("p h d -> p (h d)"), PSt[:], STs[:].rearrange("p h d -> p (h d)"), ALU.add, [PSt, STs], [STs])
                self.copy("act", STbf[:], STs[:], [STs], [STbf])
                y2 = yf.next()
                self.tt("pool", y2[:].rearrange("p (h d) -> p h d", d=64), xs[:], dsk.unsqueeze(2).broadcast_to([128, 16, 64]), ALU.mult, [xs, cst], [y2])
                self.tt("dve", y2[:], PY[:], y2[:], ALU.add, [PY, y2], [y2])
                self.tt("pool", y2[:], y2[:], z[:], ALU.mult, [y2, z], [y2])
                r8 = rs4.next()
                for gi in range(2):
                    S.op("act", lambda e, gi=gi, y2=y2, r8=r8: e.activation(out=junk[:], in_=y2[:, gi * 512:(gi + 1) * 512], func=AF.Square, accum_out=r8[:, gi:gi + 1]),
                         reads=[y2], writes=[junk, r8])
                self.act(r8[:, 0:2], r8[:, 0:2], AF.Ln, [r8], [r8], scale=1.0 / 512, bias=self.epsb[:])
                self.act(r8[:, 0:2], r8[:, 0:2], AF.Exp, [r8], [r8], scale=-0.5)
                yb = ybb.next()
                for gi in range(2):
                    self.stt(yb[:, gi * 512:(gi + 1) * 512], y2[:, gi * 512:(gi + 1) * 512], r8[:, gi:gi + 1], nrow[:, gi * 512:(gi + 1) * 512],
                             ALU.mult, ALU.mult, [y2, r8, nrr_b], [yb])
                yo = yTo.next()
                for half, src in ((0, ya), (1, yb)):
                    Pt = self.PS.next()
                    for cc in range(8):
                        self.transpose_to(Pt, cc * 128, src, src[:, cc * 128:(cc + 1) * 128])
                    self.copy("act", yo[:, half * 8:(half + 1) * 8, :].rearrange("p c i -> p (c i)"), Pt[:].bitcast(BF16), [Pt], [yo])
                S.dma("pool", yT.ap[:, ts_].rearrange("(c p) t -> p c t", p=128), yo[:], reads=[yo], writes=[yT.blk[b]], par=True)
            S.barrier()

    def build(self):
        nc, es = self.nc, self.es
        T, NB = self.T, self.NB
        S = self.S = Sched(nc, es)
        ne = sum(1 for x in self.layers if x == "e")
        no = sum(1 for x in self.layers if x == "o")
        L = len(self.layers)
        w = self.w = {}

        def inp(name, shape):
            w[name] = nc.dram_tensor(name, list(shape), F32, kind="ExternalInput").ap()

        self.xin = DT(S, "xT", [D, T], F32, NB, kind="ExternalInput")
        self.xout = DT(S, "outT", [D, T], F32, NB, kind="ExternalOutput")
        inp("norm_mix", [L, D]); inp("norm_mlp", [L, D])
        inp("mlp_w1", [L, D, 4096]); inp("mlp_w2", [L, 4096, D])
        if no:
            inp("od_w_in", [no, D, 3584]); inp("od_w_out", [no, 1536, D])
            inp("lru_conv_w", [no, 4, D]); inp("lru_conv_b", [no, D])
            inp("lru_wa", [no, 8, 128, 128]); inp("lru_ba", [no, 8, 128])
            inp("lru_wx", [no, 8, 128, 128]); inp("lru_bx", [no, 8, 128])
            inp("lru_lam", [no, D]); inp("sb_qn", [no, 64]); inp("sb_kn", [no, 64])
        if ne:
            inp("ev_w_in", [ne, D, 5648]); inp("ev_w_out", [ne, 2048, D])
            inp("ret_qn", [ne, 128]); inp("ret_kn", [ne, 128]); inp("ret_gn", [ne, 1024])
            inp("ssd_conv_w", [ne, 4, 1536]); inp("ssd_conv_b", [ne, 1536])
            inp("ssd_dt_bias", [ne, 16]); inp("ssd_a_log", [ne, 16]); inp("ssd_d", [ne, 16]); inp("ssd_norm", [ne, 1024])
            inp("c_cos", [128, T]); inp("c_sin", [128, T])
            inp("c_ident", [128, 128]); inp("c_swap", [128, 128]); inp("c_tri", [128, 128]); inp("c_tri2", [128, 256])
            inp("c_rdecay", [128, 512]); inp("c_rqdec", [128, 512]); inp("c_rkdec", [128, 4])
        inp("c_masklt", [128, 128]); inp("c_umat", [128, 128]); inp("c_bones", [128, 128])

        self.psb = [S.ps("ps%d" % i, [128, 512]) for i in range(4)]
        self.PS = Rot(self.psb)
        pw = [S.ps("pw%d" % i, [128, 1024]) for i in range(2)]
        self.PW = Rot(pw)
        self.psh = [Buf("psh%d" % i, pw[i // 2].t[:, (i % 2) * 512:(i % 2 + 1) * 512]) for i in range(4)]

        cst = self.cst = Buf("cst")

        def csb(name, shape, dt=F32):
            return es.enter_context(nc.sbuf_tensor(name, list(shape), dt))

        self.ones = S.sb("ones", [128, 128], BF16)
        S.op("dve", lambda e: e.memset(self.ones[:], 1.0), writes=[self.ones])
        self.epsb = csb("epsb", [128, 1]); self.oneb = csb("oneb", [128, 1])
        S.op("dve", lambda e: e.memset(self.epsb[:], EPS), writes=[cst])
        S.op("dve", lambda e: e.memset(self.oneb[:], 1.0), writes=[cst])
        self.mask_lt = csb("mask_lt", [128, 128], BF16)
        self.umat = S.sb("umat", [128, 128], BF16)
        self.bones = S.sb("bones", [128, 128], BF16)
        S.dma("pool", self.mask_lt[:], w["c_masklt"], writes=[cst])
        S.dma("pool", self.umat[:], w["c_umat"], writes=[self.umat])
        S.dma("pool", self.bones[:], w["c_bones"], writes=[self.bones])
        self.gains = Buf("gains")
        gmix = csb("gmix", [128, L, KC]); gmlp = csb("gmlp", [128, L, KC])
        S.dma("sp", gmix[:], w["norm_mix"].rearrange("l (kc p) -> p l kc", p=128), writes=[self.gains], allow_slow_non_contiguous=True)
        S.dma("sp", gmlp[:], w["norm_mlp"].rearrange("l (kc p) -> p l kc", p=128), writes=[self.gains], allow_slow_non_contiguous=True)
        self.g_mix = [gmix[:, l, :] for l in range(L)]
        self.g_mlp = [gmlp[:, l, :] for l in range(L)]
        if no:
            cw = csb("lru_cw", [128, no, 8, 4]); cb = csb("lru_cb", [128, no, 8])
            ba = csb("lru_ba_s", [128, no, 8]); bx = csb("lru_bx_s", [128, no, 8])
            cl = csb("lru_cl", [128, no, 8])
            qn = csb("sbqn", [128, no]); kn = csb("sbkn", [128, no])
            for o_ in range(no):
                for k_ in range(4):
                    S.dma("sp", cw[:, o_, :, k_], w["lru_conv_w"][o_, k_].rearrange("(c p) -> p c", p=128), writes=[cst], allow_slow_non_contiguous=True)
            S.dma("sp", cb[:], w["lru_conv_b"].rearrange("o (c p) -> p o c", p=128), writes=[cst], allow_slow_non_contiguous=True)
            S.dma("sp", ba[:], w["lru_ba"].rearrange("o c p -> p o c"), writes=[cst], allow_slow_non_contiguous=True)
            S.dma("sp", bx[:], w["lru_bx"].rearrange("o c p -> p o c"), writes=[cst], allow_slow_non_contiguous=True)
            S.dma("sp", cl[:], w["lru_lam"].rearrange("o (c p) -> p o c", p=128), writes=[cst], allow_slow_non_contiguous=True)
            for half in range(2):
                S.dma("sp", qn[half * 64:(half + 1) * 64, :], w["sb_qn"].rearrange("o d -> d o"), writes=[cst], allow_slow_non_contiguous=True)
                S.dma("sp", kn[half * 64:(half + 1) * 64, :], w["sb_kn"].rearrange("o d -> d o"), writes=[cst], allow_slow_non_contiguous=True)
            S.op("act", lambda e: e.activation(out=cl[:], in_=cl[:], func=AF.Exp, scale=-1.0), reads=[cst], writes=[cst])
            S.op("act", lambda e: e.activation(out=cl[:], in_=cl[:], func=AF.Ln, bias=self.oneb[:]), reads=[cst], writes=[cst])
            S.op("dve", lambda e: e.tensor_scalar(out=cl[:], in0=cl[:], scalar1=-8.0, scalar2=None, op0=ALU.mult), reads=[cst], writes=[cst])
            self.lru_cw = [cw[:, o] for o in range(no)]
            self.lru_cb = [cb[:, o] for o in range(no)]
            self.lru_ba = [ba[:, o] for o in range(no)]
            self.lru_bx = [bx[:, o] for o in range(no)]
            self.lru_cl = [cl[:, o] for o in range(no)]
            self.sb_qn = [qn[:, o:o + 1] for o in range(no)]
            self.sb_kn = [kn[:, o:o + 1] for o in range(no)]
        if ne:
            self.ident = S.sb("ident", [128, 128], BF16)
            S.dma("pool", self.ident[:], w["c_ident"], writes=[self.ident])
            self.swapm = S.sb("swapm", [128, 128], F32)
            S.dma("sp", self.swapm[:], w["c_swap"], writes=[self.swapm])
            self.tri_f = S.sb("tri_f", [128, 128], F32)
            S.dma("sp", self.tri_f[:], w["c_tri"], writes=[self.tri_f])
            self.ones_f = S.sb("ones_f", [128, 128], F32)
            S.op("dve", lambda e: e.memset(self.ones_f[:], 1.0), writes=[self.ones_f])
            self.tri2 = csb("tri2", [128, 256]); self.ret_decay = csb("rdecay", [128, 512]); self.ret_qdec = csb("rqdec", [128, 512])
            self.ret_kdec = csb("rkdec", [128, 4])
            S.dma("sp", self.tri2[:], w["c_tri2"], writes=[cst])
            S.dma("sp", self.ret_decay[:], w["c_rdecay"], writes=[cst])
            S.dma("sp", self.ret_qdec[:], w["c_rqdec"], writes=[cst])
            S.dma("sp", self.ret_kdec[:], w["c_rkdec"], writes=[cst])
            scw = csb("ssd_cw", [128, ne, 12, 4]); scb = csb("ssd_cb", [128, ne, 12])
            rqn = csb("ret_qn_s", [128, ne, 2]); rkn = csb("ret_kn_s", [128, ne, 2])
            dsk = csb("ssd_D_r", [128, ne, 16]); dtbr = csb("ssd_dtb_r", [128, ne, 16]); Ar = csb("ssd_A_r", [128, ne, 16])
            for e_ in range(ne):
                for k_ in range(4):
                    S.dma("sp", scw[:, e_, :, k_], w["ssd_conv_w"][e_, k_].rearrange("(c p) -> p c", p=128), writes=[cst], allow_slow_non_contiguous=True)
                S.dma("sp", scb[:, e_, :], w["ssd_conv_b"][e_].rearrange("(c p) -> p c", p=128), writes=[cst], allow_slow_non_contiguous=True)
                for nm, tl in (("ret_qn", rqn), ("ret_kn", rkn)):
                    col = w[nm][e_].rearrange("(d o) -> d o", o=1)
                    S.dma("sp", tl[:, e_, 0:1], col, writes=[cst], allow_slow_non_contiguous=True)
                    S.dma("sp", tl[0:64, e_, 1:2], col[64:128], writes=[cst], allow_slow_non_contiguous=True)
                    S.dma("sp", tl[64:128, e_, 1:2], col[0:64], writes=[cst], allow_slow_non_contiguous=True)
                S.dma("sp", dsk[:, e_, :], w["ssd_d"][e_].partition_broadcast(128), writes=[cst])
                S.dma("sp", dtbr[:, e_, :], w["ssd_dt_bias"][e_].partition_broadcast(128), writes=[cst])
                S.dma("sp", Ar[:, e_, :], w["ssd_a_log"][e_].partition_broadcast(128), writes=[cst])
            S.op("act", lambda e: e.activation(out=Ar[:], in_=Ar[:], func=AF.Exp), reads=[cst], writes=[cst])
            S.op("dve", lambda e: e.tensor_scalar(out=Ar[:], in0=Ar[:], scalar1=-1.0, scalar2=None, op0=ALU.mult), reads=[cst], writes=[cst])
            self.ssd_cw = [scw[:, e_] for e_ in range(ne)]; self.ssd_cb = [scb[:, e_] for e_ in range(ne)]
            self.ret_qn = [rqn[:, e_] for e_ in range(ne)]; self.ret_kn = [rkn[:, e_] for e_ in range(ne)]
            self.ssd_D = [dsk[:, e_] for e_ in range(ne)]; self.ssd_dtb = [dtbr[:, e_] for e_ in range(ne)]; self.ssd_A = [Ar[:, e_] for e_ in range(ne)]
        S.barrier()

        xa = DT(S, "xa", [D, T], F32, NB)
        xb_ = DT(S, "xb", [D, T], F32, NB)
        yT = DT(S, "yT", [2048, T], BF16, NB)
        qT = DT(S, "qT", [512, T], BF16, NB)
        kT = DT(S, "kT", [512, T], BF16, NB)
        vtm = DT(S, "vtm", [T, 1024], BF16, NB)
        sc = {"qr": qT, "kr": kT, "v": vtm}
        if ne:
            for nm in ("g", "z", "xs"):
                sc[nm] = DT(S, "sc_" + nm, [T, 1024], BF16, NB)
            sc["Btm"] = DT(S, "sc_Btm", [T, 256], BF16, NB)
            sc["BT"] = DT(S, "sc_BT", [256, T], BF16, NB)
            sc["CT"] = DT(S, "sc_CT", [256, T], BF16, NB)
            sc["dtA"] = DT(S, "sc_dtA", [T, 32], F32, NB)

        cur = self.xin
        ie = io = 0
        for l, kind in enumerate(self.layers):
            last = l == L - 1
            skip = getattr(self, "skip", ())
            if kind == "o":
                if "odd_a" not in skip:
                    self.phase_odd_a(io, l, cur, yT, qT, kT, vtm)
                if "sb" not in skip:
                    self.phase_sb(qT, kT, vtm, yT)
                if "outproj" not in skip:
                    self.phase_outproj("od_w_out", io, 12, yT, cur, xa)
                io += 1
            else:
                if "even_a" not in skip:
                    self.phase_even_a(ie, l, cur, sc)
                if "even_b" not in skip:
                    self.phase_even_b(ie, sc, yT)
                if "outproj" not in skip:
                    self.phase_outproj("ev_w_out", ie, 16, yT, cur, xa)
                ie += 1
            if "mlp" not in skip:
                self.phase_mlp(l, xa, self.xout if last else xb_)
            cur = xb_
        S.finish()
        es.close()
        return nc


def consts(T=None, even=False):
    i = np.arange(128)
    c = _consts_base(i)
    if even:
        f32 = np.float32
        inv = (f32(10000.0) ** (-(np.arange(64, dtype=f32)) / f32(64))).astype(f32)
        ang = (np.arange(T, dtype=f32)[None, :] * inv[:, None]).astype(f32).astype(np.float64)
        cos = np.cos(ang); sin = np.sin(ang)
        c["c_cos"] = np.concatenate([cos, cos], 0).astype(f32)
        c["c_sin"] = np.concatenate([-sin, sin], 0).astype(f32)
        c["c_ident"] = np.eye(128, dtype=f32)
        c["c_swap"] = (i[:, None] == ((i[None, :] + 64) % 128)).astype(f32)
        tri = (i[:, None] <= i[None, :]).astype(f32)
        c["c_tri"] = tri
        c["c_tri2"] = np.concatenate([tri, tri], 1)
        lg = np.log1p(-np.exp2(-5.0 - np.arange(4, dtype=np.float64)))
        rel = (i[None, :] - i[:, None]).astype(np.float64)
        dec = np.where(rel[:, None, :] >= 0, np.exp(lg[None, :, None] * np.maximum(rel, 0)[:, None, :]), 0.0)
        c["c_rdecay"] = dec.reshape(128, 512).astype(f32)
        qd = np.exp(lg[:, None] * (i[None, :] + 1.0))
        c["c_rqdec"] = np.broadcast_to(qd.reshape(1, 512), (128, 512)).astype(f32).copy()
        c["c_rkdec"] = np.exp(lg[None, :] * (127.0 - i[:, None])).astype(f32)
    return c


def _consts_base(i):
    return {
        "c_masklt": (i[:, None] < i[None, :]).astype(np.float32),
        "c_umat": (i[:, None] > i[None, :]).astype(np.float32),
        "c_bones": ((i[:, None] // 64) == (i[None, :] // 64)).astype(np.float32),
    }


def make_inputs(inputs, b, kinds):
    m = {"xT": np.ascontiguousarray(np.asarray(inputs["x"])[b].T)}
    L = len(kinds)
    for n in ("norm_mix", "norm_mlp", "mlp_w1", "mlp_w2"):
        m[n] = np.ascontiguousarray(np.asarray(inputs[n])[:L])
    if "e" in kinds:
        for n in ("ev_w_in", "ev_w_out", "ret_qn", "ret_kn", "ret_gn", "ssd_conv_w", "ssd_conv_b", "ssd_dt_bias", "ssd_a_log", "ssd_d", "ssd_norm"):
            m[n] = np.ascontiguousarray(np.asarray(inputs[n]))
    if "o" in kinds:
        for n in ("od_w_in", "od_w_out", "lru_conv_w", "lru_conv_b", "lru_wa", "lru_ba", "lru_wx", "lru_bx", "lru_lam", "sb_qn", "sb_kn"):
            m[n] = np.ascontiguousarray(np.asarray(inputs[n]))
    m.update(consts(m["xT"].shape[1], "e" in kinds))
    return m


KINDS = "eoeo"
SEQ = 8192
_NC_CACHE = {}


def kernel(**inputs):
    if "nc" not in _NC_CACHE:
        _NC_CACHE["nc"] = K(SEQ, list(KINDS)).build()
    nc = _NC_CACHE["nc"]
    nb = np.asarray(inputs["x"]).shape[0]
    in_maps = [make_inputs(inputs, b, KINDS) for b in range(nb)]
    res = run_bass_kernel_spmd(nc, in_maps, core_ids=list(range(nb)))
    out = np.stack([np.asarray(res.results[b]["outT"]).T for b in range(nb)])
    return np.ascontiguousarray(out.astype(np.float32))
```

```python
import contextlib
import math
import numpy as np
import concourse.bass as bass
import concourse.mybir as mybir
from concourse.bass_utils import run_bass_kernel_spmd

F32 = mybir.dt.float32
BF16 = mybir.dt.bfloat16
AF = mybir.ActivationFunctionType
ALU = mybir.AluOpType
AX = mybir.AxisListType

D = 1024
KC = 8
TB = 512
EPS = 1e-6


class Buf:
    __slots__ = ("name", "t", "lws", "rd")

    def __init__(self, name, t=None):
        self.name = name
        self.t = t
        self.lws = []
        self.rd = {}

    def __getitem__(self, k):
        return self.t[k]


class Rot:
    def __init__(self, bufs):
        self.bufs = list(bufs)
        self.i = 0

    def next(self):
        b = self.bufs[self.i % len(self.bufs)]
        self.i += 1
        return b


class Sched:
    ENG = ("pe", "act", "dve", "pool", "sp")

    def __init__(self, nc, es, n_dma_sems=32):
        self.nc = nc
        self.es = es
        self.q = {e: [] for e in self.ENG}
        self.cnt = {e: 0 for e in self.ENG}
        self.sem = {e: es.enter_context(nc.semaphore("c_" + e)) for e in ("pe", "act", "dve", "pool")}
        self.dsem = [es.enter_context(nc.semaphore("d%d" % i)) for i in range(n_dma_sems)]
        self.dcnt = [0] * n_dma_sems
        self.dnext = 0
        self.pnext = 0
        self.waited = {e: {} for e in self.ENG}
        self.ndma = 0

    def sb(self, name, shape, dt, es=None):
        self.uid = getattr(self, "uid", 0) + 1
        name = "%s_u%d" % (name, self.uid)
        return Buf(name, (es or self.es).enter_context(self.nc.sbuf_tensor(name, list(shape), dt)))

    def ps(self, name, shape, dt=F32):
        return Buf(name, self.es.enter_context(self.nc.psum_tensor(name, list(shape), dt)))

    def dram(self, name, shape, dt, kind="Internal"):
        return Buf(name, self.nc.dram_tensor(name, list(shape), dt, kind=kind).ap())

    def _need(self, eng, dep, waits):
        if dep is None:
            return
        key, val, semh = dep
        w = self.waited[eng]
        if w.get(key, 0) >= val:
            return
        w[key] = val
        waits.append((semh, val))

    def _deps(self, eng, reads, writes, same, par=False):
        waits = []
        for b in reads:
            for lw in b.lws:
                if same or lw[0] != eng:
                    self._need(eng, lw, waits)
        for b in writes:
            if not par:
                for lw in b.lws:
                    if same or lw[0] != eng:
                        self._need(eng, lw, waits)
            for k, d in b.rd.items():
                if same or k != eng:
                    self._need(eng, d, waits)
        return waits

    def op(self, eng, fn, reads=(), writes=()):
        same = eng != "pe"
        waits = self._deps(eng, reads, writes, same)
        self.cnt[eng] += 1
        tok = (eng, self.cnt[eng], self.sem[eng])
        self.q[eng].append((waits, fn, (self.sem[eng], 1)))
        for b in reads:
            b.rd[eng] = tok
        for b in writes:
            b.lws = [tok]
            b.rd = {}
        return tok

    def dma(self, qeng, out_ap, in_ap, reads=(), writes=(), par=False, **kw):
        waits = self._deps(qeng, reads, writes, True, par)
        if qeng == "pool":
            s = self.pnext
            self.pnext = (self.pnext + 1) % 4
        else:
            s = 4 + self.dnext
            self.dnext = (self.dnext + 1) % (len(self.dsem) - 4)
        if self.dcnt[s] > 0:
            self._need(qeng, ("d%d" % s, self.dcnt[s], self.dsem[s]), waits)
        self.dcnt[s] += 16
        tok = ("d%d" % s, self.dcnt[s], self.dsem[s])
        self.q[qeng].append((waits, lambda e: e.dma_start(out=out_ap, in_=in_ap, **kw), (self.dsem[s], 16)))
        for b in reads:
            b.rd["dma%d" % self.ndma] = tok
        for b in writes:
            if par:
                if b.rd:
                    b.lws = []
                    b.rd = {}
                b.lws.append(tok)
            else:
                b.lws = [tok]
                b.rd = {}
        self.ndma += 1
        return tok

    def barrier(self):
        for e in self.ENG:
            waits = []
            for e2 in ("pe", "act", "dve", "pool"):
                if e2 != e and self.cnt[e2] > 0:
                    self._need(e, (e2, self.cnt[e2], self.sem[e2]), waits)
            for s in range(len(self.dsem)):
                if self.dcnt[s] > 0:
                    self._need(e, ("d%d" % s, self.dcnt[s], self.dsem[s]), waits)
            if waits:
                self.q[e].append((waits, None, None))

    def finish(self):
        self.barrier()
        nc = self.nc
        q = self.q

        def replay(eh, items):
            for waits, fn, inc in items:
                for semh, val in waits:
                    eh.wait_ge(semh, val)
                if fn is not None:
                    ins = fn(eh)
                    if inc is not None:
                        ins.then_inc(inc[0], inc[1])

        with nc.Block() as block:
            @block.sync
            def _(e):
                replay(e, q["sp"])

            @block.tensor
            def _(e):
                replay(e, q["pe"])

            @block.scalar
            def _(e):
                replay(e, q["act"])

            @block.vector
            def _(e):
                replay(e, q["dve"])

            @block.gpsimd
            def _(e):
                replay(e, q["pool"])


class DT:
    def __init__(self, S, name, shape, dt, nblk, kind="Internal"):
        self.ap = S.nc.dram_tensor(name, list(shape), dt, kind=kind).ap()
        self.blk = [Buf("%s_b%d" % (name, i)) for i in range(nblk)]
        self.all = self.blk


class K:
    def __init__(self, T, layers):
        self.T = T
        self.NB = T // TB
        self.layers = layers
        self.nc = bass.Bass("TRN2", target_bir_lowering=False)
        self.es = contextlib.ExitStack()

    def mm(self, P, out_ap, lhsT, rhs, start, stop, reads):
        self.S.op("pe", lambda e: e.matmul(out_ap, lhsT=lhsT, rhs=rhs, start=start, stop=stop), reads=reads, writes=[P])

    def act(self, out_ap, in_ap, func, reads, writes, **kw):
        self.S.op("act", lambda e: e.activation(out=out_ap, in_=in_ap, func=func, **kw), reads=reads, writes=writes)

    def tt(self, eng, out_ap, a, b, op, reads, writes):
        self.S.op(eng, lambda e: e.tensor_tensor(out=out_ap, in0=a, in1=b, op=op), reads=reads, writes=writes)

    def ts(self, eng, out_ap, a, s1, op0, reads, writes, s2=None, op1=None):
        if op1 is None:
            self.S.op(eng, lambda e: e.tensor_scalar(out=out_ap, in0=a, scalar1=s1, scalar2=None, op0=op0), reads=reads, writes=writes)
        else:
            self.S.op(eng, lambda e: e.tensor_scalar(out=out_ap, in0=a, scalar1=s1, scalar2=s2, op0=op0, op1=op1), reads=reads, writes=writes)

    def stt(self, out_ap, a, s, b, op0, op1, reads, writes):
        self.S.op("dve", lambda e: e.scalar_tensor_tensor(out=out_ap, in0=a, scalar=s, in1=b, op0=op0, op1=op1), reads=reads, writes=writes)

    def copy(self, eng, out_ap, in_ap, reads, writes):
        if eng == "act":
            self.S.op("act", lambda e: e.copy(out=out_ap, in_=in_ap), reads=reads, writes=writes)
        else:
            self.S.op(eng, lambda e: e.tensor_copy(out=out_ap, in_=in_ap), reads=reads, writes=writes)

    def xview(self, xdt, b):
        return xdt.ap.rearrange("(kc p) t -> p kc t", p=128)[:, :, b * TB:(b + 1) * TB]

    def load_wcast(self, wbuf, w_ap, kc_n, ncols, grp=1):
        S = self.S
        wv = w_ap.rearrange("(kc p) f -> p kc f", p=128)
        for k0 in range(0, kc_n, grp):
            S.dma("pool", wbuf[:, k0:k0 + grp, :], wv[:, k0:k0 + grp, :], reads=[], writes=[wbuf], max_dma_last_dim=4096)

    def norm(self, xs, g_ap, hb, es_bufs):
        S = self.S
        sq, rst = es_bufs
        sqv = sq[:, 0:KC, :]
        P = self.PS.next()
        self.act(sqv, xs[:], AF.Square, [xs], [sq])
        for kc in range(KC):
            self.mm(P, P[:], self.ones[:], sqv[:, kc, :], kc == 0, kc == KC - 1, [self.ones, sq])
        self.act(rst[:], P[:], AF.Ln, [P], [rst], scale=1.0 / D, bias=self.epsb[:])
        self.act(rst[:], rst[:], AF.Exp, [rst], [rst], scale=-0.5)
        for kc in range(KC):
            self.stt(hb[:, kc, :], xs[:, kc, :], g_ap[:, kc:kc + 1], rst[:], ALU.mult, ALU.mult, [xs, rst, self.gains], [hb])

    def phase_mlp(self, l, xin, xout):
        S = self.S
        with contextlib.ExitStack() as es:
            w1b = S.sb("w1b", [128, KC, 4096], BF16, es)
            w2b = S.sb("w2b", [128, 32, D], BF16, es)
            xs = S.sb("m_xs", [128, KC, TB], F32, es)
            hb = S.sb("m_hb", [128, KC, TB], BF16, es)
            ab = S.sb("m_ab", [128, 32, TB], BF16, es)
            sq = ab
            rst = S.sb("m_rst", [128, TB], F32, es)
            tmps = Rot([S.sb("m_tmp%d" % i, [128, TB], F32, es) for i in range(2)])
            self.load_wcast(w1b, self.w["mlp_w1"][l], KC, 4096)
            self.load_wcast(w2b, self.w["mlp_w2"][l], 32, D, grp=4)
            for b in range(self.NB):
                S.dma("sp", xs[:], self.xview(xin, b), reads=[xin.blk[b]], writes=[xs])
                self.norm(xs, self.g_mlp[l], hb, (sq, rst))
                for fc in range(32):
                    P = self.PS.next()
                    for kc in range(KC):
                        self.mm(P, P[:], w1b[:, kc, fc * 128:(fc + 1) * 128], hb[:, kc, :], kc == 0, kc == KC - 1, [w1b, hb])
                    tmp = tmps.next()
                    self.act(tmp[:], P[:], AF.Relu, [P], [tmp])
                    self.tt("dve" if fc % 2 == 0 else "pool", ab[:, fc, :], tmp[:], tmp[:], ALU.mult, [tmp], [ab])
                for oc in range(KC):
                    P = self.PS.next()
                    for fc in range(32):
                        self.mm(P, P[:], w2b[:, fc, oc * 128:(oc + 1) * 128], ab[:, fc, :], fc == 0, fc == 31, [w2b, ab])
                    self.tt("dve", xs[:, oc, :], P[:], xs[:, oc, :], ALU.add, [P, xs], [xs])
                S.dma("pool", self.xview(xout, b), xs[:], reads=[xs], writes=[xout.blk[b]])
            S.barrier()

    def phase_outproj(self, wname, widx, nkc, yT, xin, xout):
        S = self.S
        with contextlib.ExitStack() as es:
            wob = S.sb("wob", [128, nkc, D], BF16, es)
            xs = S.sb("o_xs", [128, KC, TB], F32, es)
            ybs = Rot([S.sb("o_yb%d" % i, [128, nkc, TB], BF16, es) for i in range(2)])
            self.load_wcast(wob, self.w[wname][widx], nkc, D, grp=4)
            yv = yT.ap.rearrange("(kc p) t -> p kc t", p=128)
            for b in range(self.NB):
                yb = ybs.next()
                S.dma("sp", yb[:], yv[:, 0:nkc, b * TB:(b + 1) * TB], reads=[yT.blk[b]], writes=[yb])
                S.dma("sp", xs[:], self.xview(xin, b), reads=[xin.blk[b]], writes=[xs])
                for oc in range(KC):
                    P = self.PS.next()
                    for kc in range(nkc):
                        self.mm(P, P[:], wob[:, kc, oc * 128:(oc + 1) * 128], yb[:, kc, :], kc == 0, kc == nkc - 1, [wob, yb])
                    self.tt("dve", xs[:, oc, :], P[:], xs[:, oc, :], ALU.add, [P, xs], [xs])
                S.dma("pool", self.xview(xout, b), xs[:], reads=[xs], writes=[xout.blk[b]])
            S.barrier()

    def phase_odd_a(self, o, l, xin, yT, qT, kT, vtm):
        S = self.S
        T = self.T
        with contextlib.ExitStack() as es:
            wib = S.sb("oa_wi", [128, KC, 3584], BF16, es)
            wab = S.sb("oa_wa", [128, 8, 128], BF16, es)
            wxb = S.sb("oa_wx", [128, 8, 128], BF16, es)
            xs = S.sb("oa_xs", [128, KC, TB], F32, es)
            hb = S.sb("oa_hb", [128, KC, TB], BF16, es)
            sq = S.sb("oa_sq", [128, KC, TB], BF16, es)
            rst = S.sb("oa_rst", [128, TB], F32, es)
            xr = [[S.sb("oa_xr%d_%d" % (c, i), [128, TB + 3], F32, es) for i in range(2)] for c in range(8)]
            hst = S.sb("oa_hst", [128, 8], F32, es)
            f32t = Rot([S.sb("oa_f%d" % i, [128, TB], F32, es) for i in range(12)])
            b16t = Rot([S.sb("oa_b%d" % i, [128, TB], BF16, es) for i in range(6)])
            vb = Rot([S.sb("oa_vb%d" % i, [128, 512], BF16, es) for i in range(2)])
            self.load_wcast(wib, self.w["od_w_in"][o], KC, 3584)
            S.dma("pool", wab[:], self.w["lru_wa"][o].rearrange("k i j -> i k j"), writes=[wab])
            S.dma("pool", wxb[:], self.w["lru_wx"][o].rearrange("k i j -> i k j"), writes=[wxb])
            S.op("dve", lambda e: e.memset(hst[:], 0.0), writes=[hst])
            for c in range(8):
                S.op("pool", lambda e, c=c: e.memset(xr[c][1][:, TB:TB + 3], 0.0), writes=[xr[c][1]])
            cw = self.lru_cw[o]
            cb = self.lru_cb[o]
            ba = self.lru_ba[o]
            bx = self.lru_bx[o]
            cl = self.lru_cl[o]
            cst = self.cst
            for b in range(self.NB):
                S.dma("sp", xs[:], self.xview(xin, b), reads=[xin.blk[b]], writes=[xs])
                self.norm(xs, self.g_mix[l], hb, (sq, rst))
                for c in range(8):
                    cur = xr[c][b % 2]
                    prv = xr[c][(b + 1) % 2]
                    Pg = self.PS.next()
                    for kc in range(KC):
                        self.mm(Pg, Pg[:], wib[:, kc, c * 128:(c + 1) * 128], hb[:, kc, :], kc == 0, kc == KC - 1, [wib, hb])
                    gl = f32t.next()
                    self.act(gl[:], Pg[:], AF.Gelu_apprx_tanh, [Pg], [gl])
                    Px = self.PS.next()
                    for kc in range(KC):
                        self.mm(Px, Px[:], wib[:, kc, 1024 + c * 128:1024 + (c + 1) * 128], hb[:, kc, :], kc == 0, kc == KC - 1, [wib, hb])
                    self.copy("act", cur[:, 3:TB + 3], Px[:], [Px], [cur])
                    self.copy("pool", cur[:, 0:3], prv[:, TB:TB + 3], [prv], [cur])
                    xcv = f32t.next()
                    self.ts("dve", xcv[:], cur[:, 3:TB + 3], cw[:, c, 3:4], ALU.mult, [cur, cst], [xcv], s2=cb[:, c:c + 1], op1=ALU.add)
                    for k in range(3):
                        self.stt(xcv[:], cur[:, k:k + TB], cw[:, c, k:k + 1], xcv[:], ALU.mult, ALU.add, [cur, xcv, cst], [xcv])
                    xcb = b16t.next()
                    self.copy("pool", xcb[:], xcv[:], [xcv], [xcb])
                    Pr = self.PS.next()
                    self.mm(Pr, Pr[:], wab[:, c, :], xcb[:], True, True, [wab, xcb])
                    Pi = self.PS.next()
                    self.mm(Pi, Pi[:], wxb[:, c, :], xcb[:], True, True, [wxb, xcb])
                    rr = f32t.next()
                    self.act(rr[:], Pr[:], AF.Sigmoid, [Pr, cst], [rr], bias=ba[:, c:c + 1])
                    ii = f32t.next()
                    self.act(ii[:], Pi[:], AF.Sigmoid, [Pi, cst], [ii], bias=bx[:, c:c + 1])
                    aa = f32t.next()
                    self.act(aa[:], rr[:], AF.Exp, [rr, cst], [aa], scale=cl[:, c:c + 1])
                    a2 = f32t.next()
                    self.tt("pool", a2[:], aa[:], aa[:], ALU.mult, [aa], [a2])
                    self.act(a2[:], a2[:], AF.Sqrt, [a2], [a2], scale=-1.0, bias=self.oneb[:])
                    self.tt("pool", ii[:], ii[:], xcv[:], ALU.mult, [ii, xcv], [ii])
                    self.tt("dve", ii[:], ii[:], a2[:], ALU.mult, [ii, a2], [ii])
                    hh = f32t.next()
                    S.op("dve", lambda e, hh=hh, aa=aa, ii=ii, c=c: e.tensor_tensor_scan(out=hh[:], data0=aa[:], data1=ii[:], initial=hst[:, c:c + 1], op0=ALU.mult, op1=ALU.add),
                         reads=[aa, ii, hst], writes=[hh])
                    self.copy("pool", hst[:, c:c + 1], hh[:, TB - 1:TB], [hh], [hst])
                    yc = b16t.next()
                    self.tt("dve", yc[:], hh[:], gl[:], ALU.mult, [hh, gl], [yc])
                    S.dma("pool", yT.ap[c * 128:(c + 1) * 128, b * TB:(b + 1) * TB], yc[:], reads=[yc], writes=[yT.blk[b]], par=True)
                for qi in range(8):
                    isq = qi < 4
                    col0 = 2048 + qi * 128
                    Pq = self.PS.next()
                    for kc in range(KC):
                        self.mm(Pq, Pq[:], wib[:, kc, col0:col0 + 128], hb[:, kc, :], kc == 0, kc == KC - 1, [wib, hb])
                    s2 = b16t.next()
                    self.act(s2[:], Pq[:], AF.Square, [Pq], [s2])
                    Ps = self.PS.next()
                    self.mm(Ps, Ps[:], self.bones[:], s2[:], True, True, [self.bones, s2])
                    r2 = f32t.next()
                    self.act(r2[:], Ps[:], AF.Ln, [Ps], [r2], scale=1.0 / 64, bias=self.epsb[:])
                    self.act(r2[:], r2[:], AF.Exp, [r2], [r2], scale=-0.5)
                    qo = b16t.next()
                    gn = (self.sb_qn if isq else self.sb_kn)[o]
                    self.stt(qo[:], Pq[:], gn[:, 0:1], r2[:], ALU.mult, ALU.mult, [Pq, r2, cst], [qo])
                    dst = qT if isq else kT
                    r0 = (qi % 4) * 128
                    S.dma("pool", dst.ap[r0:r0 + 128, b * TB:(b + 1) * TB], qo[:], reads=[qo], writes=[dst.blk[b]], par=True)
                for tt_ in range(4):
                    Pv = self.PS.next()
                    for kc in range(KC):
                        self.mm(Pv, Pv[:], hb[:, kc, tt_ * 128:(tt_ + 1) * 128], wib[:, kc, 3072:3584], kc == 0, kc == KC - 1, [wib, hb])
                    vv = vb.next()
                    self.copy("act", vv[:], Pv[:], [Pv], [vv])
                    t0 = b * TB + tt_ * 128
                    S.dma("pool", vtm.ap[t0:t0 + 128, 0:512], vv[:], reads=[vv], writes=[vtm.blk[b]], par=True)
            S.barrier()

    def phase_sb(self, qT, kT, vtm, yT):
        S = self.S
        T = self.T
        NKB = T // 128
        scale = 64 ** -0.5
        with contextlib.ExitStack() as es:
            def rot(name, shape, dt, n):
                return Rot([S.sb("%s%d" % (name, i), shape, dt, es) for i in range(n)])
            qh = [S.sb("sb_q%d" % i, [64, T], BF16, es) for i in range(2)]
            kh = [S.sb("sb_k%d" % i, [64, T], BF16, es) for i in range(2)]
            vh = [S.sb("sb_v%d" % i, [128, NKB, 64], BF16, es) for i in range(2)]
            eeR = rot("sb_ee", [128, TB], F32, 2); spR = rot("sb_sp", [128, TB], F32, 7)
            l1R = rot("sb_l1", [128, TB], BF16, 7)
            argR = rot("sb_arg", [128, TB], F32, 3); wwR = rot("sb_ww", [128, TB], BF16, 4)
            ot = rot("sb_o", [64, TB], BF16, 2)
            PzR = Rot([self.psb[0], self.psb[1], self.psb[2], self.psb[3]]); PsfR = Rot([self.psh[0], self.psh[1]])
            PcR = Rot([self.psh[2]]); PaccR = Rot([self.psh[3]])

            def load_head(h):
                q, k, v = qh[h % 2], kh[h % 2], vh[h % 2]
                S.dma("sp", q[:], qT.ap[h * 64:(h + 1) * 64, :], reads=qT.all, writes=[q])
                S.dma("sp", k[:], kT.ap[h * 64:(h + 1) * 64, :], reads=kT.all, writes=[k])
                S.dma("sp", v[:], vtm.ap[:, h * 64:(h + 1) * 64].rearrange("(n p) d -> p n d", p=128), reads=vtm.all, writes=[v])

            units = []
            for h in range(8):
                for Q in range(self.NB):
                    top = 4 * Q + 3
                    pc0 = None
                    for kb in range(top, -1, -1):
                        loc = kb - 4 * Q
                        c0 = max(loc, 0) * 128
                        units.append(dict(h=h, Q=Q, kb=kb, first=(kb == top), last=(kb == 0), newhead=(Q == 0 and kb == top),
                                          c0=c0, pc0=pc0, diag=(loc >= 0)))
                        pc0 = c0
            st = {}

            def pe_z(u):
                h, Q, kb, c0 = u["h"], u["Q"], u["kb"], u["c0"]
                if u["newhead"] and h == 0:
                    load_head(0)
                q, k = qh[h % 2], kh[h % 2]
                Pz = PzR.next()
                self.mm(Pz, Pz[:, c0:TB], k[:, kb * 128:(kb + 1) * 128], q[:, Q * TB + c0:(Q + 1) * TB], True, True, [k, q])
                u["Pz"] = Pz

            def act_sp(u):
                c0, Pz = u["c0"], u["Pz"]
                ee = eeR.next()
                self.act(ee[:, c0:TB], Pz[:, c0:TB], AF.Exp, [Pz], [ee], scale=-scale)
                sp_ = spR.next()
                self.act(sp_[:, c0:TB], ee[:, c0:TB], AF.Ln, [ee], [sp_], bias=self.oneb[:])
                u["sp"] = sp_

            def dve_l1(u):
                c0 = u["c0"]
                Pz, sp_ = u["Pz"], u["sp"]
                l1 = l1R.next()
                self.stt(l1[:, c0:TB], Pz[:, c0:TB], -scale, sp_[:, c0:TB], ALU.mult, ALU.subtract, [Pz, sp_], [l1])
                if u["diag"]:
                    self.tt("pool", l1[:, c0:c0 + 128], l1[:, c0:c0 + 128], self.mask_lt[:], ALU.mult, [l1, self.cst], [l1])
                    if c0 > 0:
                        S.op("pool", lambda e, l1=l1, c0=c0: e.memset(l1[:, 0:c0], 0.0), writes=[l1])
                u["l1"] = l1

            def pe_suffix(u):
                c0 = u["c0"]
                Psf = PsfR.next()
                self.mm(Psf, Psf[:, c0:TB], self.umat[:], u["l1"][:, c0:TB], True, True, [self.umat, u["l1"]])
                u["Psf"] = Psf

            def dve_arg(u):
                c0, pc0 = u["c0"], u["pc0"]
                sp_, Psf = u["sp"], u["Psf"]
                if u["first"]:
                    st["Pc_r"] = PcR.next()
                Pc = u["Pc"] = st["Pc_r"]
                arg = argR.next()
                self.tt("dve", arg[:, c0:TB], Psf[:, c0:TB], sp_[:, c0:TB], ALU.subtract, [Psf, sp_], [arg])
                if not u["first"]:
                    self.tt("dve", arg[:, c0:TB], Pc[:, c0:TB], arg[:, c0:TB], ALU.add, [Pc, arg], [arg])
                u["arg"] = arg

            def pe_colsum(u):
                c0 = u["c0"]
                if not u["last"]:
                    self.mm(u["Pc"], u["Pc"][:, 0:TB], self.ones[:], u["l1"][:, 0:TB], u["first"], False, [self.ones, u["l1"]])

            def act_w(u):
                c0 = u["c0"]
                ww = wwR.next()
                self.act(ww[:, c0:TB], u["arg"][:, c0:TB], AF.Exp, [u["arg"]], [ww])
                if u["diag"]:
                    self.tt("pool", ww[:, c0:c0 + 128], ww[:, c0:c0 + 128], self.mask_lt[:], ALU.mult, [ww, self.cst], [ww])
                    if c0 > 0:
                        S.op("pool", lambda e, ww=ww, c0=c0: e.memset(ww[:, 0:c0], 0.0), writes=[ww])
                u["ww"] = ww

            def pe_wv(u):
                h, Q, kb, c0 = u["h"], u["Q"], u["kb"], u["c0"]
                v = vh[h % 2]
                if u["newhead"] and h + 1 < 8:
                    load_head(h + 1)
                if u["first"]:
                    st["Pacc"] = PaccR.next()
                Pacc = st["Pacc"]
                self.mm(Pacc, Pacc[0:64, 0:TB], v[:, kb, :], u["ww"][:, 0:TB], u["first"], u["last"], [v, u["ww"]])
                if u["last"]:
                    oo = ot.next()
                    self.copy("act", oo[:], Pacc[0:64, 0:TB], [Pacc], [oo])
                    S.dma("pool", yT.ap[1024 + h * 64:1024 + (h + 1) * 64, Q * TB:(Q + 1) * TB], oo[:], reads=[oo], writes=[yT.blk[Q]], par=True)
                u.clear()

            n = len(units)
            import os
            fmap = dict(pe_z=pe_z, pe_suffix=pe_suffix, pe_wv=pe_wv, pe_colsum=pe_colsum, act_sp=act_sp, act_w=act_w, dve_arg=dve_arg, dve_l1=dve_l1)
            if os.environ.get("SB_SCHED"):
                sched = tuple((fmap[x.split(":")[0]], int(x.split(":")[1])) for x in os.environ["SB_SCHED"].split(","))
            else:
                sched = ((pe_z, 0), (pe_wv, 7), (pe_suffix, 4), (pe_colsum, 6), (act_sp, 1), (act_w, 6), (dve_arg, 5), (dve_l1, 2))
            for i in range(n + 8):
                for fn, off in sched:
                    j = i - off
                    if 0 <= j < n:
                        fn(units[j])
            S.barrier()

    def transpose_to(self, P, out_cols, src_buf, src_ap):
        pv = P[:].bitcast(BF16)
        self.S.op("pe", lambda e: e.transpose(out=pv[:, out_cols:out_cols + 128], in_=src_ap, identity=self.ident[:]),
                  reads=[src_buf, self.ident], writes=[P])

    def phase_even_a(self, ei, l, xin, sc):
        S = self.S
        T = self.T
        w = self.w
        with contextlib.ExitStack() as es:
            wib = S.sb("ea_wi", [128, KC, 5648], BF16, es)
            xs = S.sb("ea_xs", [128, KC, TB], F32, es)
            hb = S.sb("ea_hb", [128, KC, TB], BF16, es)
            sq = S.sb("ea_sq", [128, KC, TB], BF16, es)
            rst = S.sb("ea_rst", [128, TB], F32, es)
            cosb = S.sb("ea_cos", [128, TB], F32, es)
            sinb = S.sb("ea_sin", [128, TB], F32, es)
            halo = S.sb("ea_halo", [128, 12, 3], F32, es)
            work = Rot([S.sb("ea_wk%d" % i, [128, TB + 3], F32, es) for i in range(2)])
            f32t = Rot([S.sb("ea_f%d" % i, [128, TB], F32, es) for i in range(8)])
            b16t = Rot([S.sb("ea_b%d" % i, [128, TB], BF16, es) for i in range(6)])
            xct = [S.sb("ea_xc%d" % i, [128, TB], BF16, es) for i in range(12)]
            tkb = Rot([S.sb("ea_tk%d" % i, [128, 1024], BF16, es) for i in range(3)])
            dts = Rot([S.sb("ea_dt%d" % i, [128, 32], F32, es) for i in range(2)])
            self.load_wcast(wib, w["ev_w_in"][ei], KC, 5648)
            S.op("dve", lambda e: e.memset(halo[:], 0.0), writes=[halo])
            cst = self.cst
            cw = self.ssd_cw[ei]; cb = self.ssd_cb[ei]
            qn = self.ret_qn[ei]; kn = self.ret_kn[ei]
            for b in range(self.NB):
                bs = slice(b * TB, (b + 1) * TB)
                S.dma("sp", xs[:], self.xview(xin, b), reads=[xin.blk[b]], writes=[xs])
                S.dma("sp", cosb[:], w["c_cos"][:, bs], writes=[cosb])
                S.dma("sp", sinb[:], w["c_sin"][:, bs], writes=[sinb])
                self.norm(xs, self.g_mix[l], hb, (sq, rst))
                for qi in range(8):
                    isq = qi < 4
                    col0 = qi * 128
                    Pq = self.PS.next()
                    for kc in range(KC):
                        self.mm(Pq, Pq[:], wib[:, kc, col0:col0 + 128], hb[:, kc, :], kc == 0, kc == KC - 1, [wib, hb])
                    qraw = f32t.next()
                    self.copy("act", qraw[:], Pq[:], [Pq], [qraw])
                    s2 = b16t.next()
                    self.act(s2[:], Pq[:], AF.Square, [Pq], [s2])
                    Ps = self.PS.next()
                    self.mm(Ps, Ps[:], self.ones[:], s2[:], True, True, [self.ones, s2])
                    r2 = f32t.next()
                    self.act(r2[:], Ps[:], AF.Ln, [Ps], [r2], scale=1.0 / 128, bias=self.epsb[:])
                    self.act(r2[:], r2[:], AF.Exp, [r2], [r2], scale=-0.5)
                    Pw = self.PS.next()
                    self.mm(Pw, Pw[:], self.swapm[:], qraw[:], True, True, [self.swapm, qraw])
                    gn = qn if isq else kn
                    t1 = f32t.next()
                    self.stt(t1[:], qraw[:], gn[:, 0:1], cosb[:], ALU.mult, ALU.mult, [qraw, cosb, cst], [t1])
                    t2 = f32t.next()
                    self.stt(t2[:], Pw[:], gn[:, 1:2], sinb[:], ALU.mult, ALU.mult, [Pw, sinb, cst], [t2])
                    self.tt("pool", t1[:], t1[:], t2[:], ALU.add, [t1, t2], [t1])
                    qo = b16t.next()
                    self.stt(qo[:], t1[:], 1.0 if isq else 128 ** -0.5, r2[:], ALU.mult, ALU.mult, [t1, r2], [qo])
                    dst = sc["qr"] if isq else sc["kr"]
                    r0 = (qi % 4) * 128
                    S.dma("pool", dst.ap[r0:r0 + 128, bs], qo[:], reads=[qo], writes=[dst.blk[b]], par=True)
                for (c0, dstn, fn) in ((1024, "v", None), (2048, "g", AF.Silu), (3072, "z", AF.Silu)):
                    for tt_ in range(4):
                        tk = tkb.next()
                        for hf in range(2):
                            Pv = self.PS.next()
                            for kc in range(KC):
                                self.mm(Pv, Pv[:], hb[:, kc, tt_ * 128:(tt_ + 1) * 128], wib[:, kc, c0 + hf * 512:c0 + (hf + 1) * 512],
                                        kc == 0, kc == KC - 1, [wib, hb])
                            if fn is None:
                                self.copy("act", tk[:, hf * 512:(hf + 1) * 512], Pv[:], [Pv], [tk])
                            else:
                                self.act(tk[:, hf * 512:(hf + 1) * 512], Pv[:], fn, [Pv], [tk])
                        t0 = b * TB + tt_ * 128
                        S.dma("pool", sc[dstn].ap[t0:t0 + 128, :], tk[:], reads=[tk], writes=[sc[dstn].blk[b]], par=True)
                for tt_ in range(4):
                    Pd = self.PS.next()
                    for kc in range(KC):
                        self.mm(Pd, Pd[:, 0:16], hb[:, kc, tt_ * 128:(tt_ + 1) * 128], wib[:, kc, 5632:5648], kc == 0, kc == KC - 1, [wib, hb])
                    dd = dts.next()
                    self.tt("dve", dd[:, 0:16], Pd[:, 0:16], self.ssd_dtb[ei], ALU.add, [Pd, cst], [dd])
                    self.act(dd[:, 0:16], dd[:, 0:16], AF.Exp, [dd], [dd])
                    self.act(dd[:, 0:16], dd[:, 0:16], AF.Ln, [dd], [dd], bias=self.oneb[:])
                    self.tt("dve", dd[:, 16:32], dd[:, 0:16], self.ssd_A[ei], ALU.mult, [dd, cst], [dd])
                    t0 = b * TB + tt_ * 128
                    S.dma("pool", sc["dtA"].ap[t0:t0 + 128, :], dd[:], reads=[dd], writes=[sc["dtA"].blk[b]], par=True)
                for c in range(12):
                    Px = self.PS.next()
                    col0 = 4096 + c * 128
                    for kc in range(KC):
                        self.mm(Px, Px[:], wib[:, kc, col0:col0 + 128], hb[:, kc, :], kc == 0, kc == KC - 1, [wib, hb])
                    wk = work.next()
                    self.copy("act", wk[:, 3:TB + 3], Px[:], [Px], [wk])
                    self.copy("pool", wk[:, 0:3], halo[:, c, :], [halo], [wk])
                    self.copy("pool", halo[:, c, :], wk[:, TB:TB + 3], [wk], [halo])
                    xcv = f32t.next()
                    self.ts("dve", xcv[:], wk[:, 3:TB + 3], cw[:, c, 3:4], ALU.mult, [wk, cst], [xcv], s2=cb[:, c:c + 1], op1=ALU.add)
                    for k in range(3):
                        self.stt(xcv[:], wk[:, k:k + TB], cw[:, c, k:k + 1], xcv[:], ALU.mult, ALU.add, [wk, xcv, cst], [xcv])
                    self.act(xct[c][:], xcv[:], AF.Silu, [xcv], [xct[c]])
                    if c >= 8:
                        dstn = "BT" if c < 10 else "CT"
                        r0 = (c % 2) * 128
                        S.dma("pool", sc[dstn].ap[r0:r0 + 128, bs], xct[c][:], reads=[xct[c]], writes=[sc[dstn].blk[b]], par=True)
                for tt_ in range(4):
                    tk = tkb.next()
                    Pt = self.PS.next()
                    for c in range(8):
                        self.transpose_to(Pt, c * 128, xct[c], xct[c][:, tt_ * 128:(tt_ + 1) * 128])
                    self.copy("act", tk[:], Pt[:].bitcast(BF16), [Pt], [tk])
                    t0 = b * TB + tt_ * 128
                    S.dma("pool", sc["xs"].ap[t0:t0 + 128, :], tk[:], reads=[tk], writes=[sc["xs"].blk[b]], par=True)
                    tk2 = tkb.next()
                    Pt2 = self.PS.next()
                    for c in range(2):
                        self.transpose_to(Pt2, c * 128, xct[8 + c], xct[8 + c][:, tt_ * 128:(tt_ + 1) * 128])
                    self.copy("act", tk2[:, 0:256], Pt2[:].bitcast(BF16)[:, 0:256], [Pt2], [tk2])
                    S.dma("pool", sc["Btm"].ap[t0:t0 + 128, :], tk2[:, 0:256], reads=[tk2], writes=[sc["Btm"].blk[b]], par=True)
            S.barrier()

    def phase_even_b(self, ei, sc, yT):
        S = self.S
        T = self.T
        NCH = T // 128
        cst = self.cst
        gam = [1.0 - 2.0 ** (-5.0 - h) for h in range(4)]
        with contextlib.ExitStack() as es:
            def rot(name, shape, dt, n=2):
                return Rot([S.sb("%s%d" % (name, i), shape, dt, es) for i in range(n)])
            qrb = rot("eb_q", [128, 4, 128], BF16); krb = rot("eb_k", [128, 4, 128], BF16)
            vb = rot("eb_v", [128, 1024], BF16); gb = rot("eb_g", [128, 1024], BF16); zb = rot("eb_z", [128, 1024], BF16)
            xsb = rot("eb_xs", [128, 16, 64], BF16); btmb = rot("eb_bt", [128, 2, 128], BF16)
            BTb = rot("eb_BT", [128, 2, 128], BF16); CTb = rot("eb_CT", [128, 2, 128], BF16)
            dtb = rot("eb_dt", [128, 32], F32)
            Sst = S.sb("eb_S", [128, 4, 256], F32, es); Sbf = S.sb("eb_Sbf", [128, 4, 256], BF16, es)
            STs = S.sb("eb_ST", [128, 16, 64], F32, es); STbf = S.sb("eb_STbf", [128, 16, 64], BF16, es)
            smb = rot("eb_sm", [128, 4, 128], BF16); ktmb = rot("eb_ktm", [128, 4, 128], BF16); q2b = rot("eb_q2", [128, 4, 128], BF16)
            Dm = S.sb("eb_D", [128, 16, 128], F32, es)
            acol = rot("eb_acol", [128, 48], F32)
            segb = rot("eb_seg", [128, 4, 128], F32, 3); ltb = rot("eb_lt", [128, 4, 128], F32, 2); erb = rot("eb_er", [128, 4, 128], F32, 2)
            cbm = rot("eb_cbm", [128, 2, 128], F32)
            MTb = S.sb("eb_MT", [128, 16, 128], BF16, es); CsTb = S.sb("eb_CsT", [128, 16, 128], BF16, es)
            xdtb = rot("eb_xdt", [128, 16, 64], BF16); xdt2b = rot("eb_xdt2", [128, 16, 64], BF16)
            yf = rot("eb_yf", [128, 1024], F32, 2)
            yab = rot("eb_ya", [128, 1024], BF16, 2); ybb = rot("eb_yb", [128, 1024], BF16, 2)
            stat = rot("eb_stat", [128, 4, 6], F32); mv = rot("eb_mv", [128, 4, 2], F32); rs4 = rot("eb_rs4", [128, 8], F32)
            yTo = rot("eb_yTo", [128, 16, 128], BF16, 2)
            junk = S.sb("eb_junk", [128, 512], BF16, es)
            S.op("dve", lambda e: e.memset(Sst[:], 0.0), writes=[Sst])
            S.op("pool", lambda e: e.memset(Sbf[:], 0.0), writes=[Sbf])
            S.op("dve", lambda e: e.memset(STs[:], 0.0), writes=[STs])
            S.op("pool", lambda e: e.memset(STbf[:], 0.0), writes=[STbf])
            gnr_b = S.sb("eb_gnr", [128, 1024], F32, es); nrr_b = S.sb("eb_nrr", [128, 1024], F32, es)
            S.dma("sp", gnr_b[:], self.w["ret_gn"][ei].partition_broadcast(128), writes=[gnr_b])
            S.dma("sp", nrr_b[:], self.w["ssd_norm"][ei].partition_broadcast(128), writes=[nrr_b])
            gnrow = gnr_b[:]; nrow = nrr_b[:]; dsk = self.ssd_D[ei]
            for c in range(NCH):
                b = c // 4
                ts_ = slice(c * 128, (c + 1) * 128)
                qr = qrb.next(); kr = krb.next(); v = vb.next(); g = gb.next(); z = zb.next(); xs = xsb.next()
                btm = btmb.next(); BT = BTb.next(); CT = CTb.next(); dt = dtb.next()
                S.dma("sp", qr[:], sc["qr"].ap[:, ts_].rearrange("(h d) t -> d h t", d=128), reads=[sc["qr"].blk[b]], writes=[qr])
                S.dma("sp", kr[:], sc["kr"].ap[:, ts_].rearrange("(h d) t -> d h t", d=128), reads=[sc["kr"].blk[b]], writes=[kr])
                S.dma("sp", v[:], sc["v"].ap[ts_, :], reads=[sc["v"].blk[b]], writes=[v])
                S.dma("sp", g[:], sc["g"].ap[ts_, :], reads=[sc["g"].blk[b]], writes=[g])
                S.dma("sp", z[:], sc["z"].ap[ts_, :], reads=[sc["z"].blk[b]], writes=[z])
                S.dma("sp", xs[:], sc["xs"].ap[ts_, :].rearrange("t (h p) -> t h p", p=64), reads=[sc["xs"].blk[b]], writes=[xs])
                S.dma("sp", btm[:], sc["Btm"].ap[ts_, :].rearrange("t (g s) -> t g s", s=128), reads=[sc["Btm"].blk[b]], writes=[btm])
                S.dma("sp", BT[:], sc["BT"].ap[:, ts_].rearrange("(g s) t -> s g t", s=128), reads=[sc["BT"].blk[b]], writes=[BT])
                S.dma("sp", CT[:], sc["CT"].ap[:, ts_].rearrange("(g s) t -> s g t", s=128), reads=[sc["CT"].blk[b]], writes=[CT])
                S.dma("sp", dt[:], sc["dtA"].ap[ts_, :], reads=[sc["dtA"].blk[b]], writes=[dt])

                Psc = self.PS.next()
                for h in range(4):
                    self.mm(Psc, Psc[:, h * 128:(h + 1) * 128], kr[:, h, :], qr[:, h, :], True, True, [kr, qr])
                sm = smb.next()
                self.tt("dve", sm[:].rearrange("p h i -> p (h i)"), Psc[:], self.ret_decay[:], ALU.mult, [Psc, cst], [sm])
                Pkt = self.PS.next()
                for h in range(4):
                    self.transpose_to(Pkt, h * 128, kr, kr[:, h, :])
                ktm = ktmb.next()
                self.tt("dve", ktm[:], Pkt[:].bitcast(BF16)[:, 0:512].rearrange("p (h d) -> p h d", d=128),
                        self.ret_kdec[:].unsqueeze(2).broadcast_to([128, 4, 128]), ALU.mult, [Pkt, cst], [ktm])
                q2 = q2b.next()
                self.tt("pool", q2[:].rearrange("p h i -> p (h i)"), qr[:].rearrange("p h i -> p (h i)"), self.ret_qdec[:], ALU.mult, [qr, cst], [q2])
                POw = self.PW.next()
                for h in range(4):
                    PO = POw[h // 2]
                    hc = slice((h % 2) * 256, (h % 2 + 1) * 256)
                    self.mm(PO, PO[:, hc], sm[:, h, :], v[:, h * 256:(h + 1) * 256], True, False, [sm, v])
                    self.mm(PO, PO[:, hc], q2[:, h, :], Sbf[:, h, :], False, True, [q2, Sbf])
                Pkvw = self.PW.next()
                for h in range(4):
                    Pkv = Pkvw[h // 2]
                    self.mm(Pkv, Pkv[:, (h % 2) * 256:(h % 2 + 1) * 256], ktm[:, h, :], v[:, h * 256:(h + 1) * 256], True, True, [ktm, v])
                for h in range(4):
                    Pkv = Pkvw[h // 2]
                    self.stt(Sst[:, h, :], Sst[:, h, :], gam[h] ** 128, Pkv[:, (h % 2) * 256:(h % 2 + 1) * 256], ALU.mult, ALU.add, [Sst, Pkv], [Sst])
                self.copy("act", Sbf[:], Sst[:], [Sst], [Sbf])
                st = stat.next(); m2 = mv.next(); r4 = rs4.next()
                for h in range(4):
                    PO = POw[h // 2]
                    S.op("dve", lambda e, h=h, st=st, PO=PO: e.bn_stats(out=st[:, h, :], in_=PO[:, (h % 2) * 256:(h % 2 + 1) * 256]), reads=[PO], writes=[st])
                for h in range(4):
                    S.op("dve", lambda e, h=h, st=st, m2=m2: e.bn_aggr(out=m2[:, h, :], in_=st[:, h, :]), reads=[st], writes=[m2])
                self.act(r4[:, 0:4], m2[:, :, 1], AF.Ln, [m2], [r4], bias=self.epsb[:])
                self.act(r4[:, 0:4], r4[:, 0:4], AF.Exp, [r4], [r4], scale=-0.5)
                y1 = yf.next()
                for h in range(4):
                    PO = POw[h // 2]
                    self.ts("dve", y1[:, h * 256:(h + 1) * 256], PO[:, (h % 2) * 256:(h % 2 + 1) * 256], m2[:, h, 0:1], ALU.subtract, [PO, m2, r4], [y1],
                            s2=r4[:, h:h + 1], op1=ALU.mult)
                self.tt("pool", y1[:], y1[:], gnrow, ALU.mult, [y1, gnr_b], [y1])
                ya = yab.next()
                self.tt("pool", ya[:], y1[:], g[:], ALU.mult, [y1, g], [ya])

                PA = self.PS.next()
                self.mm(PA, PA[:, 0:16], self.tri_f[:], dt[:, 16:32], True, True, [self.tri_f, dt])
                self.mm(PA, PA[:, 16:32], self.ones_f[:], dt[:, 16:32], True, True, [self.ones_f, dt])
                ac = acol.next()
                self.copy("act", ac[:, 0:32], PA[:, 0:32], [PA], [ac])
                self.tt("dve", ac[:, 32:48], ac[:, 16:32], ac[:, 0:16], ALU.subtract, [ac], [ac])
                self.act(ac[:, 32:48], ac[:, 32:48], AF.Exp, [ac], [ac])
                self.act(ac[:, 16:32], ac[:, 16:32], AF.Exp, [ac], [ac])
                self.tt("dve", Dm[:], dt[:, 16:32].unsqueeze(2).broadcast_to([128, 16, 128]),
                        self.tri_f[:].unsqueeze(1).broadcast_to([128, 16, 128]), ALU.mult, [dt, self.tri_f], [Dm])
                PCB = self.PS.next()
                for gi in range(2):
                    self.mm(PCB, PCB[:, gi * 128:(gi + 1) * 128], BT[:, gi, :], CT[:, gi, :], True, True, [BT, CT])
                cb_ = cbm.next()
                self.tt("dve", cb_[:].rearrange("p g i -> p (g i)"), PCB[:, 0:256], self.tri2[:], ALU.mult, [PCB, cst], [cb_])
                for q4 in range(4):
                    gi = q4 // 2
                    hs = slice(q4 * 4, q4 * 4 + 4)
                    Prb = self.PS.next()
                    self.mm(Prb, Prb[:], self.ones_f[:], Dm[:, hs, :].rearrange("p h i -> p (h i)"), True, True, [self.ones_f, Dm])
                    sg = segb.next()
                    self.tt("dve", sg[:], Prb[:].rearrange("p (h i) -> p h i", i=128), ac[:, q4 * 4:q4 * 4 + 4].unsqueeze(2).broadcast_to([128, 4, 128]),
                            ALU.subtract, [Prb, ac], [sg])
                    self.ts("pool", sg[:], sg[:], 0.0, ALU.min, [sg], [sg])
                    lt = ltb.next()
                    self.act(lt[:], sg[:], AF.Exp, [sg], [lt])
                    er = erb.next()
                    self.act(er[:].rearrange("p h i -> p (h i)"), Prb[:], AF.Exp, [Prb], [er])
                    self.tt("dve", MTb[:, hs, :], lt[:], cb_[:, gi, :].unsqueeze(1).broadcast_to([128, 4, 128]), ALU.mult, [lt, cb_], [MTb])
                    self.tt("pool", CsTb[:, hs, :], er[:], CT[:, gi, :].unsqueeze(1).broadcast_to([128, 4, 128]), ALU.mult, [er, CT], [CsTb])
                xdt = xdtb.next(); xdt2 = xdt2b.next()
                self.tt("dve", xdt[:], xs[:], dt[:, 0:16].unsqueeze(2).broadcast_to([128, 16, 64]), ALU.mult, [xs, dt], [xdt])
                self.tt("pool", xdt2[:], xdt[:], ac[:, 32:48].unsqueeze(2).broadcast_to([128, 16, 64]), ALU.mult, [xdt, ac], [xdt2])
                PYw = self.PW.next()
                for h in range(16):
                    PY = PYw[h // 8]
                    hc = slice((h % 8) * 64, (h % 8 + 1) * 64)
                    self.mm(PY, PY[:, hc], MTb[:, h, :], xdt[:, h, :], True, False, [MTb, xdt])
                    self.mm(PY, PY[:, hc], CsTb[:, h, :], STbf[:, h, :], False, True, [CsTb, STbf])
                PStw = self.PW.next()
                for h in range(16):
                    PSt = PStw[h // 8]
                    self.mm(PSt, PSt[:, (h % 8) * 64:(h % 8 + 1) * 64], btm[:, h // 8, :], xdt2[:, h, :], True, True, [btm, xdt2])
                self.tt("dve", STs[:], STs[:], ac[:, 16:32].unsqueeze(2).broadcast_to([128, 16, 64]), ALU.mult, [STs, ac], [STs])
                STf = STs[:].rearrange("p h d -> p (h d)")
                for gi in range(2):
                    self.tt("dve", STf[:, gi * 512:(gi + 1) * 512], PStw[gi][:], STf[:, gi * 512:(gi + 1) * 512], ALU.add, [PStw[gi], STs], [STs])
                self.copy("act", STbf[:], STs[:], [STs], [STbf])
                y2 = yf.next()
                self.tt("pool", y2[:].rearrange("p (h d) -> p h d", d=64), xs[:], dsk.unsqueeze(2).broadcast_to([128, 16, 64]), ALU.mult, [xs, cst], [y2])
                for gi in range(2):
                    self.tt("dve", y2[:, gi * 512:(gi + 1) * 512], PYw[gi][:], y2[:, gi * 512:(gi + 1) * 512], ALU.add, [PYw[gi], y2], [y2])
                self.tt("pool", y2[:], y2[:], z[:], ALU.mult, [y2, z], [y2])
                r8 = rs4.next()
                for gi in range(2):
                    S.op("act", lambda e, gi=gi, y2=y2, r8=r8: e.activation(out=junk[:], in_=y2[:, gi * 512:(gi + 1) * 512], func=AF.Square, accum_out=r8[:, gi:gi + 1]),
                         reads=[y2], writes=[junk, r8])
                self.act(r8[:, 0:2], r8[:, 0:2], AF.Ln, [r8], [r8], scale=1.0 / 512, bias=self.epsb[:])
                self.act(r8[:, 0:2], r8[:, 0:2], AF.Exp, [r8], [r8], scale=-0.5)
                yb = ybb.next()
                for gi in range(2):
                    self.stt(yb[:, gi * 512:(gi + 1) * 512], y2[:, gi * 512:(gi + 1) * 512], r8[:, gi:gi + 1], nrow[:, gi * 512:(gi + 1) * 512],
                             ALU.mult, ALU.mult, [y2, r8, nrr_b], [yb])
                yo = yTo.next()
                for half, src in ((0, ya), (1, yb)):
                    Pt = self.PS.next()
                    for cc in range(8):
                        self.transpose_to(Pt, cc * 128, src, src[:, cc * 128:(cc + 1) * 128])
                    self.copy("act", yo[:, half * 8:(half + 1) * 8, :].rearrange("p c i -> p (c i)"), Pt[:].bitcast(BF16), [Pt], [yo])
                S.dma("pool", yT.ap[:, ts_].rearrange("(c p) t -> p c t", p=128), yo[:], reads=[yo], writes=[yT.blk[b]], par=True)
            S.barrier()

    def build(self):
        nc, es = self.nc, self.es
        T, NB = self.T, self.NB
        S = self.S = Sched(nc, es)
        ne = sum(1 for x in self.layers if x == "e")
        no = sum(1 for x in self.layers if x == "o")
        L = len(self.layers)
        w = self.w = {}

        def inp(name, shape):
            w[name] = nc.dram_tensor(name, list(shape), F32, kind="ExternalInput").ap()

        self.xin = DT(S, "xT", [D, T], F32, NB, kind="ExternalInput")
        self.xout = DT(S, "outT", [D, T], F32, NB, kind="ExternalOutput")
        inp("norm_mix", [L, D]); inp("norm_mlp", [L, D])
        inp("mlp_w1", [L, D, 4096]); inp("mlp_w2", [L, 4096, D])
        if no:
            inp("od_w_in", [no, D, 3584]); inp("od_w_out", [no, 1536, D])
            inp("lru_conv_w", [no, 4, D]); inp("lru_conv_b", [no, D])
            inp("lru_wa", [no, 8, 128, 128]); inp("lru_ba", [no, 8, 128])
            inp("lru_wx", [no, 8, 128, 128]); inp("lru_bx", [no, 8, 128])
            inp("lru_lam", [no, D]); inp("sb_qn", [no, 64]); inp("sb_kn", [no, 64])
        if ne:
            inp("ev_w_in", [ne, D, 5648]); inp("ev_w_out", [ne, 2048, D])
            inp("ret_qn", [ne, 128]); inp("ret_kn", [ne, 128]); inp("ret_gn", [ne, 1024])
            inp("ssd_conv_w", [ne, 4, 1536]); inp("ssd_conv_b", [ne, 1536])
            inp("ssd_dt_bias", [ne, 16]); inp("ssd_a_log", [ne, 16]); inp("ssd_d", [ne, 16]); inp("ssd_norm", [ne, 1024])
            inp("c_cos", [128, T]); inp("c_sin", [128, T])
            inp("c_ident", [128, 128]); inp("c_swap", [128, 128]); inp("c_tri", [128, 128]); inp("c_tri2", [128, 256])
            inp("c_rdecay", [128, 512]); inp("c_rqdec", [128, 512]); inp("c_rkdec", [128, 4])
        inp("c_masklt", [128, 128]); inp("c_umat", [128, 128]); inp("c_bones", [128, 128])

        allb = [S.ps("ps%d" % i, [128, 512]) for i in range(8)]
        self.psb = allb[0:4]
        self.psh = allb[4:8]
        self.PS = Rot(self.psb)
        self.PW = Rot([(allb[4], allb[5]), (allb[6], allb[7])])

        cst = self.cst = Buf("cst")

        def csb(name, shape, dt=F32):
            return es.enter_context(nc.sbuf_tensor(name, list(shape), dt))

        self.ones = S.sb("ones", [128, 128], BF16)
        S.op("dve", lambda e: e.memset(self.ones[:], 1.0), writes=[self.ones])
        self.epsb = csb("epsb", [128, 1]); self.oneb = csb("oneb", [128, 1])
        S.op("dve", lambda e: e.memset(self.epsb[:], EPS), writes=[cst])
        S.op("dve", lambda e: e.memset(self.oneb[:], 1.0), writes=[cst])
        self.mask_lt = csb("mask_lt", [128, 128], BF16)
        self.umat = S.sb("umat", [128, 128], BF16)
        self.bones = S.sb("bones", [128, 128], BF16)
        S.dma("pool", self.mask_lt[:], w["c_masklt"], writes=[cst])
        S.dma("pool", self.umat[:], w["c_umat"], writes=[self.umat])
        S.dma("pool", self.bones[:], w["c_bones"], writes=[self.bones])
        self.gains = Buf("gains")
        gmix = csb("gmix", [128, L, KC]); gmlp = csb("gmlp", [128, L, KC])
        S.dma("sp", gmix[:], w["norm_mix"].rearrange("l (kc p) -> p l kc", p=128), writes=[self.gains], allow_slow_non_contiguous=True)
        S.dma("sp", gmlp[:], w["norm_mlp"].rearrange("l (kc p) -> p l kc", p=128), writes=[self.gains], allow_slow_non_contiguous=True)
        self.g_mix = [gmix[:, l, :] for l in range(L)]
        self.g_mlp = [gmlp[:, l, :] for l in range(L)]
        if no:
            cw = csb("lru_cw", [128, no, 8, 4]); cb = csb("lru_cb", [128, no, 8])
            ba = csb("lru_ba_s", [128, no, 8]); bx = csb("lru_bx_s", [128, no, 8])
            cl = csb("lru_cl", [128, no, 8])
            qn = csb("sbqn", [128, no]); kn = csb("sbkn", [128, no])
            for o_ in range(no):
                for k_ in range(4):
                    S.dma("sp", cw[:, o_, :, k_], w["lru_conv_w"][o_, k_].rearrange("(c p) -> p c", p=128), writes=[cst], allow_slow_non_contiguous=True)
            S.dma("sp", cb[:], w["lru_conv_b"].rearrange("o (c p) -> p o c", p=128), writes=[cst], allow_slow_non_contiguous=True)
            S.dma("sp", ba[:], w["lru_ba"].rearrange("o c p -> p o c"), writes=[cst], allow_slow_non_contiguous=True)
            S.dma("sp", bx[:], w["lru_bx"].rearrange("o c p -> p o c"), writes=[cst], allow_slow_non_contiguous=True)
            S.dma("sp", cl[:], w["lru_lam"].rearrange("o (c p) -> p o c", p=128), writes=[cst], allow_slow_non_contiguous=True)
            for half in range(2):
                S.dma("sp", qn[half * 64:(half + 1) * 64, :], w["sb_qn"].rearrange("o d -> d o"), writes=[cst], allow_slow_non_contiguous=True)
                S.dma("sp", kn[half * 64:(half + 1) * 64, :], w["sb_kn"].rearrange("o d -> d o"), writes=[cst], allow_slow_non_contiguous=True)
            S.op("act", lambda e: e.activation(out=cl[:], in_=cl[:], func=AF.Exp, scale=-1.0), reads=[cst], writes=[cst])
            S.op("act", lambda e: e.activation(out=cl[:], in_=cl[:], func=AF.Ln, bias=self.oneb[:]), reads=[cst], writes=[cst])
            S.op("dve", lambda e: e.tensor_scalar(out=cl[:], in0=cl[:], scalar1=-8.0, scalar2=None, op0=ALU.mult), reads=[cst], writes=[cst])
            self.lru_cw = [cw[:, o] for o in range(no)]
            self.lru_cb = [cb[:, o] for o in range(no)]
            self.lru_ba = [ba[:, o] for o in range(no)]
            self.lru_bx = [bx[:, o] for o in range(no)]
            self.lru_cl = [cl[:, o] for o in range(no)]
            self.sb_qn = [qn[:, o:o + 1] for o in range(no)]
            self.sb_kn = [kn[:, o:o + 1] for o in range(no)]
        if ne:
            self.ident = S.sb("ident", [128, 128], BF16)
            S.dma("pool", self.ident[:], w["c_ident"], writes=[self.ident])
            self.swapm = S.sb("swapm", [128, 128], F32)
            S.dma("sp", self.swapm[:], w["c_swap"], writes=[self.swapm])
            self.tri_f = S.sb("tri_f", [128, 128], F32)
            S.dma("sp", self.tri_f[:], w["c_tri"], writes=[self.tri_f])
            self.ones_f = S.sb("ones_f", [128, 128], F32)
            S.op("dve", lambda e: e.memset(self.ones_f[:], 1.0), writes=[self.ones_f])
            self.tri2 = csb("tri2", [128, 256]); self.ret_decay = csb("rdecay", [128, 512]); self.ret_qdec = csb("rqdec", [128, 512])
            self.ret_kdec = csb("rkdec", [128, 4])
            S.dma("sp", self.tri2[:], w["c_tri2"], writes=[cst])
            S.dma("sp", self.ret_decay[:], w["c_rdecay"], writes=[cst])
            S.dma("sp", self.ret_qdec[:], w["c_rqdec"], writes=[cst])
            S.dma("sp", self.ret_kdec[:], w["c_rkdec"], writes=[cst])
            scw = csb("ssd_cw", [128, ne, 12, 4]); scb = csb("ssd_cb", [128, ne, 12])
            rqn = csb("ret_qn_s", [128, ne, 2]); rkn = csb("ret_kn_s", [128, ne, 2])
            dsk = csb("ssd_D_r", [128, ne, 16]); dtbr = csb("ssd_dtb_r", [128, ne, 16]); Ar = csb("ssd_A_r", [128, ne, 16])
            for e_ in range(ne):
                for k_ in range(4):
                    S.dma("sp", scw[:, e_, :, k_], w["ssd_conv_w"][e_, k_].rearrange("(c p) -> p c", p=128), writes=[cst], allow_slow_non_contiguous=True)
                S.dma("sp", scb[:, e_, :], w["ssd_conv_b"][e_].rearrange("(c p) -> p c", p=128), writes=[cst], allow_slow_non_contiguous=True)
                for nm, tl in (("ret_qn", rqn), ("ret_kn", rkn)):
                    col = w[nm][e_].rearrange("(d o) -> d o", o=1)
                    S.dma("sp", tl[:, e_, 0:1], col, writes=[cst], allow_slow_non_contiguous=True)
                    S.dma("sp", tl[0:64, e_, 1:2], col[64:128], writes=[cst], allow_slow_non_contiguous=True)
                    S.dma("sp", tl[64:128, e_, 1:2], col[0:64], writes=[cst], allow_slow_non_contiguous=True)
                S.dma("sp", dsk[:, e_, :], w["ssd_d"][e_].partition_broadcast(128), writes=[cst])
                S.dma("sp", dtbr[:, e_, :], w["ssd_dt_bias"][e_].partition_broadcast(128), writes=[cst])
                S.dma("sp", Ar[:, e_, :], w["ssd_a_log"][e_].partition_broadcast(128), writes=[cst])
            S.op("act", lambda e: e.activation(out=Ar[:], in_=Ar[:], func=AF.Exp), reads=[cst], writes=[cst])
            S.op("dve", lambda e: e.tensor_scalar(out=Ar[:], in0=Ar[:], scalar1=-1.0, scalar2=None, op0=ALU.mult), reads=[cst], writes=[cst])
            self.ssd_cw = [scw[:, e_] for e_ in range(ne)]; self.ssd_cb = [scb[:, e_] for e_ in range(ne)]
            self.ret_qn = [rqn[:, e_] for e_ in range(ne)]; self.ret_kn = [rkn[:, e_] for e_ in range(ne)]
            self.ssd_D = [dsk[:, e_] for e_ in range(ne)]; self.ssd_dtb = [dtbr[:, e_] for e_ in range(ne)]; self.ssd_A = [Ar[:, e_] for e_ in range(ne)]
        S.barrier()

        xa = DT(S, "xa", [D, T], F32, NB)
        xb_ = DT(S, "xb", [D, T], F32, NB)
        yT = DT(S, "yT", [2048, T], BF16, NB)
        qT = DT(S, "qT", [512, T], BF16, NB)
        kT = DT(S, "kT", [512, T], BF16, NB)
        vtm = DT(S, "vtm", [T, 1024], BF16, NB)
        sc = {"qr": qT, "kr": kT, "v": vtm}
        if ne:
            for nm in ("g", "z", "xs"):
                sc[nm] = DT(S, "sc_" + nm, [T, 1024], BF16, NB)
            sc["Btm"] = DT(S, "sc_Btm", [T, 256], BF16, NB)
            sc["BT"] = DT(S, "sc_BT", [256, T], BF16, NB)
            sc["CT"] = DT(S, "sc_CT", [256, T], BF16, NB)
            sc["dtA"] = DT(S, "sc_dtA", [T, 32], F32, NB)

        cur = self.xin
        ie = io = 0
        for l, kind in enumerate(self.layers):
            last = l == L - 1
            skip = getattr(self, "skip", ())
            if kind == "o":
                if "odd_a" not in skip:
                    self.phase_odd_a(io, l, cur, yT, qT, kT, vtm)
                if "sb" not in skip:
                    self.phase_sb(qT, kT, vtm, yT)
                if "outproj" not in skip:
                    self.phase_outproj("od_w_out", io, 12, yT, cur, xa)
                io += 1
            else:
                if "even_a" not in skip:
                    self.phase_even_a(ie, l, cur, sc)
                if "even_b" not in skip:
                    self.phase_even_b(ie, sc, yT)
                if "outproj" not in skip:
                    self.phase_outproj("ev_w_out", ie, 16, yT, cur, xa)
                ie += 1
            if "mlp" not in skip:
                self.phase_mlp(l, xa, self.xout if last else xb_)
            cur = xb_
        S.finish()
        es.close()
        return nc


def consts(T=None, even=False):
    i = np.arange(128)
    c = _consts_base(i)
    if even:
        f32 = np.float32
        inv = (f32(10000.0) ** (-(np.arange(64, dtype=f32)) / f32(64))).astype(f32)
        ang = (np.arange(T, dtype=f32)[None, :] * inv[:, None]).astype(f32).astype(np.float64)
        cos = np.cos(ang); sin = np.sin(ang)
        c["c_cos"] = np.concatenate([cos, cos], 0).astype(f32)
        c["c_sin"] = np.concatenate([-sin, sin], 0).astype(f32)
        c["c_ident"] = np.eye(128, dtype=f32)
        c["c_swap"] = (i[:, None] == ((i[None, :] + 64) % 128)).astype(f32)
        tri = (i[:, None] <= i[None, :]).astype(f32)
        c["c_tri"] = tri
        c["c_tri2"] = np.concatenate([tri, tri], 1)
        lg = np.log1p(-np.exp2(-5.0 - np.arange(4, dtype=np.float64)))
        rel = (i[None, :] - i[:, None]).astype(np.float64)
        dec = np.where(rel[:, None, :] >= 0, np.exp(lg[None, :, None] * np.maximum(rel, 0)[:, None, :]), 0.0)
        c["c_rdecay"] = dec.reshape(128, 512).astype(f32)
        qd = np.exp(lg[:, None] * (i[None, :] + 1.0))
        c["c_rqdec"] = np.broadcast_to(qd.reshape(1, 512), (128, 512)).astype(f32).copy()
        c["c_rkdec"] = np.exp(lg[None, :] * (127.0 - i[:, None])).astype(f32)
    return c


def _consts_base(i):
    return {
        "c_masklt": (i[:, None] < i[None, :]).astype(np.float32),
        "c_umat": (i[:, None] > i[None, :]).astype(np.float32),
        "c_bones": ((i[:, None] // 64) == (i[None, :] // 64)).astype(np.float32),
    }


def make_inputs(inputs, b, kinds):
    m = {"xT": np.ascontiguousarray(np.asarray(inputs["x"])[b].T)}
    L = len(kinds)
    for n in ("norm_mix", "norm_mlp", "mlp_w1", "mlp_w2"):
        m[n] = np.ascontiguousarray(np.asarray(inputs[n])[:L])
    if "e" in kinds:
        for n in ("ev_w_in", "ev_w_out", "ret_qn", "ret_kn", "ret_gn", "ssd_conv_w", "ssd_conv_b", "ssd_dt_bias", "ssd_a_log", "ssd_d", "ssd_norm"):
            m[n] = np.ascontiguousarray(np.asarray(inputs[n]))
    if "o" in kinds:
        for n in ("od_w_in", "od_w_out", "lru_conv_w", "lru_conv_b", "lru_wa", "lru_ba", "lru_wx", "lru_bx", "lru_lam", "sb_qn", "sb_kn"):
            m[n] = np.ascontiguousarray(np.asarray(inputs[n]))
    m.update(consts(m["xT"].shape[1], "e" in kinds))
    return m


KINDS = "eoeo"
SEQ = 8192
_NC_CACHE = {}


def kernel(**inputs):
    if "nc" not in _NC_CACHE:
        _NC_CACHE["nc"] = K(SEQ, list(KINDS)).build()
    nc = _NC_CACHE["nc"]
    nb = np.asarray(inputs["x"]).shape[0]
    in_maps = [make_inputs(inputs, b, KINDS) for b in range(nb)]
    res = run_bass_kernel_spmd(nc, in_maps, core_ids=list(range(nb)))
    out = np.stack([np.asarray(res.results[b]["outT"]).T for b in range(nb)])
    return np.ascontiguousarray(out.astype(np.float32))
```

```python
import contextlib
import math
import numpy as np
import concourse.bass as bass
import concourse.mybir as mybir
from concourse.bass_utils import run_bass_kernel_spmd

F32 = mybir.dt.float32
BF16 = mybir.dt.bfloat16
AF = mybir.ActivationFunctionType
ALU = mybir.AluOpType
AX = mybir.AxisListType

D = 1024
KC = 8
TB = 512
EPS = 1e-6


class Buf:
    __slots__ = ("name", "t", "lws", "rd")

    def __init__(self, name, t=None):
        self.name = name
        self.t = t
        self.lws = []
        self.rd = {}

    def __getitem__(self, k):
        return self.t[k]


class Rot:
    def __init__(self, bufs):
        self.bufs = list(bufs)
        self.i = 0

    def next(self):
        b = self.bufs[self.i % len(self.bufs)]
        self.i += 1
        return b


class Sched:
    ENG = ("pe", "act", "dve", "pool", "sp")

    def __init__(self, nc, es, n_dma_sems=32):
        self.nc = nc
        self.es = es
        self.q = {e: [] for e in self.ENG}
        self.cnt = {e: 0 for e in self.ENG}
        self.sem = {e: es.enter_context(nc.semaphore("c_" + e)) for e in ("pe", "act", "dve", "pool")}
        self.dsem = [es.enter_context(nc.semaphore("d%d" % i)) for i in range(n_dma_sems)]
        self.dcnt = [0] * n_dma_sems
        self.dnext = 0
        self.pnext = 0
        self.rec = None
        self.waited = {e: {} for e in self.ENG}
        self.ndma = 0

    def sb(self, name, shape, dt, es=None):
        self.uid = getattr(self, "uid", 0) + 1
        name = "%s_u%d" % (name, self.uid)
        return Buf(name, (es or self.es).enter_context(self.nc.sbuf_tensor(name, list(shape), dt)))

    def ps(self, name, shape, dt=F32):
        return Buf(name, self.es.enter_context(self.nc.psum_tensor(name, list(shape), dt)))

    def dram(self, name, shape, dt, kind="Internal"):
        return Buf(name, self.nc.dram_tensor(name, list(shape), dt, kind=kind).ap())

    def _need(self, eng, dep, waits):
        if dep is None:
            return
        key, val, semh = dep
        w = self.waited[eng]
        if w.get(key, 0) >= val:
            return
        w[key] = val
        waits.append((semh, val))

    def _deps(self, eng, reads, writes, same, par=False):
        waits = []
        for b in reads:
            for lw in b.lws:
                if same or lw[0] != eng:
                    self._need(eng, lw, waits)
        for b in writes:
            if not par:
                for lw in b.lws:
                    if same or lw[0] != eng:
                        self._need(eng, lw, waits)
            for k, d in b.rd.items():
                if same or k != eng:
                    self._need(eng, d, waits)
        return waits

    def record(self, body):
        old = self.rec
        self.rec = []
        body()
        r = self.rec
        self.rec = old
        return r

    def interleave(self, lists):
        its = [list(l) for l in lists]
        pos = [0] * len(its)
        left = sum(len(l) for l in its)
        while left:
            for k, l in enumerate(its):
                if pos[k] < len(l):
                    it = l[pos[k]]
                    pos[k] += 1
                    left -= 1
                    if it[0] == 0:
                        self.op(it[1], it[2], it[3], it[4])
                    else:
                        self.dma(it[1], it[2], it[3], it[4], it[5], it[6], **it[7])

    def op(self, eng, fn, reads=(), writes=()):
        if self.rec is not None:
            self.rec.append((0, eng, fn, list(reads), list(writes)))
            return None
        same = eng != "pe"
        waits = self._deps(eng, reads, writes, same)
        self.cnt[eng] += 1
        tok = (eng, self.cnt[eng], self.sem[eng])
        self.q[eng].append((waits, fn, (self.sem[eng], 1)))
        for b in reads:
            b.rd[eng] = tok
        for b in writes:
            b.lws = [tok]
            b.rd = {}
        return tok

    def dma(self, qeng, out_ap, in_ap, reads=(), writes=(), par=False, **kw):
        if self.rec is not None:
            self.rec.append((1, qeng, out_ap, in_ap, list(reads), list(writes), par, kw))
            return None
        waits = self._deps(qeng, reads, writes, True, par)
        if qeng == "pool":
            s = self.pnext
            self.pnext = (self.pnext + 1) % 4
        else:
            s = 4 + self.dnext
            self.dnext = (self.dnext + 1) % (len(self.dsem) - 4)
        if self.dcnt[s] > 0:
            self._need(qeng, ("d%d" % s, self.dcnt[s], self.dsem[s]), waits)
        self.dcnt[s] += 16
        tok = ("d%d" % s, self.dcnt[s], self.dsem[s])
        self.q[qeng].append((waits, lambda e: e.dma_start(out=out_ap, in_=in_ap, **kw), (self.dsem[s], 16)))
        for b in reads:
            b.rd["dma%d" % self.ndma] = tok
        for b in writes:
            if par:
                if b.rd:
                    b.lws = []
                    b.rd = {}
                b.lws.append(tok)
            else:
                b.lws = [tok]
                b.rd = {}
        self.ndma += 1
        return tok

    def barrier(self):
        for e in self.ENG:
            waits = []
            for e2 in ("pe", "act", "dve", "pool"):
                if e2 != e and self.cnt[e2] > 0:
                    self._need(e, (e2, self.cnt[e2], self.sem[e2]), waits)
            for s in range(len(self.dsem)):
                if self.dcnt[s] > 0:
                    self._need(e, ("d%d" % s, self.dcnt[s], self.dsem[s]), waits)
            if waits:
                self.q[e].append((waits, None, None))

    def finish(self):
        self.barrier()
        nc = self.nc
        q = self.q

        def replay(eh, items):
            for waits, fn, inc in items:
                for semh, val in waits:
                    eh.wait_ge(semh, val)
                if fn is not None:
                    ins = fn(eh)
                    if inc is not None:
                        ins.then_inc(inc[0], inc[1])

        with nc.Block() as block:
            @block.sync
            def _(e):
                replay(e, q["sp"])

            @block.tensor
            def _(e):
                replay(e, q["pe"])

            @block.scalar
            def _(e):
                replay(e, q["act"])

            @block.vector
            def _(e):
                replay(e, q["dve"])

            @block.gpsimd
            def _(e):
                replay(e, q["pool"])


class DT:
    def __init__(self, S, name, shape, dt, nblk, kind="Internal"):
        self.ap = S.nc.dram_tensor(name, list(shape), dt, kind=kind).ap()
        self.blk = [Buf("%s_b%d" % (name, i)) for i in range(nblk)]
        self.all = self.blk


class K:
    def __init__(self, T, layers):
        self.T = T
        self.NB = T // TB
        self.layers = layers
        self.nc = bass.Bass("TRN2", target_bir_lowering=False)
        self.es = contextlib.ExitStack()

    def mm(self, P, out_ap, lhsT, rhs, start, stop, reads):
        self.S.op("pe", lambda e: e.matmul(out_ap, lhsT=lhsT, rhs=rhs, start=start, stop=stop), reads=reads, writes=[P])

    def act(self, out_ap, in_ap, func, reads, writes, **kw):
        self.S.op("act", lambda e: e.activation(out=out_ap, in_=in_ap, func=func, **kw), reads=reads, writes=writes)

    def tt(self, eng, out_ap, a, b, op, reads, writes):
        self.S.op(eng, lambda e: e.tensor_tensor(out=out_ap, in0=a, in1=b, op=op), reads=reads, writes=writes)

    def ts(self, eng, out_ap, a, s1, op0, reads, writes, s2=None, op1=None):
        if op1 is None:
            self.S.op(eng, lambda e: e.tensor_scalar(out=out_ap, in0=a, scalar1=s1, scalar2=None, op0=op0), reads=reads, writes=writes)
        else:
            self.S.op(eng, lambda e: e.tensor_scalar(out=out_ap, in0=a, scalar1=s1, scalar2=s2, op0=op0, op1=op1), reads=reads, writes=writes)

    def stt(self, out_ap, a, s, b, op0, op1, reads, writes):
        self.S.op("dve", lambda e: e.scalar_tensor_tensor(out=out_ap, in0=a, scalar=s, in1=b, op0=op0, op1=op1), reads=reads, writes=writes)

    def copy(self, eng, out_ap, in_ap, reads, writes):
        if eng == "act":
            self.S.op("act", lambda e: e.copy(out=out_ap, in_=in_ap), reads=reads, writes=writes)
        else:
            self.S.op(eng, lambda e: e.tensor_copy(out=out_ap, in_=in_ap), reads=reads, writes=writes)

    def xview(self, xdt, b):
        return xdt.ap.rearrange("(kc p) t -> p kc t", p=128)[:, :, b * TB:(b + 1) * TB]

    def load_wcast(self, wbuf, w_ap, kc_n, ncols, grp=1):
        S = self.S
        wv = w_ap.rearrange("(kc p) f -> p kc f", p=128)
        for k0 in range(0, kc_n, grp):
            S.dma("pool", wbuf[:, k0:k0 + grp, :], wv[:, k0:k0 + grp, :], reads=[], writes=[wbuf], max_dma_last_dim=4096)

    def norm(self, xs, g_ap, hb, es_bufs):
        S = self.S
        sq, rst = es_bufs
        sqv = sq[:, 0:KC, :]
        P = self.PS.next()
        self.act(sqv, xs[:], AF.Square, [xs], [sq])
        for kc in range(KC):
            self.mm(P, P[:], self.ones[:], sqv[:, kc, :], kc == 0, kc == KC - 1, [self.ones, sq])
        self.act(rst[:], P[:], AF.Ln, [P], [rst], scale=1.0 / D, bias=self.epsb[:])
        self.act(rst[:], rst[:], AF.Exp, [rst], [rst], scale=-0.5)
        for kc in range(KC):
            self.stt(hb[:, kc, :], xs[:, kc, :], g_ap[:, kc:kc + 1], rst[:], ALU.mult, ALU.mult, [xs, rst, self.gains], [hb])

    def phase_mlp(self, l, xin, xout):
        S = self.S
        with contextlib.ExitStack() as es:
            w1b = S.sb("w1b", [128, KC, 4096], BF16, es)
            w2b = S.sb("w2b", [128, 32, D], BF16, es)
            xs = S.sb("m_xs", [128, KC, TB], F32, es)
            hb = S.sb("m_hb", [128, KC, TB], BF16, es)
            ab = S.sb("m_ab", [128, 32, TB], BF16, es)
            sq = ab
            rst = S.sb("m_rst", [128, TB], F32, es)
            tmps = Rot([S.sb("m_tmp%d" % i, [128, TB], F32, es) for i in range(2)])
            self.load_wcast(w1b, self.w["mlp_w1"][l], KC, 4096)
            self.load_wcast(w2b, self.w["mlp_w2"][l], 32, D, grp=4)
            for b in range(self.NB):
                S.dma("sp", xs[:], self.xview(xin, b), reads=[xin.blk[b]], writes=[xs])
                self.norm(xs, self.g_mlp[l], hb, (sq, rst))
                for fc in range(32):
                    P = self.PS.next()
                    for kc in range(KC):
                        self.mm(P, P[:], w1b[:, kc, fc * 128:(fc + 1) * 128], hb[:, kc, :], kc == 0, kc == KC - 1, [w1b, hb])
                    tmp = tmps.next()
                    self.act(tmp[:], P[:], AF.Relu, [P], [tmp])
                    self.tt("dve" if fc % 2 == 0 else "pool", ab[:, fc, :], tmp[:], tmp[:], ALU.mult, [tmp], [ab])
                for oc in range(KC):
                    P = self.PS.next()
                    for fc in range(32):
                        self.mm(P, P[:], w2b[:, fc, oc * 128:(oc + 1) * 128], ab[:, fc, :], fc == 0, fc == 31, [w2b, ab])
                    self.tt("dve", xs[:, oc, :], P[:], xs[:, oc, :], ALU.add, [P, xs], [xs])
                S.dma("pool", self.xview(xout, b), xs[:], reads=[xs], writes=[xout.blk[b]])
            S.barrier()

    def phase_outproj(self, wname, widx, nkc, yT, xin, xout):
        S = self.S
        with contextlib.ExitStack() as es:
            wob = S.sb("wob", [128, nkc, D], BF16, es)
            xs = S.sb("o_xs", [128, KC, TB], F32, es)
            ybs = Rot([S.sb("o_yb%d" % i, [128, nkc, TB], BF16, es) for i in range(2)])
            self.load_wcast(wob, self.w[wname][widx], nkc, D, grp=4)
            yv = yT.ap.rearrange("(kc p) t -> p kc t", p=128)
            for b in range(self.NB):
                yb = ybs.next()
                S.dma("sp", yb[:], yv[:, 0:nkc, b * TB:(b + 1) * TB], reads=[yT.blk[b]], writes=[yb])
                S.dma("sp", xs[:], self.xview(xin, b), reads=[xin.blk[b]], writes=[xs])
                for oc in range(KC):
                    P = self.PS.next()
                    for kc in range(nkc):
                        self.mm(P, P[:], wob[:, kc, oc * 128:(oc + 1) * 128], yb[:, kc, :], kc == 0, kc == nkc - 1, [wob, yb])
                    self.tt("dve", xs[:, oc, :], P[:], xs[:, oc, :], ALU.add, [P, xs], [xs])
                S.dma("pool", self.xview(xout, b), xs[:], reads=[xs], writes=[xout.blk[b]])
            S.barrier()

    def phase_odd_a(self, o, l, xin, yT, qT, kT, vtm):
        S = self.S
        T = self.T
        with contextlib.ExitStack() as es:
            wib = S.sb("oa_wi", [128, KC, 3584], BF16, es)
            wab = S.sb("oa_wa", [128, 8, 128], BF16, es)
            wxb = S.sb("oa_wx", [128, 8, 128], BF16, es)
            xs = S.sb("oa_xs", [128, KC, TB], F32, es)
            hb = S.sb("oa_hb", [128, KC, TB], BF16, es)
            sq = S.sb("oa_sq", [128, KC, TB], BF16, es)
            rst = S.sb("oa_rst", [128, TB], F32, es)
            xr = [[S.sb("oa_xr%d_%d" % (c, i), [128, TB + 3], F32, es) for i in range(2)] for c in range(8)]
            hst = [S.sb("oa_hst%d" % i, [128, 1], F32, es) for i in range(8)]
            f32t = Rot([S.sb("oa_f%d" % i, [128, TB], F32, es) for i in range(16)])
            b16t = Rot([S.sb("oa_b%d" % i, [128, TB], BF16, es) for i in range(10)])
            PS8 = Rot(self.psb + self.psh)
            vb = Rot([S.sb("oa_vb%d" % i, [128, 512], BF16, es) for i in range(2)])
            self.load_wcast(wib, self.w["od_w_in"][o], KC, 3584)
            S.dma("pool", wab[:], self.w["lru_wa"][o].rearrange("k i j -> i k j"), writes=[wab])
            S.dma("pool", wxb[:], self.w["lru_wx"][o].rearrange("k i j -> i k j"), writes=[wxb])
            for c in range(8):
                S.op("dve", lambda e, c=c: e.memset(hst[c][:], 0.0), writes=[hst[c]])
                S.op("pool", lambda e, c=c: e.memset(xr[c][1][:, TB:TB + 3], 0.0), writes=[xr[c][1]])
            cw = self.lru_cw[o]
            cb = self.lru_cb[o]
            ba = self.lru_ba[o]
            bx = self.lru_bx[o]
            cl = self.lru_cl[o]
            cst = self.cst
            for b in range(self.NB):
                S.dma("sp", xs[:], self.xview(xin, b), reads=[xin.blk[b]], writes=[xs])
                self.norm(xs, self.g_mix[l], hb, (sq, rst))
                def lru_chunk(c):
                    cur = xr[c][b % 2]
                    prv = xr[c][(b + 1) % 2]
                    Pg = PS8.next()
                    for kc in range(KC):
                        self.mm(Pg, Pg[:], wib[:, kc, c * 128:(c + 1) * 128], hb[:, kc, :], kc == 0, kc == KC - 1, [wib, hb])
                    gl = f32t.next()
                    self.act(gl[:], Pg[:], AF.Gelu_apprx_tanh, [Pg], [gl])
                    Px = PS8.next()
                    for kc in range(KC):
                        self.mm(Px, Px[:], wib[:, kc, 1024 + c * 128:1024 + (c + 1) * 128], hb[:, kc, :], kc == 0, kc == KC - 1, [wib, hb])
                    self.copy("act", cur[:, 3:TB + 3], Px[:], [Px], [cur])
                    self.copy("pool", cur[:, 0:3], prv[:, TB:TB + 3], [prv], [cur])
                    xcv = f32t.next()
                    self.ts("dve", xcv[:], cur[:, 3:TB + 3], cw[:, c, 3:4], ALU.mult, [cur, cst], [xcv], s2=cb[:, c:c + 1], op1=ALU.add)
                    for k in range(3):
                        self.stt(xcv[:], cur[:, k:k + TB], cw[:, c, k:k + 1], xcv[:], ALU.mult, ALU.add, [cur, xcv, cst], [xcv])
                    xcb = b16t.next()
                    self.copy("pool", xcb[:], xcv[:], [xcv], [xcb])
                    Pr = PS8.next()
                    self.mm(Pr, Pr[:], wab[:, c, :], xcb[:], True, True, [wab, xcb])
                    Pi = PS8.next()
                    self.mm(Pi, Pi[:], wxb[:, c, :], xcb[:], True, True, [wxb, xcb])
                    rr = f32t.next()
                    self.act(rr[:], Pr[:], AF.Sigmoid, [Pr, cst], [rr], bias=ba[:, c:c + 1])
                    ii = f32t.next()
                    self.act(ii[:], Pi[:], AF.Sigmoid, [Pi, cst], [ii], bias=bx[:, c:c + 1])
                    aa = f32t.next()
                    self.act(aa[:], rr[:], AF.Exp, [rr, cst], [aa], scale=cl[:, c:c + 1])
                    a2 = f32t.next()
                    self.tt("pool", a2[:], aa[:], aa[:], ALU.mult, [aa], [a2])
                    self.act(a2[:], a2[:], AF.Sqrt, [a2], [a2], scale=-1.0, bias=self.oneb[:])
                    self.tt("pool", ii[:], ii[:], xcv[:], ALU.mult, [ii, xcv], [ii])
                    self.tt("dve", ii[:], ii[:], a2[:], ALU.mult, [ii, a2], [ii])
                    hh = f32t.next()
                    S.op("dve", lambda e, hh=hh, aa=aa, ii=ii, c=c: e.tensor_tensor_scan(out=hh[:], data0=aa[:], data1=ii[:], initial=hst[c][:, 0:1], op0=ALU.mult, op1=ALU.add),
                         reads=[aa, ii, hst[c]], writes=[hh])
                    self.copy("pool", hst[c][:, 0:1], hh[:, TB - 1:TB], [hh], [hst[c]])
                    yc = b16t.next()
                    self.tt("dve", yc[:], hh[:], gl[:], ALU.mult, [hh, gl], [yc])
                    S.dma("pool", yT.ap[c * 128:(c + 1) * 128, b * TB:(b + 1) * TB], yc[:], reads=[yc], writes=[yT.blk[b]], par=True)
                for c in range(0, 8, 2):
                    S.interleave([S.record(lambda c=c: lru_chunk(c)), S.record(lambda c=c: lru_chunk(c + 1))])
                def qk_tile(qi):
                    isq = qi < 4
                    col0 = 2048 + qi * 128
                    Pq = PS8.next()
                    for kc in range(KC):
                        self.mm(Pq, Pq[:], wib[:, kc, col0:col0 + 128], hb[:, kc, :], kc == 0, kc == KC - 1, [wib, hb])
                    s2 = b16t.next()
                    self.act(s2[:], Pq[:], AF.Square, [Pq], [s2])
                    Ps = PS8.next()
                    self.mm(Ps, Ps[:], self.bones[:], s2[:], True, True, [self.bones, s2])
                    r2 = f32t.next()
                    self.act(r2[:], Ps[:], AF.Ln, [Ps], [r2], scale=1.0 / 64, bias=self.epsb[:])
                    self.act(r2[:], r2[:], AF.Exp, [r2], [r2], scale=-0.5)
                    qo = b16t.next()
                    gn = (self.sb_qn if isq else self.sb_kn)[o]
                    self.stt(qo[:], Pq[:], gn[:, 0:1], r2[:], ALU.mult, ALU.mult, [Pq, r2, cst], [qo])
                    dst = qT if isq else kT
                    r0 = (qi % 4) * 128
                    S.dma("pool", dst.ap[r0:r0 + 128, b * TB:(b + 1) * TB], qo[:], reads=[qo], writes=[dst.blk[b]], par=True)
                for q0 in range(0, 8, 4):
                    S.interleave([S.record(lambda qi=q0 + j: qk_tile(qi)) for j in range(4)])
                for tt_ in range(4):
                    Pv = self.PS.next()
                    for kc in range(KC):
                        self.mm(Pv, Pv[:], hb[:, kc, tt_ * 128:(tt_ + 1) * 128], wib[:, kc, 3072:3584], kc == 0, kc == KC - 1, [wib, hb])
                    vv = vb.next()
                    self.copy("act", vv[:], Pv[:], [Pv], [vv])
                    t0 = b * TB + tt_ * 128
                    S.dma("pool", vtm.ap[t0:t0 + 128, 0:512], vv[:], reads=[vv], writes=[vtm.blk[b]], par=True)
            S.barrier()

    def phase_sb(self, qT, kT, vtm, yT):
        S = self.S
        T = self.T
        NKB = T // 128
        scale = 64 ** -0.5
        with contextlib.ExitStack() as es:
            def rot(name, shape, dt, n):
                return Rot([S.sb("%s%d" % (name, i), shape, dt, es) for i in range(n)])
            qh = [S.sb("sb_q%d" % i, [64, T], BF16, es) for i in range(2)]
            kh = [S.sb("sb_k%d" % i, [64, T], BF16, es) for i in range(2)]
            vh = [S.sb("sb_v%d" % i, [128, NKB, 64], BF16, es) for i in range(2)]
            eeR = rot("sb_ee", [128, TB], F32, 2); spR = rot("sb_sp", [128, TB], F32, 7)
            l1R = rot("sb_l1", [128, TB], BF16, 7)
            argR = rot("sb_arg", [128, TB], F32, 3); wwR = rot("sb_ww", [128, TB], BF16, 4)
            ot = rot("sb_o", [64, TB], BF16, 2)
            PzR = Rot([self.psb[0], self.psb[1], self.psb[2], self.psb[3]]); PsfR = Rot([self.psh[0], self.psh[1]])
            PcR = Rot([self.psh[2]]); PaccR = Rot([self.psh[3]])

            def load_head(h):
                q, k, v = qh[h % 2], kh[h % 2], vh[h % 2]
                S.dma("sp", q[:], qT.ap[h * 64:(h + 1) * 64, :], reads=qT.all, writes=[q])
                S.dma("sp", k[:], kT.ap[h * 64:(h + 1) * 64, :], reads=kT.all, writes=[k])
                S.dma("sp", v[:], vtm.ap[:, h * 64:(h + 1) * 64].rearrange("(n p) d -> p n d", p=128), reads=vtm.all, writes=[v])

            units = []
            for h in range(8):
                for Q in range(self.NB):
                    top = 4 * Q + 3
                    pc0 = None
                    for kb in range(top, -1, -1):
                        loc = kb - 4 * Q
                        c0 = max(loc, 0) * 128
                        units.append(dict(h=h, Q=Q, kb=kb, first=(kb == top), last=(kb == 0), newhead=(Q == 0 and kb == top),
                                          c0=c0, pc0=pc0, diag=(loc >= 0)))
                        pc0 = c0
            st = {}

            def pe_z(u):
                h, Q, kb, c0 = u["h"], u["Q"], u["kb"], u["c0"]
                if u["newhead"] and h == 0:
                    load_head(0)
                q, k = qh[h % 2], kh[h % 2]
                Pz = PzR.next()
                self.mm(Pz, Pz[:, c0:TB], k[:, kb * 128:(kb + 1) * 128], q[:, Q * TB + c0:(Q + 1) * TB], True, True, [k, q])
                u["Pz"] = Pz

            def act_sp(u):
                c0, Pz = u["c0"], u["Pz"]
                ee = eeR.next()
                self.act(ee[:, c0:TB], Pz[:, c0:TB], AF.Exp, [Pz], [ee], scale=-scale)
                sp_ = spR.next()
                self.act(sp_[:, c0:TB], ee[:, c0:TB], AF.Ln, [ee], [sp_], bias=self.oneb[:])
                u["sp"] = sp_

            def dve_l1(u):
                c0 = u["c0"]
                Pz, sp_ = u["Pz"], u["sp"]
                l1 = l1R.next()
                self.stt(l1[:, c0:TB], Pz[:, c0:TB], -scale, sp_[:, c0:TB], ALU.mult, ALU.subtract, [Pz, sp_], [l1])
                if u["diag"]:
                    self.tt("pool", l1[:, c0:c0 + 128], l1[:, c0:c0 + 128], self.mask_lt[:], ALU.mult, [l1, self.cst], [l1])
                    if c0 > 0:
                        S.op("pool", lambda e, l1=l1, c0=c0: e.memset(l1[:, 0:c0], 0.0), writes=[l1])
                u["l1"] = l1

            def pe_suffix(u):
                c0 = u["c0"]
                Psf = PsfR.next()
                self.mm(Psf, Psf[:, c0:TB], self.umat[:], u["l1"][:, c0:TB], True, True, [self.umat, u["l1"]])
                u["Psf"] = Psf

            def dve_arg(u):
                c0, pc0 = u["c0"], u["pc0"]
                sp_, Psf = u["sp"], u["Psf"]
                if u["first"]:
                    st["Pc_r"] = PcR.next()
                Pc = u["Pc"] = st["Pc_r"]
                arg = argR.next()
                self.tt("dve", arg[:, c0:TB], Psf[:, c0:TB], sp_[:, c0:TB], ALU.subtract, [Psf, sp_], [arg])
                if not u["first"]:
                    self.tt("dve", arg[:, c0:TB], Pc[:, c0:TB], arg[:, c0:TB], ALU.add, [Pc, arg], [arg])
                u["arg"] = arg

            def pe_colsum(u):
                c0 = u["c0"]
                if not u["last"]:
                    self.mm(u["Pc"], u["Pc"][:, 0:TB], self.ones[:], u["l1"][:, 0:TB], u["first"], False, [self.ones, u["l1"]])

            def act_w(u):
                c0 = u["c0"]
                ww = wwR.next()
                self.act(ww[:, c0:TB], u["arg"][:, c0:TB], AF.Exp, [u["arg"]], [ww])
                if u["diag"]:
                    self.tt("pool", ww[:, c0:c0 + 128], ww[:, c0:c0 + 128], self.mask_lt[:], ALU.mult, [ww, self.cst], [ww])
                    if c0 > 0:
                        S.op("pool", lambda e, ww=ww, c0=c0: e.memset(ww[:, 0:c0], 0.0), writes=[ww])
                u["ww"] = ww

            def pe_wv(u):
                h, Q, kb, c0 = u["h"], u["Q"], u["kb"], u["c0"]
                v = vh[h % 2]
                if u["newhead"] and h + 1 < 8:
                    load_head(h + 1)
                if u["first"]:
                    st["Pacc"] = PaccR.next()
                Pacc = st["Pacc"]
                self.mm(Pacc, Pacc[0:64, 0:TB], v[:, kb, :], u["ww"][:, 0:TB], u["first"], u["last"], [v, u["ww"]])
                if u["last"]:
                    oo = ot.next()
                    self.copy("act", oo[:], Pacc[0:64, 0:TB], [Pacc], [oo])
                    S.dma("pool", yT.ap[1024 + h * 64:1024 + (h + 1) * 64, Q * TB:(Q + 1) * TB], oo[:], reads=[oo], writes=[yT.blk[Q]], par=True)
                u.clear()

            n = len(units)
            import os
            fmap = dict(pe_z=pe_z, pe_suffix=pe_suffix, pe_wv=pe_wv, pe_colsum=pe_colsum, act_sp=act_sp, act_w=act_w, dve_arg=dve_arg, dve_l1=dve_l1)
            if os.environ.get("SB_SCHED"):
                sched = tuple((fmap[x.split(":")[0]], int(x.split(":")[1])) for x in os.environ["SB_SCHED"].split(","))
            else:
                sched = ((pe_z, 0), (pe_wv, 7), (pe_suffix, 4), (pe_colsum, 6), (act_sp, 1), (act_w, 6), (dve_arg, 5), (dve_l1, 2))
            for i in range(n + 8):
                for fn, off in sched:
                    j = i - off
                    if 0 <= j < n:
                        fn(units[j])
            S.barrier()

    def transpose_to(self, P, out_cols, src_buf, src_ap):
        pv = P[:].bitcast(BF16)
        self.S.op("pe", lambda e: e.transpose(out=pv[:, out_cols:out_cols + 128], in_=src_ap, identity=self.ident[:]),
                  reads=[src_buf, self.ident], writes=[P])

    def phase_even_a(self, ei, l, xin, sc):
        S = self.S
        T = self.T
        w = self.w
        with contextlib.ExitStack() as es:
            wib = S.sb("ea_wi", [128, KC, 5648], BF16, es)
            xs = S.sb("ea_xs", [128, KC, TB], F32, es)
            hb = S.sb("ea_hb", [128, KC, TB], BF16, es)
            sq = S.sb("ea_sq", [128, KC, TB], BF16, es)
            rst = S.sb("ea_rst", [128, TB], F32, es)
            cosb = S.sb("ea_cos", [128, TB], F32, es)
            sinb = S.sb("ea_sin", [128, TB], F32, es)
            halo = S.sb("ea_halo", [128, 12, 3], F32, es)
            work = Rot([S.sb("ea_wk%d" % i, [128, TB + 3], F32, es) for i in range(2)])
            f32t = Rot([S.sb("ea_f%d" % i, [128, TB], F32, es) for i in range(8)])
            b16t = Rot([S.sb("ea_b%d" % i, [128, TB], BF16, es) for i in range(6)])
            xct = [S.sb("ea_xc%d" % i, [128, TB], BF16, es) for i in range(12)]
            tkb = Rot([S.sb("ea_tk%d" % i, [128, 1024], BF16, es) for i in range(3)])
            dts = Rot([S.sb("ea_dt%d" % i, [128, 32], F32, es) for i in range(2)])
            self.load_wcast(wib, w["ev_w_in"][ei], KC, 5648)
            S.op("dve", lambda e: e.memset(halo[:], 0.0), writes=[halo])
            cst = self.cst
            cw = self.ssd_cw[ei]; cb = self.ssd_cb[ei]
            qn = self.ret_qn[ei]; kn = self.ret_kn[ei]
            for b in range(self.NB):
                bs = slice(b * TB, (b + 1) * TB)
                S.dma("sp", xs[:], self.xview(xin, b), reads=[xin.blk[b]], writes=[xs])
                S.dma("sp", cosb[:], w["c_cos"][:, bs], writes=[cosb])
                S.dma("sp", sinb[:], w["c_sin"][:, bs], writes=[sinb])
                self.norm(xs, self.g_mix[l], hb, (sq, rst))
                for qi in range(8):
                    isq = qi < 4
                    col0 = qi * 128
                    Pq = self.PS.next()
                    for kc in range(KC):
                        self.mm(Pq, Pq[:], wib[:, kc, col0:col0 + 128], hb[:, kc, :], kc == 0, kc == KC - 1, [wib, hb])
                    qraw = f32t.next()
                    self.copy("act", qraw[:], Pq[:], [Pq], [qraw])
                    s2 = b16t.next()
                    self.act(s2[:], Pq[:], AF.Square, [Pq], [s2])
                    Ps = self.PS.next()
                    self.mm(Ps, Ps[:], self.ones[:], s2[:], True, True, [self.ones, s2])
                    r2 = f32t.next()
                    self.act(r2[:], Ps[:], AF.Ln, [Ps], [r2], scale=1.0 / 128, bias=self.epsb[:])
                    self.act(r2[:], r2[:], AF.Exp, [r2], [r2], scale=-0.5)
                    Pw = self.PS.next()
                    self.mm(Pw, Pw[:], self.swapm[:], qraw[:], True, True, [self.swapm, qraw])
                    gn = qn if isq else kn
                    t1 = f32t.next()
                    self.stt(t1[:], qraw[:], gn[:, 0:1], cosb[:], ALU.mult, ALU.mult, [qraw, cosb, cst], [t1])
                    t2 = f32t.next()
                    self.stt(t2[:], Pw[:], gn[:, 1:2], sinb[:], ALU.mult, ALU.mult, [Pw, sinb, cst], [t2])
                    self.tt("pool", t1[:], t1[:], t2[:], ALU.add, [t1, t2], [t1])
                    qo = b16t.next()
                    self.stt(qo[:], t1[:], 1.0 if isq else 128 ** -0.5, r2[:], ALU.mult, ALU.mult, [t1, r2], [qo])
                    dst = sc["qr"] if isq else sc["kr"]
                    r0 = (qi % 4) * 128
                    S.dma("pool", dst.ap[r0:r0 + 128, bs], qo[:], reads=[qo], writes=[dst.blk[b]], par=True)
                for (c0, dstn, fn) in ((1024, "v", None), (2048, "g", AF.Silu), (3072, "z", AF.Silu)):
                    for tt_ in range(4):
                        tk = tkb.next()
                        for hf in range(2):
                            Pv = self.PS.next()
                            for kc in range(KC):
                                self.mm(Pv, Pv[:], hb[:, kc, tt_ * 128:(tt_ + 1) * 128], wib[:, kc, c0 + hf * 512:c0 + (hf + 1) * 512],
                                        kc == 0, kc == KC - 1, [wib, hb])
                            if fn is None:
                                self.copy("act", tk[:, hf * 512:(hf + 1) * 512], Pv[:], [Pv], [tk])
                            else:
                                self.act(tk[:, hf * 512:(hf + 1) * 512], Pv[:], fn, [Pv], [tk])
                        t0 = b * TB + tt_ * 128
                        S.dma("pool", sc[dstn].ap[t0:t0 + 128, :], tk[:], reads=[tk], writes=[sc[dstn].blk[b]], par=True)
                for tt_ in range(4):
                    Pd = self.PS.next()
                    for kc in range(KC):
                        self.mm(Pd, Pd[:, 0:16], hb[:, kc, tt_ * 128:(tt_ + 1) * 128], wib[:, kc, 5632:5648], kc == 0, kc == KC - 1, [wib, hb])
                    dd = dts.next()
                    self.tt("dve", dd[:, 0:16], Pd[:, 0:16], self.ssd_dtb[ei], ALU.add, [Pd, cst], [dd])
                    self.act(dd[:, 0:16], dd[:, 0:16], AF.Exp, [dd], [dd])
                    self.act(dd[:, 0:16], dd[:, 0:16], AF.Ln, [dd], [dd], bias=self.oneb[:])
                    self.tt("dve", dd[:, 16:32], dd[:, 0:16], self.ssd_A[ei], ALU.mult, [dd, cst], [dd])
                    t0 = b * TB + tt_ * 128
                    S.dma("pool", sc["dtA"].ap[t0:t0 + 128, :], dd[:], reads=[dd], writes=[sc["dtA"].blk[b]], par=True)
                for c in range(12):
                    Px = self.PS.next()
                    col0 = 4096 + c * 128
                    for kc in range(KC):
                        self.mm(Px, Px[:], wib[:, kc, col0:col0 + 128], hb[:, kc, :], kc == 0, kc == KC - 1, [wib, hb])
                    wk = work.next()
                    self.copy("act", wk[:, 3:TB + 3], Px[:], [Px], [wk])
                    self.copy("pool", wk[:, 0:3], halo[:, c, :], [halo], [wk])
                    self.copy("pool", halo[:, c, :], wk[:, TB:TB + 3], [wk], [halo])
                    xcv = f32t.next()
                    self.ts("dve", xcv[:], wk[:, 3:TB + 3], cw[:, c, 3:4], ALU.mult, [wk, cst], [xcv], s2=cb[:, c:c + 1], op1=ALU.add)
                    for k in range(3):
                        self.stt(xcv[:], wk[:, k:k + TB], cw[:, c, k:k + 1], xcv[:], ALU.mult, ALU.add, [wk, xcv, cst], [xcv])
                    self.act(xct[c][:], xcv[:], AF.Silu, [xcv], [xct[c]])
                    if c >= 8:
                        dstn = "BT" if c < 10 else "CT"
                        r0 = (c % 2) * 128
                        S.dma("pool", sc[dstn].ap[r0:r0 + 128, bs], xct[c][:], reads=[xct[c]], writes=[sc[dstn].blk[b]], par=True)
                for tt_ in range(4):
                    tk = tkb.next()
                    Pt = self.PS.next()
                    for c in range(8):
                        self.transpose_to(Pt, c * 128, xct[c], xct[c][:, tt_ * 128:(tt_ + 1) * 128])
                    self.copy("act", tk[:], Pt[:].bitcast(BF16), [Pt], [tk])
                    t0 = b * TB + tt_ * 128
                    S.dma("pool", sc["xs"].ap[t0:t0 + 128, :], tk[:], reads=[tk], writes=[sc["xs"].blk[b]], par=True)
                    tk2 = tkb.next()
                    Pt2 = self.PS.next()
                    for c in range(2):
                        self.transpose_to(Pt2, c * 128, xct[8 + c], xct[8 + c][:, tt_ * 128:(tt_ + 1) * 128])
                    self.copy("act", tk2[:, 0:256], Pt2[:].bitcast(BF16)[:, 0:256], [Pt2], [tk2])
                    S.dma("pool", sc["Btm"].ap[t0:t0 + 128, :], tk2[:, 0:256], reads=[tk2], writes=[sc["Btm"].blk[b]], par=True)
            S.barrier()

    def phase_even_b(self, ei, sc, yT):
        S = self.S
        T = self.T
        NCH = T // 128
        cst = self.cst
        gam = [1.0 - 2.0 ** (-5.0 - h) for h in range(4)]
        with contextlib.ExitStack() as es:
            def rot(name, shape, dt, n=2):
                return Rot([S.sb("%s%d" % (name, i), shape, dt, es) for i in range(n)])
            qrb = rot("eb_q", [128, 4, 128], BF16); krb = rot("eb_k", [128, 4, 128], BF16)
            vb = rot("eb_v", [128, 1024], BF16); gb = rot("eb_g", [128, 1024], BF16); zb = rot("eb_z", [128, 1024], BF16)
            xsb = rot("eb_xs", [128, 16, 64], BF16); btmb = rot("eb_bt", [128, 2, 128], BF16)
            BTb = rot("eb_BT", [128, 2, 128], BF16); CTb = rot("eb_CT", [128, 2, 128], BF16)
            dtb = rot("eb_dt", [128, 32], F32)
            Sst = S.sb("eb_S", [128, 4, 256], F32, es); Sbf = S.sb("eb_Sbf", [128, 4, 256], BF16, es)
            STs = S.sb("eb_ST", [128, 16, 64], F32, es); STbf = S.sb("eb_STbf", [128, 16, 64], BF16, es)
            smb = rot("eb_sm", [128, 4, 128], BF16); ktmb = rot("eb_ktm", [128, 4, 128], BF16); q2b = rot("eb_q2", [128, 4, 128], BF16)
            Dm = S.sb("eb_D", [128, 16, 128], F32, es)
            acol = rot("eb_acol", [128, 48], F32)
            segb = rot("eb_seg", [128, 4, 128], F32, 3); ltb = rot("eb_lt", [128, 4, 128], F32, 2); erb = rot("eb_er", [128, 4, 128], F32, 2)
            cbm = rot("eb_cbm", [128, 2, 128], F32)
            MTb = S.sb("eb_MT", [128, 16, 128], BF16, es); CsTb = S.sb("eb_CsT", [128, 16, 128], BF16, es)
            xdtb = rot("eb_xdt", [128, 16, 64], BF16); xdt2b = rot("eb_xdt2", [128, 16, 64], BF16)
            yf = rot("eb_yf", [128, 1024], F32, 2)
            yab = rot("eb_ya", [128, 1024], BF16, 2); ybb = rot("eb_yb", [128, 1024], BF16, 2)
            stat = rot("eb_stat", [128, 4, 6], F32); mv = rot("eb_mv", [128, 4, 2], F32); rs4 = rot("eb_rs4", [128, 8], F32)
            yTo = rot("eb_yTo", [128, 16, 128], BF16, 2)
            junk = S.sb("eb_junk", [128, 512], BF16, es)
            PSr = Rot(self.psb)
            PSs = Rot(self.psh)
            S.op("dve", lambda e: e.memset(Sst[:], 0.0), writes=[Sst])
            S.op("pool", lambda e: e.memset(Sbf[:], 0.0), writes=[Sbf])
            S.op("dve", lambda e: e.memset(STs[:], 0.0), writes=[STs])
            S.op("pool", lambda e: e.memset(STbf[:], 0.0), writes=[STbf])
            gnr_b = S.sb("eb_gnr", [128, 1024], F32, es); nrr_b = S.sb("eb_nrr", [128, 1024], F32, es)
            S.dma("sp", gnr_b[:], self.w["ret_gn"][ei].partition_broadcast(128), writes=[gnr_b])
            S.dma("sp", nrr_b[:], self.w["ssd_norm"][ei].partition_broadcast(128), writes=[nrr_b])
            gnrow = gnr_b[:]; nrow = nrr_b[:]; dsk = self.ssd_D[ei]
            for c in range(NCH):
                b = c // 4
                ts_ = slice(c * 128, (c + 1) * 128)
                qr = qrb.next(); kr = krb.next(); v = vb.next(); g = gb.next(); z = zb.next(); xs = xsb.next()
                btm = btmb.next(); BT = BTb.next(); CT = CTb.next(); dt = dtb.next()
                S.dma("sp", qr[:], sc["qr"].ap[:, ts_].rearrange("(h d) t -> d h t", d=128), reads=[sc["qr"].blk[b]], writes=[qr])
                S.dma("sp", kr[:], sc["kr"].ap[:, ts_].rearrange("(h d) t -> d h t", d=128), reads=[sc["kr"].blk[b]], writes=[kr])
                S.dma("sp", v[:], sc["v"].ap[ts_, :], reads=[sc["v"].blk[b]], writes=[v])
                S.dma("sp", g[:], sc["g"].ap[ts_, :], reads=[sc["g"].blk[b]], writes=[g])
                S.dma("sp", z[:], sc["z"].ap[ts_, :], reads=[sc["z"].blk[b]], writes=[z])
                S.dma("sp", xs[:], sc["xs"].ap[ts_, :].rearrange("t (h p) -> t h p", p=64), reads=[sc["xs"].blk[b]], writes=[xs])
                S.dma("sp", btm[:], sc["Btm"].ap[ts_, :].rearrange("t (g s) -> t g s", s=128), reads=[sc["Btm"].blk[b]], writes=[btm])
                S.dma("sp", BT[:], sc["BT"].ap[:, ts_].rearrange("(g s) t -> s g t", s=128), reads=[sc["BT"].blk[b]], writes=[BT])
                S.dma("sp", CT[:], sc["CT"].ap[:, ts_].rearrange("(g s) t -> s g t", s=128), reads=[sc["CT"].blk[b]], writes=[CT])
                S.dma("sp", dt[:], sc["dtA"].ap[ts_, :], reads=[sc["dtA"].blk[b]], writes=[dt])

                def ret_stream():
                    Psc = PSr.next()
                    for h in range(4):
                        self.mm(Psc, Psc[:, h * 128:(h + 1) * 128], kr[:, h, :], qr[:, h, :], True, True, [kr, qr])
                    sm = smb.next()
                    self.tt("dve", sm[:].rearrange("p h i -> p (h i)"), Psc[:], self.ret_decay[:], ALU.mult, [Psc, cst], [sm])
                    Pkt = PSr.next()
                    for h in range(4):
                        self.transpose_to(Pkt, h * 128, kr, kr[:, h, :])
                    ktm = ktmb.next()
                    self.tt("dve", ktm[:], Pkt[:].bitcast(BF16)[:, 0:512].rearrange("p (h d) -> p h d", d=128),
                            self.ret_kdec[:].unsqueeze(2).broadcast_to([128, 4, 128]), ALU.mult, [Pkt, cst], [ktm])
                    q2 = q2b.next()
                    self.tt("pool", q2[:].rearrange("p h i -> p (h i)"), qr[:].rearrange("p h i -> p (h i)"), self.ret_qdec[:], ALU.mult, [qr, cst], [q2])
                    POw = (PSr.next(), PSr.next())
                    for h in range(4):
                        PO = POw[h // 2]
                        hc = slice((h % 2) * 256, (h % 2 + 1) * 256)
                        self.mm(PO, PO[:, hc], sm[:, h, :], v[:, h * 256:(h + 1) * 256], True, False, [sm, v])
                        self.mm(PO, PO[:, hc], q2[:, h, :], Sbf[:, h, :], False, True, [q2, Sbf])
                    Pkvw = (PSr.next(), PSr.next())
                    for h in range(4):
                        Pkv = Pkvw[h // 2]
                        self.mm(Pkv, Pkv[:, (h % 2) * 256:(h % 2 + 1) * 256], ktm[:, h, :], v[:, h * 256:(h + 1) * 256], True, True, [ktm, v])
                    for h in range(4):
                        Pkv = Pkvw[h // 2]
                        self.stt(Sst[:, h, :], Sst[:, h, :], gam[h] ** 128, Pkv[:, (h % 2) * 256:(h % 2 + 1) * 256], ALU.mult, ALU.add, [Sst, Pkv], [Sst])
                    self.copy("act", Sbf[:], Sst[:], [Sst], [Sbf])
                    st = stat.next(); m2 = mv.next(); r4 = rs4.next()
                    for h in range(4):
                        PO = POw[h // 2]
                        S.op("dve", lambda e, h=h, st=st, PO=PO: e.bn_stats(out=st[:, h, :], in_=PO[:, (h % 2) * 256:(h % 2 + 1) * 256]), reads=[PO], writes=[st])
                    for h in range(4):
                        S.op("dve", lambda e, h=h, st=st, m2=m2: e.bn_aggr(out=m2[:, h, :], in_=st[:, h, :]), reads=[st], writes=[m2])
                    self.act(r4[:, 0:4], m2[:, :, 1], AF.Ln, [m2], [r4], bias=self.epsb[:])
                    self.act(r4[:, 0:4], r4[:, 0:4], AF.Exp, [r4], [r4], scale=-0.5)
                    y1 = yf.next()
                    for h in range(4):
                        PO = POw[h // 2]
                        self.ts("dve", y1[:, h * 256:(h + 1) * 256], PO[:, (h % 2) * 256:(h % 2 + 1) * 256], m2[:, h, 0:1], ALU.subtract, [PO, m2, r4], [y1],
                                s2=r4[:, h:h + 1], op1=ALU.mult)
                    self.tt("pool", y1[:], y1[:], gnrow, ALU.mult, [y1, gnr_b], [y1])
                    ya = yab.next()
                    self.tt("pool", ya[:], y1[:], g[:], ALU.mult, [y1, g], [ya])

                    out_['ya'] = ya

                def ssd_stream():
                    PA = PSs.next()
                    self.mm(PA, PA[:, 0:16], self.tri_f[:], dt[:, 16:32], True, True, [self.tri_f, dt])
                    self.mm(PA, PA[:, 16:32], self.ones_f[:], dt[:, 16:32], True, True, [self.ones_f, dt])
                    ac = acol.next()
                    self.copy("act", ac[:, 0:32], PA[:, 0:32], [PA], [ac])
                    self.tt("dve", ac[:, 32:48], ac[:, 16:32], ac[:, 0:16], ALU.subtract, [ac], [ac])
                    self.act(ac[:, 32:48], ac[:, 32:48], AF.Exp, [ac], [ac])
                    self.act(ac[:, 16:32], ac[:, 16:32], AF.Exp, [ac], [ac])
                    self.tt("dve", Dm[:], dt[:, 16:32].unsqueeze(2).broadcast_to([128, 16, 128]),
                            self.tri_f[:].unsqueeze(1).broadcast_to([128, 16, 128]), ALU.mult, [dt, self.tri_f], [Dm])
                    PCB = PSs.next()
                    for gi in range(2):
                        self.mm(PCB, PCB[:, gi * 128:(gi + 1) * 128], BT[:, gi, :], CT[:, gi, :], True, True, [BT, CT])
                    cb_ = cbm.next()
                    self.tt("dve", cb_[:].rearrange("p g i -> p (g i)"), PCB[:, 0:256], self.tri2[:], ALU.mult, [PCB, cst], [cb_])
                    for q4 in range(4):
                        gi = q4 // 2
                        hs = slice(q4 * 4, q4 * 4 + 4)
                        Prb = PSs.next()
                        self.mm(Prb, Prb[:], self.ones_f[:], Dm[:, hs, :].rearrange("p h i -> p (h i)"), True, True, [self.ones_f, Dm])
                        sg = segb.next()
                        self.tt("dve", sg[:], Prb[:].rearrange("p (h i) -> p h i", i=128), ac[:, q4 * 4:q4 * 4 + 4].unsqueeze(2).broadcast_to([128, 4, 128]),
                                ALU.subtract, [Prb, ac], [sg])
                        lt = ltb.next()
                        self.act(lt[:], sg[:], AF.Exp, [sg], [lt])
                        er = erb.next()
                        self.act(er[:].rearrange("p h i -> p (h i)"), Prb[:], AF.Exp, [Prb], [er])
                        self.stt(MTb[:, hs, :], lt[:], 1.0, cb_[:, gi, :].unsqueeze(1).broadcast_to([128, 4, 128]), ALU.min, ALU.mult, [lt, cb_], [MTb])
                        self.tt("pool", CsTb[:, hs, :], er[:], CT[:, gi, :].unsqueeze(1).broadcast_to([128, 4, 128]), ALU.mult, [er, CT], [CsTb])
                    xdt = xdtb.next(); xdt2 = xdt2b.next()
                    self.tt("dve", xdt[:], xs[:], dt[:, 0:16].unsqueeze(2).broadcast_to([128, 16, 64]), ALU.mult, [xs, dt], [xdt])
                    self.tt("pool", xdt2[:], xdt[:], ac[:, 32:48].unsqueeze(2).broadcast_to([128, 16, 64]), ALU.mult, [xdt, ac], [xdt2])
                    PYw = (PSs.next(), PSs.next())
                    for h in range(16):
                        PY = PYw[h // 8]
                        hc = slice((h % 8) * 64, (h % 8 + 1) * 64)
                        self.mm(PY, PY[:, hc], MTb[:, h, :], xdt[:, h, :], True, False, [MTb, xdt])
                        self.mm(PY, PY[:, hc], CsTb[:, h, :], STbf[:, h, :], False, True, [CsTb, STbf])
                    PStw = (PSs.next(), PSs.next())
                    for h in range(16):
                        PSt = PStw[h // 8]
                        self.mm(PSt, PSt[:, (h % 8) * 64:(h % 8 + 1) * 64], btm[:, h // 8, :], xdt2[:, h, :], True, True, [btm, xdt2])
                    self.tt("dve", STs[:], STs[:], ac[:, 16:32].unsqueeze(2).broadcast_to([128, 16, 64]), ALU.mult, [STs, ac], [STs])
                    STf = STs[:].rearrange("p h d -> p (h d)")
                    for gi in range(2):
                        self.tt("dve", STf[:, gi * 512:(gi + 1) * 512], PStw[gi][:], STf[:, gi * 512:(gi + 1) * 512], ALU.add, [PStw[gi], STs], [STs])
                    self.copy("act", STbf[:], STs[:], [STs], [STbf])
                    y2 = yf.next()
                    self.tt("pool", y2[:].rearrange("p (h d) -> p h d", d=64), xs[:], dsk.unsqueeze(2).broadcast_to([128, 16, 64]), ALU.mult, [xs, cst], [y2])
                    for gi in range(2):
                        self.tt("dve", y2[:, gi * 512:(gi + 1) * 512], PYw[gi][:], y2[:, gi * 512:(gi + 1) * 512], ALU.add, [PYw[gi], y2], [y2])
                    self.tt("pool", y2[:], y2[:], z[:], ALU.mult, [y2, z], [y2])
                    r8 = rs4.next()
                    for gi in range(2):
                        S.op("act", lambda e, gi=gi, y2=y2, r8=r8: e.activation(out=junk[:], in_=y2[:, gi * 512:(gi + 1) * 512], func=AF.Square, accum_out=r8[:, gi:gi + 1]),
                             reads=[y2], writes=[junk, r8])
                    self.act(r8[:, 0:2], r8[:, 0:2], AF.Ln, [r8], [r8], scale=1.0 / 512, bias=self.epsb[:])
                    self.act(r8[:, 0:2], r8[:, 0:2], AF.Exp, [r8], [r8], scale=-0.5)
                    yb = ybb.next()
                    for gi in range(2):
                        self.stt(yb[:, gi * 512:(gi + 1) * 512], y2[:, gi * 512:(gi + 1) * 512], r8[:, gi:gi + 1], nrow[:, gi * 512:(gi + 1) * 512],
                                 ALU.mult, ALU.mult, [y2, r8, nrr_b], [yb])
                    out_['yb'] = yb

                out_ = {}
                S.interleave([S.record(ret_stream), S.record(ssd_stream)])
                ya, yb = out_['ya'], out_['yb']
                yo = yTo.next()
                for half, src in ((0, ya), (1, yb)):
                    Pt = self.PS.next()
                    for cc in range(8):
                        self.transpose_to(Pt, cc * 128, src, src[:, cc * 128:(cc + 1) * 128])
                    self.copy("act", yo[:, half * 8:(half + 1) * 8, :].rearrange("p c i -> p (c i)"), Pt[:].bitcast(BF16), [Pt], [yo])
                S.dma("pool", yT.ap[:, ts_].rearrange("(c p) t -> p c t", p=128), yo[:], reads=[yo], writes=[yT.blk[b]], par=True)
            S.barrier()

    def build(self):
        nc, es = self.nc, self.es
        T, NB = self.T, self.NB
        S = self.S = Sched(nc, es)
        ne = sum(1 for x in self.layers if x == "e")
        no = sum(1 for x in self.layers if x == "o")
        L = len(self.layers)
        w = self.w = {}

        def inp(name, shape):
            w[name] = nc.dram_tensor(name, list(shape), F32, kind="ExternalInput").ap()

        self.xin = DT(S, "xT", [D, T], F32, NB, kind="ExternalInput")
        self.xout = DT(S, "outT", [D, T], F32, NB, kind="ExternalOutput")
        inp("norm_mix", [L, D]); inp("norm_mlp", [L, D])
        inp("mlp_w1", [L, D, 4096]); inp("mlp_w2", [L, 4096, D])
        if no:
            inp("od_w_in", [no, D, 3584]); inp("od_w_out", [no, 1536, D])
            inp("lru_conv_w", [no, 4, D]); inp("lru_conv_b", [no, D])
            inp("lru_wa", [no, 8, 128, 128]); inp("lru_ba", [no, 8, 128])
            inp("lru_wx", [no, 8, 128, 128]); inp("lru_bx", [no, 8, 128])
            inp("lru_lam", [no, D]); inp("sb_qn", [no, 64]); inp("sb_kn", [no, 64])
        if ne:
            inp("ev_w_in", [ne, D, 5648]); inp("ev_w_out", [ne, 2048, D])
            inp("ret_qn", [ne, 128]); inp("ret_kn", [ne, 128]); inp("ret_gn", [ne, 1024])
            inp("ssd_conv_w", [ne, 4, 1536]); inp("ssd_conv_b", [ne, 1536])
            inp("ssd_dt_bias", [ne, 16]); inp("ssd_a_log", [ne, 16]); inp("ssd_d", [ne, 16]); inp("ssd_norm", [ne, 1024])
            inp("c_cos", [128, T]); inp("c_sin", [128, T])
            inp("c_ident", [128, 128]); inp("c_swap", [128, 128]); inp("c_tri", [128, 128]); inp("c_tri2", [128, 256])
            inp("c_rdecay", [128, 512]); inp("c_rqdec", [128, 512]); inp("c_rkdec", [128, 4])
        inp("c_masklt", [128, 128]); inp("c_umat", [128, 128]); inp("c_bones", [128, 128])

        allb = [S.ps("ps%d" % i, [128, 512]) for i in range(8)]
        self.psb = allb[0:4]
        self.psh = allb[4:8]
        self.PS = Rot(self.psb)
        self.PW = Rot([(allb[4], allb[5]), (allb[6], allb[7])])

        cst = self.cst = Buf("cst")

        def csb(name, shape, dt=F32):
            return es.enter_context(nc.sbuf_tensor(name, list(shape), dt))

        self.ones = S.sb("ones", [128, 128], BF16)
        S.op("dve", lambda e: e.memset(self.ones[:], 1.0), writes=[self.ones])
        self.epsb = csb("epsb", [128, 1]); self.oneb = csb("oneb", [128, 1])
        S.op("dve", lambda e: e.memset(self.epsb[:], EPS), writes=[cst])
        S.op("dve", lambda e: e.memset(self.oneb[:], 1.0), writes=[cst])
        self.mask_lt = csb("mask_lt", [128, 128], BF16)
        self.umat = S.sb("umat", [128, 128], BF16)
        self.bones = S.sb("bones", [128, 128], BF16)
        S.dma("pool", self.mask_lt[:], w["c_masklt"], writes=[cst])
        S.dma("pool", self.umat[:], w["c_umat"], writes=[self.umat])
        S.dma("pool", self.bones[:], w["c_bones"], writes=[self.bones])
        self.gains = Buf("gains")
        gmix = csb("gmix", [128, L, KC]); gmlp = csb("gmlp", [128, L, KC])
        S.dma("sp", gmix[:], w["norm_mix"].rearrange("l (kc p) -> p l kc", p=128), writes=[self.gains], allow_slow_non_contiguous=True)
        S.dma("sp", gmlp[:], w["norm_mlp"].rearrange("l (kc p) -> p l kc", p=128), writes=[self.gains], allow_slow_non_contiguous=True)
        self.g_mix = [gmix[:, l, :] for l in range(L)]
        self.g_mlp = [gmlp[:, l, :] for l in range(L)]
        if no:
            cw = csb("lru_cw", [128, no, 8, 4]); cb = csb("lru_cb", [128, no, 8])
            ba = csb("lru_ba_s", [128, no, 8]); bx = csb("lru_bx_s", [128, no, 8])
            cl = csb("lru_cl", [128, no, 8])
            qn = csb("sbqn", [128, no]); kn = csb("sbkn", [128, no])
            for o_ in range(no):
                for k_ in range(4):
                    S.dma("sp", cw[:, o_, :, k_], w["lru_conv_w"][o_, k_].rearrange("(c p) -> p c", p=128), writes=[cst], allow_slow_non_contiguous=True)
            S.dma("sp", cb[:], w["lru_conv_b"].rearrange("o (c p) -> p o c", p=128), writes=[cst], allow_slow_non_contiguous=True)
            S.dma("sp", ba[:], w["lru_ba"].rearrange("o c p -> p o c"), writes=[cst], allow_slow_non_contiguous=True)
            S.dma("sp", bx[:], w["lru_bx"].rearrange("o c p -> p o c"), writes=[cst], allow_slow_non_contiguous=True)
            S.dma("sp", cl[:], w["lru_lam"].rearrange("o (c p) -> p o c", p=128), writes=[cst], allow_slow_non_contiguous=True)
            for half in range(2):
                S.dma("sp", qn[half * 64:(half + 1) * 64, :], w["sb_qn"].rearrange("o d -> d o"), writes=[cst], allow_slow_non_contiguous=True)
                S.dma("sp", kn[half * 64:(half + 1) * 64, :], w["sb_kn"].rearrange("o d -> d o"), writes=[cst], allow_slow_non_contiguous=True)
            S.op("act", lambda e: e.activation(out=cl[:], in_=cl[:], func=AF.Exp, scale=-1.0), reads=[cst], writes=[cst])
            S.op("act", lambda e: e.activation(out=cl[:], in_=cl[:], func=AF.Ln, bias=self.oneb[:]), reads=[cst], writes=[cst])
            S.op("dve", lambda e: e.tensor_scalar(out=cl[:], in0=cl[:], scalar1=-8.0, scalar2=None, op0=ALU.mult), reads=[cst], writes=[cst])
            self.lru_cw = [cw[:, o] for o in range(no)]
            self.lru_cb = [cb[:, o] for o in range(no)]
            self.lru_ba = [ba[:, o] for o in range(no)]
            self.lru_bx = [bx[:, o] for o in range(no)]
            self.lru_cl = [cl[:, o] for o in range(no)]
            self.sb_qn = [qn[:, o:o + 1] for o in range(no)]
            self.sb_kn = [kn[:, o:o + 1] for o in range(no)]
        if ne:
            self.ident = S.sb("ident", [128, 128], BF16)
            S.dma("pool", self.ident[:], w["c_ident"], writes=[self.ident])
            self.swapm = S.sb("swapm", [128, 128], F32)
            S.dma("sp", self.swapm[:], w["c_swap"], writes=[self.swapm])
            self.tri_f = S.sb("tri_f", [128, 128], F32)
            S.dma("sp", self.tri_f[:], w["c_tri"], writes=[self.tri_f])
            self.ones_f = S.sb("ones_f", [128, 128], F32)
            S.op("dve", lambda e: e.memset(self.ones_f[:], 1.0), writes=[self.ones_f])
            self.tri2 = csb("tri2", [128, 256]); self.ret_decay = csb("rdecay", [128, 512]); self.ret_qdec = csb("rqdec", [128, 512])
            self.ret_kdec = csb("rkdec", [128, 4])
            S.dma("sp", self.tri2[:], w["c_tri2"], writes=[cst])
            S.dma("sp", self.ret_decay[:], w["c_rdecay"], writes=[cst])
            S.dma("sp", self.ret_qdec[:], w["c_rqdec"], writes=[cst])
            S.dma("sp", self.ret_kdec[:], w["c_rkdec"], writes=[cst])
            scw = csb("ssd_cw", [128, ne, 12, 4]); scb = csb("ssd_cb", [128, ne, 12])
            rqn = csb("ret_qn_s", [128, ne, 2]); rkn = csb("ret_kn_s", [128, ne, 2])
            dsk = csb("ssd_D_r", [128, ne, 16]); dtbr = csb("ssd_dtb_r", [128, ne, 16]); Ar = csb("ssd_A_r", [128, ne, 16])
            for e_ in range(ne):
                for k_ in range(4):
                    S.dma("sp", scw[:, e_, :, k_], w["ssd_conv_w"][e_, k_].rearrange("(c p) -> p c", p=128), writes=[cst], allow_slow_non_contiguous=True)
                S.dma("sp", scb[:, e_, :], w["ssd_conv_b"][e_].rearrange("(c p) -> p c", p=128), writes=[cst], allow_slow_non_contiguous=True)
                for nm, tl in (("ret_qn", rqn), ("ret_kn", rkn)):
                    col = w[nm][e_].rearrange("(d o) -> d o", o=1)
                    S.dma("sp", tl[:, e_, 0:1], col, writes=[cst], allow_slow_non_contiguous=True)
                    S.dma("sp", tl[0:64, e_, 1:2], col[64:128], writes=[cst], allow_slow_non_contiguous=True)
                    S.dma("sp", tl[64:128, e_, 1:2], col[0:64], writes=[cst], allow_slow_non_contiguous=True)
                S.dma("sp", dsk[:, e_, :], w["ssd_d"][e_].partition_broadcast(128), writes=[cst])
                S.dma("sp", dtbr[:, e_, :], w["ssd_dt_bias"][e_].partition_broadcast(128), writes=[cst])
                S.dma("sp", Ar[:, e_, :], w["ssd_a_log"][e_].partition_broadcast(128), writes=[cst])
            S.op("act", lambda e: e.activation(out=Ar[:], in_=Ar[:], func=AF.Exp), reads=[cst], writes=[cst])
            S.op("dve", lambda e: e.tensor_scalar(out=Ar[:], in0=Ar[:], scalar1=-1.0, scalar2=None, op0=ALU.mult), reads=[cst], writes=[cst])
            self.ssd_cw = [scw[:, e_] for e_ in range(ne)]; self.ssd_cb = [scb[:, e_] for e_ in range(ne)]
            self.ret_qn = [rqn[:, e_] for e_ in range(ne)]; self.ret_kn = [rkn[:, e_] for e_ in range(ne)]
            self.ssd_D = [dsk[:, e_] for e_ in range(ne)]; self.ssd_dtb = [dtbr[:, e_] for e_ in range(ne)]; self.ssd_A = [Ar[:, e_] for e_ in range(ne)]
        S.barrier()

        xa = DT(S, "xa", [D, T], F32, NB)
        xb_ = DT(S, "xb", [D, T], F32, NB)
        yT = DT(S, "yT", [2048, T], BF16, NB)
        qT = DT(S, "qT", [512, T], BF16, NB)
        kT = DT(S, "kT", [512, T], BF16, NB)
        vtm = DT(S, "vtm", [T, 1024], BF16, NB)
        sc = {"qr": qT, "kr": kT, "v": vtm}
        if ne:
            for nm in ("g", "z", "xs"):
                sc[nm] = DT(S, "sc_" + nm, [T, 1024], BF16, NB)
            sc["Btm"] = DT(S, "sc_Btm", [T, 256], BF16, NB)
            sc["BT"] = DT(S, "sc_BT", [256, T], BF16, NB)
            sc["CT"] = DT(S, "sc_CT", [256, T], BF16, NB)
            sc["dtA"] = DT(S, "sc_dtA", [T, 32], F32, NB)

        cur = self.xin
        ie = io = 0
        for l, kind in enumerate(self.layers):
            last = l == L - 1
            skip = getattr(self, "skip", ())
            if kind == "o":
                if "odd_a" not in skip:
                    self.phase_odd_a(io, l, cur, yT, qT, kT, vtm)
                if "sb" not in skip:
                    self.phase_sb(qT, kT, vtm, yT)
                if "outproj" not in skip:
                    self.phase_outproj("od_w_out", io, 12, yT, cur, xa)
                io += 1
            else:
                if "even_a" not in skip:
                    self.phase_even_a(ie, l, cur, sc)
                if "even_b" not in skip:
                    self.phase_even_b(ie, sc, yT)
                if "outproj" not in skip:
                    self.phase_outproj("ev_w_out", ie, 16, yT, cur, xa)
                ie += 1
            if "mlp" not in skip:
                self.phase_mlp(l, xa, self.xout if last else xb_)
            cur = xb_
        S.finish()
        es.close()
        return nc


def consts(T=None, even=False):
    i = np.arange(128)
    c = _consts_base(i)
    if even:
        f32 = np.float32
        inv = (f32(10000.0) ** (-(np.arange(64, dtype=f32)) / f32(64))).astype(f32)
        ang = (np.arange(T, dtype=f32)[None, :] * inv[:, None]).astype(f32).astype(np.float64)
        cos = np.cos(ang); sin = np.sin(ang)
        c["c_cos"] = np.concatenate([cos, cos], 0).astype(f32)
        c["c_sin"] = np.concatenate([-sin, sin], 0).astype(f32)
        c["c_ident"] = np.eye(128, dtype=f32)
        c["c_swap"] = (i[:, None] == ((i[None, :] + 64) % 128)).astype(f32)
        tri = (i[:, None] <= i[None, :]).astype(f32)
        c["c_tri"] = tri
        c["c_tri2"] = np.concatenate([tri, tri], 1)
        lg = np.log1p(-np.exp2(-5.0 - np.arange(4, dtype=np.float64)))
        rel = (i[None, :] - i[:, None]).astype(np.float64)
        dec = np.where(rel[:, None, :] >= 0, np.exp(lg[None, :, None] * np.maximum(rel, 0)[:, None, :]), 0.0)
        c["c_rdecay"] = dec.reshape(128, 512).astype(f32)
        qd = np.exp(lg[:, None] * (i[None, :] + 1.0))
        c["c_rqdec"] = np.broadcast_to(qd.reshape(1, 512), (128, 512)).astype(f32).copy()
        c["c_rkdec"] = np.exp(lg[None, :] * (127.0 - i[:, None])).astype(f32)
    return c


def _consts_base(i):
    return {
        "c_masklt": (i[:, None] < i[None, :]).astype(np.float32),
        "c_umat": (i[:, None] > i[None, :]).astype(np.float32),
        "c_bones": ((i[:, None] // 64) == (i[None, :] // 64)).astype(np.float32),
    }


def make_inputs(inputs, b, kinds):
    m = {"xT": np.ascontiguousarray(np.asarray(inputs["x"])[b].T)}
    L = len(kinds)
    for n in ("norm_mix", "norm_mlp", "mlp_w1", "mlp_w2"):
        m[n] = np.ascontiguousarray(np.asarray(inputs[n])[:L])
    if "e" in kinds:
        for n in ("ev_w_in", "ev_w_out", "ret_qn", "ret_kn", "ret_gn", "ssd_conv_w", "ssd_conv_b", "ssd_dt_bias", "ssd_a_log", "ssd_d", "ssd_norm"):
            m[n] = np.ascontiguousarray(np.asarray(inputs[n]))
    if "o" in kinds:
        for n in ("od_w_in", "od_w_out", "lru_conv_w", "lru_conv_b", "lru_wa", "lru_ba", "lru_wx", "lru_bx", "lru_lam", "sb_qn", "sb_kn"):
            m[n] = np.ascontiguousarray(np.asarray(inputs[n]))
    m.update(consts(m["xT"].shape[1], "e" in kinds))
    return m


KINDS = "eoeo"
SEQ = 8192
_NC_CACHE = {}


def kernel(**inputs):
    if "nc" not in _NC_CACHE:
        _NC_CACHE["nc"] = K(SEQ, list(KINDS)).build()
    nc = _NC_CACHE["nc"]
    nb = np.asarray(inputs["x"]).shape[0]
    in_maps = [make_inputs(inputs, b, KINDS) for b in range(nb)]
    res = run_bass_kernel_spmd(nc, in_maps, core_ids=list(range(nb)))
    out = np.stack([np.asarray(res.results[b]["outT"]).T for b in range(nb)])
    return np.ascontiguousarray(out.astype(np.float32))
```

```python
import contextlib
import math
import numpy as np
import concourse.bass as bass
import concourse.mybir as mybir
from concourse.bass_utils import run_bass_kernel_spmd

F32 = mybir.dt.float32
BF16 = mybir.dt.bfloat16
AF = mybir.ActivationFunctionType
ALU = mybir.AluOpType
AX = mybir.AxisListType

D = 1024
KC = 8
TB = 512
EPS = 1e-6


class Buf:
    __slots__ = ("name", "t", "lws", "rd")

    def __init__(self, name, t=None):
        self.name = name
        self.t = t
        self.lws = []
        self.rd = {}

    def __getitem__(self, k):
        return self.t[k]


class Rot:
    def __init__(self, bufs):
        self.bufs = list(bufs)
        self.i = 0

    def next(self):
        b = self.bufs[self.i % len(self.bufs)]
        self.i += 1
        return b


class Sched:
    ENG = ("pe", "act", "dve", "pool", "sp")

    def __init__(self, nc, es, n_dma_sems=32):
        self.nc = nc
        self.es = es
        self.q = {e: [] for e in self.ENG}
        self.cnt = {e: 0 for e in self.ENG}
        self.sem = {e: es.enter_context(nc.semaphore("c_" + e)) for e in ("pe", "act", "dve", "pool")}
        self.dsem = [es.enter_context(nc.semaphore("d%d" % i)) for i in range(n_dma_sems)]
        self.dcnt = [0] * n_dma_sems
        self.dnext = 0
        self.pnext = 0
        self.rec = None
        self.waited = {e: {} for e in self.ENG}
        self.ndma = 0

    def sb(self, name, shape, dt, es=None):
        self.uid = getattr(self, "uid", 0) + 1
        name = "%s_u%d" % (name, self.uid)
        return Buf(name, (es or self.es).enter_context(self.nc.sbuf_tensor(name, list(shape), dt)))

    def ps(self, name, shape, dt=F32):
        return Buf(name, self.es.enter_context(self.nc.psum_tensor(name, list(shape), dt)))

    def dram(self, name, shape, dt, kind="Internal"):
        return Buf(name, self.nc.dram_tensor(name, list(shape), dt, kind=kind).ap())

    def _need(self, eng, dep, waits):
        if dep is None:
            return
        key, val, semh = dep
        w = self.waited[eng]
        if w.get(key, 0) >= val:
            return
        w[key] = val
        waits.append((semh, val))

    def _deps(self, eng, reads, writes, same, par=False):
        waits = []
        for b in reads:
            for lw in b.lws:
                if same or lw[0] != eng:
                    self._need(eng, lw, waits)
        for b in writes:
            if not par:
                for lw in b.lws:
                    if same or lw[0] != eng:
                        self._need(eng, lw, waits)
            for k, d in b.rd.items():
                if same or k != eng:
                    self._need(eng, d, waits)
        return waits

    def hoist_begin(self, eng):
        self._hm = (eng, len(self.q[eng]))

    def hoist_end(self):
        eng, m = self._hm
        items = self.q[eng][m:]
        if len(items) < 2:
            return
        allw = []
        for waits, fn, inc in items:
            allw.extend(waits)
        best = {}
        for semh, val in allw:
            k = id(semh)
            if k not in best or best[k][1] < val:
                best[k] = (semh, val)
        self.q[eng][m] = (list(best.values()), items[0][1], items[0][2])
        for j in range(1, len(items)):
            self.q[eng][m + j] = ([], items[j][1], items[j][2])

    def record(self, body):
        old = self.rec
        self.rec = []
        body()
        r = self.rec
        self.rec = old
        return r

    def interleave(self, lists):
        its = [list(l) for l in lists]
        pos = [0] * len(its)
        left = sum(len(l) for l in its)
        while left:
            for k, l in enumerate(its):
                if pos[k] < len(l):
                    it = l[pos[k]]
                    pos[k] += 1
                    left -= 1
                    if it[0] == 0:
                        self.op(it[1], it[2], it[3], it[4])
                    else:
                        self.dma(it[1], it[2], it[3], it[4], it[5], it[6], **it[7])

    def op(self, eng, fn, reads=(), writes=()):
        if self.rec is not None:
            self.rec.append((0, eng, fn, list(reads), list(writes)))
            return None
        same = eng != "pe"
        waits = self._deps(eng, reads, writes, same)
        self.cnt[eng] += 1
        tok = (eng, self.cnt[eng], self.sem[eng])
        self.q[eng].append((waits, fn, (self.sem[eng], 1)))
        for b in reads:
            b.rd[eng] = tok
        for b in writes:
            b.lws = [tok]
            b.rd = {}
        return tok

    def dma(self, qeng, out_ap, in_ap, reads=(), writes=(), par=False, **kw):
        if self.rec is not None:
            self.rec.append((1, qeng, out_ap, in_ap, list(reads), list(writes), par, kw))
            return None
        waits = self._deps(qeng, reads, writes, True, par)
        if qeng == "pool":
            s = self.pnext
            self.pnext = (self.pnext + 1) % 4
        else:
            s = 4 + self.dnext
            self.dnext = (self.dnext + 1) % (len(self.dsem) - 4)
        if self.dcnt[s] > 0:
            self._need(qeng, ("d%d" % s, self.dcnt[s], self.dsem[s]), waits)
        self.dcnt[s] += 16
        tok = ("d%d" % s, self.dcnt[s], self.dsem[s])
        self.q[qeng].append((waits, lambda e: e.dma_start(out=out_ap, in_=in_ap, **kw), (self.dsem[s], 16)))
        for b in reads:
            b.rd["dma%d" % self.ndma] = tok
        for b in writes:
            if par:
                if b.rd:
                    b.lws = []
                    b.rd = {}
                b.lws.append(tok)
            else:
                b.lws = [tok]
                b.rd = {}
        self.ndma += 1
        return tok

    def barrier(self):
        for e in self.ENG:
            waits = []
            for e2 in ("pe", "act", "dve", "pool"):
                if e2 != e and self.cnt[e2] > 0:
                    self._need(e, (e2, self.cnt[e2], self.sem[e2]), waits)
            for s in range(len(self.dsem)):
                if self.dcnt[s] > 0:
                    self._need(e, ("d%d" % s, self.dcnt[s], self.dsem[s]), waits)
            if waits:
                self.q[e].append((waits, None, None))

    def finish(self):
        self.barrier()
        nc = self.nc
        q = self.q

        def replay(eh, items):
            for waits, fn, inc in items:
                for semh, val in waits:
                    eh.wait_ge(semh, val)
                if fn is not None:
                    ins = fn(eh)
                    if inc is not None:
                        ins.then_inc(inc[0], inc[1])

        with nc.Block() as block:
            @block.sync
            def _(e):
                replay(e, q["sp"])

            @block.tensor
            def _(e):
                replay(e, q["pe"])

            @block.scalar
            def _(e):
                replay(e, q["act"])

            @block.vector
            def _(e):
                replay(e, q["dve"])

            @block.gpsimd
            def _(e):
                replay(e, q["pool"])


class DT:
    def __init__(self, S, name, shape, dt, nblk, kind="Internal"):
        self.ap = S.nc.dram_tensor(name, list(shape), dt, kind=kind).ap()
        self.blk = [Buf("%s_b%d" % (name, i)) for i in range(nblk)]
        self.all = self.blk


class K:
    def __init__(self, T, layers):
        self.T = T
        self.NB = T // TB
        self.layers = layers
        self.nc = bass.Bass("TRN2", target_bir_lowering=False)
        self.es = contextlib.ExitStack()

    def mm(self, P, out_ap, lhsT, rhs, start, stop, reads):
        self.S.op("pe", lambda e: e.matmul(out_ap, lhsT=lhsT, rhs=rhs, start=start, stop=stop), reads=reads, writes=[P])

    def act(self, out_ap, in_ap, func, reads, writes, **kw):
        self.S.op("act", lambda e: e.activation(out=out_ap, in_=in_ap, func=func, **kw), reads=reads, writes=writes)

    def tt(self, eng, out_ap, a, b, op, reads, writes):
        self.S.op(eng, lambda e: e.tensor_tensor(out=out_ap, in0=a, in1=b, op=op), reads=reads, writes=writes)

    def ts(self, eng, out_ap, a, s1, op0, reads, writes, s2=None, op1=None):
        if op1 is None:
            self.S.op(eng, lambda e: e.tensor_scalar(out=out_ap, in0=a, scalar1=s1, scalar2=None, op0=op0), reads=reads, writes=writes)
        else:
            self.S.op(eng, lambda e: e.tensor_scalar(out=out_ap, in0=a, scalar1=s1, scalar2=s2, op0=op0, op1=op1), reads=reads, writes=writes)

    def stt(self, out_ap, a, s, b, op0, op1, reads, writes):
        self.S.op("dve", lambda e: e.scalar_tensor_tensor(out=out_ap, in0=a, scalar=s, in1=b, op0=op0, op1=op1), reads=reads, writes=writes)

    def copy(self, eng, out_ap, in_ap, reads, writes):
        if eng == "act":
            self.S.op("act", lambda e: e.copy(out=out_ap, in_=in_ap), reads=reads, writes=writes)
        else:
            self.S.op(eng, lambda e: e.tensor_copy(out=out_ap, in_=in_ap), reads=reads, writes=writes)

    def xview(self, xdt, b):
        return xdt.ap.rearrange("(kc p) t -> p kc t", p=128)[:, :, b * TB:(b + 1) * TB]

    def load_wcast(self, wbuf, w_ap, kc_n, ncols, grp=1, defer=False):
        S = self.S
        wv = w_ap.rearrange("(kc p) f -> p kc f", p=128)
        th = []
        for k0 in range(0, kc_n, grp):
            th.append(lambda k0=k0: S.dma("pool", wbuf[:, k0:k0 + grp, :], wv[:, k0:k0 + grp, :], reads=[], writes=[wbuf], par=True, max_dma_last_dim=4096))
        if defer:
            return th
        for t in th:
            t()

    def norm(self, xs, g_ap, hb, es_bufs):
        S = self.S
        sq, rst = es_bufs
        sqv = sq[:, 0:KC, :]
        P = self.PS.next()
        self.act(sqv, xs[:], AF.Square, [xs], [sq])
        for kc in range(KC):
            self.mm(P, P[:], self.ones[:], sqv[:, kc, :], kc == 0, kc == KC - 1, [self.ones, sq])
        self.act(rst[:], P[:], AF.Ln, [P], [rst], scale=1.0 / D, bias=self.epsb[:])
        self.act(rst[:], rst[:], AF.Exp, [rst], [rst], scale=-0.5)
        for kc in range(KC):
            self.stt(hb[:, kc, :], xs[:, kc, :], g_ap[:, kc:kc + 1], rst[:], ALU.mult, ALU.mult, [xs, rst, self.gains], [hb])

    def phase_mlp(self, l, xin, xout, w1b=None):
        S = self.S
        NB = self.NB
        with contextlib.ExitStack() as es:
            pre = w1b is not None
            if not pre:
                w1b = S.sb("w1b", [128, KC, 4096], BF16, es)
            w2b = S.sb("w2b", [128, 32, D], BF16, es)
            xs = S.sb("m_xs", [128, KC, TB], F32, es)
            hb = S.sb("m_hb", [128, KC, TB], BF16, es)
            ab = S.sb("m_ab", [128, 32, TB], BF16, es)
            rst = S.sb("m_rst", [128, TB], F32, es)
            tmps = Rot([S.sb("m_tmp%d" % i, [128, TB], F32, es) for i in range(2)])
            xos = Rot([S.sb("m_xo%d" % i, [128, TB], F32, es) for i in range(3)])
            PS8 = Rot(self.psb + self.psh)
            if not pre:
                self.load_wcast(w1b, self.w["mlp_w1"][l], KC, 4096)
            self.load_wcast(w2b, self.w["mlp_w2"][l], 32, D, grp=4)
            xinv = xin.ap.rearrange("(kc p) t -> p kc t", p=128)
            xoutv = xout.ap.rearrange("(kc p) t -> p kc t", p=128)

            def load_x(b):
                S.dma("sp", xs[:], self.xview(xin, b), reads=[xin.blk[b]], writes=[xs])

            def norm_next(b):
                self.norm(xs, self.g_mlp[l], hb, (hb, rst))

            load_x(0)
            norm_next(0)
            for b in range(NB):
                if b + 1 < NB:
                    load_x(b + 1)
                for fc in range(32):
                    P = PS8.next()
                    for kc in range(KC):
                        self.mm(P, P[:], w1b[:, kc, fc * 128:(fc + 1) * 128], hb[:, kc, :], kc == 0, kc == KC - 1, [w1b, hb])
                    tmp = tmps.next()
                    self.act(tmp[:], P[:], AF.Relu, [P], [tmp])
                    self.tt("dve" if fc % 2 == 0 else "pool", ab[:, fc, :], tmp[:], tmp[:], ALU.mult, [tmp], [ab])
                xo_pref = {}
                for oc in range(KC):
                    for o2 in (oc, oc + 1, oc + 2):
                        if o2 < KC and o2 not in xo_pref:
                            xo = xos.next()
                            S.dma("sp", xo[:], xinv[:, o2, b * TB:(b + 1) * TB], reads=[xin.blk[b]], writes=[xo])
                            xo_pref[o2] = xo
                    P = PS8.next()
                    for fc in range(32):
                        self.mm(P, P[:], w2b[:, fc, oc * 128:(oc + 1) * 128], ab[:, fc, :], fc == 0, fc == 31, [w2b, ab])
                    xo = xo_pref[oc]
                    self.tt("dve", xo[:], P[:], xo[:], ALU.add, [P, xo], [xo])
                    S.dma("pool", xoutv[:, oc, b * TB:(b + 1) * TB], xo[:], reads=[xo], writes=[xout.blk[b]], par=True)
                    if oc == 1 and b + 1 < NB:
                        norm_next(b + 1)
            S.barrier()

    def phase_outproj(self, wname, widx, nkc, yT, xin, xout, extra=()):
        S = self.S
        with contextlib.ExitStack() as es:
            wob = S.sb("wob", [128, nkc, D], BF16, es)
            xs = S.sb("o_xs", [128, KC, TB], F32, es)
            ybs = Rot([S.sb("o_yb%d" % i, [128, nkc, TB], BF16, es) for i in range(2)])
            self.load_wcast(wob, self.w[wname][widx], nkc, D, grp=4)
            yv = yT.ap.rearrange("(kc p) t -> p kc t", p=128)
            for b in range(self.NB):
                yb = ybs.next()
                S.dma("sp", yb[:], yv[:, 0:nkc, b * TB:(b + 1) * TB], reads=[yT.blk[b]], writes=[yb])
                S.dma("sp", xs[:], self.xview(xin, b), reads=[xin.blk[b]], writes=[xs])
                for oc in range(KC):
                    P = self.PS.next()
                    for kc in range(nkc):
                        self.mm(P, P[:], wob[:, kc, oc * 128:(oc + 1) * 128], yb[:, kc, :], kc == 0, kc == nkc - 1, [wob, yb])
                    self.tt("dve", xs[:, oc, :], P[:], xs[:, oc, :], ALU.add, [P, xs], [xs])
                S.dma("pool", self.xview(xout, b), xs[:], reads=[xs], writes=[xout.blk[b]])
                if b < len(extra):
                    extra[b]()
            for t in extra[self.NB:]:
                t()
            S.barrier()

    def phase_odd_a(self, o, l, xin, yT, qT, kT, vtm):
        S = self.S
        T = self.T
        with contextlib.ExitStack() as es:
            wib = S.sb("oa_wi", [128, KC, 3584], BF16, es)
            wab = S.sb("oa_wa", [128, 8, 128], BF16, es)
            wxb = S.sb("oa_wx", [128, 8, 128], BF16, es)
            xs = S.sb("oa_xs", [128, KC, TB], F32, es)
            hb = S.sb("oa_hb", [128, KC, TB], BF16, es)
            sq = S.sb("oa_sq", [128, KC, TB], BF16, es)
            rst = S.sb("oa_rst", [128, TB], F32, es)
            xr = [[S.sb("oa_xr%d_%d" % (c, i), [128, TB + 3], F32, es) for i in range(2)] for c in range(8)]
            hst = [S.sb("oa_hst%d" % i, [128, 1], F32, es) for i in range(8)]
            f32t = Rot([S.sb("oa_f%d" % i, [128, TB], F32, es) for i in range(16)])
            b16t = Rot([S.sb("oa_b%d" % i, [128, TB], BF16, es) for i in range(10)])
            PS8 = Rot(self.psb + self.psh)
            vb = Rot([S.sb("oa_vb%d" % i, [128, 512], BF16, es) for i in range(2)])
            self.load_wcast(wib, self.w["od_w_in"][o], KC, 3584)
            S.dma("pool", wab[:], self.w["lru_wa"][o].rearrange("k i j -> i k j"), writes=[wab])
            S.dma("pool", wxb[:], self.w["lru_wx"][o].rearrange("k i j -> i k j"), writes=[wxb])
            for c in range(8):
                S.op("dve", lambda e, c=c: e.memset(hst[c][:], 0.0), writes=[hst[c]])
                S.op("pool", lambda e, c=c: e.memset(xr[c][1][:, TB:TB + 3], 0.0), writes=[xr[c][1]])
            cw = self.lru_cw[o]
            cb = self.lru_cb[o]
            ba = self.lru_ba[o]
            bx = self.lru_bx[o]
            cl = self.lru_cl[o]
            cst = self.cst
            for b in range(self.NB):
                S.dma("sp", xs[:], self.xview(xin, b), reads=[xin.blk[b]], writes=[xs])
                self.norm(xs, self.g_mix[l], hb, (sq, rst))
                def lru_chunk(c):
                    cur = xr[c][b % 2]
                    prv = xr[c][(b + 1) % 2]
                    Pg = PS8.next()
                    for kc in range(KC):
                        self.mm(Pg, Pg[:], wib[:, kc, c * 128:(c + 1) * 128], hb[:, kc, :], kc == 0, kc == KC - 1, [wib, hb])
                    gl = f32t.next()
                    self.act(gl[:], Pg[:], AF.Gelu_apprx_tanh, [Pg], [gl])
                    Px = PS8.next()
                    for kc in range(KC):
                        self.mm(Px, Px[:], wib[:, kc, 1024 + c * 128:1024 + (c + 1) * 128], hb[:, kc, :], kc == 0, kc == KC - 1, [wib, hb])
                    self.copy("act", cur[:, 3:TB + 3], Px[:], [Px], [cur])
                    self.copy("pool", cur[:, 0:3], prv[:, TB:TB + 3], [prv], [cur])
                    xcv = f32t.next()
                    self.ts("dve", xcv[:], cur[:, 3:TB + 3], cw[:, c, 3:4], ALU.mult, [cur, cst], [xcv], s2=cb[:, c:c + 1], op1=ALU.add)
                    for k in range(3):
                        self.stt(xcv[:], cur[:, k:k + TB], cw[:, c, k:k + 1], xcv[:], ALU.mult, ALU.add, [cur, xcv, cst], [xcv])
                    xcb = b16t.next()
                    self.copy("pool", xcb[:], xcv[:], [xcv], [xcb])
                    Pr = PS8.next()
                    self.mm(Pr, Pr[:], wab[:, c, :], xcb[:], True, True, [wab, xcb])
                    Pi = PS8.next()
                    self.mm(Pi, Pi[:], wxb[:, c, :], xcb[:], True, True, [wxb, xcb])
                    rr = f32t.next()
                    self.act(rr[:], Pr[:], AF.Sigmoid, [Pr, cst], [rr], bias=ba[:, c:c + 1])
                    ii = f32t.next()
                    self.act(ii[:], Pi[:], AF.Sigmoid, [Pi, cst], [ii], bias=bx[:, c:c + 1])
                    aa = f32t.next()
                    self.act(aa[:], rr[:], AF.Exp, [rr, cst], [aa], scale=cl[:, c:c + 1])
                    a2 = f32t.next()
                    self.tt("pool", a2[:], aa[:], aa[:], ALU.mult, [aa], [a2])
                    self.act(a2[:], a2[:], AF.Sqrt, [a2], [a2], scale=-1.0, bias=self.oneb[:])
                    self.tt("pool", ii[:], ii[:], xcv[:], ALU.mult, [ii, xcv], [ii])
                    self.tt("dve", ii[:], ii[:], a2[:], ALU.mult, [ii, a2], [ii])
                    hh = f32t.next()
                    S.op("dve", lambda e, hh=hh, aa=aa, ii=ii, c=c: e.tensor_tensor_scan(out=hh[:], data0=aa[:], data1=ii[:], initial=hst[c][:, 0:1], op0=ALU.mult, op1=ALU.add),
                         reads=[aa, ii, hst[c]], writes=[hh])
                    self.copy("pool", hst[c][:, 0:1], hh[:, TB - 1:TB], [hh], [hst[c]])
                    yc = b16t.next()
                    self.tt("dve", yc[:], hh[:], gl[:], ALU.mult, [hh, gl], [yc])
                    S.dma("pool", yT.ap[c * 128:(c + 1) * 128, b * TB:(b + 1) * TB], yc[:], reads=[yc], writes=[yT.blk[b]], par=True)
                for c in range(0, 8, 2):
                    S.interleave([S.record(lambda c=c: lru_chunk(c)), S.record(lambda c=c: lru_chunk(c + 1))])
                def qk_tile(qi):
                    isq = qi < 4
                    col0 = 2048 + qi * 128
                    Pq = PS8.next()
                    for kc in range(KC):
                        self.mm(Pq, Pq[:], wib[:, kc, col0:col0 + 128], hb[:, kc, :], kc == 0, kc == KC - 1, [wib, hb])
                    s2 = b16t.next()
                    self.act(s2[:], Pq[:], AF.Square, [Pq], [s2])
                    Ps = PS8.next()
                    self.mm(Ps, Ps[:], self.bones[:], s2[:], True, True, [self.bones, s2])
                    r2 = f32t.next()
                    self.act(r2[:], Ps[:], AF.Ln, [Ps], [r2], scale=1.0 / 64, bias=self.epsb[:])
                    self.act(r2[:], r2[:], AF.Exp, [r2], [r2], scale=-0.5)
                    qo = b16t.next()
                    gn = (self.sb_qn if isq else self.sb_kn)[o]
                    self.stt(qo[:], Pq[:], gn[:, 0:1], r2[:], ALU.mult, ALU.mult, [Pq, r2, cst], [qo])
                    dst = qT if isq else kT
                    r0 = (qi % 4) * 128
                    S.dma("pool", dst.ap[r0:r0 + 128, b * TB:(b + 1) * TB], qo[:], reads=[qo], writes=[dst.blk[b]], par=True)
                for q0 in range(0, 8, 4):
                    S.interleave([S.record(lambda qi=q0 + j: qk_tile(qi)) for j in range(4)])
                for tt_ in range(4):
                    Pv = self.PS.next()
                    for kc in range(KC):
                        self.mm(Pv, Pv[:], hb[:, kc, tt_ * 128:(tt_ + 1) * 128], wib[:, kc, 3072:3584], kc == 0, kc == KC - 1, [wib, hb])
                    vv = vb.next()
                    self.copy("act", vv[:], Pv[:], [Pv], [vv])
                    t0 = b * TB + tt_ * 128
                    S.dma("pool", vtm.ap[t0:t0 + 128, 0:512], vv[:], reads=[vv], writes=[vtm.blk[b]], par=True)
            S.barrier()

    def phase_sb(self, qT, kT, vtm, yT):
        S = self.S
        T = self.T
        NKB = T // 128
        scale = 64 ** -0.5
        with contextlib.ExitStack() as es:
            def rot(name, shape, dt, n):
                return Rot([S.sb("%s%d" % (name, i), shape, dt, es) for i in range(n)])
            qh = [S.sb("sb_q%d" % i, [64, T], BF16, es) for i in range(2)]
            kh = [S.sb("sb_k%d" % i, [64, T], BF16, es) for i in range(2)]
            vh = [S.sb("sb_v%d" % i, [128, NKB, 64], BF16, es) for i in range(2)]
            eeR = rot("sb_ee", [128, TB], F32, 2); spR = rot("sb_sp", [128, TB], F32, 7)
            l1R = rot("sb_l1", [128, TB], BF16, 7)
            argR = rot("sb_arg", [128, TB], F32, 3); wwR = rot("sb_ww", [128, TB], BF16, 4)
            ot = rot("sb_o", [64, TB], BF16, 2)
            PzR = Rot([self.psb[0], self.psb[1], self.psb[2], self.psb[3]]); PsfR = Rot([self.psh[0], self.psh[1]])
            PcR = Rot([self.psh[2]]); PaccR = Rot([self.psh[3]])

            def load_head(h):
                q, k, v = qh[h % 2], kh[h % 2], vh[h % 2]
                S.dma("sp", q[:], qT.ap[h * 64:(h + 1) * 64, :], reads=qT.all, writes=[q])
                S.dma("sp", k[:], kT.ap[h * 64:(h + 1) * 64, :], reads=kT.all, writes=[k])
                S.dma("sp", v[:], vtm.ap[:, h * 64:(h + 1) * 64].rearrange("(n p) d -> p n d", p=128), reads=vtm.all, writes=[v])

            units = []
            for h in range(8):
                for Q in range(self.NB):
                    top = 4 * Q + 3
                    pc0 = None
                    for kb in range(top, -1, -1):
                        loc = kb - 4 * Q
                        c0 = max(loc, 0) * 128
                        units.append(dict(h=h, Q=Q, kb=kb, first=(kb == top), last=(kb == 0), newhead=(Q == 0 and kb == top),
                                          c0=c0, pc0=pc0, diag=(loc >= 0)))
                        pc0 = c0
            st = {}

            def pe_z(u):
                h, Q, kb, c0 = u["h"], u["Q"], u["kb"], u["c0"]
                if u["newhead"] and h == 0:
                    load_head(0)
                q, k = qh[h % 2], kh[h % 2]
                Pz = PzR.next()
                self.mm(Pz, Pz[:, c0:TB], k[:, kb * 128:(kb + 1) * 128], q[:, Q * TB + c0:(Q + 1) * TB], True, True, [k, q])
                u["Pz"] = Pz

            def act_sp(u):
                c0, Pz = u["c0"], u["Pz"]
                ee = eeR.next()
                self.act(ee[:, c0:TB], Pz[:, c0:TB], AF.Exp, [Pz], [ee], scale=-scale)
                sp_ = spR.next()
                self.act(sp_[:, c0:TB], ee[:, c0:TB], AF.Ln, [ee], [sp_], bias=self.oneb[:])
                u["sp"] = sp_

            def dve_l1(u):
                c0 = u["c0"]
                Pz, sp_ = u["Pz"], u["sp"]
                l1 = l1R.next()
                self.stt(l1[:, c0:TB], Pz[:, c0:TB], -scale, sp_[:, c0:TB], ALU.mult, ALU.subtract, [Pz, sp_], [l1])
                if u["diag"]:
                    self.tt("pool", l1[:, c0:c0 + 128], l1[:, c0:c0 + 128], self.mask_lt[:], ALU.mult, [l1, self.cst], [l1])
                    if c0 > 0:
                        S.op("pool", lambda e, l1=l1, c0=c0: e.memset(l1[:, 0:c0], 0.0), writes=[l1])
                u["l1"] = l1

            def pe_suffix(u):
                c0 = u["c0"]
                Psf = PsfR.next()
                self.mm(Psf, Psf[:, c0:TB], self.umat[:], u["l1"][:, c0:TB], True, True, [self.umat, u["l1"]])
                u["Psf"] = Psf

            def dve_arg(u):
                c0, pc0 = u["c0"], u["pc0"]
                sp_, Psf = u["sp"], u["Psf"]
                if u["first"]:
                    st["Pc_r"] = PcR.next()
                Pc = u["Pc"] = st["Pc_r"]
                arg = argR.next()
                self.tt("dve", arg[:, c0:TB], Psf[:, c0:TB], sp_[:, c0:TB], ALU.subtract, [Psf, sp_], [arg])
                if not u["first"]:
                    self.tt("dve", arg[:, c0:TB], Pc[:, c0:TB], arg[:, c0:TB], ALU.add, [Pc, arg], [arg])
                u["arg"] = arg

            def pe_colsum(u):
                c0 = u["c0"]
                if not u["last"]:
                    self.mm(u["Pc"], u["Pc"][:, 0:TB], self.ones[:], u["l1"][:, 0:TB], u["first"], False, [self.ones, u["l1"]])

            def act_w(u):
                c0 = u["c0"]
                ww = wwR.next()
                self.act(ww[:, c0:TB], u["arg"][:, c0:TB], AF.Exp, [u["arg"]], [ww])
                if u["diag"]:
                    self.tt("pool", ww[:, c0:c0 + 128], ww[:, c0:c0 + 128], self.mask_lt[:], ALU.mult, [ww, self.cst], [ww])
                    if c0 > 0:
                        S.op("pool", lambda e, ww=ww, c0=c0: e.memset(ww[:, 0:c0], 0.0), writes=[ww])
                u["ww"] = ww

            def pe_wv(u):
                h, Q, kb, c0 = u["h"], u["Q"], u["kb"], u["c0"]
                v = vh[h % 2]
                if u["newhead"] and h + 1 < 8:
                    load_head(h + 1)
                if u["first"]:
                    st["Pacc"] = PaccR.next()
                Pacc = st["Pacc"]
                self.mm(Pacc, Pacc[0:64, 0:TB], v[:, kb, :], u["ww"][:, 0:TB], u["first"], u["last"], [v, u["ww"]])
                if u["last"]:
                    oo = ot.next()
                    self.copy("act", oo[:], Pacc[0:64, 0:TB], [Pacc], [oo])
                    S.dma("pool", yT.ap[1024 + h * 64:1024 + (h + 1) * 64, Q * TB:(Q + 1) * TB], oo[:], reads=[oo], writes=[yT.blk[Q]], par=True)
                u.clear()

            n = len(units)
            import os
            fmap = dict(pe_z=pe_z, pe_suffix=pe_suffix, pe_wv=pe_wv, pe_colsum=pe_colsum, act_sp=act_sp, act_w=act_w, dve_arg=dve_arg, dve_l1=dve_l1)
            if os.environ.get("SB_SCHED"):
                sched = tuple((fmap[x.split(":")[0]], int(x.split(":")[1])) for x in os.environ["SB_SCHED"].split(","))
            else:
                sched = ((pe_z, 0), (pe_wv, 7), (pe_suffix, 4), (pe_colsum, 6), (act_sp, 1), (act_w, 6), (dve_arg, 5), (dve_l1, 2))
            for i in range(n + 8):
                for fn, off in sched:
                    j = i - off
                    if 0 <= j < n:
                        fn(units[j])
            S.barrier()

    def transpose_to(self, P, out_cols, src_buf, src_ap):
        pv = P[:].bitcast(BF16)
        self.S.op("pe", lambda e: e.transpose(out=pv[:, out_cols:out_cols + 128], in_=src_ap, identity=self.ident[:]),
                  reads=[src_buf, self.ident], writes=[P])

    def phase_even_a(self, ei, l, xin, sc):
        S = self.S
        T = self.T
        w = self.w
        with contextlib.ExitStack() as es:
            wib = S.sb("ea_wi", [128, KC, 5648], BF16, es)
            xs = S.sb("ea_xs", [128, KC, TB], F32, es)
            hb = S.sb("ea_hb", [128, KC, TB], BF16, es)
            sq = S.sb("ea_sq", [128, KC, TB], BF16, es)
            rst = S.sb("ea_rst", [128, TB], F32, es)
            cosb = S.sb("ea_cos", [128, TB], F32, es)
            sinb = S.sb("ea_sin", [128, TB], F32, es)
            halo = S.sb("ea_halo", [128, 12, 3], F32, es)
            work = Rot([S.sb("ea_wk%d" % i, [128, TB + 3], F32, es) for i in range(2)])
            f32t = Rot([S.sb("ea_f%d" % i, [128, TB], F32, es) for i in range(8)])
            b16t = Rot([S.sb("ea_b%d" % i, [128, TB], BF16, es) for i in range(6)])
            xct = [S.sb("ea_xc%d" % i, [128, TB], BF16, es) for i in range(12)]
            tkb = Rot([S.sb("ea_tk%d" % i, [128, 1024], BF16, es) for i in range(3)])
            dts = Rot([S.sb("ea_dt%d" % i, [128, 32], F32, es) for i in range(2)])
            self.load_wcast(wib, w["ev_w_in"][ei], KC, 5648)
            S.op("dve", lambda e: e.memset(halo[:], 0.0), writes=[halo])
            cst = self.cst
            cw = self.ssd_cw[ei]; cb = self.ssd_cb[ei]
            qn = self.ret_qn[ei]; kn = self.ret_kn[ei]
            for b in range(self.NB):
                bs = slice(b * TB, (b + 1) * TB)
                S.dma("sp", xs[:], self.xview(xin, b), reads=[xin.blk[b]], writes=[xs])
                S.dma("sp", cosb[:], w["c_cos"][:, bs], writes=[cosb])
                S.dma("sp", sinb[:], w["c_sin"][:, bs], writes=[sinb])
                self.norm(xs, self.g_mix[l], hb, (sq, rst))
                for qi in range(8):
                    isq = qi < 4
                    col0 = qi * 128
                    Pq = self.PS.next()
                    for kc in range(KC):
                        self.mm(Pq, Pq[:], wib[:, kc, col0:col0 + 128], hb[:, kc, :], kc == 0, kc == KC - 1, [wib, hb])
                    qraw = f32t.next()
                    self.copy("act", qraw[:], Pq[:], [Pq], [qraw])
                    s2 = b16t.next()
                    self.act(s2[:], Pq[:], AF.Square, [Pq], [s2])
                    Ps = self.PS.next()
                    self.mm(Ps, Ps[:], self.ones[:], s2[:], True, True, [self.ones, s2])
                    r2 = f32t.next()
                    self.act(r2[:], Ps[:], AF.Ln, [Ps], [r2], scale=1.0 / 128, bias=self.epsb[:])
                    self.act(r2[:], r2[:], AF.Exp, [r2], [r2], scale=-0.5)
                    Pw = self.PS.next()
                    self.mm(Pw, Pw[:], self.swapm[:], qraw[:], True, True, [self.swapm, qraw])
                    gn = qn if isq else kn
                    t1 = f32t.next()
                    self.stt(t1[:], qraw[:], gn[:, 0:1], cosb[:], ALU.mult, ALU.mult, [qraw, cosb, cst], [t1])
                    t2 = f32t.next()
                    self.stt(t2[:], Pw[:], gn[:, 1:2], sinb[:], ALU.mult, ALU.mult, [Pw, sinb, cst], [t2])
                    self.tt("pool", t1[:], t1[:], t2[:], ALU.add, [t1, t2], [t1])
                    qo = b16t.next()
                    self.stt(qo[:], t1[:], 1.0 if isq else 128 ** -0.5, r2[:], ALU.mult, ALU.mult, [t1, r2], [qo])
                    dst = sc["qr"] if isq else sc["kr"]
                    r0 = (qi % 4) * 128
                    S.dma("pool", dst.ap[r0:r0 + 128, bs], qo[:], reads=[qo], writes=[dst.blk[b]], par=True)
                for (c0, dstn, fn) in ((1024, "v", None), (2048, "g", AF.Silu), (3072, "z", AF.Silu)):
                    for tt_ in range(4):
                        tk = tkb.next()
                        for hf in range(2):
                            Pv = self.PS.next()
                            for kc in range(KC):
                                self.mm(Pv, Pv[:], hb[:, kc, tt_ * 128:(tt_ + 1) * 128], wib[:, kc, c0 + hf * 512:c0 + (hf + 1) * 512],
                                        kc == 0, kc == KC - 1, [wib, hb])
                            if fn is None:
                                self.copy("act", tk[:, hf * 512:(hf + 1) * 512], Pv[:], [Pv], [tk])
                            else:
                                self.act(tk[:, hf * 512:(hf + 1) * 512], Pv[:], fn, [Pv], [tk])
                        t0 = b * TB + tt_ * 128
                        S.dma("pool", sc[dstn].ap[t0:t0 + 128, :], tk[:], reads=[tk], writes=[sc[dstn].blk[b]], par=True)
                for tt_ in range(4):
                    Pd = self.PS.next()
                    for kc in range(KC):
                        self.mm(Pd, Pd[:, 0:16], hb[:, kc, tt_ * 128:(tt_ + 1) * 128], wib[:, kc, 5632:5648], kc == 0, kc == KC - 1, [wib, hb])
                    dd = dts.next()
                    self.tt("dve", dd[:, 0:16], Pd[:, 0:16], self.ssd_dtb[ei], ALU.add, [Pd, cst], [dd])
                    self.act(dd[:, 0:16], dd[:, 0:16], AF.Exp, [dd], [dd])
                    self.act(dd[:, 0:16], dd[:, 0:16], AF.Ln, [dd], [dd], bias=self.oneb[:])
                    self.tt("dve", dd[:, 16:32], dd[:, 0:16], self.ssd_A[ei], ALU.mult, [dd, cst], [dd])
                    t0 = b * TB + tt_ * 128
                    S.dma("pool", sc["dtA"].ap[t0:t0 + 128, :], dd[:], reads=[dd], writes=[sc["dtA"].blk[b]], par=True)
                for c in range(12):
                    Px = self.PS.next()
                    col0 = 4096 + c * 128
                    for kc in range(KC):
                        self.mm(Px, Px[:], wib[:, kc, col0:col0 + 128], hb[:, kc, :], kc == 0, kc == KC - 1, [wib, hb])
                    wk = work.next()
                    self.copy("act", wk[:, 3:TB + 3], Px[:], [Px], [wk])
                    self.copy("pool", wk[:, 0:3], halo[:, c, :], [halo], [wk])
                    self.copy("pool", halo[:, c, :], wk[:, TB:TB + 3], [wk], [halo])
                    xcv = f32t.next()
                    self.ts("dve", xcv[:], wk[:, 3:TB + 3], cw[:, c, 3:4], ALU.mult, [wk, cst], [xcv], s2=cb[:, c:c + 1], op1=ALU.add)
                    for k in range(3):
                        self.stt(xcv[:], wk[:, k:k + TB], cw[:, c, k:k + 1], xcv[:], ALU.mult, ALU.add, [wk, xcv, cst], [xcv])
                    self.act(xct[c][:], xcv[:], AF.Silu, [xcv], [xct[c]])
                    if c >= 8:
                        dstn = "BT" if c < 10 else "CT"
                        r0 = (c % 2) * 128
                        S.dma("pool", sc[dstn].ap[r0:r0 + 128, bs], xct[c][:], reads=[xct[c]], writes=[sc[dstn].blk[b]], par=True)
                for tt_ in range(4):
                    tk = tkb.next()
                    Pt = self.PS.next()
                    for c in range(8):
                        self.transpose_to(Pt, c * 128, xct[c], xct[c][:, tt_ * 128:(tt_ + 1) * 128])
                    self.copy("act", tk[:], Pt[:].bitcast(BF16), [Pt], [tk])
                    t0 = b * TB + tt_ * 128
                    S.dma("pool", sc["xs"].ap[t0:t0 + 128, :], tk[:], reads=[tk], writes=[sc["xs"].blk[b]], par=True)
                    tk2 = tkb.next()
                    Pt2 = self.PS.next()
                    for c in range(2):
                        self.transpose_to(Pt2, c * 128, xct[8 + c], xct[8 + c][:, tt_ * 128:(tt_ + 1) * 128])
                    self.copy("act", tk2[:, 0:256], Pt2[:].bitcast(BF16)[:, 0:256], [Pt2], [tk2])
                    S.dma("pool", sc["Btm"].ap[t0:t0 + 128, :], tk2[:, 0:256], reads=[tk2], writes=[sc["Btm"].blk[b]], par=True)
            S.barrier()

    def phase_even_b(self, ei, sc, yT):
        S = self.S
        T = self.T
        NCH = T // 128
        cst = self.cst
        gam = [1.0 - 2.0 ** (-5.0 - h) for h in range(4)]
        with contextlib.ExitStack() as es:
            def rot(name, shape, dt, n=2):
                return Rot([S.sb("%s%d" % (name, i), shape, dt, es) for i in range(n)])
            qrb = rot("eb_q", [128, 4, 128], BF16); krb = rot("eb_k", [128, 4, 128], BF16)
            vb = rot("eb_v", [128, 1024], BF16); gb = rot("eb_g", [128, 1024], BF16); zb = rot("eb_z", [128, 1024], BF16)
            xsb = rot("eb_xs", [128, 16, 64], BF16); btmb = rot("eb_bt", [128, 2, 128], BF16)
            BTb = rot("eb_BT", [128, 2, 128], BF16); CTb = rot("eb_CT", [128, 2, 128], BF16)
            dtb = rot("eb_dt", [128, 32], F32)
            Sst = S.sb("eb_S", [128, 4, 256], F32, es); Sbf = S.sb("eb_Sbf", [128, 4, 256], BF16, es)
            STs = S.sb("eb_ST", [128, 16, 64], F32, es); STbf = S.sb("eb_STbf", [128, 16, 64], BF16, es)
            smb = rot("eb_sm", [128, 4, 128], BF16); ktmb = rot("eb_ktm", [128, 4, 128], BF16); q2b = rot("eb_q2", [128, 4, 128], BF16)
            Dm = S.sb("eb_D", [128, 16, 128], F32, es)
            acol = rot("eb_acol", [128, 48], F32)
            segb = rot("eb_seg", [128, 4, 128], F32, 3); ltb = rot("eb_lt", [128, 4, 128], F32, 2); erb = rot("eb_er", [128, 4, 128], F32, 2)
            cbm = rot("eb_cbm", [128, 2, 128], F32)
            MTb = S.sb("eb_MT", [128, 16, 128], BF16, es); CsTb = S.sb("eb_CsT", [128, 16, 128], BF16, es)
            xdtb = rot("eb_xdt", [128, 16, 64], BF16); xdt2b = rot("eb_xdt2", [128, 16, 64], BF16)
            yf = rot("eb_yf", [128, 1024], F32, 2)
            yab = rot("eb_ya", [128, 1024], BF16, 2); ybb = rot("eb_yb", [128, 1024], BF16, 2)
            stat = rot("eb_stat", [128, 4, 6], F32); mv = rot("eb_mv", [128, 4, 2], F32); rs4 = rot("eb_rs4", [128, 8], F32)
            yTo = rot("eb_yTo", [128, 16, 128], BF16, 2)
            junk = S.sb("eb_junk", [128, 512], BF16, es)
            PSr = Rot(self.psb)
            PSs = Rot(self.psh)
            S.op("dve", lambda e: e.memset(Sst[:], 0.0), writes=[Sst])
            S.op("pool", lambda e: e.memset(Sbf[:], 0.0), writes=[Sbf])
            S.op("dve", lambda e: e.memset(STs[:], 0.0), writes=[STs])
            S.op("pool", lambda e: e.memset(STbf[:], 0.0), writes=[STbf])
            gnr_b = S.sb("eb_gnr", [128, 1024], F32, es); nrr_b = S.sb("eb_nrr", [128, 1024], F32, es)
            S.dma("sp", gnr_b[:], self.w["ret_gn"][ei].partition_broadcast(128), writes=[gnr_b])
            S.dma("sp", nrr_b[:], self.w["ssd_norm"][ei].partition_broadcast(128), writes=[nrr_b])
            gnrow = gnr_b[:]; nrow = nrr_b[:]; dsk = self.ssd_D[ei]
            for c in range(NCH):
                b = c // 4
                ts_ = slice(c * 128, (c + 1) * 128)
                qr = qrb.next(); kr = krb.next(); v = vb.next(); g = gb.next(); z = zb.next(); xs = xsb.next()
                btm = btmb.next(); BT = BTb.next(); CT = CTb.next(); dt = dtb.next()
                S.dma("sp", qr[:], sc["qr"].ap[:, ts_].rearrange("(h d) t -> d h t", d=128), reads=[sc["qr"].blk[b]], writes=[qr])
                S.dma("sp", kr[:], sc["kr"].ap[:, ts_].rearrange("(h d) t -> d h t", d=128), reads=[sc["kr"].blk[b]], writes=[kr])
                S.dma("sp", v[:], sc["v"].ap[ts_, :], reads=[sc["v"].blk[b]], writes=[v])
                S.dma("sp", g[:], sc["g"].ap[ts_, :], reads=[sc["g"].blk[b]], writes=[g])
                S.dma("sp", z[:], sc["z"].ap[ts_, :], reads=[sc["z"].blk[b]], writes=[z])
                S.dma("sp", xs[:], sc["xs"].ap[ts_, :].rearrange("t (h p) -> t h p", p=64), reads=[sc["xs"].blk[b]], writes=[xs])
                S.dma("sp", btm[:], sc["Btm"].ap[ts_, :].rearrange("t (g s) -> t g s", s=128), reads=[sc["Btm"].blk[b]], writes=[btm])
                S.dma("sp", BT[:], sc["BT"].ap[:, ts_].rearrange("(g s) t -> s g t", s=128), reads=[sc["BT"].blk[b]], writes=[BT])
                S.dma("sp", CT[:], sc["CT"].ap[:, ts_].rearrange("(g s) t -> s g t", s=128), reads=[sc["CT"].blk[b]], writes=[CT])
                S.dma("sp", dt[:], sc["dtA"].ap[ts_, :], reads=[sc["dtA"].blk[b]], writes=[dt])

                def ret_stream():
                    Psc = PSr.next()
                    for h in range(4):
                        self.mm(Psc, Psc[:, h * 128:(h + 1) * 128], kr[:, h, :], qr[:, h, :], True, True, [kr, qr])
                    sm = smb.next()
                    self.tt("dve", sm[:].rearrange("p h i -> p (h i)"), Psc[:], self.ret_decay[:], ALU.mult, [Psc, cst], [sm])
                    Pkt = PSr.next()
                    for h in range(4):
                        self.transpose_to(Pkt, h * 128, kr, kr[:, h, :])
                    ktm = ktmb.next()
                    self.tt("dve", ktm[:], Pkt[:].bitcast(BF16)[:, 0:512].rearrange("p (h d) -> p h d", d=128),
                            self.ret_kdec[:].unsqueeze(2).broadcast_to([128, 4, 128]), ALU.mult, [Pkt, cst], [ktm])
                    q2 = q2b.next()
                    self.tt("pool", q2[:].rearrange("p h i -> p (h i)"), qr[:].rearrange("p h i -> p (h i)"), self.ret_qdec[:], ALU.mult, [qr, cst], [q2])
                    POw = (PSr.next(), PSr.next())
                    for h in range(4):
                        PO = POw[h // 2]
                        hc = slice((h % 2) * 256, (h % 2 + 1) * 256)
                        self.mm(PO, PO[:, hc], sm[:, h, :], v[:, h * 256:(h + 1) * 256], True, False, [sm, v])
                        self.mm(PO, PO[:, hc], q2[:, h, :], Sbf[:, h, :], False, True, [q2, Sbf])
                    Pkvw = (PSr.next(), PSr.next())
                    for h in range(4):
                        Pkv = Pkvw[h // 2]
                        self.mm(Pkv, Pkv[:, (h % 2) * 256:(h % 2 + 1) * 256], ktm[:, h, :], v[:, h * 256:(h + 1) * 256], True, True, [ktm, v])
                    for h in range(4):
                        Pkv = Pkvw[h // 2]
                        self.stt(Sst[:, h, :], Sst[:, h, :], gam[h] ** 128, Pkv[:, (h % 2) * 256:(h % 2 + 1) * 256], ALU.mult, ALU.add, [Sst, Pkv], [Sst])
                    self.copy("act", Sbf[:], Sst[:], [Sst], [Sbf])
                    st = stat.next(); m2 = mv.next(); r4 = rs4.next()
                    for h in range(4):
                        PO = POw[h // 2]
                        S.op("dve", lambda e, h=h, st=st, PO=PO: e.bn_stats(out=st[:, h, :], in_=PO[:, (h % 2) * 256:(h % 2 + 1) * 256]), reads=[PO], writes=[st])
                    for h in range(4):
                        S.op("dve", lambda e, h=h, st=st, m2=m2: e.bn_aggr(out=m2[:, h, :], in_=st[:, h, :]), reads=[st], writes=[m2])
                    self.act(r4[:, 0:4], m2[:, :, 1], AF.Ln, [m2], [r4], bias=self.epsb[:])
                    self.act(r4[:, 0:4], r4[:, 0:4], AF.Exp, [r4], [r4], scale=-0.5)
                    y1 = yf.next()
                    for h in range(4):
                        PO = POw[h // 2]
                        self.ts("dve", y1[:, h * 256:(h + 1) * 256], PO[:, (h % 2) * 256:(h % 2 + 1) * 256], m2[:, h, 0:1], ALU.subtract, [PO, m2, r4], [y1],
                                s2=r4[:, h:h + 1], op1=ALU.mult)
                    self.tt("pool", y1[:], y1[:], gnrow, ALU.mult, [y1, gnr_b], [y1])
                    ya = yab.next()
                    self.tt("pool", ya[:], y1[:], g[:], ALU.mult, [y1, g], [ya])

                    out_['ya'] = ya

                def ssd_stream():
                    PA = PSs.next()
                    self.mm(PA, PA[:, 0:16], self.tri_f[:], dt[:, 16:32], True, True, [self.tri_f, dt])
                    self.mm(PA, PA[:, 16:32], self.ones_f[:], dt[:, 16:32], True, True, [self.ones_f, dt])
                    ac = acol.next()
                    self.copy("act", ac[:, 0:32], PA[:, 0:32], [PA], [ac])
                    self.tt("dve", ac[:, 32:48], ac[:, 16:32], ac[:, 0:16], ALU.subtract, [ac], [ac])
                    self.act(ac[:, 32:48], ac[:, 32:48], AF.Exp, [ac], [ac])
                    self.act(ac[:, 16:32], ac[:, 16:32], AF.Exp, [ac], [ac])
                    self.tt("dve", Dm[:], dt[:, 16:32].unsqueeze(2).broadcast_to([128, 16, 128]),
                            self.tri_f[:].unsqueeze(1).broadcast_to([128, 16, 128]), ALU.mult, [dt, self.tri_f], [Dm])
                    PCB = PSs.next()
                    for gi in range(2):
                        self.mm(PCB, PCB[:, gi * 128:(gi + 1) * 128], BT[:, gi, :], CT[:, gi, :], True, True, [BT, CT])
                    cb_ = cbm.next()
                    self.tt("dve", cb_[:].rearrange("p g i -> p (g i)"), PCB[:, 0:256], self.tri2[:], ALU.mult, [PCB, cst], [cb_])
                    for q4 in range(4):
                        gi = q4 // 2
                        hs = slice(q4 * 4, q4 * 4 + 4)
                        Prb = PSs.next()
                        self.mm(Prb, Prb[:], self.ones_f[:], Dm[:, hs, :].rearrange("p h i -> p (h i)"), True, True, [self.ones_f, Dm])
                        sg = segb.next()
                        self.tt("dve", sg[:], Prb[:].rearrange("p (h i) -> p h i", i=128), ac[:, q4 * 4:q4 * 4 + 4].unsqueeze(2).broadcast_to([128, 4, 128]),
                                ALU.subtract, [Prb, ac], [sg])
                        lt = ltb.next()
                        self.act(lt[:], sg[:], AF.Exp, [sg], [lt])
                        er = erb.next()
                        self.act(er[:].rearrange("p h i -> p (h i)"), Prb[:], AF.Exp, [Prb], [er])
                        self.stt(MTb[:, hs, :], lt[:], 1.0, cb_[:, gi, :].unsqueeze(1).broadcast_to([128, 4, 128]), ALU.min, ALU.mult, [lt, cb_], [MTb])
                        self.tt("pool", CsTb[:, hs, :], er[:], CT[:, gi, :].unsqueeze(1).broadcast_to([128, 4, 128]), ALU.mult, [er, CT], [CsTb])
                    xdt = xdtb.next(); xdt2 = xdt2b.next()
                    self.tt("dve", xdt[:], xs[:], dt[:, 0:16].unsqueeze(2).broadcast_to([128, 16, 64]), ALU.mult, [xs, dt], [xdt])
                    self.tt("pool", xdt2[:], xdt[:], ac[:, 32:48].unsqueeze(2).broadcast_to([128, 16, 64]), ALU.mult, [xdt, ac], [xdt2])
                    PYw = (PSs.next(), PSs.next())
                    for h in range(16):
                        PY = PYw[h // 8]
                        hc = slice((h % 8) * 64, (h % 8 + 1) * 64)
                        self.mm(PY, PY[:, hc], MTb[:, h, :], xdt[:, h, :], True, False, [MTb, xdt])
                        self.mm(PY, PY[:, hc], CsTb[:, h, :], STbf[:, h, :], False, True, [CsTb, STbf])
                    PStw = (PSs.next(), PSs.next())
                    for h in range(16):
                        PSt = PStw[h // 8]
                        self.mm(PSt, PSt[:, (h % 8) * 64:(h % 8 + 1) * 64], btm[:, h // 8, :], xdt2[:, h, :], True, True, [btm, xdt2])
                    self.tt("dve", STs[:], STs[:], ac[:, 16:32].unsqueeze(2).broadcast_to([128, 16, 64]), ALU.mult, [STs, ac], [STs])
                    STf = STs[:].rearrange("p h d -> p (h d)")
                    for gi in range(2):
                        self.tt("dve", STf[:, gi * 512:(gi + 1) * 512], PStw[gi][:], STf[:, gi * 512:(gi + 1) * 512], ALU.add, [PStw[gi], STs], [STs])
                    self.copy("act", STbf[:], STs[:], [STs], [STbf])
                    y2 = yf.next()
                    self.tt("pool", y2[:].rearrange("p (h d) -> p h d", d=64), xs[:], dsk.unsqueeze(2).broadcast_to([128, 16, 64]), ALU.mult, [xs, cst], [y2])
                    for gi in range(2):
                        self.tt("dve", y2[:, gi * 512:(gi + 1) * 512], PYw[gi][:], y2[:, gi * 512:(gi + 1) * 512], ALU.add, [PYw[gi], y2], [y2])
                    self.tt("pool", y2[:], y2[:], z[:], ALU.mult, [y2, z], [y2])
                    r8 = rs4.next()
                    for gi in range(2):
                        S.op("act", lambda e, gi=gi, y2=y2, r8=r8: e.activation(out=junk[:], in_=y2[:, gi * 512:(gi + 1) * 512], func=AF.Square, accum_out=r8[:, gi:gi + 1]),
                             reads=[y2], writes=[junk, r8])
                    self.act(r8[:, 0:2], r8[:, 0:2], AF.Ln, [r8], [r8], scale=1.0 / 512, bias=self.epsb[:])
                    self.act(r8[:, 0:2], r8[:, 0:2], AF.Exp, [r8], [r8], scale=-0.5)
                    yb = ybb.next()
                    for gi in range(2):
                        self.stt(yb[:, gi * 512:(gi + 1) * 512], y2[:, gi * 512:(gi + 1) * 512], r8[:, gi:gi + 1], nrow[:, gi * 512:(gi + 1) * 512],
                                 ALU.mult, ALU.mult, [y2, r8, nrr_b], [yb])
                    out_['yb'] = yb

                out_ = {}
                S.interleave([S.record(ret_stream), S.record(ssd_stream)])
                ya, yb = out_['ya'], out_['yb']
                yo = yTo.next()
                for half, src in ((0, ya), (1, yb)):
                    Pt = self.PS.next()
                    for cc in range(8):
                        self.transpose_to(Pt, cc * 128, src, src[:, cc * 128:(cc + 1) * 128])
                    self.copy("act", yo[:, half * 8:(half + 1) * 8, :].rearrange("p c i -> p (c i)"), Pt[:].bitcast(BF16), [Pt], [yo])
                S.dma("pool", yT.ap[:, ts_].rearrange("(c p) t -> p c t", p=128), yo[:], reads=[yo], writes=[yT.blk[b]], par=True)
            S.barrier()

    def build(self):
        nc, es = self.nc, self.es
        T, NB = self.T, self.NB
        S = self.S = Sched(nc, es)
        ne = sum(1 for x in self.layers if x == "e")
        no = sum(1 for x in self.layers if x == "o")
        L = len(self.layers)
        w = self.w = {}

        def inp(name, shape):
            w[name] = nc.dram_tensor(name, list(shape), F32, kind="ExternalInput").ap()

        self.xin = DT(S, "xT", [D, T], F32, NB, kind="ExternalInput")
        self.xout = DT(S, "outT", [D, T], F32, NB, kind="ExternalOutput")
        inp("norm_mix", [L, D]); inp("norm_mlp", [L, D])
        inp("mlp_w1", [L, D, 4096]); inp("mlp_w2", [L, 4096, D])
        if no:
            inp("od_w_in", [no, D, 3584]); inp("od_w_out", [no, 1536, D])
            inp("lru_conv_w", [no, 4, D]); inp("lru_conv_b", [no, D])
            inp("lru_wa", [no, 8, 128, 128]); inp("lru_ba", [no, 8, 128])
            inp("lru_wx", [no, 8, 128, 128]); inp("lru_bx", [no, 8, 128])
            inp("lru_lam", [no, D]); inp("sb_qn", [no, 64]); inp("sb_kn", [no, 64])
        if ne:
            inp("ev_w_in", [ne, D, 5648]); inp("ev_w_out", [ne, 2048, D])
            inp("ret_qn", [ne, 128]); inp("ret_kn", [ne, 128]); inp("ret_gn", [ne, 1024])
            inp("ssd_conv_w", [ne, 4, 1536]); inp("ssd_conv_b", [ne, 1536])
            inp("ssd_dt_bias", [ne, 16]); inp("ssd_a_log", [ne, 16]); inp("ssd_d", [ne, 16]); inp("ssd_norm", [ne, 1024])
            inp("c_cos", [128, T]); inp("c_sin", [128, T])
            inp("c_ident", [128, 128]); inp("c_swap", [128, 128]); inp("c_tri", [128, 128]); inp("c_tri2", [128, 256])
            inp("c_rdecay", [128, 512]); inp("c_rqdec", [128, 512]); inp("c_rkdec", [128, 4])
        inp("c_masklt", [128, 128]); inp("c_umat", [128, 128]); inp("c_bones", [128, 128])

        allb = [S.ps("ps%d" % i, [128, 512]) for i in range(8)]
        self.psb = allb[0:4]
        self.psh = allb[4:8]
        self.PS = Rot(self.psb)
        self.PW = Rot([(allb[4], allb[5]), (allb[6], allb[7])])

        cst = self.cst = Buf("cst")

        def csb(name, shape, dt=F32):
            return es.enter_context(nc.sbuf_tensor(name, list(shape), dt))

        self.ones = S.sb("ones", [128, 128], BF16)
        S.op("dve", lambda e: e.memset(self.ones[:], 1.0), writes=[self.ones])
        self.epsb = csb("epsb", [128, 1]); self.oneb = csb("oneb", [128, 1])
        S.op("dve", lambda e: e.memset(self.epsb[:], EPS), writes=[cst])
        S.op("dve", lambda e: e.memset(self.oneb[:], 1.0), writes=[cst])
        self.mask_lt = csb("mask_lt", [128, 128], BF16)
        self.umat = S.sb("umat", [128, 128], BF16)
        self.bones = S.sb("bones", [128, 128], BF16)
        S.dma("pool", self.mask_lt[:], w["c_masklt"], writes=[cst])
        S.dma("pool", self.umat[:], w["c_umat"], writes=[self.umat])
        S.dma("pool", self.bones[:], w["c_bones"], writes=[self.bones])
        self.gains = Buf("gains")
        gmix = csb("gmix", [128, L, KC]); gmlp = csb("gmlp", [128, L, KC])
        S.dma("sp", gmix[:], w["norm_mix"].rearrange("l (kc p) -> p l kc", p=128), writes=[self.gains], allow_slow_non_contiguous=True)
        S.dma("sp", gmlp[:], w["norm_mlp"].rearrange("l (kc p) -> p l kc", p=128), writes=[self.gains], allow_slow_non_contiguous=True)
        self.g_mix = [gmix[:, l, :] for l in range(L)]
        self.g_mlp = [gmlp[:, l, :] for l in range(L)]
        if no:
            cw = csb("lru_cw", [128, no, 8, 4]); cb = csb("lru_cb", [128, no, 8])
            ba = csb("lru_ba_s", [128, no, 8]); bx = csb("lru_bx_s", [128, no, 8])
            cl = csb("lru_cl", [128, no, 8])
            qn = csb("sbqn", [128, no]); kn = csb("sbkn", [128, no])
            for o_ in range(no):
                for k_ in range(4):
                    S.dma("sp", cw[:, o_, :, k_], w["lru_conv_w"][o_, k_].rearrange("(c p) -> p c", p=128), writes=[cst], allow_slow_non_contiguous=True)
            S.dma("sp", cb[:], w["lru_conv_b"].rearrange("o (c p) -> p o c", p=128), writes=[cst], allow_slow_non_contiguous=True)
            S.dma("sp", ba[:], w["lru_ba"].rearrange("o c p -> p o c"), writes=[cst], allow_slow_non_contiguous=True)
            S.dma("sp", bx[:], w["lru_bx"].rearrange("o c p -> p o c"), writes=[cst], allow_slow_non_contiguous=True)
            S.dma("sp", cl[:], w["lru_lam"].rearrange("o (c p) -> p o c", p=128), writes=[cst], allow_slow_non_contiguous=True)
            for half in range(2):
                S.dma("sp", qn[half * 64:(half + 1) * 64, :], w["sb_qn"].rearrange("o d -> d o"), writes=[cst], allow_slow_non_contiguous=True)
                S.dma("sp", kn[half * 64:(half + 1) * 64, :], w["sb_kn"].rearrange("o d -> d o"), writes=[cst], allow_slow_non_contiguous=True)
            S.op("act", lambda e: e.activation(out=cl[:], in_=cl[:], func=AF.Exp, scale=-1.0), reads=[cst], writes=[cst])
            S.op("act", lambda e: e.activation(out=cl[:], in_=cl[:], func=AF.Ln, bias=self.oneb[:]), reads=[cst], writes=[cst])
            S.op("dve", lambda e: e.tensor_scalar(out=cl[:], in0=cl[:], scalar1=-8.0, scalar2=None, op0=ALU.mult), reads=[cst], writes=[cst])
            self.lru_cw = [cw[:, o] for o in range(no)]
            self.lru_cb = [cb[:, o] for o in range(no)]
            self.lru_ba = [ba[:, o] for o in range(no)]
            self.lru_bx = [bx[:, o] for o in range(no)]
            self.lru_cl = [cl[:, o] for o in range(no)]
            self.sb_qn = [qn[:, o:o + 1] for o in range(no)]
            self.sb_kn = [kn[:, o:o + 1] for o in range(no)]
        if ne:
            self.ident = S.sb("ident", [128, 128], BF16)
            S.dma("pool", self.ident[:], w["c_ident"], writes=[self.ident])
            self.swapm = S.sb("swapm", [128, 128], F32)
            S.dma("sp", self.swapm[:], w["c_swap"], writes=[self.swapm])
            self.tri_f = S.sb("tri_f", [128, 128], F32)
            S.dma("sp", self.tri_f[:], w["c_tri"], writes=[self.tri_f])
            self.ones_f = S.sb("ones_f", [128, 128], F32)
            S.op("dve", lambda e: e.memset(self.ones_f[:], 1.0), writes=[self.ones_f])
            self.tri2 = csb("tri2", [128, 256]); self.ret_decay = csb("rdecay", [128, 512]); self.ret_qdec = csb("rqdec", [128, 512])
            self.ret_kdec = csb("rkdec", [128, 4])
            S.dma("sp", self.tri2[:], w["c_tri2"], writes=[cst])
            S.dma("sp", self.ret_decay[:], w["c_rdecay"], writes=[cst])
            S.dma("sp", self.ret_qdec[:], w["c_rqdec"], writes=[cst])
            S.dma("sp", self.ret_kdec[:], w["c_rkdec"], writes=[cst])
            scw = csb("ssd_cw", [128, ne, 12, 4]); scb = csb("ssd_cb", [128, ne, 12])
            rqn = csb("ret_qn_s", [128, ne, 2]); rkn = csb("ret_kn_s", [128, ne, 2])
            dsk = csb("ssd_D_r", [128, ne, 16]); dtbr = csb("ssd_dtb_r", [128, ne, 16]); Ar = csb("ssd_A_r", [128, ne, 16])
            for e_ in range(ne):
                for k_ in range(4):
                    S.dma("sp", scw[:, e_, :, k_], w["ssd_conv_w"][e_, k_].rearrange("(c p) -> p c", p=128), writes=[cst], allow_slow_non_contiguous=True)
                S.dma("sp", scb[:, e_, :], w["ssd_conv_b"][e_].rearrange("(c p) -> p c", p=128), writes=[cst], allow_slow_non_contiguous=True)
                for nm, tl in (("ret_qn", rqn), ("ret_kn", rkn)):
                    col = w[nm][e_].rearrange("(d o) -> d o", o=1)
                    S.dma("sp", tl[:, e_, 0:1], col, writes=[cst], allow_slow_non_contiguous=True)
                    S.dma("sp", tl[0:64, e_, 1:2], col[64:128], writes=[cst], allow_slow_non_contiguous=True)
                    S.dma("sp", tl[64:128, e_, 1:2], col[0:64], writes=[cst], allow_slow_non_contiguous=True)
                S.dma("sp", dsk[:, e_, :], w["ssd_d"][e_].partition_broadcast(128), writes=[cst])
                S.dma("sp", dtbr[:, e_, :], w["ssd_dt_bias"][e_].partition_broadcast(128), writes=[cst])
                S.dma("sp", Ar[:, e_, :], w["ssd_a_log"][e_].partition_broadcast(128), writes=[cst])
            S.op("act", lambda e: e.activation(out=Ar[:], in_=Ar[:], func=AF.Exp), reads=[cst], writes=[cst])
            S.op("dve", lambda e: e.tensor_scalar(out=Ar[:], in0=Ar[:], scalar1=-1.0, scalar2=None, op0=ALU.mult), reads=[cst], writes=[cst])
            self.ssd_cw = [scw[:, e_] for e_ in range(ne)]; self.ssd_cb = [scb[:, e_] for e_ in range(ne)]
            self.ret_qn = [rqn[:, e_] for e_ in range(ne)]; self.ret_kn = [rkn[:, e_] for e_ in range(ne)]
            self.ssd_D = [dsk[:, e_] for e_ in range(ne)]; self.ssd_dtb = [dtbr[:, e_] for e_ in range(ne)]; self.ssd_A = [Ar[:, e_] for e_ in range(ne)]
        S.barrier()

        xa = DT(S, "xa", [D, T], F32, NB)
        xb_ = DT(S, "xb", [D, T], F32, NB)
        yT = DT(S, "yT", [2048, T], BF16, NB)
        qT = DT(S, "qT", [512, T], BF16, NB)
        kT = DT(S, "kT", [512, T], BF16, NB)
        vtm = DT(S, "vtm", [T, 1024], BF16, NB)
        sc = {"qr": qT, "kr": kT, "v": vtm}
        if ne:
            for nm in ("g", "z", "xs"):
                sc[nm] = DT(S, "sc_" + nm, [T, 1024], BF16, NB)
            sc["Btm"] = DT(S, "sc_Btm", [T, 256], BF16, NB)
            sc["BT"] = DT(S, "sc_BT", [256, T], BF16, NB)
            sc["CT"] = DT(S, "sc_CT", [256, T], BF16, NB)
            sc["dtA"] = DT(S, "sc_dtA", [T, 32], F32, NB)

        cur = self.xin
        ie = io = 0
        for l, kind in enumerate(self.layers):
            last = l == L - 1
            skip = getattr(self, "skip", ())
            lay_es = contextlib.ExitStack()
            w1b = None
            w1_th = ()

            def prep_w1():
                nonlocal w1b, w1_th
                if "mlp" in skip:
                    return
                w1b = S.sb("w1b", [128, KC, 4096], BF16, lay_es)
                w1_th = self.load_wcast(w1b, self.w["mlp_w1"][l], KC, 4096, defer=True)
            if kind == "o":
                if "odd_a" not in skip:
                    self.phase_odd_a(io, l, cur, yT, qT, kT, vtm)
                if "sb" not in skip:
                    self.phase_sb(qT, kT, vtm, yT)
                if "outproj" not in skip:
                    prep_w1()
                    self.phase_outproj("od_w_out", io, 12, yT, cur, xa, extra=w1_th)
                io += 1
            else:
                if "even_a" not in skip:
                    self.phase_even_a(ie, l, cur, sc)
                if "even_b" not in skip:
                    self.phase_even_b(ie, sc, yT)
                if "outproj" not in skip:
                    prep_w1()
                    self.phase_outproj("ev_w_out", ie, 16, yT, cur, xa, extra=w1_th)
                ie += 1
            if "mlp" not in skip:
                self.phase_mlp(l, xa, self.xout if last else xb_, w1b=(w1b if "outproj" not in skip else None))
            lay_es.close()
            cur = xb_
        S.finish()
        es.close()
        return nc


def consts(T=None, even=False):
    i = np.arange(128)
    c = _consts_base(i)
    if even:
        f32 = np.float32
        inv = (f32(10000.0) ** (-(np.arange(64, dtype=f32)) / f32(64))).astype(f32)
        ang = (np.arange(T, dtype=f32)[None, :] * inv[:, None]).astype(f32).astype(np.float64)
        cos = np.cos(ang); sin = np.sin(ang)
        c["c_cos"] = np.concatenate([cos, cos], 0).astype(f32)
        c["c_sin"] = np.concatenate([-sin, sin], 0).astype(f32)
        c["c_ident"] = np.eye(128, dtype=f32)
        c["c_swap"] = (i[:, None] == ((i[None, :] + 64) % 128)).astype(f32)
        tri = (i[:, None] <= i[None, :]).astype(f32)
        c["c_tri"] = tri
        c["c_tri2"] = np.concatenate([tri, tri], 1)
        lg = np.log1p(-np.exp2(-5.0 - np.arange(4, dtype=np.float64)))
        rel = (i[None, :] - i[:, None]).astype(np.float64)
        dec = np.where(rel[:, None, :] >= 0, np.exp(lg[None, :, None] * np.maximum(rel, 0)[:, None, :]), 0.0)
        c["c_rdecay"] = dec.reshape(128, 512).astype(f32)
        qd = np.exp(lg[:, None] * (i[None, :] + 1.0))
        c["c_rqdec"] = np.broadcast_to(qd.reshape(1, 512), (128, 512)).astype(f32).copy()
        c["c_rkdec"] = np.exp(lg[None, :] * (127.0 - i[:, None])).astype(f32)
    return c


def _consts_base(i):
    return {
        "c_masklt": (i[:, None] < i[None, :]).astype(np.float32),
        "c_umat": (i[:, None] > i[None, :]).astype(np.float32),
        "c_bones": ((i[:, None] // 64) == (i[None, :] // 64)).astype(np.float32),
    }


def make_inputs(inputs, b, kinds):
    m = {"xT": np.ascontiguousarray(np.asarray(inputs["x"])[b].T)}
    L = len(kinds)
    for n in ("norm_mix", "norm_mlp", "mlp_w1", "mlp_w2"):
        m[n] = np.ascontiguousarray(np.asarray(inputs[n])[:L])
    if "e" in kinds:
        for n in ("ev_w_in", "ev_w_out", "ret_qn", "ret_kn", "ret_gn", "ssd_conv_w", "ssd_conv_b", "ssd_dt_bias", "ssd_a_log", "ssd_d", "ssd_norm"):
            m[n] = np.ascontiguousarray(np.asarray(inputs[n]))
    if "o" in kinds:
        for n in ("od_w_in", "od_w_out", "lru_conv_w", "lru_conv_b", "lru_wa", "lru_ba", "lru_wx", "lru_bx", "lru_lam", "sb_qn", "sb_kn"):
            m[n] = np.ascontiguousarray(np.asarray(inputs[n]))
    m.update(consts(m["xT"].shape[1], "e" in kinds))
    return m


KINDS = "eoeo"
SEQ = 8192
_NC_CACHE = {}


def kernel(**inputs):
    if "nc" not in _NC_CACHE:
        _NC_CACHE["nc"] = K(SEQ, list(KINDS)).build()
    nc = _NC_CACHE["nc"]
    nb = np.asarray(inputs["x"]).shape[0]
    in_maps = [make_inputs(inputs, b, KINDS) for b in range(nb)]
    res = run_bass_kernel_spmd(nc, in_maps, core_ids=list(range(nb)))
    out = np.stack([np.asarray(res.results[b]["outT"]).T for b in range(nb)])
    return np.ascontiguousarray(out.astype(np.float32))
```

```python
import contextlib
import math
import numpy as np
import concourse.bass as bass
import concourse.mybir as mybir
from concourse.bass_utils import run_bass_kernel_spmd

F32 = mybir.dt.float32
BF16 = mybir.dt.bfloat16
AF = mybir.ActivationFunctionType
ALU = mybir.AluOpType
AX = mybir.AxisListType

D = 1024
KC = 8
TB = 512
EPS = 1e-6


class Buf:
    __slots__ = ("name", "t", "lws", "rd")

    def __init__(self, name, t=None):
        self.name = name
        self.t = t
        self.lws = []
        self.rd = {}

    def __getitem__(self, k):
        return self.t[k]


class Rot:
    def __init__(self, bufs):
        self.bufs = list(bufs)
        self.i = 0

    def next(self):
        b = self.bufs[self.i % len(self.bufs)]
        self.i += 1
        return b


class Sched:
    ENG = ("pe", "act", "dve", "pool", "sp")

    def __init__(self, nc, es, n_dma_sems=32):
        self.nc = nc
        self.es = es
        self.q = {e: [] for e in self.ENG}
        self.cnt = {e: 0 for e in self.ENG}
        self.sem = {e: es.enter_context(nc.semaphore("c_" + e)) for e in ("pe", "act", "dve", "pool")}
        self.dsem = [es.enter_context(nc.semaphore("d%d" % i)) for i in range(n_dma_sems)]
        self.dcnt = [0] * n_dma_sems
        self.dnext = 0
        self.pnext = 0
        self.rec = None
        self.waited = {e: {} for e in self.ENG}
        self.ndma = 0

    def sb(self, name, shape, dt, es=None):
        self.uid = getattr(self, "uid", 0) + 1
        name = "%s_u%d" % (name, self.uid)
        return Buf(name, (es or self.es).enter_context(self.nc.sbuf_tensor(name, list(shape), dt)))

    def ps(self, name, shape, dt=F32):
        return Buf(name, self.es.enter_context(self.nc.psum_tensor(name, list(shape), dt)))

    def dram(self, name, shape, dt, kind="Internal"):
        return Buf(name, self.nc.dram_tensor(name, list(shape), dt, kind=kind).ap())

    def _need(self, eng, dep, waits):
        if dep is None:
            return
        key, val, semh = dep
        w = self.waited[eng]
        if w.get(key, 0) >= val:
            return
        w[key] = val
        waits.append((semh, val))

    def _deps(self, eng, reads, writes, same, par=False):
        waits = []
        for b in reads:
            for lw in b.lws:
                if same or lw[0] != eng:
                    self._need(eng, lw, waits)
        for b in writes:
            if not par:
                for lw in b.lws:
                    if same or lw[0] != eng:
                        self._need(eng, lw, waits)
            for k, d in b.rd.items():
                if same or k != eng:
                    self._need(eng, d, waits)
        return waits

    def hoist_begin(self, eng):
        self._hm = (eng, len(self.q[eng]))

    def hoist_end(self):
        eng, m = self._hm
        items = self.q[eng][m:]
        if len(items) < 2:
            return
        allw = []
        for waits, fn, inc in items:
            allw.extend(waits)
        best = {}
        for semh, val in allw:
            k = id(semh)
            if k not in best or best[k][1] < val:
                best[k] = (semh, val)
        self.q[eng][m] = (list(best.values()), items[0][1], items[0][2])
        for j in range(1, len(items)):
            self.q[eng][m + j] = ([], items[j][1], items[j][2])

    def record(self, body):
        old = self.rec
        self.rec = []
        body()
        r = self.rec
        self.rec = old
        return r

    def interleave(self, lists):
        its = [list(l) for l in lists]
        pos = [0] * len(its)
        left = sum(len(l) for l in its)
        while left:
            for k, l in enumerate(its):
                if pos[k] < len(l):
                    it = l[pos[k]]
                    pos[k] += 1
                    left -= 1
                    if it[0] == 0:
                        self.op(it[1], it[2], it[3], it[4])
                    else:
                        self.dma(it[1], it[2], it[3], it[4], it[5], it[6], **it[7])

    def op(self, eng, fn, reads=(), writes=()):
        if self.rec is not None:
            self.rec.append((0, eng, fn, list(reads), list(writes)))
            return None
        same = eng != "pe"
        waits = self._deps(eng, reads, writes, same)
        self.cnt[eng] += 1
        tok = (eng, self.cnt[eng], self.sem[eng])
        self.q[eng].append((waits, fn, (self.sem[eng], 1)))
        for b in reads:
            b.rd[eng] = tok
        for b in writes:
            b.lws = [tok]
            b.rd = {}
        return tok

    def dma(self, qeng, out_ap, in_ap, reads=(), writes=(), par=False, **kw):
        if self.rec is not None:
            self.rec.append((1, qeng, out_ap, in_ap, list(reads), list(writes), par, kw))
            return None
        waits = self._deps(qeng, reads, writes, True, par)
        if qeng == "pool":
            s = self.pnext
            self.pnext = (self.pnext + 1) % 4
        else:
            s = 4 + self.dnext
            self.dnext = (self.dnext + 1) % (len(self.dsem) - 4)
        if self.dcnt[s] > 0:
            self._need(qeng, ("d%d" % s, self.dcnt[s], self.dsem[s]), waits)
        self.dcnt[s] += 16
        tok = ("d%d" % s, self.dcnt[s], self.dsem[s])
        self.q[qeng].append((waits, lambda e: e.dma_start(out=out_ap, in_=in_ap, **kw), (self.dsem[s], 16)))
        for b in reads:
            b.rd["dma%d" % self.ndma] = tok
        for b in writes:
            if par:
                if b.rd:
                    b.lws = []
                    b.rd = {}
                b.lws.append(tok)
            else:
                b.lws = [tok]
                b.rd = {}
        self.ndma += 1
        return tok

    def barrier(self):
        for e in self.ENG:
            waits = []
            for e2 in ("pe", "act", "dve", "pool"):
                if e2 != e and self.cnt[e2] > 0:
                    self._need(e, (e2, self.cnt[e2], self.sem[e2]), waits)
            for s in range(len(self.dsem)):
                if self.dcnt[s] > 0:
                    self._need(e, ("d%d" % s, self.dcnt[s], self.dsem[s]), waits)
            if waits:
                self.q[e].append((waits, None, None))

    def finish(self):
        self.barrier()
        nc = self.nc
        q = self.q

        def replay(eh, items):
            for waits, fn, inc in items:
                for semh, val in waits:
                    eh.wait_ge(semh, val)
                if fn is not None:
                    ins = fn(eh)
                    if inc is not None:
                        ins.then_inc(inc[0], inc[1])

        with nc.Block() as block:
            @block.sync
            def _(e):
                replay(e, q["sp"])

            @block.tensor
            def _(e):
                replay(e, q["pe"])

            @block.scalar
            def _(e):
                replay(e, q["act"])

            @block.vector
            def _(e):
                replay(e, q["dve"])

            @block.gpsimd
            def _(e):
                replay(e, q["pool"])


class DT:
    def __init__(self, S, name, shape, dt, nblk, kind="Internal"):
        self.ap = S.nc.dram_tensor(name, list(shape), dt, kind=kind).ap()
        self.blk = [Buf("%s_b%d" % (name, i)) for i in range(nblk)]
        self.all = self.blk


class K:
    def __init__(self, T, layers):
        self.T = T
        self.NB = T // TB
        self.layers = layers
        self.nc = bass.Bass("TRN2", target_bir_lowering=False)
        self.es = contextlib.ExitStack()

    def mm(self, P, out_ap, lhsT, rhs, start, stop, reads):
        self.S.op("pe", lambda e: e.matmul(out_ap, lhsT=lhsT, rhs=rhs, start=start, stop=stop), reads=reads, writes=[P])

    def act(self, out_ap, in_ap, func, reads, writes, **kw):
        self.S.op("act", lambda e: e.activation(out=out_ap, in_=in_ap, func=func, **kw), reads=reads, writes=writes)

    def tt(self, eng, out_ap, a, b, op, reads, writes):
        self.S.op(eng, lambda e: e.tensor_tensor(out=out_ap, in0=a, in1=b, op=op), reads=reads, writes=writes)

    def ts(self, eng, out_ap, a, s1, op0, reads, writes, s2=None, op1=None):
        if op1 is None:
            self.S.op(eng, lambda e: e.tensor_scalar(out=out_ap, in0=a, scalar1=s1, scalar2=None, op0=op0), reads=reads, writes=writes)
        else:
            self.S.op(eng, lambda e: e.tensor_scalar(out=out_ap, in0=a, scalar1=s1, scalar2=s2, op0=op0, op1=op1), reads=reads, writes=writes)

    def stt(self, out_ap, a, s, b, op0, op1, reads, writes):
        self.S.op("dve", lambda e: e.scalar_tensor_tensor(out=out_ap, in0=a, scalar=s, in1=b, op0=op0, op1=op1), reads=reads, writes=writes)

    def copy(self, eng, out_ap, in_ap, reads, writes):
        if eng == "act":
            self.S.op("act", lambda e: e.copy(out=out_ap, in_=in_ap), reads=reads, writes=writes)
        else:
            self.S.op(eng, lambda e: e.tensor_copy(out=out_ap, in_=in_ap), reads=reads, writes=writes)

    def xview(self, xdt, b):
        return xdt.ap.rearrange("(kc p) t -> p kc t", p=128)[:, :, b * TB:(b + 1) * TB]

    def load_wcast(self, wbuf, w_ap, kc_n, ncols, grp=1, defer=False):
        S = self.S
        wv = w_ap.rearrange("(kc p) f -> p kc f", p=128)
        th = []
        for k0 in range(0, kc_n, grp):
            th.append(lambda k0=k0: S.dma("pool", wbuf[:, k0:k0 + grp, :], wv[:, k0:k0 + grp, :], reads=[], writes=[wbuf], par=True, max_dma_last_dim=4096))
        if defer:
            return th
        for t in th:
            t()

    def norm(self, xs, g_ap, hb, es_bufs):
        S = self.S
        sq, rst = es_bufs
        sqv = sq[:, 0:KC, :]
        P = self.PS.next()
        self.act(sqv, xs[:], AF.Square, [xs], [sq])
        for kc in range(KC):
            self.mm(P, P[:], self.ones[:], sqv[:, kc, :], kc == 0, kc == KC - 1, [self.ones, sq])
        self.act(rst[:], P[:], AF.Ln, [P], [rst], scale=1.0 / D, bias=self.epsb[:])
        self.act(rst[:], rst[:], AF.Exp, [rst], [rst], scale=-0.5)
        for kc in range(KC):
            self.stt(hb[:, kc, :], xs[:, kc, :], g_ap[:, kc:kc + 1], rst[:], ALU.mult, ALU.mult, [xs, rst, self.gains], [hb])

    def phase_mlp(self, l, xin, xout, w1b=None):
        S = self.S
        NB = self.NB
        with contextlib.ExitStack() as es:
            pre = w1b is not None
            if not pre:
                w1b = S.sb("w1b", [128, KC, 4096], BF16, es)
            w2b = S.sb("w2b", [128, 32, D], BF16, es)
            xs = S.sb("m_xs", [128, KC, TB], F32, es)
            hb = S.sb("m_hb", [128, KC, TB], BF16, es)
            ab = S.sb("m_ab", [128, 32, TB], BF16, es)
            rst = S.sb("m_rst", [128, TB], F32, es)
            tmps = Rot([S.sb("m_tmp%d" % i, [128, TB], F32, es) for i in range(2)])
            xos = Rot([S.sb("m_xo%d" % i, [128, TB], F32, es) for i in range(3)])
            PS8 = Rot(self.psb + self.psh)
            if not pre:
                self.load_wcast(w1b, self.w["mlp_w1"][l], KC, 4096)
            self.load_wcast(w2b, self.w["mlp_w2"][l], 32, D, grp=4)
            xinv = xin.ap.rearrange("(kc p) t -> p kc t", p=128)
            xoutv = xout.ap.rearrange("(kc p) t -> p kc t", p=128)

            def load_x(b):
                S.dma("sp", xs[:], self.xview(xin, b), reads=[xin.blk[b]], writes=[xs])

            def norm_next(b):
                self.norm(xs, self.g_mlp[l], hb, (hb, rst))

            load_x(0)
            norm_next(0)
            for b in range(NB):
                if b + 1 < NB:
                    load_x(b + 1)
                for fc in range(32):
                    P = PS8.next()
                    for kc in range(KC):
                        self.mm(P, P[:], w1b[:, kc, fc * 128:(fc + 1) * 128], hb[:, kc, :], kc == 0, kc == KC - 1, [w1b, hb])
                    tmp = tmps.next()
                    self.act(tmp[:], P[:], AF.Relu, [P], [tmp])
                    self.tt("dve" if fc % 2 == 0 else "pool", ab[:, fc, :], tmp[:], tmp[:], ALU.mult, [tmp], [ab])
                xo_pref = {}
                for oc in range(KC):
                    for o2 in (oc, oc + 1, oc + 2):
                        if o2 < KC and o2 not in xo_pref:
                            xo = xos.next()
                            S.dma("sp", xo[:], xinv[:, o2, b * TB:(b + 1) * TB], reads=[xin.blk[b]], writes=[xo])
                            xo_pref[o2] = xo
                    P = PS8.next()
                    for fc in range(32):
                        self.mm(P, P[:], w2b[:, fc, oc * 128:(oc + 1) * 128], ab[:, fc, :], fc == 0, fc == 31, [w2b, ab])
                    xo = xo_pref[oc]
                    self.tt("dve", xo[:], P[:], xo[:], ALU.add, [P, xo], [xo])
                    S.dma("pool", xoutv[:, oc, b * TB:(b + 1) * TB], xo[:], reads=[xo], writes=[xout.blk[b]], par=True)
                    if oc == 1 and b + 1 < NB:
                        norm_next(b + 1)
            S.barrier()

    def phase_outproj(self, wname, widx, nkc, yT, xin, xout, extra=()):
        S = self.S
        with contextlib.ExitStack() as es:
            wob = S.sb("wob", [128, nkc, D], BF16, es)
            xss = Rot([S.sb("o_xs%d" % i, [128, KC, TB], F32, es) for i in range(2)])
            ybs = Rot([S.sb("o_yb%d" % i, [128, nkc, TB], BF16, es) for i in range(2)])
            PS8 = Rot(self.psb + self.psh)
            self.load_wcast(wob, self.w[wname][widx], nkc, D, grp=4)
            yv = yT.ap.rearrange("(kc p) t -> p kc t", p=128)
            nxt = {}

            def loads(b):
                yb = ybs.next()
                xs = xss.next()
                S.dma("sp", yb[:], yv[:, 0:nkc, b * TB:(b + 1) * TB], reads=[yT.blk[b]], writes=[yb])
                S.dma("sp", xs[:], self.xview(xin, b), reads=[xin.blk[b]], writes=[xs])
                nxt[b] = (yb, xs)

            loads(0)
            for b in range(self.NB):
                if b + 1 < self.NB:
                    loads(b + 1)
                yb, xs = nxt.pop(b)
                for oc in range(KC):
                    P = PS8.next()
                    for kc in range(nkc):
                        self.mm(P, P[:], wob[:, kc, oc * 128:(oc + 1) * 128], yb[:, kc, :], kc == 0, kc == nkc - 1, [wob, yb])
                    self.tt("dve", xs[:, oc, :], P[:], xs[:, oc, :], ALU.add, [P, xs], [xs])
                S.dma("pool", self.xview(xout, b), xs[:], reads=[xs], writes=[xout.blk[b]])
                if b < len(extra):
                    extra[b]()
            for t in extra[self.NB:]:
                t()
            S.barrier()

    def phase_odd_a(self, o, l, xin, yT, qT, kT, vtm):
        S = self.S
        T = self.T
        with contextlib.ExitStack() as es:
            wib = S.sb("oa_wi", [128, KC, 3584], BF16, es)
            wab = S.sb("oa_wa", [128, 8, 128], BF16, es)
            wxb = S.sb("oa_wx", [128, 8, 128], BF16, es)
            xs = S.sb("oa_xs", [128, KC, TB], F32, es)
            hbs = [S.sb("oa_hb%d" % i, [128, KC, TB], BF16, es) for i in range(2)]
            sq = S.sb("oa_sq", [128, KC, TB], BF16, es)
            rst = S.sb("oa_rst", [128, TB], F32, es)
            xr = [[S.sb("oa_xr%d_%d" % (c, i), [128, TB + 3], F32, es) for i in range(2)] for c in range(8)]
            hst = [S.sb("oa_hst%d" % i, [128, 1], F32, es) for i in range(8)]
            f32t = Rot([S.sb("oa_f%d" % i, [128, TB], F32, es) for i in range(16)])
            b16t = Rot([S.sb("oa_b%d" % i, [128, TB], BF16, es) for i in range(10)])
            PS8 = Rot(self.psb + self.psh)
            vb = Rot([S.sb("oa_vb%d" % i, [128, 512], BF16, es) for i in range(2)])
            self.load_wcast(wib, self.w["od_w_in"][o], KC, 3584)
            S.dma("pool", wab[:], self.w["lru_wa"][o].rearrange("k i j -> i k j"), writes=[wab])
            S.dma("pool", wxb[:], self.w["lru_wx"][o].rearrange("k i j -> i k j"), writes=[wxb])
            for c in range(8):
                S.op("dve", lambda e, c=c: e.memset(hst[c][:], 0.0), writes=[hst[c]])
                S.op("pool", lambda e, c=c: e.memset(xr[c][1][:, TB:TB + 3], 0.0), writes=[xr[c][1]])
            cw = self.lru_cw[o]
            cb = self.lru_cb[o]
            ba = self.lru_ba[o]
            bx = self.lru_bx[o]
            cl = self.lru_cl[o]
            cst = self.cst
            def prep(b):
                S.dma("sp", xs[:], self.xview(xin, b), reads=[xin.blk[b]], writes=[xs])
                self.norm(xs, self.g_mix[l], hbs[b % 2], (sq, rst))

            prep(0)
            for b in range(self.NB):
                hb = hbs[b % 2]
                def lru_chunk(c):
                    cur = xr[c][b % 2]
                    prv = xr[c][(b + 1) % 2]
                    Pg = PS8.next()
                    for kc in range(KC):
                        self.mm(Pg, Pg[:], wib[:, kc, c * 128:(c + 1) * 128], hb[:, kc, :], kc == 0, kc == KC - 1, [wib, hb])
                    gl = f32t.next()
                    self.act(gl[:], Pg[:], AF.Gelu_apprx_tanh, [Pg], [gl])
                    Px = PS8.next()
                    for kc in range(KC):
                        self.mm(Px, Px[:], wib[:, kc, 1024 + c * 128:1024 + (c + 1) * 128], hb[:, kc, :], kc == 0, kc == KC - 1, [wib, hb])
                    self.copy("act", cur[:, 3:TB + 3], Px[:], [Px], [cur])
                    self.copy("pool", cur[:, 0:3], prv[:, TB:TB + 3], [prv], [cur])
                    xcv = f32t.next()
                    self.ts("dve", xcv[:], cur[:, 3:TB + 3], cw[:, c, 3:4], ALU.mult, [cur, cst], [xcv], s2=cb[:, c:c + 1], op1=ALU.add)
                    for k in range(3):
                        self.stt(xcv[:], cur[:, k:k + TB], cw[:, c, k:k + 1], xcv[:], ALU.mult, ALU.add, [cur, xcv, cst], [xcv])
                    xcb = b16t.next()
                    self.copy("pool", xcb[:], xcv[:], [xcv], [xcb])
                    Pr = PS8.next()
                    self.mm(Pr, Pr[:], wab[:, c, :], xcb[:], True, True, [wab, xcb])
                    Pi = PS8.next()
                    self.mm(Pi, Pi[:], wxb[:, c, :], xcb[:], True, True, [wxb, xcb])
                    rr = f32t.next()
                    self.act(rr[:], Pr[:], AF.Sigmoid, [Pr, cst], [rr], bias=ba[:, c:c + 1])
                    ii = f32t.next()
                    self.act(ii[:], Pi[:], AF.Sigmoid, [Pi, cst], [ii], bias=bx[:, c:c + 1])
                    aa = f32t.next()
                    self.act(aa[:], rr[:], AF.Exp, [rr, cst], [aa], scale=cl[:, c:c + 1])
                    a2 = f32t.next()
                    self.tt("pool", a2[:], aa[:], aa[:], ALU.mult, [aa], [a2])
                    self.act(a2[:], a2[:], AF.Sqrt, [a2], [a2], scale=-1.0, bias=self.oneb[:])
                    self.tt("pool", ii[:], ii[:], xcv[:], ALU.mult, [ii, xcv], [ii])
                    self.tt("dve", ii[:], ii[:], a2[:], ALU.mult, [ii, a2], [ii])
                    hh = f32t.next()
                    S.op("dve", lambda e, hh=hh, aa=aa, ii=ii, c=c: e.tensor_tensor_scan(out=hh[:], data0=aa[:], data1=ii[:], initial=hst[c][:, 0:1], op0=ALU.mult, op1=ALU.add),
                         reads=[aa, ii, hst[c]], writes=[hh])
                    self.copy("pool", hst[c][:, 0:1], hh[:, TB - 1:TB], [hh], [hst[c]])
                    yc = b16t.next()
                    self.tt("dve", yc[:], hh[:], gl[:], ALU.mult, [hh, gl], [yc])
                    S.dma("pool", yT.ap[c * 128:(c + 1) * 128, b * TB:(b + 1) * TB], yc[:], reads=[yc], writes=[yT.blk[b]], par=True)
                for c in range(0, 8, 2):
                    S.interleave([S.record(lambda c=c: lru_chunk(c)), S.record(lambda c=c: lru_chunk(c + 1))])
                    if c == 2 and b + 1 < self.NB:
                        prep(b + 1)
                def qk_tile(qi):
                    isq = qi < 4
                    col0 = 2048 + qi * 128
                    Pq = PS8.next()
                    for kc in range(KC):
                        self.mm(Pq, Pq[:], wib[:, kc, col0:col0 + 128], hb[:, kc, :], kc == 0, kc == KC - 1, [wib, hb])
                    s2 = b16t.next()
                    self.act(s2[:], Pq[:], AF.Square, [Pq], [s2])
                    Ps = PS8.next()
                    self.mm(Ps, Ps[:], self.bones[:], s2[:], True, True, [self.bones, s2])
                    r2 = f32t.next()
                    self.act(r2[:], Ps[:], AF.Ln, [Ps], [r2], scale=1.0 / 64, bias=self.epsb[:])
                    self.act(r2[:], r2[:], AF.Exp, [r2], [r2], scale=-0.5)
                    qo = b16t.next()
                    gn = (self.sb_qn if isq else self.sb_kn)[o]
                    self.stt(qo[:], Pq[:], gn[:, 0:1], r2[:], ALU.mult, ALU.mult, [Pq, r2, cst], [qo])
                    dst = qT if isq else kT
                    r0 = (qi % 4) * 128
                    S.dma("pool", dst.ap[r0:r0 + 128, b * TB:(b + 1) * TB], qo[:], reads=[qo], writes=[dst.blk[b]], par=True)
                for q0 in range(0, 8, 4):
                    S.interleave([S.record(lambda qi=q0 + j: qk_tile(qi)) for j in range(4)])
                for tt_ in range(4):
                    Pv = self.PS.next()
                    for kc in range(KC):
                        self.mm(Pv, Pv[:], hb[:, kc, tt_ * 128:(tt_ + 1) * 128], wib[:, kc, 3072:3584], kc == 0, kc == KC - 1, [wib, hb])
                    vv = vb.next()
                    self.copy("act", vv[:], Pv[:], [Pv], [vv])
                    t0 = b * TB + tt_ * 128
                    S.dma("pool", vtm.ap[t0:t0 + 128, 0:512], vv[:], reads=[vv], writes=[vtm.blk[b]], par=True)
            S.barrier()

    def phase_sb(self, qT, kT, vtm, yT):
        S = self.S
        T = self.T
        NKB = T // 128
        scale = 64 ** -0.5
        with contextlib.ExitStack() as es:
            def rot(name, shape, dt, n):
                return Rot([S.sb("%s%d" % (name, i), shape, dt, es) for i in range(n)])
            qh = [S.sb("sb_q%d" % i, [64, T], BF16, es) for i in range(2)]
            kh = [S.sb("sb_k%d" % i, [64, T], BF16, es) for i in range(2)]
            vh = [S.sb("sb_v%d" % i, [128, NKB, 64], BF16, es) for i in range(2)]
            eeR = rot("sb_ee", [128, TB], F32, 2); spR = rot("sb_sp", [128, TB], F32, 7)
            l1R = rot("sb_l1", [128, TB], BF16, 7)
            argR = rot("sb_arg", [128, TB], F32, 3); wwR = rot("sb_ww", [128, TB], BF16, 4)
            ot = rot("sb_o", [64, TB], BF16, 2)
            PzR = Rot([self.psb[0], self.psb[1], self.psb[2], self.psb[3]]); PsfR = Rot([self.psh[0], self.psh[1]])
            PcR = Rot([self.psh[2]]); PaccR = Rot([self.psh[3]])

            def load_head(h):
                q, k, v = qh[h % 2], kh[h % 2], vh[h % 2]
                S.dma("sp", q[:], qT.ap[h * 64:(h + 1) * 64, :], reads=qT.all, writes=[q])
                S.dma("sp", k[:], kT.ap[h * 64:(h + 1) * 64, :], reads=kT.all, writes=[k])
                S.dma("sp", v[:], vtm.ap[:, h * 64:(h + 1) * 64].rearrange("(n p) d -> p n d", p=128), reads=vtm.all, writes=[v])

            units = []
            for h in range(8):
                for Q in range(self.NB):
                    top = 4 * Q + 3
                    pc0 = None
                    for kb in range(top, -1, -1):
                        loc = kb - 4 * Q
                        c0 = max(loc, 0) * 128
                        units.append(dict(h=h, Q=Q, kb=kb, first=(kb == top), last=(kb == 0), newhead=(Q == 0 and kb == top),
                                          c0=c0, pc0=pc0, diag=(loc >= 0)))
                        pc0 = c0
            st = {}

            def pe_z(u):
                h, Q, kb, c0 = u["h"], u["Q"], u["kb"], u["c0"]
                if u["newhead"] and h == 0:
                    load_head(0)
                q, k = qh[h % 2], kh[h % 2]
                Pz = PzR.next()
                self.mm(Pz, Pz[:, c0:TB], k[:, kb * 128:(kb + 1) * 128], q[:, Q * TB + c0:(Q + 1) * TB], True, True, [k, q])
                u["Pz"] = Pz

            def act_sp(u):
                c0, Pz = u["c0"], u["Pz"]
                ee = eeR.next()
                self.act(ee[:, c0:TB], Pz[:, c0:TB], AF.Exp, [Pz], [ee], scale=-scale)
                sp_ = spR.next()
                self.act(sp_[:, c0:TB], ee[:, c0:TB], AF.Ln, [ee], [sp_], bias=self.oneb[:])
                u["sp"] = sp_

            def dve_l1(u):
                c0 = u["c0"]
                Pz, sp_ = u["Pz"], u["sp"]
                l1 = l1R.next()
                self.stt(l1[:, c0:TB], Pz[:, c0:TB], -scale, sp_[:, c0:TB], ALU.mult, ALU.subtract, [Pz, sp_], [l1])
                if u["diag"]:
                    self.tt("pool", l1[:, c0:c0 + 128], l1[:, c0:c0 + 128], self.mask_lt[:], ALU.mult, [l1, self.cst], [l1])
                    if c0 > 0:
                        S.op("pool", lambda e, l1=l1, c0=c0: e.memset(l1[:, 0:c0], 0.0), writes=[l1])
                u["l1"] = l1

            def pe_suffix(u):
                c0 = u["c0"]
                Psf = PsfR.next()
                self.mm(Psf, Psf[:, c0:TB], self.umat[:], u["l1"][:, c0:TB], True, True, [self.umat, u["l1"]])
                u["Psf"] = Psf

            def dve_arg(u):
                c0, pc0 = u["c0"], u["pc0"]
                sp_, Psf = u["sp"], u["Psf"]
                if u["first"]:
                    st["Pc_r"] = PcR.next()
                Pc = u["Pc"] = st["Pc_r"]
                arg = argR.next()
                self.tt("dve", arg[:, c0:TB], Psf[:, c0:TB], sp_[:, c0:TB], ALU.subtract, [Psf, sp_], [arg])
                if not u["first"]:
                    self.tt("dve", arg[:, c0:TB], Pc[:, c0:TB], arg[:, c0:TB], ALU.add, [Pc, arg], [arg])
                u["arg"] = arg

            def pe_colsum(u):
                c0 = u["c0"]
                if not u["last"]:
                    self.mm(u["Pc"], u["Pc"][:, 0:TB], self.ones[:], u["l1"][:, 0:TB], u["first"], False, [self.ones, u["l1"]])

            def act_w(u):
                c0 = u["c0"]
                ww = wwR.next()
                self.act(ww[:, c0:TB], u["arg"][:, c0:TB], AF.Exp, [u["arg"]], [ww])
                if u["diag"]:
                    self.tt("pool", ww[:, c0:c0 + 128], ww[:, c0:c0 + 128], self.mask_lt[:], ALU.mult, [ww, self.cst], [ww])
                    if c0 > 0:
                        S.op("pool", lambda e, ww=ww, c0=c0: e.memset(ww[:, 0:c0], 0.0), writes=[ww])
                u["ww"] = ww

            def pe_wv(u):
                h, Q, kb, c0 = u["h"], u["Q"], u["kb"], u["c0"]
                v = vh[h % 2]
                if u["newhead"] and h + 1 < 8:
                    load_head(h + 1)
                if u["first"]:
                    st["Pacc"] = PaccR.next()
                Pacc = st["Pacc"]
                self.mm(Pacc, Pacc[0:64, 0:TB], v[:, kb, :], u["ww"][:, 0:TB], u["first"], u["last"], [v, u["ww"]])
                if u["last"]:
                    oo = ot.next()
                    self.copy("act", oo[:], Pacc[0:64, 0:TB], [Pacc], [oo])
                    S.dma("pool", yT.ap[1024 + h * 64:1024 + (h + 1) * 64, Q * TB:(Q + 1) * TB], oo[:], reads=[oo], writes=[yT.blk[Q]], par=True)
                u.clear()

            n = len(units)
            import os
            fmap = dict(pe_z=pe_z, pe_suffix=pe_suffix, pe_wv=pe_wv, pe_colsum=pe_colsum, act_sp=act_sp, act_w=act_w, dve_arg=dve_arg, dve_l1=dve_l1)
            if os.environ.get("SB_SCHED"):
                sched = tuple((fmap[x.split(":")[0]], int(x.split(":")[1])) for x in os.environ["SB_SCHED"].split(","))
            else:
                sched = ((pe_z, 0), (pe_wv, 7), (pe_suffix, 4), (pe_colsum, 6), (act_sp, 1), (act_w, 6), (dve_arg, 5), (dve_l1, 2))
            for i in range(n + 8):
                for fn, off in sched:
                    j = i - off
                    if 0 <= j < n:
                        fn(units[j])
            S.barrier()

    def transpose_to(self, P, out_cols, src_buf, src_ap):
        pv = P[:].bitcast(BF16)
        self.S.op("pe", lambda e: e.transpose(out=pv[:, out_cols:out_cols + 128], in_=src_ap, identity=self.ident[:]),
                  reads=[src_buf, self.ident], writes=[P])

    def phase_even_a(self, ei, l, xin, sc):
        S = self.S
        T = self.T
        w = self.w
        with contextlib.ExitStack() as es:
            wib = S.sb("ea_wi", [128, KC, 5648], BF16, es)
            xs = S.sb("ea_xs", [128, KC, TB], F32, es)
            hbs = [S.sb("ea_hb%d" % i, [128, KC, TB], BF16, es) for i in range(2)]
            sq = S.sb("ea_sq", [128, KC, TB], BF16, es)
            rst = S.sb("ea_rst", [128, TB], F32, es)
            cosb = S.sb("ea_cos", [128, TB], F32, es)
            sinb = S.sb("ea_sin", [128, TB], F32, es)
            halo = S.sb("ea_halo", [128, 12, 3], F32, es)
            work = Rot([S.sb("ea_wk%d" % i, [128, TB + 3], F32, es) for i in range(2)])
            f32t = Rot([S.sb("ea_f%d" % i, [128, TB], F32, es) for i in range(8)])
            b16t = Rot([S.sb("ea_b%d" % i, [128, TB], BF16, es) for i in range(6)])
            xct = [S.sb("ea_xc%d" % i, [128, TB], BF16, es) for i in range(12)]
            tkb = Rot([S.sb("ea_tk%d" % i, [128, 1024], BF16, es) for i in range(3)])
            dts = Rot([S.sb("ea_dt%d" % i, [128, 32], F32, es) for i in range(2)])
            self.load_wcast(wib, w["ev_w_in"][ei], KC, 5648)
            S.op("dve", lambda e: e.memset(halo[:], 0.0), writes=[halo])
            cst = self.cst
            cw = self.ssd_cw[ei]; cb = self.ssd_cb[ei]
            qn = self.ret_qn[ei]; kn = self.ret_kn[ei]
            def prep(b):
                bs_ = slice(b * TB, (b + 1) * TB)
                S.dma("sp", xs[:], self.xview(xin, b), reads=[xin.blk[b]], writes=[xs])
                S.dma("sp", cosb[:], w["c_cos"][:, bs_], writes=[cosb])
                S.dma("sp", sinb[:], w["c_sin"][:, bs_], writes=[sinb])
                self.norm(xs, self.g_mix[l], hbs[b % 2], (sq, rst))

            prep(0)
            for b in range(self.NB):
                bs = slice(b * TB, (b + 1) * TB)
                hb = hbs[b % 2]
                for qi in range(8):
                    isq = qi < 4
                    col0 = qi * 128
                    Pq = self.PS.next()
                    for kc in range(KC):
                        self.mm(Pq, Pq[:], wib[:, kc, col0:col0 + 128], hb[:, kc, :], kc == 0, kc == KC - 1, [wib, hb])
                    qraw = f32t.next()
                    self.copy("act", qraw[:], Pq[:], [Pq], [qraw])
                    s2 = b16t.next()
                    self.act(s2[:], Pq[:], AF.Square, [Pq], [s2])
                    Ps = self.PS.next()
                    self.mm(Ps, Ps[:], self.ones[:], s2[:], True, True, [self.ones, s2])
                    r2 = f32t.next()
                    self.act(r2[:], Ps[:], AF.Ln, [Ps], [r2], scale=1.0 / 128, bias=self.epsb[:])
                    self.act(r2[:], r2[:], AF.Exp, [r2], [r2], scale=-0.5)
                    Pw = self.PS.next()
                    self.mm(Pw, Pw[:], self.swapm[:], qraw[:], True, True, [self.swapm, qraw])
                    gn = qn if isq else kn
                    t1 = f32t.next()
                    self.stt(t1[:], qraw[:], gn[:, 0:1], cosb[:], ALU.mult, ALU.mult, [qraw, cosb, cst], [t1])
                    t2 = f32t.next()
                    self.stt(t2[:], Pw[:], gn[:, 1:2], sinb[:], ALU.mult, ALU.mult, [Pw, sinb, cst], [t2])
                    self.tt("pool", t1[:], t1[:], t2[:], ALU.add, [t1, t2], [t1])
                    qo = b16t.next()
                    self.stt(qo[:], t1[:], 1.0 if isq else 128 ** -0.5, r2[:], ALU.mult, ALU.mult, [t1, r2], [qo])
                    dst = sc["qr"] if isq else sc["kr"]
                    r0 = (qi % 4) * 128
                    S.dma("pool", dst.ap[r0:r0 + 128, bs], qo[:], reads=[qo], writes=[dst.blk[b]], par=True)
                if b + 1 < self.NB:
                    prep(b + 1)
                for (c0, dstn, fn) in ((1024, "v", None), (2048, "g", AF.Silu), (3072, "z", AF.Silu)):
                    for tt_ in range(4):
                        tk = tkb.next()
                        for hf in range(2):
                            Pv = self.PS.next()
                            for kc in range(KC):
                                self.mm(Pv, Pv[:], hb[:, kc, tt_ * 128:(tt_ + 1) * 128], wib[:, kc, c0 + hf * 512:c0 + (hf + 1) * 512],
                                        kc == 0, kc == KC - 1, [wib, hb])
                            if fn is None:
                                self.copy("act", tk[:, hf * 512:(hf + 1) * 512], Pv[:], [Pv], [tk])
                            else:
                                self.act(tk[:, hf * 512:(hf + 1) * 512], Pv[:], fn, [Pv], [tk])
                        t0 = b * TB + tt_ * 128
                        S.dma("pool", sc[dstn].ap[t0:t0 + 128, :], tk[:], reads=[tk], writes=[sc[dstn].blk[b]], par=True)
                for tt_ in range(4):
                    Pd = self.PS.next()
                    for kc in range(KC):
                        self.mm(Pd, Pd[:, 0:16], hb[:, kc, tt_ * 128:(tt_ + 1) * 128], wib[:, kc, 5632:5648], kc == 0, kc == KC - 1, [wib, hb])
                    dd = dts.next()
                    self.tt("dve", dd[:, 0:16], Pd[:, 0:16], self.ssd_dtb[ei], ALU.add, [Pd, cst], [dd])
                    self.act(dd[:, 0:16], dd[:, 0:16], AF.Exp, [dd], [dd])
                    self.act(dd[:, 0:16], dd[:, 0:16], AF.Ln, [dd], [dd], bias=self.oneb[:])
                    self.tt("dve", dd[:, 16:32], dd[:, 0:16], self.ssd_A[ei], ALU.mult, [dd, cst], [dd])
                    t0 = b * TB + tt_ * 128
                    S.dma("pool", sc["dtA"].ap[t0:t0 + 128, :], dd[:], reads=[dd], writes=[sc["dtA"].blk[b]], par=True)
                for c in range(12):
                    Px = self.PS.next()
                    col0 = 4096 + c * 128
                    for kc in range(KC):
                        self.mm(Px, Px[:], wib[:, kc, col0:col0 + 128], hb[:, kc, :], kc == 0, kc == KC - 1, [wib, hb])
                    wk = work.next()
                    self.copy("act", wk[:, 3:TB + 3], Px[:], [Px], [wk])
                    self.copy("pool", wk[:, 0:3], halo[:, c, :], [halo], [wk])
                    self.copy("pool", halo[:, c, :], wk[:, TB:TB + 3], [wk], [halo])
                    xcv = f32t.next()
                    self.ts("dve", xcv[:], wk[:, 3:TB + 3], cw[:, c, 3:4], ALU.mult, [wk, cst], [xcv], s2=cb[:, c:c + 1], op1=ALU.add)
                    for k in range(3):
                        self.stt(xcv[:], wk[:, k:k + TB], cw[:, c, k:k + 1], xcv[:], ALU.mult, ALU.add, [wk, xcv, cst], [xcv])
                    self.act(xct[c][:], xcv[:], AF.Silu, [xcv], [xct[c]])
                    if c >= 8:
                        dstn = "BT" if c < 10 else "CT"
                        r0 = (c % 2) * 128
                        S.dma("pool", sc[dstn].ap[r0:r0 + 128, bs], xct[c][:], reads=[xct[c]], writes=[sc[dstn].blk[b]], par=True)
                for tt_ in range(4):
                    tk = tkb.next()
                    Pt = self.PS.next()
                    for c in range(8):
                        self.transpose_to(Pt, c * 128, xct[c], xct[c][:, tt_ * 128:(tt_ + 1) * 128])
                    self.copy("act", tk[:], Pt[:].bitcast(BF16), [Pt], [tk])
                    t0 = b * TB + tt_ * 128
                    S.dma("pool", sc["xs"].ap[t0:t0 + 128, :], tk[:], reads=[tk], writes=[sc["xs"].blk[b]], par=True)
                    tk2 = tkb.next()
                    Pt2 = self.PS.next()
                    for c in range(2):
                        self.transpose_to(Pt2, c * 128, xct[8 + c], xct[8 + c][:, tt_ * 128:(tt_ + 1) * 128])
                    self.copy("act", tk2[:, 0:256], Pt2[:].bitcast(BF16)[:, 0:256], [Pt2], [tk2])
                    S.dma("pool", sc["Btm"].ap[t0:t0 + 128, :], tk2[:, 0:256], reads=[tk2], writes=[sc["Btm"].blk[b]], par=True)
            S.barrier()

    def phase_even_b(self, ei, sc, yT):
        S = self.S
        T = self.T
        NCH = T // 128
        cst = self.cst
        gam = [1.0 - 2.0 ** (-5.0 - h) for h in range(4)]
        with contextlib.ExitStack() as es:
            def rot(name, shape, dt, n=2):
                return Rot([S.sb("%s%d" % (name, i), shape, dt, es) for i in range(n)])
            qrb = rot("eb_q", [128, 4, 128], BF16); krb = rot("eb_k", [128, 4, 128], BF16)
            vb = rot("eb_v", [128, 1024], BF16); gb = rot("eb_g", [128, 1024], BF16); zb = rot("eb_z", [128, 1024], BF16)
            xsb = rot("eb_xs", [128, 16, 64], BF16); btmb = rot("eb_bt", [128, 2, 128], BF16)
            BTb = rot("eb_BT", [128, 2, 128], BF16); CTb = rot("eb_CT", [128, 2, 128], BF16)
            dtb = rot("eb_dt", [128, 32], F32)
            Sst = S.sb("eb_S", [128, 4, 256], F32, es); Sbf = S.sb("eb_Sbf", [128, 4, 256], BF16, es)
            STs = S.sb("eb_ST", [128, 16, 64], F32, es); STbf = S.sb("eb_STbf", [128, 16, 64], BF16, es)
            smb = rot("eb_sm", [128, 4, 128], BF16); ktmb = rot("eb_ktm", [128, 4, 128], BF16); q2b = rot("eb_q2", [128, 4, 128], BF16)
            Dm = S.sb("eb_D", [128, 16, 128], F32, es)
            acol = rot("eb_acol", [128, 48], F32)
            segb = rot("eb_seg", [128, 4, 128], F32, 3); ltb = rot("eb_lt", [128, 4, 128], F32, 2); erb = rot("eb_er", [128, 4, 128], F32, 2)
            cbm = rot("eb_cbm", [128, 2, 128], F32)
            MTb = S.sb("eb_MT", [128, 16, 128], BF16, es); CsTb = S.sb("eb_CsT", [128, 16, 128], BF16, es)
            xdtb = rot("eb_xdt", [128, 16, 64], BF16); xdt2b = rot("eb_xdt2", [128, 16, 64], BF16)
            yf = rot("eb_yf", [128, 1024], F32, 2)
            yab = rot("eb_ya", [128, 1024], BF16, 2); ybb = rot("eb_yb", [128, 1024], BF16, 2)
            stat = rot("eb_stat", [128, 4, 6], F32); mv = rot("eb_mv", [128, 4, 2], F32); rs4 = rot("eb_rs4", [128, 8], F32)
            yTo = rot("eb_yTo", [128, 16, 128], BF16, 2)
            junk = S.sb("eb_junk", [128, 512], BF16, es)
            PSr = Rot(self.psb)
            PSs = Rot(self.psh)
            S.op("dve", lambda e: e.memset(Sst[:], 0.0), writes=[Sst])
            S.op("pool", lambda e: e.memset(Sbf[:], 0.0), writes=[Sbf])
            S.op("dve", lambda e: e.memset(STs[:], 0.0), writes=[STs])
            S.op("pool", lambda e: e.memset(STbf[:], 0.0), writes=[STbf])
            gnr_b = S.sb("eb_gnr", [128, 1024], F32, es); nrr_b = S.sb("eb_nrr", [128, 1024], F32, es)
            S.dma("sp", gnr_b[:], self.w["ret_gn"][ei].partition_broadcast(128), writes=[gnr_b])
            S.dma("sp", nrr_b[:], self.w["ssd_norm"][ei].partition_broadcast(128), writes=[nrr_b])
            gnrow = gnr_b[:]; nrow = nrr_b[:]; dsk = self.ssd_D[ei]
            for c in range(NCH):
                b = c // 4
                ts_ = slice(c * 128, (c + 1) * 128)
                qr = qrb.next(); kr = krb.next(); v = vb.next(); g = gb.next(); z = zb.next(); xs = xsb.next()
                btm = btmb.next(); BT = BTb.next(); CT = CTb.next(); dt = dtb.next()
                S.dma("sp", qr[:], sc["qr"].ap[:, ts_].rearrange("(h d) t -> d h t", d=128), reads=[sc["qr"].blk[b]], writes=[qr])
                S.dma("sp", kr[:], sc["kr"].ap[:, ts_].rearrange("(h d) t -> d h t", d=128), reads=[sc["kr"].blk[b]], writes=[kr])
                S.dma("sp", v[:], sc["v"].ap[ts_, :], reads=[sc["v"].blk[b]], writes=[v])
                S.dma("sp", g[:], sc["g"].ap[ts_, :], reads=[sc["g"].blk[b]], writes=[g])
                S.dma("sp", z[:], sc["z"].ap[ts_, :], reads=[sc["z"].blk[b]], writes=[z])
                S.dma("sp", xs[:], sc["xs"].ap[ts_, :].rearrange("t (h p) -> t h p", p=64), reads=[sc["xs"].blk[b]], writes=[xs])
                S.dma("sp", btm[:], sc["Btm"].ap[ts_, :].rearrange("t (g s) -> t g s", s=128), reads=[sc["Btm"].blk[b]], writes=[btm])
                S.dma("sp", BT[:], sc["BT"].ap[:, ts_].rearrange("(g s) t -> s g t", s=128), reads=[sc["BT"].blk[b]], writes=[BT])
                S.dma("sp", CT[:], sc["CT"].ap[:, ts_].rearrange("(g s) t -> s g t", s=128), reads=[sc["CT"].blk[b]], writes=[CT])
                S.dma("sp", dt[:], sc["dtA"].ap[ts_, :], reads=[sc["dtA"].blk[b]], writes=[dt])

                def ret_stream():
                    Psc = PSr.next()
                    for h in range(4):
                        self.mm(Psc, Psc[:, h * 128:(h + 1) * 128], kr[:, h, :], qr[:, h, :], True, True, [kr, qr])
                    sm = smb.next()
                    self.tt("dve", sm[:].rearrange("p h i -> p (h i)"), Psc[:], self.ret_decay[:], ALU.mult, [Psc, cst], [sm])
                    Pkt = PSr.next()
                    for h in range(4):
                        self.transpose_to(Pkt, h * 128, kr, kr[:, h, :])
                    ktm = ktmb.next()
                    self.tt("dve", ktm[:], Pkt[:].bitcast(BF16)[:, 0:512].rearrange("p (h d) -> p h d", d=128),
                            self.ret_kdec[:].unsqueeze(2).broadcast_to([128, 4, 128]), ALU.mult, [Pkt, cst], [ktm])
                    q2 = q2b.next()
                    self.tt("pool", q2[:].rearrange("p h i -> p (h i)"), qr[:].rearrange("p h i -> p (h i)"), self.ret_qdec[:], ALU.mult, [qr, cst], [q2])
                    POw = (PSr.next(), PSr.next())
                    for h in range(4):
                        PO = POw[h // 2]
                        hc = slice((h % 2) * 256, (h % 2 + 1) * 256)
                        self.mm(PO, PO[:, hc], sm[:, h, :], v[:, h * 256:(h + 1) * 256], True, False, [sm, v])
                        self.mm(PO, PO[:, hc], q2[:, h, :], Sbf[:, h, :], False, True, [q2, Sbf])
                    Pkvw = (PSr.next(), PSr.next())
                    for h in range(4):
                        Pkv = Pkvw[h // 2]
                        self.mm(Pkv, Pkv[:, (h % 2) * 256:(h % 2 + 1) * 256], ktm[:, h, :], v[:, h * 256:(h + 1) * 256], True, True, [ktm, v])
                    for h in range(4):
                        Pkv = Pkvw[h // 2]
                        self.stt(Sst[:, h, :], Sst[:, h, :], gam[h] ** 128, Pkv[:, (h % 2) * 256:(h % 2 + 1) * 256], ALU.mult, ALU.add, [Sst, Pkv], [Sst])
                    self.copy("act", Sbf[:], Sst[:], [Sst], [Sbf])
                    st = stat.next(); m2 = mv.next(); r4 = rs4.next()
                    for h in range(4):
                        PO = POw[h // 2]
                        S.op("dve", lambda e, h=h, st=st, PO=PO: e.bn_stats(out=st[:, h, :], in_=PO[:, (h % 2) * 256:(h % 2 + 1) * 256]), reads=[PO], writes=[st])
                    for h in range(4):
                        S.op("dve", lambda e, h=h, st=st, m2=m2: e.bn_aggr(out=m2[:, h, :], in_=st[:, h, :]), reads=[st], writes=[m2])
                    self.act(r4[:, 0:4], m2[:, :, 1], AF.Ln, [m2], [r4], bias=self.epsb[:])
                    self.act(r4[:, 0:4], r4[:, 0:4], AF.Exp, [r4], [r4], scale=-0.5)
                    y1 = yf.next()
                    for h in range(4):
                        PO = POw[h // 2]
                        self.ts("dve", y1[:, h * 256:(h + 1) * 256], PO[:, (h % 2) * 256:(h % 2 + 1) * 256], m2[:, h, 0:1], ALU.subtract, [PO, m2, r4], [y1],
                                s2=r4[:, h:h + 1], op1=ALU.mult)
                    self.tt("pool", y1[:], y1[:], gnrow, ALU.mult, [y1, gnr_b], [y1])
                    ya = yab.next()
                    self.tt("pool", ya[:], y1[:], g[:], ALU.mult, [y1, g], [ya])

                    out_['ya'] = ya

                def ssd_stream():
                    PA = PSs.next()
                    self.mm(PA, PA[:, 0:16], self.tri_f[:], dt[:, 16:32], True, True, [self.tri_f, dt])
                    self.mm(PA, PA[:, 16:32], self.ones_f[:], dt[:, 16:32], True, True, [self.ones_f, dt])
                    ac = acol.next()
                    self.copy("act", ac[:, 0:32], PA[:, 0:32], [PA], [ac])
                    self.tt("dve", ac[:, 32:48], ac[:, 16:32], ac[:, 0:16], ALU.subtract, [ac], [ac])
                    self.act(ac[:, 32:48], ac[:, 32:48], AF.Exp, [ac], [ac])
                    self.act(ac[:, 16:32], ac[:, 16:32], AF.Exp, [ac], [ac])
                    self.tt("dve", Dm[:], dt[:, 16:32].unsqueeze(2).broadcast_to([128, 16, 128]),
                            self.tri_f[:].unsqueeze(1).broadcast_to([128, 16, 128]), ALU.mult, [dt, self.tri_f], [Dm])
                    PCB = PSs.next()
                    for gi in range(2):
                        self.mm(PCB, PCB[:, gi * 128:(gi + 1) * 128], BT[:, gi, :], CT[:, gi, :], True, True, [BT, CT])
                    cb_ = cbm.next()
                    self.tt("dve", cb_[:].rearrange("p g i -> p (g i)"), PCB[:, 0:256], self.tri2[:], ALU.mult, [PCB, cst], [cb_])
                    for q4 in range(4):
                        gi = q4 // 2
                        hs = slice(q4 * 4, q4 * 4 + 4)
                        Prb = PSs.next()
                        self.mm(Prb, Prb[:], self.ones_f[:], Dm[:, hs, :].rearrange("p h i -> p (h i)"), True, True, [self.ones_f, Dm])
                        sg = segb.next()
                        self.tt("dve", sg[:], Prb[:].rearrange("p (h i) -> p h i", i=128), ac[:, q4 * 4:q4 * 4 + 4].unsqueeze(2).broadcast_to([128, 4, 128]),
                                ALU.subtract, [Prb, ac], [sg])
                        lt = ltb.next()
                        self.act(lt[:], sg[:], AF.Exp, [sg], [lt])
                        er = erb.next()
                        self.act(er[:].rearrange("p h i -> p (h i)"), Prb[:], AF.Exp, [Prb], [er])
                        self.stt(MTb[:, hs, :], lt[:], 1.0, cb_[:, gi, :].unsqueeze(1).broadcast_to([128, 4, 128]), ALU.min, ALU.mult, [lt, cb_], [MTb])
                        self.tt("pool", CsTb[:, hs, :], er[:], CT[:, gi, :].unsqueeze(1).broadcast_to([128, 4, 128]), ALU.mult, [er, CT], [CsTb])
                    xdt = xdtb.next(); xdt2 = xdt2b.next()
                    self.tt("dve", xdt[:], xs[:], dt[:, 0:16].unsqueeze(2).broadcast_to([128, 16, 64]), ALU.mult, [xs, dt], [xdt])
                    self.tt("pool", xdt2[:], xdt[:], ac[:, 32:48].unsqueeze(2).broadcast_to([128, 16, 64]), ALU.mult, [xdt, ac], [xdt2])
                    PYw = (PSs.next(), PSs.next())
                    for h in range(16):
                        PY = PYw[h // 8]
                        hc = slice((h % 8) * 64, (h % 8 + 1) * 64)
                        self.mm(PY, PY[:, hc], MTb[:, h, :], xdt[:, h, :], True, False, [MTb, xdt])
                        self.mm(PY, PY[:, hc], CsTb[:, h, :], STbf[:, h, :], False, True, [CsTb, STbf])
                    PStw = (PSs.next(), PSs.next())
                    for h in range(16):
                        PSt = PStw[h // 8]
                        self.mm(PSt, PSt[:, (h % 8) * 64:(h % 8 + 1) * 64], btm[:, h // 8, :], xdt2[:, h, :], True, True, [btm, xdt2])
                    self.tt("dve", STs[:], STs[:], ac[:, 16:32].unsqueeze(2).broadcast_to([128, 16, 64]), ALU.mult, [STs, ac], [STs])
                    STf = STs[:].rearrange("p h d -> p (h d)")
                    for gi in range(2):
                        self.tt("dve", STf[:, gi * 512:(gi + 1) * 512], PStw[gi][:], STf[:, gi * 512:(gi + 1) * 512], ALU.add, [PStw[gi], STs], [STs])
                    self.copy("act", STbf[:], STs[:], [STs], [STbf])
                    y2 = yf.next()
                    self.tt("pool", y2[:].rearrange("p (h d) -> p h d", d=64), xs[:], dsk.unsqueeze(2).broadcast_to([128, 16, 64]), ALU.mult, [xs, cst], [y2])
                    for gi in range(2):
                        self.tt("dve", y2[:, gi * 512:(gi + 1) * 512], PYw[gi][:], y2[:, gi * 512:(gi + 1) * 512], ALU.add, [PYw[gi], y2], [y2])
                    self.tt("pool", y2[:], y2[:], z[:], ALU.mult, [y2, z], [y2])
                    r8 = rs4.next()
                    for gi in range(2):
                        S.op("act", lambda e, gi=gi, y2=y2, r8=r8: e.activation(out=junk[:], in_=y2[:, gi * 512:(gi + 1) * 512], func=AF.Square, accum_out=r8[:, gi:gi + 1]),
                             reads=[y2], writes=[junk, r8])
                    self.act(r8[:, 0:2], r8[:, 0:2], AF.Ln, [r8], [r8], scale=1.0 / 512, bias=self.epsb[:])
                    self.act(r8[:, 0:2], r8[:, 0:2], AF.Exp, [r8], [r8], scale=-0.5)
                    yb = ybb.next()
                    for gi in range(2):
                        self.stt(yb[:, gi * 512:(gi + 1) * 512], y2[:, gi * 512:(gi + 1) * 512], r8[:, gi:gi + 1], nrow[:, gi * 512:(gi + 1) * 512],
                                 ALU.mult, ALU.mult, [y2, r8, nrr_b], [yb])
                    out_['yb'] = yb

                out_ = {}
                S.interleave([S.record(ret_stream), S.record(ssd_stream)])
                ya, yb = out_['ya'], out_['yb']
                yo = yTo.next()
                for half, src in ((0, ya), (1, yb)):
                    Pt = self.PS.next()
                    for cc in range(8):
                        self.transpose_to(Pt, cc * 128, src, src[:, cc * 128:(cc + 1) * 128])
                    self.copy("act", yo[:, half * 8:(half + 1) * 8, :].rearrange("p c i -> p (c i)"), Pt[:].bitcast(BF16), [Pt], [yo])
                S.dma("pool", yT.ap[:, ts_].rearrange("(c p) t -> p c t", p=128), yo[:], reads=[yo], writes=[yT.blk[b]], par=True)
            S.barrier()

    def build(self):
        nc, es = self.nc, self.es
        T, NB = self.T, self.NB
        S = self.S = Sched(nc, es)
        ne = sum(1 for x in self.layers if x == "e")
        no = sum(1 for x in self.layers if x == "o")
        L = len(self.layers)
        w = self.w = {}

        def inp(name, shape):
            w[name] = nc.dram_tensor(name, list(shape), F32, kind="ExternalInput").ap()

        self.xin = DT(S, "xT", [D, T], F32, NB, kind="ExternalInput")
        self.xout = DT(S, "outT", [D, T], F32, NB, kind="ExternalOutput")
        inp("norm_mix", [L, D]); inp("norm_mlp", [L, D])
        inp("mlp_w1", [L, D, 4096]); inp("mlp_w2", [L, 4096, D])
        if no:
            inp("od_w_in", [no, D, 3584]); inp("od_w_out", [no, 1536, D])
            inp("lru_conv_w", [no, 4, D]); inp("lru_conv_b", [no, D])
            inp("lru_wa", [no, 8, 128, 128]); inp("lru_ba", [no, 8, 128])
            inp("lru_wx", [no, 8, 128, 128]); inp("lru_bx", [no, 8, 128])
            inp("lru_lam", [no, D]); inp("sb_qn", [no, 64]); inp("sb_kn", [no, 64])
        if ne:
            inp("ev_w_in", [ne, D, 5648]); inp("ev_w_out", [ne, 2048, D])
            inp("ret_qn", [ne, 128]); inp("ret_kn", [ne, 128]); inp("ret_gn", [ne, 1024])
            inp("ssd_conv_w", [ne, 4, 1536]); inp("ssd_conv_b", [ne, 1536])
            inp("ssd_dt_bias", [ne, 16]); inp("ssd_a_log", [ne, 16]); inp("ssd_d", [ne, 16]); inp("ssd_norm", [ne, 1024])
            inp("c_cos", [128, T]); inp("c_sin", [128, T])
            inp("c_ident", [128, 128]); inp("c_swap", [128, 128]); inp("c_tri", [128, 128]); inp("c_tri2", [128, 256])
            inp("c_rdecay", [128, 512]); inp("c_rqdec", [128, 512]); inp("c_rkdec", [128, 4])
        inp("c_masklt", [128, 128]); inp("c_umat", [128, 128]); inp("c_bones", [128, 128])

        allb = [S.ps("ps%d" % i, [128, 512]) for i in range(8)]
        self.psb = allb[0:4]
        self.psh = allb[4:8]
        self.PS = Rot(self.psb)
        self.PW = Rot([(allb[4], allb[5]), (allb[6], allb[7])])

        cst = self.cst = Buf("cst")

        def csb(name, shape, dt=F32):
            return es.enter_context(nc.sbuf_tensor(name, list(shape), dt))

        self.ones = S.sb("ones", [128, 128], BF16)
        S.op("dve", lambda e: e.memset(self.ones[:], 1.0), writes=[self.ones])
        self.epsb = csb("epsb", [128, 1]); self.oneb = csb("oneb", [128, 1])
        S.op("dve", lambda e: e.memset(self.epsb[:], EPS), writes=[cst])
        S.op("dve", lambda e: e.memset(self.oneb[:], 1.0), writes=[cst])
        self.mask_lt = csb("mask_lt", [128, 128], BF16)
        self.umat = S.sb("umat", [128, 128], BF16)
        self.bones = S.sb("bones", [128, 128], BF16)
        S.dma("pool", self.mask_lt[:], w["c_masklt"], writes=[cst])
        S.dma("pool", self.umat[:], w["c_umat"], writes=[self.umat])
        S.dma("pool", self.bones[:], w["c_bones"], writes=[self.bones])
        self.gains = Buf("gains")
        gmix = csb("gmix", [128, L, KC]); gmlp = csb("gmlp", [128, L, KC])
        S.dma("sp", gmix[:], w["norm_mix"].rearrange("l (kc p) -> p l kc", p=128), writes=[self.gains], allow_slow_non_contiguous=True)
        S.dma("sp", gmlp[:], w["norm_mlp"].rearrange("l (kc p) -> p l kc", p=128), writes=[self.gains], allow_slow_non_contiguous=True)
        self.g_mix = [gmix[:, l, :] for l in range(L)]
        self.g_mlp = [gmlp[:, l, :] for l in range(L)]
        if no:
            cw = csb("lru_cw", [128, no, 8, 4]); cb = csb("lru_cb", [128, no, 8])
            ba = csb("lru_ba_s", [128, no, 8]); bx = csb("lru_bx_s", [128, no, 8])
            cl = csb("lru_cl", [128, no, 8])
            qn = csb("sbqn", [128, no]); kn = csb("sbkn", [128, no])
            for o_ in range(no):
                for k_ in range(4):
                    S.dma("sp", cw[:, o_, :, k_], w["lru_conv_w"][o_, k_].rearrange("(c p) -> p c", p=128), writes=[cst], allow_slow_non_contiguous=True)
            S.dma("sp", cb[:], w["lru_conv_b"].rearrange("o (c p) -> p o c", p=128), writes=[cst], allow_slow_non_contiguous=True)
            S.dma("sp", ba[:], w["lru_ba"].rearrange("o c p -> p o c"), writes=[cst], allow_slow_non_contiguous=True)
            S.dma("sp", bx[:], w["lru_bx"].rearrange("o c p -> p o c"), writes=[cst], allow_slow_non_contiguous=True)
            S.dma("sp", cl[:], w["lru_lam"].rearrange("o (c p) -> p o c", p=128), writes=[cst], allow_slow_non_contiguous=True)
            for half in range(2):
                S.dma("sp", qn[half * 64:(half + 1) * 64, :], w["sb_qn"].rearrange("o d -> d o"), writes=[cst], allow_slow_non_contiguous=True)
                S.dma("sp", kn[half * 64:(half + 1) * 64, :], w["sb_kn"].rearrange("o d -> d o"), writes=[cst], allow_slow_non_contiguous=True)
            S.op("act", lambda e: e.activation(out=cl[:], in_=cl[:], func=AF.Exp, scale=-1.0), reads=[cst], writes=[cst])
            S.op("act", lambda e: e.activation(out=cl[:], in_=cl[:], func=AF.Ln, bias=self.oneb[:]), reads=[cst], writes=[cst])
            S.op("dve", lambda e: e.tensor_scalar(out=cl[:], in0=cl[:], scalar1=-8.0, scalar2=None, op0=ALU.mult), reads=[cst], writes=[cst])
            self.lru_cw = [cw[:, o] for o in range(no)]
            self.lru_cb = [cb[:, o] for o in range(no)]
            self.lru_ba = [ba[:, o] for o in range(no)]
            self.lru_bx = [bx[:, o] for o in range(no)]
            self.lru_cl = [cl[:, o] for o in range(no)]
            self.sb_qn = [qn[:, o:o + 1] for o in range(no)]
            self.sb_kn = [kn[:, o:o + 1] for o in range(no)]
        if ne:
            self.ident = S.sb("ident", [128, 128], BF16)
            S.dma("pool", self.ident[:], w["c_ident"], writes=[self.ident])
            self.swapm = S.sb("swapm", [128, 128], F32)
            S.dma("sp", self.swapm[:], w["c_swap"], writes=[self.swapm])
            self.tri_f = S.sb("tri_f", [128, 128], F32)
            S.dma("sp", self.tri_f[:], w["c_tri"], writes=[self.tri_f])
            self.ones_f = S.sb("ones_f", [128, 128], F32)
            S.op("dve", lambda e: e.memset(self.ones_f[:], 1.0), writes=[self.ones_f])
            self.tri2 = csb("tri2", [128, 256]); self.ret_decay = csb("rdecay", [128, 512]); self.ret_qdec = csb("rqdec", [128, 512])
            self.ret_kdec = csb("rkdec", [128, 4])
            S.dma("sp", self.tri2[:], w["c_tri2"], writes=[cst])
            S.dma("sp", self.ret_decay[:], w["c_rdecay"], writes=[cst])
            S.dma("sp", self.ret_qdec[:], w["c_rqdec"], writes=[cst])
            S.dma("sp", self.ret_kdec[:], w["c_rkdec"], writes=[cst])
            scw = csb("ssd_cw", [128, ne, 12, 4]); scb = csb("ssd_cb", [128, ne, 12])
            rqn = csb("ret_qn_s", [128, ne, 2]); rkn = csb("ret_kn_s", [128, ne, 2])
            dsk = csb("ssd_D_r", [128, ne, 16]); dtbr = csb("ssd_dtb_r", [128, ne, 16]); Ar = csb("ssd_A_r", [128, ne, 16])
            for e_ in range(ne):
                for k_ in range(4):
                    S.dma("sp", scw[:, e_, :, k_], w["ssd_conv_w"][e_, k_].rearrange("(c p) -> p c", p=128), writes=[cst], allow_slow_non_contiguous=True)
                S.dma("sp", scb[:, e_, :], w["ssd_conv_b"][e_].rearrange("(c p) -> p c", p=128), writes=[cst], allow_slow_non_contiguous=True)
                for nm, tl in (("ret_qn", rqn), ("ret_kn", rkn)):
                    col = w[nm][e_].rearrange("(d o) -> d o", o=1)
                    S.dma("sp", tl[:, e_, 0:1], col, writes=[cst], allow_slow_non_contiguous=True)
                    S.dma("sp", tl[0:64, e_, 1:2], col[64:128], writes=[cst], allow_slow_non_contiguous=True)
                    S.dma("sp", tl[64:128, e_, 1:2], col[0:64], writes=[cst], allow_slow_non_contiguous=True)
                S.dma("sp", dsk[:, e_, :], w["ssd_d"][e_].partition_broadcast(128), writes=[cst])
                S.dma("sp", dtbr[:, e_, :], w["ssd_dt_bias"][e_].partition_broadcast(128), writes=[cst])
                S.dma("sp", Ar[:, e_, :], w["ssd_a_log"][e_].partition_broadcast(128), writes=[cst])
            S.op("act", lambda e: e.activation(out=Ar[:], in_=Ar[:], func=AF.Exp), reads=[cst], writes=[cst])
            S.op("dve", lambda e: e.tensor_scalar(out=Ar[:], in0=Ar[:], scalar1=-1.0, scalar2=None, op0=ALU.mult), reads=[cst], writes=[cst])
            self.ssd_cw = [scw[:, e_] for e_ in range(ne)]; self.ssd_cb = [scb[:, e_] for e_ in range(ne)]
            self.ret_qn = [rqn[:, e_] for e_ in range(ne)]; self.ret_kn = [rkn[:, e_] for e_ in range(ne)]
            self.ssd_D = [dsk[:, e_] for e_ in range(ne)]; self.ssd_dtb = [dtbr[:, e_] for e_ in range(ne)]; self.ssd_A = [Ar[:, e_] for e_ in range(ne)]
        S.barrier()

        xa = DT(S, "xa", [D, T], F32, NB)
        xb_ = DT(S, "xb", [D, T], F32, NB)
        yT = DT(S, "yT", [2048, T], BF16, NB)
        qT = DT(S, "qT", [512, T], BF16, NB)
        kT = DT(S, "kT", [512, T], BF16, NB)
        vtm = DT(S, "vtm", [T, 1024], BF16, NB)
        sc = {"qr": qT, "kr": kT, "v": vtm}
        if ne:
            for nm in ("g", "z", "xs"):
                sc[nm] = DT(S, "sc_" + nm, [T, 1024], BF16, NB)
            sc["Btm"] = DT(S, "sc_Btm", [T, 256], BF16, NB)
            sc["BT"] = DT(S, "sc_BT", [256, T], BF16, NB)
            sc["CT"] = DT(S, "sc_CT", [256, T], BF16, NB)
            sc["dtA"] = DT(S, "sc_dtA", [T, 32], F32, NB)

        cur = self.xin
        ie = io = 0
        for l, kind in enumerate(self.layers):
            last = l == L - 1
            skip = getattr(self, "skip", ())
            lay_es = contextlib.ExitStack()
            w1b = None
            w1_th = ()

            def prep_w1():
                nonlocal w1b, w1_th
                if "mlp" in skip:
                    return
                w1b = S.sb("w1b", [128, KC, 4096], BF16, lay_es)
                w1_th = self.load_wcast(w1b, self.w["mlp_w1"][l], KC, 4096, defer=True)
            if kind == "o":
                if "odd_a" not in skip:
                    self.phase_odd_a(io, l, cur, yT, qT, kT, vtm)
                if "sb" not in skip:
                    self.phase_sb(qT, kT, vtm, yT)
                if "outproj" not in skip:
                    prep_w1()
                    self.phase_outproj("od_w_out", io, 12, yT, cur, xa, extra=w1_th)
                io += 1
            else:
                if "even_a" not in skip:
                    self.phase_even_a(ie, l, cur, sc)
                if "even_b" not in skip:
                    self.phase_even_b(ie, sc, yT)
                if "outproj" not in skip:
                    prep_w1()
                    self.phase_outproj("ev_w_out", ie, 16, yT, cur, xa, extra=w1_th)
                ie += 1
            if "mlp" not in skip:
                self.phase_mlp(l, xa, self.xout if last else xb_, w1b=(w1b if "outproj" not in skip else None))
            lay_es.close()
            cur = xb_
        S.finish()
        es.close()
        return nc


def consts(T=None, even=False):
    i = np.arange(128)
    c = _consts_base(i)
    if even:
        f32 = np.float32
        inv = (f32(10000.0) ** (-(np.arange(64, dtype=f32)) / f32(64))).astype(f32)
        ang = (np.arange(T, dtype=f32)[None, :] * inv[:, None]).astype(f32).astype(np.float64)
        cos = np.cos(ang); sin = np.sin(ang)
        c["c_cos"] = np.concatenate([cos, cos], 0).astype(f32)
        c["c_sin"] = np.concatenate([-sin, sin], 0).astype(f32)
        c["c_ident"] = np.eye(128, dtype=f32)
        c["c_swap"] = (i[:, None] == ((i[None, :] + 64) % 128)).astype(f32)
        tri = (i[:, None] <= i[None, :]).astype(f32)
        c["c_tri"] = tri
        c["c_tri2"] = np.concatenate([tri, tri], 1)
        lg = np.log1p(-np.exp2(-5.0 - np.arange(4, dtype=np.float64)))
        rel = (i[None, :] - i[:, None]).astype(np.float64)
        dec = np.where(rel[:, None, :] >= 0, np.exp(lg[None, :, None] * np.maximum(rel, 0)[:, None, :]), 0.0)
        c["c_rdecay"] = dec.reshape(128, 512).astype(f32)
        qd = np.exp(lg[:, None] * (i[None, :] + 1.0))
        c["c_rqdec"] = np.broadcast_to(qd.reshape(1, 512), (128, 512)).astype(f32).copy()
        c["c_rkdec"] = np.exp(lg[None, :] * (127.0 - i[:, None])).astype(f32)
    return c


def _consts_base(i):
    return {
        "c_masklt": (i[:, None] < i[None, :]).astype(np.float32),
        "c_umat": (i[:, None] > i[None, :]).astype(np.float32),
        "c_bones": ((i[:, None] // 64) == (i[None, :] // 64)).astype(np.float32),
    }


def make_inputs(inputs, b, kinds):
    m = {"xT": np.ascontiguousarray(np.asarray(inputs["x"])[b].T)}
    L = len(kinds)
    for n in ("norm_mix", "norm_mlp", "mlp_w1", "mlp_w2"):
        m[n] = np.ascontiguousarray(np.asarray(inputs[n])[:L])
    if "e" in kinds:
        for n in ("ev_w_in", "ev_w_out", "ret_qn", "ret_kn", "ret_gn", "ssd_conv_w", "ssd_conv_b", "ssd_dt_bias", "ssd_a_log", "ssd_d", "ssd_norm"):
            m[n] = np.ascontiguousarray(np.asarray(inputs[n]))
    if "o" in kinds:
        for n in ("od_w_in", "od_w_out", "lru_conv_w", "lru_conv_b", "lru_wa", "lru_ba", "lru_wx", "lru_bx", "lru_lam", "sb_qn", "sb_kn"):
            m[n] = np.ascontiguousarray(np.asarray(inputs[n]))
    m.update(consts(m["xT"].shape[1], "e" in kinds))
    return m


KINDS = "eoeo"
SEQ = 8192
_NC_CACHE = {}


def kernel(**inputs):
    if "nc" not in _NC_CACHE:
        _NC_CACHE["nc"] = K(SEQ, list(KINDS)).build()
    nc = _NC_CACHE["nc"]
    nb = np.asarray(inputs["x"]).shape[0]
    in_maps = [make_inputs(inputs, b, KINDS) for b in range(nb)]
    res = run_bass_kernel_spmd(nc, in_maps, core_ids=list(range(nb)))
    out = np.stack([np.asarray(res.results[b]["outT"]).T for b in range(nb)])
    return np.ascontiguousarray(out.astype(np.float32))
```

```python
import contextlib
import math
import numpy as np
import concourse.bass as bass
import concourse.mybir as mybir
from concourse.bass_utils import run_bass_kernel_spmd

F32 = mybir.dt.float32
BF16 = mybir.dt.bfloat16
AF = mybir.ActivationFunctionType
ALU = mybir.AluOpType
AX = mybir.AxisListType

D = 1024
KC = 8
TB = 512
EPS = 1e-6


class Buf:
    __slots__ = ("name", "t", "lws", "rd")

    def __init__(self, name, t=None):
        self.name = name
        self.t = t
        self.lws = []
        self.rd = {}

    def __getitem__(self, k):
        return self.t[k]


class Rot:
    def __init__(self, bufs):
        self.bufs = list(bufs)
        self.i = 0

    def next(self):
        b = self.bufs[self.i % len(self.bufs)]
        self.i += 1
        return b


class Sched:
    ENG = ("pe", "act", "dve", "pool", "sp")

    def __init__(self, nc, es, n_dma_sems=32):
        self.nc = nc
        self.es = es
        self.q = {e: [] for e in self.ENG}
        self.cnt = {e: 0 for e in self.ENG}
        self.sem = {e: es.enter_context(nc.semaphore("c_" + e)) for e in ("pe", "act", "dve", "pool")}
        self.dsem = [es.enter_context(nc.semaphore("d%d" % i)) for i in range(n_dma_sems)]
        self.dcnt = [0] * n_dma_sems
        self.dnext = 0
        self.pnext = 0
        self.rec = None
        self.waited = {e: {} for e in self.ENG}
        self.ndma = 0

    def sb(self, name, shape, dt, es=None):
        self.uid = getattr(self, "uid", 0) + 1
        name = "%s_u%d" % (name, self.uid)
        return Buf(name, (es or self.es).enter_context(self.nc.sbuf_tensor(name, list(shape), dt)))

    def ps(self, name, shape, dt=F32):
        return Buf(name, self.es.enter_context(self.nc.psum_tensor(name, list(shape), dt)))

    def dram(self, name, shape, dt, kind="Internal"):
        return Buf(name, self.nc.dram_tensor(name, list(shape), dt, kind=kind).ap())

    def _need(self, eng, dep, waits):
        if dep is None:
            return
        key, val, semh = dep
        w = self.waited[eng]
        if w.get(key, 0) >= val:
            return
        w[key] = val
        waits.append((semh, val))

    def _deps(self, eng, reads, writes, same, par=False):
        waits = []
        for b in reads:
            for lw in b.lws:
                if same or lw[0] != eng:
                    self._need(eng, lw, waits)
        for b in writes:
            if not par:
                for lw in b.lws:
                    if same or lw[0] != eng:
                        self._need(eng, lw, waits)
            for k, d in b.rd.items():
                if same or k != eng:
                    self._need(eng, d, waits)
        return waits

    def hoist_begin(self, eng):
        self._hm = (eng, len(self.q[eng]))

    def hoist_end(self):
        eng, m = self._hm
        items = self.q[eng][m:]
        if len(items) < 2:
            return
        allw = []
        for waits, fn, inc in items:
            allw.extend(waits)
        best = {}
        for semh, val in allw:
            k = id(semh)
            if k not in best or best[k][1] < val:
                best[k] = (semh, val)
        self.q[eng][m] = (list(best.values()), items[0][1], items[0][2])
        for j in range(1, len(items)):
            self.q[eng][m + j] = ([], items[j][1], items[j][2])

    def record(self, body):
        old = self.rec
        self.rec = []
        body()
        r = self.rec
        self.rec = old
        return r

    def interleave(self, lists):
        its = [list(l) for l in lists]
        pos = [0] * len(its)
        left = sum(len(l) for l in its)
        while left:
            for k, l in enumerate(its):
                if pos[k] < len(l):
                    it = l[pos[k]]
                    pos[k] += 1
                    left -= 1
                    if it[0] == 0:
                        self.op(it[1], it[2], it[3], it[4])
                    else:
                        self.dma(it[1], it[2], it[3], it[4], it[5], it[6], **it[7])

    def op(self, eng, fn, reads=(), writes=()):
        if self.rec is not None:
            self.rec.append((0, eng, fn, list(reads), list(writes)))
            return None
        same = eng != "pe"
        waits = self._deps(eng, reads, writes, same)
        self.cnt[eng] += 1
        tok = (eng, self.cnt[eng], self.sem[eng])
        self.q[eng].append((waits, fn, (self.sem[eng], 1)))
        for b in reads:
            b.rd[eng] = tok
        for b in writes:
            b.lws = [tok]
            b.rd = {}
        return tok

    def dma(self, qeng, out_ap, in_ap, reads=(), writes=(), par=False, **kw):
        if self.rec is not None:
            self.rec.append((1, qeng, out_ap, in_ap, list(reads), list(writes), par, kw))
            return None
        waits = self._deps(qeng, reads, writes, True, par)
        if qeng == "pool":
            s = self.pnext
            self.pnext = (self.pnext + 1) % 4
        else:
            s = 4 + self.dnext
            self.dnext = (self.dnext + 1) % (len(self.dsem) - 4)
        if self.dcnt[s] > 0:
            self._need(qeng, ("d%d" % s, self.dcnt[s], self.dsem[s]), waits)
        self.dcnt[s] += 16
        tok = ("d%d" % s, self.dcnt[s], self.dsem[s])
        self.q[qeng].append((waits, lambda e: e.dma_start(out=out_ap, in_=in_ap, **kw), (self.dsem[s], 16)))
        for b in reads:
            b.rd["dma%d" % self.ndma] = tok
        for b in writes:
            if par:
                if b.rd:
                    b.lws = []
                    b.rd = {}
                b.lws.append(tok)
            else:
                b.lws = [tok]
                b.rd = {}
        self.ndma += 1
        return tok

    def barrier(self):
        for e in self.ENG:
            waits = []
            for e2 in ("pe", "act", "dve", "pool"):
                if e2 != e and self.cnt[e2] > 0:
                    self._need(e, (e2, self.cnt[e2], self.sem[e2]), waits)
            for s in range(len(self.dsem)):
                if self.dcnt[s] > 0:
                    self._need(e, ("d%d" % s, self.dcnt[s], self.dsem[s]), waits)
            if waits:
                self.q[e].append((waits, None, None))

    def finish(self):
        self.barrier()
        nc = self.nc
        q = self.q

        def replay(eh, items):
            for waits, fn, inc in items:
                for semh, val in waits:
                    eh.wait_ge(semh, val)
                if fn is not None:
                    ins = fn(eh)
                    if inc is not None:
                        ins.then_inc(inc[0], inc[1])

        with nc.Block() as block:
            @block.sync
            def _(e):
                replay(e, q["sp"])

            @block.tensor
            def _(e):
                replay(e, q["pe"])

            @block.scalar
            def _(e):
                replay(e, q["act"])

            @block.vector
            def _(e):
                replay(e, q["dve"])

            @block.gpsimd
            def _(e):
                replay(e, q["pool"])


class DT:
    def __init__(self, S, name, shape, dt, nblk, kind="Internal"):
        self.ap = S.nc.dram_tensor(name, list(shape), dt, kind=kind).ap()
        self.blk = [Buf("%s_b%d" % (name, i)) for i in range(nblk)]
        self.all = self.blk


class K:
    def __init__(self, T, layers):
        self.T = T
        self.NB = T // TB
        self.layers = layers
        self.nc = bass.Bass("TRN2", target_bir_lowering=False)
        self.es = contextlib.ExitStack()

    def mm(self, P, out_ap, lhsT, rhs, start, stop, reads):
        self.S.op("pe", lambda e: e.matmul(out_ap, lhsT=lhsT, rhs=rhs, start=start, stop=stop), reads=reads, writes=[P])

    def act(self, out_ap, in_ap, func, reads, writes, **kw):
        self.S.op("act", lambda e: e.activation(out=out_ap, in_=in_ap, func=func, **kw), reads=reads, writes=writes)

    def tt(self, eng, out_ap, a, b, op, reads, writes):
        self.S.op(eng, lambda e: e.tensor_tensor(out=out_ap, in0=a, in1=b, op=op), reads=reads, writes=writes)

    def ts(self, eng, out_ap, a, s1, op0, reads, writes, s2=None, op1=None):
        if op1 is None:
            self.S.op(eng, lambda e: e.tensor_scalar(out=out_ap, in0=a, scalar1=s1, scalar2=None, op0=op0), reads=reads, writes=writes)
        else:
            self.S.op(eng, lambda e: e.tensor_scalar(out=out_ap, in0=a, scalar1=s1, scalar2=s2, op0=op0, op1=op1), reads=reads, writes=writes)

    def stt(self, out_ap, a, s, b, op0, op1, reads, writes):
        self.S.op("dve", lambda e: e.scalar_tensor_tensor(out=out_ap, in0=a, scalar=s, in1=b, op0=op0, op1=op1), reads=reads, writes=writes)

    def copy(self, eng, out_ap, in_ap, reads, writes):
        if eng == "act":
            self.S.op("act", lambda e: e.copy(out=out_ap, in_=in_ap), reads=reads, writes=writes)
        else:
            self.S.op(eng, lambda e: e.tensor_copy(out=out_ap, in_=in_ap), reads=reads, writes=writes)

    def xview(self, xdt, b):
        return xdt.ap.rearrange("(kc p) t -> p kc t", p=128)[:, :, b * TB:(b + 1) * TB]

    def load_wcast(self, wbuf, w_ap, kc_n, ncols, grp=1, defer=False):
        S = self.S
        wv = w_ap.rearrange("(kc p) f -> p kc f", p=128)
        th = []
        for k0 in range(0, kc_n, grp):
            th.append(lambda k0=k0: S.dma("pool", wbuf[:, k0:k0 + grp, :], wv[:, k0:k0 + grp, :], reads=[], writes=[wbuf], par=True, max_dma_last_dim=4096))
        if defer:
            return th
        for t in th:
            t()

    def norm(self, xs, g_ap, hb, es_bufs):
        S = self.S
        sq, rst = es_bufs
        sqv = sq[:, 0:KC, :]
        P = self.PS.next()
        self.act(sqv, xs[:], AF.Square, [xs], [sq])
        for kc in range(KC):
            self.mm(P, P[:], self.ones[:], sqv[:, kc, :], kc == 0, kc == KC - 1, [self.ones, sq])
        self.act(rst[:], P[:], AF.Ln, [P], [rst], scale=1.0 / D, bias=self.epsb[:])
        self.act(rst[:], rst[:], AF.Exp, [rst], [rst], scale=-0.5)
        for kc in range(KC):
            self.stt(hb[:, kc, :], xs[:, kc, :], g_ap[:, kc:kc + 1], rst[:], ALU.mult, ALU.mult, [xs, rst, self.gains], [hb])

    def phase_mlp(self, l, xin, xout, w1b=None):
        S = self.S
        NB = self.NB
        with contextlib.ExitStack() as es:
            pre = w1b is not None
            if not pre:
                w1b = S.sb("w1b", [128, KC, 4096], BF16, es)
            w2b = S.sb("w2b", [128, 32, D], BF16, es)
            xs = S.sb("m_xs", [128, KC, TB], F32, es)
            hb = S.sb("m_hb", [128, KC, TB], BF16, es)
            ab = S.sb("m_ab", [128, 32, TB], BF16, es)
            rst = S.sb("m_rst", [128, TB], F32, es)
            tmps = Rot([S.sb("m_tmp%d" % i, [128, TB], F32, es) for i in range(2)])
            xos = Rot([S.sb("m_xo%d" % i, [128, TB], F32, es) for i in range(3)])
            PS8 = Rot(self.psb + self.psh)
            if not pre:
                self.load_wcast(w1b, self.w["mlp_w1"][l], KC, 4096)
            self.load_wcast(w2b, self.w["mlp_w2"][l], 32, D, grp=4)
            xinv = xin.ap.rearrange("(kc p) t -> p kc t", p=128)
            xoutv = xout.ap.rearrange("(kc p) t -> p kc t", p=128)

            def load_x(b):
                S.dma("sp", xs[:], self.xview(xin, b), reads=[xin.blk[b]], writes=[xs])

            def norm_next(b):
                self.norm(xs, self.g_mlp[l], hb, (hb, rst))

            load_x(0)
            norm_next(0)
            for b in range(NB):
                if b + 1 < NB:
                    load_x(b + 1)
                for fc in range(32):
                    P = PS8.next()
                    for kc in range(KC):
                        self.mm(P, P[:], w1b[:, kc, fc * 128:(fc + 1) * 128], hb[:, kc, :], kc == 0, kc == KC - 1, [w1b, hb])
                    tmp = tmps.next()
                    self.act(tmp[:], P[:], AF.Relu, [P], [tmp])
                    self.tt("dve" if fc % 2 == 0 else "pool", ab[:, fc, :], tmp[:], tmp[:], ALU.mult, [tmp], [ab])
                xo_pref = {}
                for oc in range(KC):
                    for o2 in (oc, oc + 1, oc + 2):
                        if o2 < KC and o2 not in xo_pref:
                            xo = xos.next()
                            S.dma("sp", xo[:], xinv[:, o2, b * TB:(b + 1) * TB], reads=[xin.blk[b]], writes=[xo])
                            xo_pref[o2] = xo
                    P = PS8.next()
                    for fc in range(32):
                        self.mm(P, P[:], w2b[:, fc, oc * 128:(oc + 1) * 128], ab[:, fc, :], fc == 0, fc == 31, [w2b, ab])
                    xo = xo_pref[oc]
                    self.tt("dve", xo[:], P[:], xo[:], ALU.add, [P, xo], [xo])
                    S.dma("pool", xoutv[:, oc, b * TB:(b + 1) * TB], xo[:], reads=[xo], writes=[xout.blk[b]], par=True)
                    if oc == 1 and b + 1 < NB:
                        norm_next(b + 1)
            S.barrier()

    def phase_outproj(self, wname, widx, nkc, yT, xin, xout, extra=()):
        S = self.S
        with contextlib.ExitStack() as es:
            wob = S.sb("wob", [128, nkc, D], BF16, es)
            xss = Rot([S.sb("o_xs%d" % i, [128, KC, TB], F32, es) for i in range(2)])
            ybs = Rot([S.sb("o_yb%d" % i, [128, nkc, TB], BF16, es) for i in range(2)])
            PS8 = Rot(self.psb + self.psh)
            self.load_wcast(wob, self.w[wname][widx], nkc, D, grp=4)
            yv = yT.ap.rearrange("(kc p) t -> p kc t", p=128)
            nxt = {}

            def loads(b):
                yb = ybs.next()
                xs = xss.next()
                S.dma("sp", yb[:], yv[:, 0:nkc, b * TB:(b + 1) * TB], reads=[yT.blk[b]], writes=[yb])
                S.dma("sp", xs[:], self.xview(xin, b), reads=[xin.blk[b]], writes=[xs])
                nxt[b] = (yb, xs)

            loads(0)
            for b in range(self.NB):
                if b + 1 < self.NB:
                    loads(b + 1)
                yb, xs = nxt.pop(b)
                for oc in range(KC):
                    P = PS8.next()
                    for kc in range(nkc):
                        self.mm(P, P[:], wob[:, kc, oc * 128:(oc + 1) * 128], yb[:, kc, :], kc == 0, kc == nkc - 1, [wob, yb])
                    self.tt("dve", xs[:, oc, :], P[:], xs[:, oc, :], ALU.add, [P, xs], [xs])
                S.dma("pool", self.xview(xout, b), xs[:], reads=[xs], writes=[xout.blk[b]])
                if b < len(extra):
                    extra[b]()
            for t in extra[self.NB:]:
                t()
            S.barrier()

    def phase_odd_a(self, o, l, xin, yT, qT, kT, vtm):
        S = self.S
        T = self.T
        with contextlib.ExitStack() as es:
            wib = S.sb("oa_wi", [128, KC, 3584], BF16, es)
            wab = S.sb("oa_wa", [128, 8, 128], BF16, es)
            wxb = S.sb("oa_wx", [128, 8, 128], BF16, es)
            xs = S.sb("oa_xs", [128, KC, TB], F32, es)
            hbs = [S.sb("oa_hb%d" % i, [128, KC, TB], BF16, es) for i in range(2)]
            sq = S.sb("oa_sq", [128, KC, TB], BF16, es)
            rst = S.sb("oa_rst", [128, TB], F32, es)
            xr = [[S.sb("oa_xr%d_%d" % (c, i), [128, TB + 3], F32, es) for i in range(2)] for c in range(8)]
            hst = [S.sb("oa_hst%d" % i, [128, 1], F32, es) for i in range(8)]
            f32t = Rot([S.sb("oa_f%d" % i, [128, TB], F32, es) for i in range(16)])
            b16t = Rot([S.sb("oa_b%d" % i, [128, TB], BF16, es) for i in range(10)])
            PS8 = Rot(self.psb + self.psh)
            vb = Rot([S.sb("oa_vb%d" % i, [128, 512], BF16, es) for i in range(2)])
            self.load_wcast(wib, self.w["od_w_in"][o], KC, 3584)
            S.dma("pool", wab[:], self.w["lru_wa"][o].rearrange("k i j -> i k j"), writes=[wab])
            S.dma("pool", wxb[:], self.w["lru_wx"][o].rearrange("k i j -> i k j"), writes=[wxb])
            for c in range(8):
                S.op("dve", lambda e, c=c: e.memset(hst[c][:], 0.0), writes=[hst[c]])
                S.op("pool", lambda e, c=c: e.memset(xr[c][1][:, TB:TB + 3], 0.0), writes=[xr[c][1]])
            cw = self.lru_cw[o]
            cb = self.lru_cb[o]
            ba = self.lru_ba[o]
            bx = self.lru_bx[o]
            cl = self.lru_cl[o]
            cst = self.cst
            def prep(b):
                S.dma("sp", xs[:], self.xview(xin, b), reads=[xin.blk[b]], writes=[xs])
                self.norm(xs, self.g_mix[l], hbs[b % 2], (sq, rst))

            prep(0)
            for b in range(self.NB):
                hb = hbs[b % 2]
                def lru_chunk(c):
                    cur = xr[c][b % 2]
                    prv = xr[c][(b + 1) % 2]
                    Pg = PS8.next()
                    for kc in range(KC):
                        self.mm(Pg, Pg[:], wib[:, kc, c * 128:(c + 1) * 128], hb[:, kc, :], kc == 0, kc == KC - 1, [wib, hb])
                    gl = f32t.next()
                    self.act(gl[:], Pg[:], AF.Gelu_apprx_tanh, [Pg], [gl])
                    Px = PS8.next()
                    for kc in range(KC):
                        self.mm(Px, Px[:], wib[:, kc, 1024 + c * 128:1024 + (c + 1) * 128], hb[:, kc, :], kc == 0, kc == KC - 1, [wib, hb])
                    self.copy("act", cur[:, 3:TB + 3], Px[:], [Px], [cur])
                    self.copy("pool", cur[:, 0:3], prv[:, TB:TB + 3], [prv], [cur])
                    xcv = f32t.next()
                    self.ts("dve", xcv[:], cur[:, 3:TB + 3], cw[:, c, 3:4], ALU.mult, [cur, cst], [xcv], s2=cb[:, c:c + 1], op1=ALU.add)
                    for k in range(3):
                        self.stt(xcv[:], cur[:, k:k + TB], cw[:, c, k:k + 1], xcv[:], ALU.mult, ALU.add, [cur, xcv, cst], [xcv])
                    xcb = b16t.next()
                    self.copy("pool", xcb[:], xcv[:], [xcv], [xcb])
                    Pr = PS8.next()
                    self.mm(Pr, Pr[:], wab[:, c, :], xcb[:], True, True, [wab, xcb])
                    Pi = PS8.next()
                    self.mm(Pi, Pi[:], wxb[:, c, :], xcb[:], True, True, [wxb, xcb])
                    rr = f32t.next()
                    self.act(rr[:], Pr[:], AF.Sigmoid, [Pr, cst], [rr], bias=ba[:, c:c + 1])
                    ii = f32t.next()
                    self.act(ii[:], Pi[:], AF.Sigmoid, [Pi, cst], [ii], bias=bx[:, c:c + 1])
                    aa = f32t.next()
                    self.act(aa[:], rr[:], AF.Exp, [rr, cst], [aa], scale=cl[:, c:c + 1])
                    a2 = f32t.next()
                    self.tt("pool", a2[:], aa[:], aa[:], ALU.mult, [aa], [a2])
                    self.act(a2[:], a2[:], AF.Sqrt, [a2], [a2], scale=-1.0, bias=self.oneb[:])
                    self.tt("pool", ii[:], ii[:], xcv[:], ALU.mult, [ii, xcv], [ii])
                    self.tt("dve", ii[:], ii[:], a2[:], ALU.mult, [ii, a2], [ii])
                    hh = f32t.next()
                    S.op("dve", lambda e, hh=hh, aa=aa, ii=ii, c=c: e.tensor_tensor_scan(out=hh[:], data0=aa[:], data1=ii[:], initial=hst[c][:, 0:1], op0=ALU.mult, op1=ALU.add),
                         reads=[aa, ii, hst[c]], writes=[hh])
                    self.copy("pool", hst[c][:, 0:1], hh[:, TB - 1:TB], [hh], [hst[c]])
                    yc = b16t.next()
                    self.tt("dve", yc[:], hh[:], gl[:], ALU.mult, [hh, gl], [yc])
                    S.dma("pool", yT.ap[c * 128:(c + 1) * 128, b * TB:(b + 1) * TB], yc[:], reads=[yc], writes=[yT.blk[b]], par=True)
                for c in range(0, 8, 2):
                    S.interleave([S.record(lambda c=c: lru_chunk(c)), S.record(lambda c=c: lru_chunk(c + 1))])
                    if c == 2 and b + 1 < self.NB:
                        prep(b + 1)
                def qk_tile(qi):
                    isq = qi < 4
                    col0 = 2048 + qi * 128
                    Pq = PS8.next()
                    for kc in range(KC):
                        self.mm(Pq, Pq[:], wib[:, kc, col0:col0 + 128], hb[:, kc, :], kc == 0, kc == KC - 1, [wib, hb])
                    s2 = b16t.next()
                    self.act(s2[:], Pq[:], AF.Square, [Pq], [s2])
                    Ps = PS8.next()
                    self.mm(Ps, Ps[:], self.bones[:], s2[:], True, True, [self.bones, s2])
                    r2 = f32t.next()
                    self.act(r2[:], Ps[:], AF.Ln, [Ps], [r2], scale=1.0 / 64, bias=self.epsb[:])
                    self.act(r2[:], r2[:], AF.Exp, [r2], [r2], scale=-0.5)
                    qo = b16t.next()
                    gn = (self.sb_qn if isq else self.sb_kn)[o]
                    self.stt(qo[:], Pq[:], gn[:, 0:1], r2[:], ALU.mult, ALU.mult, [Pq, r2, cst], [qo])
                    dst = qT if isq else kT
                    r0 = (qi % 4) * 128
                    S.dma("pool", dst.ap[r0:r0 + 128, b * TB:(b + 1) * TB], qo[:], reads=[qo], writes=[dst.blk[b]], par=True)
                for q0 in range(0, 8, 4):
                    S.interleave([S.record(lambda qi=q0 + j: qk_tile(qi)) for j in range(4)])
                for tt_ in range(4):
                    Pv = self.PS.next()
                    for kc in range(KC):
                        self.mm(Pv, Pv[:], hb[:, kc, tt_ * 128:(tt_ + 1) * 128], wib[:, kc, 3072:3584], kc == 0, kc == KC - 1, [wib, hb])
                    vv = vb.next()
                    self.copy("act", vv[:], Pv[:], [Pv], [vv])
                    t0 = b * TB + tt_ * 128
                    S.dma("pool", vtm.ap[t0:t0 + 128, 0:512], vv[:], reads=[vv], writes=[vtm.blk[b]], par=True)
            S.barrier()

    def phase_sb(self, qT, kT, vtm, yT):
        S = self.S
        T = self.T
        NKB = T // 128
        scale = 64 ** -0.5
        with contextlib.ExitStack() as es:
            def rot(name, shape, dt, n):
                return Rot([S.sb("%s%d" % (name, i), shape, dt, es) for i in range(n)])
            qh = [S.sb("sb_q%d" % i, [64, T], BF16, es) for i in range(2)]
            kh = [S.sb("sb_k%d" % i, [64, T], BF16, es) for i in range(2)]
            vh = [S.sb("sb_v%d" % i, [128, NKB, 64], BF16, es) for i in range(2)]
            eeR = rot("sb_ee", [128, TB], F32, 2); spR = rot("sb_sp", [128, TB], F32, 7)
            l1R = rot("sb_l1", [128, TB], BF16, 7)
            argR = rot("sb_arg", [128, TB], F32, 3); wwR = rot("sb_ww", [128, TB], BF16, 4)
            ot = rot("sb_o", [64, TB], BF16, 2)
            PzR = Rot([self.psb[0], self.psb[1], self.psb[2], self.psb[3]]); PsfR = Rot([self.psh[0], self.psh[1]])
            PcR = Rot([self.psh[2]]); PaccR = Rot([self.psh[3]])

            def load_head(h):
                q, k, v = qh[h % 2], kh[h % 2], vh[h % 2]
                S.dma("sp", q[:], qT.ap[h * 64:(h + 1) * 64, :], reads=qT.all, writes=[q])
                S.dma("sp", k[:], kT.ap[h * 64:(h + 1) * 64, :], reads=kT.all, writes=[k])
                S.dma("sp", v[:], vtm.ap[:, h * 64:(h + 1) * 64].rearrange("(n p) d -> p n d", p=128), reads=vtm.all, writes=[v])

            units = []
            for h in range(8):
                for Q in range(self.NB):
                    top = 4 * Q + 3
                    pc0 = None
                    for kb in range(top, -1, -1):
                        loc = kb - 4 * Q
                        c0 = max(loc, 0) * 128
                        units.append(dict(h=h, Q=Q, kb=kb, first=(kb == top), last=(kb == 0), newhead=(Q == 0 and kb == top),
                                          c0=c0, pc0=pc0, diag=(loc >= 0)))
                        pc0 = c0
            st = {}

            def pe_z(u):
                h, Q, kb, c0 = u["h"], u["Q"], u["kb"], u["c0"]
                if u["newhead"] and h == 0:
                    load_head(0)
                q, k = qh[h % 2], kh[h % 2]
                Pz = PzR.next()
                self.mm(Pz, Pz[:, c0:TB], k[:, kb * 128:(kb + 1) * 128], q[:, Q * TB + c0:(Q + 1) * TB], True, True, [k, q])
                u["Pz"] = Pz

            def act_sp(u):
                c0, Pz = u["c0"], u["Pz"]
                ee = eeR.next()
                self.act(ee[:, c0:TB], Pz[:, c0:TB], AF.Exp, [Pz], [ee], scale=-scale)
                sp_ = spR.next()
                self.act(sp_[:, c0:TB], ee[:, c0:TB], AF.Ln, [ee], [sp_], bias=self.oneb[:])
                u["sp"] = sp_

            def dve_l1(u):
                c0 = u["c0"]
                Pz, sp_ = u["Pz"], u["sp"]
                l1 = l1R.next()
                self.stt(l1[:, c0:TB], Pz[:, c0:TB], -scale, sp_[:, c0:TB], ALU.mult, ALU.subtract, [Pz, sp_], [l1])
                if u["diag"]:
                    self.tt("pool", l1[:, c0:c0 + 128], l1[:, c0:c0 + 128], self.mask_lt[:], ALU.mult, [l1, self.cst], [l1])
                    if c0 > 0:
                        S.op("pool", lambda e, l1=l1, c0=c0: e.memset(l1[:, 0:c0], 0.0), writes=[l1])
                u["l1"] = l1

            def pe_suffix(u):
                c0 = u["c0"]
                Psf = PsfR.next()
                self.mm(Psf, Psf[:, c0:TB], self.umat[:], u["l1"][:, c0:TB], True, True, [self.umat, u["l1"]])
                u["Psf"] = Psf

            def dve_arg(u):
                c0, pc0 = u["c0"], u["pc0"]
                sp_, Psf = u["sp"], u["Psf"]
                if u["first"]:
                    st["Pc_r"] = PcR.next()
                Pc = u["Pc"] = st["Pc_r"]
                arg = argR.next()
                self.tt("dve", arg[:, c0:TB], Psf[:, c0:TB], sp_[:, c0:TB], ALU.subtract, [Psf, sp_], [arg])
                if not u["first"]:
                    self.tt("dve", arg[:, c0:TB], Pc[:, c0:TB], arg[:, c0:TB], ALU.add, [Pc, arg], [arg])
                u["arg"] = arg

            def pe_colsum(u):
                c0 = u["c0"]
                if not u["last"]:
                    self.mm(u["Pc"], u["Pc"][:, 0:TB], self.ones[:], u["l1"][:, 0:TB], u["first"], False, [self.ones, u["l1"]])

            def act_w(u):
                c0 = u["c0"]
                ww = wwR.next()
                self.act(ww[:, c0:TB], u["arg"][:, c0:TB], AF.Exp, [u["arg"]], [ww])
                if u["diag"]:
                    self.tt("pool", ww[:, c0:c0 + 128], ww[:, c0:c0 + 128], self.mask_lt[:], ALU.mult, [ww, self.cst], [ww])
                    if c0 > 0:
                        S.op("pool", lambda e, ww=ww, c0=c0: e.memset(ww[:, 0:c0], 0.0), writes=[ww])
                u["ww"] = ww

            def pe_wv(u):
                h, Q, kb, c0 = u["h"], u["Q"], u["kb"], u["c0"]
                v = vh[h % 2]
                if u["newhead"] and h + 1 < 8:
                    load_head(h + 1)
                if u["first"]:
                    st["Pacc"] = PaccR.next()
                Pacc = st["Pacc"]
                self.mm(Pacc, Pacc[0:64, 0:TB], v[:, kb, :], u["ww"][:, 0:TB], u["first"], u["last"], [v, u["ww"]])
                if u["last"]:
                    oo = ot.next()
                    self.copy("act", oo[:], Pacc[0:64, 0:TB], [Pacc], [oo])
                    S.dma("pool", yT.ap[1024 + h * 64:1024 + (h + 1) * 64, Q * TB:(Q + 1) * TB], oo[:], reads=[oo], writes=[yT.blk[Q]], par=True)
                u.clear()

            n = len(units)
            import os
            fmap = dict(pe_z=pe_z, pe_suffix=pe_suffix, pe_wv=pe_wv, pe_colsum=pe_colsum, act_sp=act_sp, act_w=act_w, dve_arg=dve_arg, dve_l1=dve_l1)
            if os.environ.get("SB_SCHED"):
                sched = tuple((fmap[x.split(":")[0]], int(x.split(":")[1])) for x in os.environ["SB_SCHED"].split(","))
            else:
                sched = ((pe_z, 0), (pe_wv, 7), (pe_suffix, 4), (pe_colsum, 6), (act_sp, 1), (act_w, 6), (dve_arg, 5), (dve_l1, 2))
            for i in range(n + 8):
                for fn, off in sched:
                    j = i - off
                    if 0 <= j < n:
                        fn(units[j])
            S.barrier()

    def transpose_to(self, P, out_cols, src_buf, src_ap):
        pv = P[:].bitcast(BF16)
        self.S.op("pe", lambda e: e.transpose(out=pv[:, out_cols:out_cols + 128], in_=src_ap, identity=self.ident[:]),
                  reads=[src_buf, self.ident], writes=[P])

    def phase_even_a(self, ei, l, xin, sc):
        S = self.S
        T = self.T
        w = self.w
        with contextlib.ExitStack() as es:
            wib = S.sb("ea_wi", [128, KC, 5648], BF16, es)
            xs = S.sb("ea_xs", [128, KC, TB], F32, es)
            hbs = [S.sb("ea_hb%d" % i, [128, KC, TB], BF16, es) for i in range(2)]
            sq = S.sb("ea_sq", [128, KC, TB], BF16, es)
            rst = S.sb("ea_rst", [128, TB], F32, es)
            cosb = S.sb("ea_cos", [128, TB], F32, es)
            sinb = S.sb("ea_sin", [128, TB], F32, es)
            halo = [S.sb("ea_halo%d" % i, [128, 3], F32, es) for i in range(12)]
            PS8 = Rot(self.psb + self.psh)
            work = Rot([S.sb("ea_wk%d" % i, [128, TB + 3], F32, es) for i in range(4)])
            f32t = Rot([S.sb("ea_f%d" % i, [128, TB], F32, es) for i in range(12)])
            b16t = Rot([S.sb("ea_b%d" % i, [128, TB], BF16, es) for i in range(6)])
            xct = [S.sb("ea_xc%d" % i, [128, TB], BF16, es) for i in range(12)]
            tkb = Rot([S.sb("ea_tk%d" % i, [128, 1024], BF16, es) for i in range(3)])
            dts = Rot([S.sb("ea_dt%d" % i, [128, 32], F32, es) for i in range(2)])
            self.load_wcast(wib, w["ev_w_in"][ei], KC, 5648)
            for c in range(12):
                S.op("dve", lambda e, c=c: e.memset(halo[c][:], 0.0), writes=[halo[c]])
            cst = self.cst
            cw = self.ssd_cw[ei]; cb = self.ssd_cb[ei]
            qn = self.ret_qn[ei]; kn = self.ret_kn[ei]
            def prep(b):
                bs_ = slice(b * TB, (b + 1) * TB)
                S.dma("sp", xs[:], self.xview(xin, b), reads=[xin.blk[b]], writes=[xs])
                S.dma("sp", cosb[:], w["c_cos"][:, bs_], writes=[cosb])
                S.dma("sp", sinb[:], w["c_sin"][:, bs_], writes=[sinb])
                self.norm(xs, self.g_mix[l], hbs[b % 2], (sq, rst))

            prep(0)
            for b in range(self.NB):
                bs = slice(b * TB, (b + 1) * TB)
                hb = hbs[b % 2]
                def qk_tile(qi):
                    isq = qi < 4
                    col0 = qi * 128
                    Pq = PS8.next()
                    for kc in range(KC):
                        self.mm(Pq, Pq[:], wib[:, kc, col0:col0 + 128], hb[:, kc, :], kc == 0, kc == KC - 1, [wib, hb])
                    qraw = f32t.next()
                    self.copy("act", qraw[:], Pq[:], [Pq], [qraw])
                    s2 = b16t.next()
                    self.act(s2[:], Pq[:], AF.Square, [Pq], [s2])
                    Ps = PS8.next()
                    self.mm(Ps, Ps[:], self.ones[:], s2[:], True, True, [self.ones, s2])
                    r2 = f32t.next()
                    self.act(r2[:], Ps[:], AF.Ln, [Ps], [r2], scale=1.0 / 128, bias=self.epsb[:])
                    self.act(r2[:], r2[:], AF.Exp, [r2], [r2], scale=-0.5)
                    Pw = PS8.next()
                    self.mm(Pw, Pw[:], self.swapm[:], qraw[:], True, True, [self.swapm, qraw])
                    gn = qn if isq else kn
                    t1 = f32t.next()
                    self.stt(t1[:], qraw[:], gn[:, 0:1], cosb[:], ALU.mult, ALU.mult, [qraw, cosb, cst], [t1])
                    t2 = f32t.next()
                    self.stt(t2[:], Pw[:], gn[:, 1:2], sinb[:], ALU.mult, ALU.mult, [Pw, sinb, cst], [t2])
                    self.tt("pool", t1[:], t1[:], t2[:], ALU.add, [t1, t2], [t1])
                    qo = b16t.next()
                    self.stt(qo[:], t1[:], 1.0 if isq else 128 ** -0.5, r2[:], ALU.mult, ALU.mult, [t1, r2], [qo])
                    dst = sc["qr"] if isq else sc["kr"]
                    r0 = (qi % 4) * 128
                    S.dma("pool", dst.ap[r0:r0 + 128, bs], qo[:], reads=[qo], writes=[dst.blk[b]], par=True)
                for q0 in range(0, 8, 2):
                    S.interleave([S.record(lambda qi=q0 + j: qk_tile(qi)) for j in range(2)])
                if b + 1 < self.NB:
                    prep(b + 1)
                for (c0, dstn, fn) in ((1024, "v", None), (2048, "g", AF.Silu), (3072, "z", AF.Silu)):
                    for tt_ in range(4):
                        tk = tkb.next()
                        for hf in range(2):
                            Pv = PS8.next()
                            for kc in range(KC):
                                self.mm(Pv, Pv[:], hb[:, kc, tt_ * 128:(tt_ + 1) * 128], wib[:, kc, c0 + hf * 512:c0 + (hf + 1) * 512],
                                        kc == 0, kc == KC - 1, [wib, hb])
                            if fn is None:
                                self.copy("act", tk[:, hf * 512:(hf + 1) * 512], Pv[:], [Pv], [tk])
                            else:
                                self.act(tk[:, hf * 512:(hf + 1) * 512], Pv[:], fn, [Pv], [tk])
                        t0 = b * TB + tt_ * 128
                        S.dma("pool", sc[dstn].ap[t0:t0 + 128, :], tk[:], reads=[tk], writes=[sc[dstn].blk[b]], par=True)
                for tt_ in range(4):
                    Pd = PS8.next()
                    for kc in range(KC):
                        self.mm(Pd, Pd[:, 0:16], hb[:, kc, tt_ * 128:(tt_ + 1) * 128], wib[:, kc, 5632:5648], kc == 0, kc == KC - 1, [wib, hb])
                    dd = dts.next()
                    self.tt("dve", dd[:, 0:16], Pd[:, 0:16], self.ssd_dtb[ei], ALU.add, [Pd, cst], [dd])
                    self.act(dd[:, 0:16], dd[:, 0:16], AF.Exp, [dd], [dd])
                    self.act(dd[:, 0:16], dd[:, 0:16], AF.Ln, [dd], [dd], bias=self.oneb[:])
                    self.tt("dve", dd[:, 16:32], dd[:, 0:16], self.ssd_A[ei], ALU.mult, [dd, cst], [dd])
                    t0 = b * TB + tt_ * 128
                    S.dma("pool", sc["dtA"].ap[t0:t0 + 128, :], dd[:], reads=[dd], writes=[sc["dtA"].blk[b]], par=True)
                def xbc_chunk(c):
                    Px = PS8.next()
                    col0 = 4096 + c * 128
                    for kc in range(KC):
                        self.mm(Px, Px[:], wib[:, kc, col0:col0 + 128], hb[:, kc, :], kc == 0, kc == KC - 1, [wib, hb])
                    wk = work.next()
                    self.copy("act", wk[:, 3:TB + 3], Px[:], [Px], [wk])
                    self.copy("pool", wk[:, 0:3], halo[c][:], [halo[c]], [wk])
                    self.copy("pool", halo[c][:], wk[:, TB:TB + 3], [wk], [halo[c]])
                    xcv = f32t.next()
                    self.ts("dve", xcv[:], wk[:, 3:TB + 3], cw[:, c, 3:4], ALU.mult, [wk, cst], [xcv], s2=cb[:, c:c + 1], op1=ALU.add)
                    for k in range(3):
                        self.stt(xcv[:], wk[:, k:k + TB], cw[:, c, k:k + 1], xcv[:], ALU.mult, ALU.add, [wk, xcv, cst], [xcv])
                    self.act(xct[c][:], xcv[:], AF.Silu, [xcv], [xct[c]])
                    if c >= 8:
                        dstn = "BT" if c < 10 else "CT"
                        r0 = (c % 2) * 128
                        S.dma("pool", sc[dstn].ap[r0:r0 + 128, bs], xct[c][:], reads=[xct[c]], writes=[sc[dstn].blk[b]], par=True)
                for c0_ in range(0, 12, 4):
                    S.interleave([S.record(lambda c=c0_ + j: xbc_chunk(c)) for j in range(4)])
                for tt_ in range(4):
                    tk = tkb.next()
                    Pt = PS8.next()
                    for c in range(8):
                        self.transpose_to(Pt, c * 128, xct[c], xct[c][:, tt_ * 128:(tt_ + 1) * 128])
                    self.copy("act", tk[:], Pt[:].bitcast(BF16), [Pt], [tk])
                    t0 = b * TB + tt_ * 128
                    S.dma("pool", sc["xs"].ap[t0:t0 + 128, :], tk[:], reads=[tk], writes=[sc["xs"].blk[b]], par=True)
                    tk2 = tkb.next()
                    Pt2 = PS8.next()
                    for c in range(2):
                        self.transpose_to(Pt2, c * 128, xct[8 + c], xct[8 + c][:, tt_ * 128:(tt_ + 1) * 128])
                    self.copy("act", tk2[:, 0:256], Pt2[:].bitcast(BF16)[:, 0:256], [Pt2], [tk2])
                    S.dma("pool", sc["Btm"].ap[t0:t0 + 128, :], tk2[:, 0:256], reads=[tk2], writes=[sc["Btm"].blk[b]], par=True)
            S.barrier()

    def phase_even_b(self, ei, sc, yT):
        S = self.S
        T = self.T
        NCH = T // 128
        cst = self.cst
        gam = [1.0 - 2.0 ** (-5.0 - h) for h in range(4)]
        with contextlib.ExitStack() as es:
            def rot(name, shape, dt, n=2):
                return Rot([S.sb("%s%d" % (name, i), shape, dt, es) for i in range(n)])
            qrb = rot("eb_q", [128, 4, 128], BF16); krb = rot("eb_k", [128, 4, 128], BF16)
            vb = rot("eb_v", [128, 1024], BF16); gb = rot("eb_g", [128, 1024], BF16); zb = rot("eb_z", [128, 1024], BF16)
            xsb = rot("eb_xs", [128, 16, 64], BF16); btmb = rot("eb_bt", [128, 2, 128], BF16)
            BTb = rot("eb_BT", [128, 2, 128], BF16); CTb = rot("eb_CT", [128, 2, 128], BF16)
            dtb = rot("eb_dt", [128, 32], F32)
            Sst = S.sb("eb_S", [128, 4, 256], F32, es); Sbf = S.sb("eb_Sbf", [128, 4, 256], BF16, es)
            STs = S.sb("eb_ST", [128, 16, 64], F32, es); STbf = S.sb("eb_STbf", [128, 16, 64], BF16, es)
            smb = rot("eb_sm", [128, 4, 128], BF16); ktmb = rot("eb_ktm", [128, 4, 128], BF16); q2b = rot("eb_q2", [128, 4, 128], BF16)
            Dm = S.sb("eb_D", [128, 16, 128], F32, es)
            acol = rot("eb_acol", [128, 48], F32)
            segb = rot("eb_seg", [128, 4, 128], F32, 3); ltb = rot("eb_lt", [128, 4, 128], F32, 2); erb = rot("eb_er", [128, 4, 128], F32, 2)
            cbm = rot("eb_cbm", [128, 2, 128], F32)
            MTb = S.sb("eb_MT", [128, 16, 128], BF16, es); CsTb = S.sb("eb_CsT", [128, 16, 128], BF16, es)
            xdtb = rot("eb_xdt", [128, 16, 64], BF16); xdt2b = rot("eb_xdt2", [128, 16, 64], BF16)
            yf = rot("eb_yf", [128, 1024], F32, 2)
            yab = rot("eb_ya", [128, 1024], BF16, 2); ybb = rot("eb_yb", [128, 1024], BF16, 2)
            stat = rot("eb_stat", [128, 4, 6], F32); mv = rot("eb_mv", [128, 4, 2], F32); rs4 = rot("eb_rs4", [128, 8], F32)
            yTo = rot("eb_yTo", [128, 16, 128], BF16, 2)
            junk = S.sb("eb_junk", [128, 512], BF16, es)
            PSr = Rot(self.psb)
            PSs = Rot(self.psh)
            S.op("dve", lambda e: e.memset(Sst[:], 0.0), writes=[Sst])
            S.op("pool", lambda e: e.memset(Sbf[:], 0.0), writes=[Sbf])
            S.op("dve", lambda e: e.memset(STs[:], 0.0), writes=[STs])
            S.op("pool", lambda e: e.memset(STbf[:], 0.0), writes=[STbf])
            gnr_b = S.sb("eb_gnr", [128, 1024], F32, es); nrr_b = S.sb("eb_nrr", [128, 1024], F32, es)
            S.dma("sp", gnr_b[:], self.w["ret_gn"][ei].partition_broadcast(128), writes=[gnr_b])
            S.dma("sp", nrr_b[:], self.w["ssd_norm"][ei].partition_broadcast(128), writes=[nrr_b])
            gnrow = gnr_b[:]; nrow = nrr_b[:]; dsk = self.ssd_D[ei]
            pend = {}

            def out_part(ya, yb, b, ts_):
                yo = yTo.next()
                for half, src in ((0, ya), (1, yb)):
                    Pt = PSr.next()
                    for cc in range(8):
                        self.transpose_to(Pt, cc * 128, src, src[:, cc * 128:(cc + 1) * 128])
                    self.copy("act", yo[:, half * 8:(half + 1) * 8, :].rearrange("p c i -> p (c i)"), Pt[:].bitcast(BF16), [Pt], [yo])
                S.dma("pool", yT.ap[:, ts_].rearrange("(c p) t -> p c t", p=128), yo[:], reads=[yo], writes=[yT.blk[b]], par=True)

            for c in range(NCH):
                b = c // 4
                ts_ = slice(c * 128, (c + 1) * 128)
                qr = qrb.next(); kr = krb.next(); v = vb.next(); g = gb.next(); z = zb.next(); xs = xsb.next()
                btm = btmb.next(); BT = BTb.next(); CT = CTb.next(); dt = dtb.next()
                S.dma("sp", qr[:], sc["qr"].ap[:, ts_].rearrange("(h d) t -> d h t", d=128), reads=[sc["qr"].blk[b]], writes=[qr])
                S.dma("sp", kr[:], sc["kr"].ap[:, ts_].rearrange("(h d) t -> d h t", d=128), reads=[sc["kr"].blk[b]], writes=[kr])
                S.dma("sp", v[:], sc["v"].ap[ts_, :], reads=[sc["v"].blk[b]], writes=[v])
                S.dma("sp", g[:], sc["g"].ap[ts_, :], reads=[sc["g"].blk[b]], writes=[g])
                S.dma("sp", z[:], sc["z"].ap[ts_, :], reads=[sc["z"].blk[b]], writes=[z])
                S.dma("sp", xs[:], sc["xs"].ap[ts_, :].rearrange("t (h p) -> t h p", p=64), reads=[sc["xs"].blk[b]], writes=[xs])
                S.dma("sp", btm[:], sc["Btm"].ap[ts_, :].rearrange("t (g s) -> t g s", s=128), reads=[sc["Btm"].blk[b]], writes=[btm])
                S.dma("sp", BT[:], sc["BT"].ap[:, ts_].rearrange("(g s) t -> s g t", s=128), reads=[sc["BT"].blk[b]], writes=[BT])
                S.dma("sp", CT[:], sc["CT"].ap[:, ts_].rearrange("(g s) t -> s g t", s=128), reads=[sc["CT"].blk[b]], writes=[CT])
                S.dma("sp", dt[:], sc["dtA"].ap[ts_, :], reads=[sc["dtA"].blk[b]], writes=[dt])

                def ret_stream():
                    Psc = PSr.next()
                    for h in range(4):
                        self.mm(Psc, Psc[:, h * 128:(h + 1) * 128], kr[:, h, :], qr[:, h, :], True, True, [kr, qr])
                    sm = smb.next()
                    self.tt("dve", sm[:].rearrange("p h i -> p (h i)"), Psc[:], self.ret_decay[:], ALU.mult, [Psc, cst], [sm])
                    Pkt = PSr.next()
                    for h in range(4):
                        self.transpose_to(Pkt, h * 128, kr, kr[:, h, :])
                    ktm = ktmb.next()
                    self.tt("dve", ktm[:], Pkt[:].bitcast(BF16)[:, 0:512].rearrange("p (h d) -> p h d", d=128),
                            self.ret_kdec[:].unsqueeze(2).broadcast_to([128, 4, 128]), ALU.mult, [Pkt, cst], [ktm])
                    q2 = q2b.next()
                    self.tt("pool", q2[:].rearrange("p h i -> p (h i)"), qr[:].rearrange("p h i -> p (h i)"), self.ret_qdec[:], ALU.mult, [qr, cst], [q2])
                    POw = (PSr.next(), PSr.next())
                    for h in range(4):
                        PO = POw[h // 2]
                        hc = slice((h % 2) * 256, (h % 2 + 1) * 256)
                        self.mm(PO, PO[:, hc], sm[:, h, :], v[:, h * 256:(h + 1) * 256], True, False, [sm, v])
                        self.mm(PO, PO[:, hc], q2[:, h, :], Sbf[:, h, :], False, True, [q2, Sbf])
                    Pkvw = (PSr.next(), PSr.next())
                    for h in range(4):
                        Pkv = Pkvw[h // 2]
                        self.mm(Pkv, Pkv[:, (h % 2) * 256:(h % 2 + 1) * 256], ktm[:, h, :], v[:, h * 256:(h + 1) * 256], True, True, [ktm, v])
                    for h in range(4):
                        Pkv = Pkvw[h // 2]
                        self.stt(Sst[:, h, :], Sst[:, h, :], gam[h] ** 128, Pkv[:, (h % 2) * 256:(h % 2 + 1) * 256], ALU.mult, ALU.add, [Sst, Pkv], [Sst])
                    self.copy("act", Sbf[:], Sst[:], [Sst], [Sbf])
                    st = stat.next(); m2 = mv.next(); r4 = rs4.next()
                    for h in range(4):
                        PO = POw[h // 2]
                        S.op("dve", lambda e, h=h, st=st, PO=PO: e.bn_stats(out=st[:, h, :], in_=PO[:, (h % 2) * 256:(h % 2 + 1) * 256]), reads=[PO], writes=[st])
                    for h in range(4):
                        S.op("dve", lambda e, h=h, st=st, m2=m2: e.bn_aggr(out=m2[:, h, :], in_=st[:, h, :]), reads=[st], writes=[m2])
                    self.act(r4[:, 0:4], m2[:, :, 1], AF.Ln, [m2], [r4], bias=self.epsb[:])
                    self.act(r4[:, 0:4], r4[:, 0:4], AF.Exp, [r4], [r4], scale=-0.5)
                    y1 = yf.next()
                    for h in range(4):
                        PO = POw[h // 2]
                        self.ts("dve", y1[:, h * 256:(h + 1) * 256], PO[:, (h % 2) * 256:(h % 2 + 1) * 256], m2[:, h, 0:1], ALU.subtract, [PO, m2, r4], [y1],
                                s2=r4[:, h:h + 1], op1=ALU.mult)
                    self.tt("pool", y1[:], y1[:], gnrow, ALU.mult, [y1, gnr_b], [y1])
                    ya = yab.next()
                    self.tt("pool", ya[:], y1[:], g[:], ALU.mult, [y1, g], [ya])

                    out_['ya'] = ya

                def ssd_stream():
                    PA = PSs.next()
                    self.mm(PA, PA[:, 0:16], self.tri_f[:], dt[:, 16:32], True, True, [self.tri_f, dt])
                    self.mm(PA, PA[:, 16:32], self.ones_f[:], dt[:, 16:32], True, True, [self.ones_f, dt])
                    ac = acol.next()
                    self.copy("act", ac[:, 0:32], PA[:, 0:32], [PA], [ac])
                    self.tt("dve", ac[:, 32:48], ac[:, 16:32], ac[:, 0:16], ALU.subtract, [ac], [ac])
                    self.act(ac[:, 32:48], ac[:, 32:48], AF.Exp, [ac], [ac])
                    self.act(ac[:, 16:32], ac[:, 16:32], AF.Exp, [ac], [ac])
                    self.tt("dve", Dm[:], dt[:, 16:32].unsqueeze(2).broadcast_to([128, 16, 128]),
                            self.tri_f[:].unsqueeze(1).broadcast_to([128, 16, 128]), ALU.mult, [dt, self.tri_f], [Dm])
                    PCB = PSs.next()
                    for gi in range(2):
                        self.mm(PCB, PCB[:, gi * 128:(gi + 1) * 128], BT[:, gi, :], CT[:, gi, :], True, True, [BT, CT])
                    cb_ = cbm.next()
                    self.tt("dve", cb_[:].rearrange("p g i -> p (g i)"), PCB[:, 0:256], self.tri2[:], ALU.mult, [PCB, cst], [cb_])
                    for q4 in range(4):
                        gi = q4 // 2
                        hs = slice(q4 * 4, q4 * 4 + 4)
                        Prb = PSs.next()
                        self.mm(Prb, Prb[:], self.ones_f[:], Dm[:, hs, :].rearrange("p h i -> p (h i)"), True, True, [self.ones_f, Dm])
                        sg = segb.next()
                        self.tt("dve", sg[:], Prb[:].rearrange("p (h i) -> p h i", i=128), ac[:, q4 * 4:q4 * 4 + 4].unsqueeze(2).broadcast_to([128, 4, 128]),
                                ALU.subtract, [Prb, ac], [sg])
                        lt = ltb.next()
                        self.act(lt[:], sg[:], AF.Exp, [sg], [lt])
                        er = erb.next()
                        self.act(er[:].rearrange("p h i -> p (h i)"), Prb[:], AF.Exp, [Prb], [er])
                        self.stt(MTb[:, hs, :], lt[:], 1.0, cb_[:, gi, :].unsqueeze(1).broadcast_to([128, 4, 128]), ALU.min, ALU.mult, [lt, cb_], [MTb])
                        self.tt("pool", CsTb[:, hs, :], er[:], CT[:, gi, :].unsqueeze(1).broadcast_to([128, 4, 128]), ALU.mult, [er, CT], [CsTb])
                    xdt = xdtb.next(); xdt2 = xdt2b.next()
                    self.tt("dve", xdt[:], xs[:], dt[:, 0:16].unsqueeze(2).broadcast_to([128, 16, 64]), ALU.mult, [xs, dt], [xdt])
                    self.tt("pool", xdt2[:], xdt[:], ac[:, 32:48].unsqueeze(2).broadcast_to([128, 16, 64]), ALU.mult, [xdt, ac], [xdt2])
                    PYw = (PSs.next(), PSs.next())
                    for h in range(16):
                        PY = PYw[h // 8]
                        hc = slice((h % 8) * 64, (h % 8 + 1) * 64)
                        self.mm(PY, PY[:, hc], MTb[:, h, :], xdt[:, h, :], True, False, [MTb, xdt])
                        self.mm(PY, PY[:, hc], CsTb[:, h, :], STbf[:, h, :], False, True, [CsTb, STbf])
                    PStw = (PSs.next(), PSs.next())
                    for h in range(16):
                        PSt = PStw[h // 8]
                        self.mm(PSt, PSt[:, (h % 8) * 64:(h % 8 + 1) * 64], btm[:, h // 8, :], xdt2[:, h, :], True, True, [btm, xdt2])
                    self.tt("dve", STs[:], STs[:], ac[:, 16:32].unsqueeze(2).broadcast_to([128, 16, 64]), ALU.mult, [STs, ac], [STs])
                    STf = STs[:].rearrange("p h d -> p (h d)")
                    for gi in range(2):
                        self.tt("dve", STf[:, gi * 512:(gi + 1) * 512], PStw[gi][:], STf[:, gi * 512:(gi + 1) * 512], ALU.add, [PStw[gi], STs], [STs])
                    self.copy("act", STbf[:], STs[:], [STs], [STbf])
                    y2 = yf.next()
                    self.tt("pool", y2[:].rearrange("p (h d) -> p h d", d=64), xs[:], dsk.unsqueeze(2).broadcast_to([128, 16, 64]), ALU.mult, [xs, cst], [y2])
                    for gi in range(2):
                        self.tt("dve", y2[:, gi * 512:(gi + 1) * 512], PYw[gi][:], y2[:, gi * 512:(gi + 1) * 512], ALU.add, [PYw[gi], y2], [y2])
                    self.tt("pool", y2[:], y2[:], z[:], ALU.mult, [y2, z], [y2])
                    r8 = rs4.next()
                    for gi in range(2):
                        S.op("act", lambda e, gi=gi, y2=y2, r8=r8: e.activation(out=junk[:], in_=y2[:, gi * 512:(gi + 1) * 512], func=AF.Square, accum_out=r8[:, gi:gi + 1]),
                             reads=[y2], writes=[junk, r8])
                    self.act(r8[:, 0:2], r8[:, 0:2], AF.Ln, [r8], [r8], scale=1.0 / 512, bias=self.epsb[:])
                    self.act(r8[:, 0:2], r8[:, 0:2], AF.Exp, [r8], [r8], scale=-0.5)
                    yb = ybb.next()
                    for gi in range(2):
                        self.stt(yb[:, gi * 512:(gi + 1) * 512], y2[:, gi * 512:(gi + 1) * 512], r8[:, gi:gi + 1], nrow[:, gi * 512:(gi + 1) * 512],
                                 ALU.mult, ALU.mult, [y2, r8, nrr_b], [yb])
                    out_['yb'] = yb

                out_ = {}
                S.interleave([S.record(ret_stream), S.record(ssd_stream)])
                out_part(out_['ya'], out_['yb'], b, ts_)
            S.barrier()

    def build(self):
        nc, es = self.nc, self.es
        T, NB = self.T, self.NB
        S = self.S = Sched(nc, es)
        ne = sum(1 for x in self.layers if x == "e")
        no = sum(1 for x in self.layers if x == "o")
        L = len(self.layers)
        w = self.w = {}

        def inp(name, shape):
            w[name] = nc.dram_tensor(name, list(shape), F32, kind="ExternalInput").ap()

        self.xin = DT(S, "xT", [D, T], F32, NB, kind="ExternalInput")
        self.xout = DT(S, "outT", [D, T], F32, NB, kind="ExternalOutput")
        inp("norm_mix", [L, D]); inp("norm_mlp", [L, D])
        inp("mlp_w1", [L, D, 4096]); inp("mlp_w2", [L, 4096, D])
        if no:
            inp("od_w_in", [no, D, 3584]); inp("od_w_out", [no, 1536, D])
            inp("lru_conv_w", [no, 4, D]); inp("lru_conv_b", [no, D])
            inp("lru_wa", [no, 8, 128, 128]); inp("lru_ba", [no, 8, 128])
            inp("lru_wx", [no, 8, 128, 128]); inp("lru_bx", [no, 8, 128])
            inp("lru_lam", [no, D]); inp("sb_qn", [no, 64]); inp("sb_kn", [no, 64])
        if ne:
            inp("ev_w_in", [ne, D, 5648]); inp("ev_w_out", [ne, 2048, D])
            inp("ret_qn", [ne, 128]); inp("ret_kn", [ne, 128]); inp("ret_gn", [ne, 1024])
            inp("ssd_conv_w", [ne, 4, 1536]); inp("ssd_conv_b", [ne, 1536])
            inp("ssd_dt_bias", [ne, 16]); inp("ssd_a_log", [ne, 16]); inp("ssd_d", [ne, 16]); inp("ssd_norm", [ne, 1024])
            inp("c_cos", [128, T]); inp("c_sin", [128, T])
            inp("c_ident", [128, 128]); inp("c_swap", [128, 128]); inp("c_tri", [128, 128]); inp("c_tri2", [128, 256])
            inp("c_rdecay", [128, 512]); inp("c_rqdec", [128, 512]); inp("c_rkdec", [128, 4])
        inp("c_masklt", [128, 128]); inp("c_umat", [128, 128]); inp("c_bones", [128, 128])

        allb = [S.ps("ps%d" % i, [128, 512]) for i in range(8)]
        self.psb = allb[0:4]
        self.psh = allb[4:8]
        self.PS = Rot(self.psb)
        self.PW = Rot([(allb[4], allb[5]), (allb[6], allb[7])])

        cst = self.cst = Buf("cst")

        def csb(name, shape, dt=F32):
            return es.enter_context(nc.sbuf_tensor(name, list(shape), dt))

        self.ones = S.sb("ones", [128, 128], BF16)
        S.op("dve", lambda e: e.memset(self.ones[:], 1.0), writes=[self.ones])
        self.epsb = csb("epsb", [128, 1]); self.oneb = csb("oneb", [128, 1])
        S.op("dve", lambda e: e.memset(self.epsb[:], EPS), writes=[cst])
        S.op("dve", lambda e: e.memset(self.oneb[:], 1.0), writes=[cst])
        self.mask_lt = csb("mask_lt", [128, 128], BF16)
        self.umat = S.sb("umat", [128, 128], BF16)
        self.bones = S.sb("bones", [128, 128], BF16)
        S.dma("pool", self.mask_lt[:], w["c_masklt"], writes=[cst])
        S.dma("pool", self.umat[:], w["c_umat"], writes=[self.umat])
        S.dma("pool", self.bones[:], w["c_bones"], writes=[self.bones])
        self.gains = Buf("gains")
        gmix = csb("gmix", [128, L, KC]); gmlp = csb("gmlp", [128, L, KC])
        S.dma("sp", gmix[:], w["norm_mix"].rearrange("l (kc p) -> p l kc", p=128), writes=[self.gains], allow_slow_non_contiguous=True)
        S.dma("sp", gmlp[:], w["norm_mlp"].rearrange("l (kc p) -> p l kc", p=128), writes=[self.gains], allow_slow_non_contiguous=True)
        self.g_mix = [gmix[:, l, :] for l in range(L)]
        self.g_mlp = [gmlp[:, l, :] for l in range(L)]
        if no:
            cw = csb("lru_cw", [128, no, 8, 4]); cb = csb("lru_cb", [128, no, 8])
            ba = csb("lru_ba_s", [128, no, 8]); bx = csb("lru_bx_s", [128, no, 8])
            cl = csb("lru_cl", [128, no, 8])
            qn = csb("sbqn", [128, no]); kn = csb("sbkn", [128, no])
            for o_ in range(no):
                for k_ in range(4):
                    S.dma("sp", cw[:, o_, :, k_], w["lru_conv_w"][o_, k_].rearrange("(c p) -> p c", p=128), writes=[cst], allow_slow_non_contiguous=True)
            S.dma("sp", cb[:], w["lru_conv_b"].rearrange("o (c p) -> p o c", p=128), writes=[cst], allow_slow_non_contiguous=True)
            S.dma("sp", ba[:], w["lru_ba"].rearrange("o c p -> p o c"), writes=[cst], allow_slow_non_contiguous=True)
            S.dma("sp", bx[:], w["lru_bx"].rearrange("o c p -> p o c"), writes=[cst], allow_slow_non_contiguous=True)
            S.dma("sp", cl[:], w["lru_lam"].rearrange("o (c p) -> p o c", p=128), writes=[cst], allow_slow_non_contiguous=True)
            for half in range(2):
                S.dma("sp", qn[half * 64:(half + 1) * 64, :], w["sb_qn"].rearrange("o d -> d o"), writes=[cst], allow_slow_non_contiguous=True)
                S.dma("sp", kn[half * 64:(half + 1) * 64, :], w["sb_kn"].rearrange("o d -> d o"), writes=[cst], allow_slow_non_contiguous=True)
            S.op("act", lambda e: e.activation(out=cl[:], in_=cl[:], func=AF.Exp, scale=-1.0), reads=[cst], writes=[cst])
            S.op("act", lambda e: e.activation(out=cl[:], in_=cl[:], func=AF.Ln, bias=self.oneb[:]), reads=[cst], writes=[cst])
            S.op("dve", lambda e: e.tensor_scalar(out=cl[:], in0=cl[:], scalar1=-8.0, scalar2=None, op0=ALU.mult), reads=[cst], writes=[cst])
            self.lru_cw = [cw[:, o] for o in range(no)]
            self.lru_cb = [cb[:, o] for o in range(no)]
            self.lru_ba = [ba[:, o] for o in range(no)]
            self.lru_bx = [bx[:, o] for o in range(no)]
            self.lru_cl = [cl[:, o] for o in range(no)]
            self.sb_qn = [qn[:, o:o + 1] for o in range(no)]
            self.sb_kn = [kn[:, o:o + 1] for o in range(no)]
        if ne:
            self.ident = S.sb("ident", [128, 128], BF16)
            S.dma("pool", self.ident[:], w["c_ident"], writes=[self.ident])
            self.swapm = S.sb("swapm", [128, 128], F32)
            S.dma("sp", self.swapm[:], w["c_swap"], writes=[self.swapm])
            self.tri_f = S.sb("tri_f", [128, 128], F32)
            S.dma("sp", self.tri_f[:], w["c_tri"], writes=[self.tri_f])
            self.ones_f = S.sb("ones_f", [128, 128], F32)
            S.op("dve", lambda e: e.memset(self.ones_f[:], 1.0), writes=[self.ones_f])
            self.tri2 = csb("tri2", [128, 256]); self.ret_decay = csb("rdecay", [128, 512]); self.ret_qdec = csb("rqdec", [128, 512])
            self.ret_kdec = csb("rkdec", [128, 4])
            S.dma("sp", self.tri2[:], w["c_tri2"], writes=[cst])
            S.dma("sp", self.ret_decay[:], w["c_rdecay"], writes=[cst])
            S.dma("sp", self.ret_qdec[:], w["c_rqdec"], writes=[cst])
            S.dma("sp", self.ret_kdec[:], w["c_rkdec"], writes=[cst])
            scw = csb("ssd_cw", [128, ne, 12, 4]); scb = csb("ssd_cb", [128, ne, 12])
            rqn = csb("ret_qn_s", [128, ne, 2]); rkn = csb("ret_kn_s", [128, ne, 2])
            dsk = csb("ssd_D_r", [128, ne, 16]); dtbr = csb("ssd_dtb_r", [128, ne, 16]); Ar = csb("ssd_A_r", [128, ne, 16])
            for e_ in range(ne):
                for k_ in range(4):
                    S.dma("sp", scw[:, e_, :, k_], w["ssd_conv_w"][e_, k_].rearrange("(c p) -> p c", p=128), writes=[cst], allow_slow_non_contiguous=True)
                S.dma("sp", scb[:, e_, :], w["ssd_conv_b"][e_].rearrange("(c p) -> p c", p=128), writes=[cst], allow_slow_non_contiguous=True)
                for nm, tl in (("ret_qn", rqn), ("ret_kn", rkn)):
                    col = w[nm][e_].rearrange("(d o) -> d o", o=1)
                    S.dma("sp", tl[:, e_, 0:1], col, writes=[cst], allow_slow_non_contiguous=True)
                    S.dma("sp", tl[0:64, e_, 1:2], col[64:128], writes=[cst], allow_slow_non_contiguous=True)
                    S.dma("sp", tl[64:128, e_, 1:2], col[0:64], writes=[cst], allow_slow_non_contiguous=True)
                S.dma("sp", dsk[:, e_, :], w["ssd_d"][e_].partition_broadcast(128), writes=[cst])
                S.dma("sp", dtbr[:, e_, :], w["ssd_dt_bias"][e_].partition_broadcast(128), writes=[cst])
                S.dma("sp", Ar[:, e_, :], w["ssd_a_log"][e_].partition_broadcast(128), writes=[cst])
            S.op("act", lambda e: e.activation(out=Ar[:], in_=Ar[:], func=AF.Exp), reads=[cst], writes=[cst])
            S.op("dve", lambda e: e.tensor_scalar(out=Ar[:], in0=Ar[:], scalar1=-1.0, scalar2=None, op0=ALU.mult), reads=[cst], writes=[cst])
            self.ssd_cw = [scw[:, e_] for e_ in range(ne)]; self.ssd_cb = [scb[:, e_] for e_ in range(ne)]
            self.ret_qn = [rqn[:, e_] for e_ in range(ne)]; self.ret_kn = [rkn[:, e_] for e_ in range(ne)]
            self.ssd_D = [dsk[:, e_] for e_ in range(ne)]; self.ssd_dtb = [dtbr[:, e_] for e_ in range(ne)]; self.ssd_A = [Ar[:, e_] for e_ in range(ne)]
        S.barrier()

        xa = DT(S, "xa", [D, T], F32, NB)
        xb_ = DT(S, "xb", [D, T], F32, NB)
        yT = DT(S, "yT", [2048, T], BF16, NB)
        qT = DT(S, "qT", [512, T], BF16, NB)
        kT = DT(S, "kT", [512, T], BF16, NB)
        vtm = DT(S, "vtm", [T, 1024], BF16, NB)
        sc = {"qr": qT, "kr": kT, "v": vtm}
        if ne:
            for nm in ("g", "z", "xs"):
                sc[nm] = DT(S, "sc_" + nm, [T, 1024], BF16, NB)
            sc["Btm"] = DT(S, "sc_Btm", [T, 256], BF16, NB)
            sc["BT"] = DT(S, "sc_BT", [256, T], BF16, NB)
            sc["CT"] = DT(S, "sc_CT", [256, T], BF16, NB)
            sc["dtA"] = DT(S, "sc_dtA", [T, 32], F32, NB)

        cur = self.xin
        ie = io = 0
        for l, kind in enumerate(self.layers):
            last = l == L - 1
            skip = getattr(self, "skip", ())
            lay_es = contextlib.ExitStack()
            w1b = None
            w1_th = ()

            def prep_w1():
                nonlocal w1b, w1_th
                if "mlp" in skip:
                    return
                w1b = S.sb("w1b", [128, KC, 4096], BF16, lay_es)
                w1_th = self.load_wcast(w1b, self.w["mlp_w1"][l], KC, 4096, defer=True)
            if kind == "o":
                if "odd_a" not in skip:
                    self.phase_odd_a(io, l, cur, yT, qT, kT, vtm)
                if "sb" not in skip:
                    self.phase_sb(qT, kT, vtm, yT)
                if "outproj" not in skip:
                    prep_w1()
                    self.phase_outproj("od_w_out", io, 12, yT, cur, xa, extra=w1_th)
                io += 1
            else:
                if "even_a" not in skip:
                    self.phase_even_a(ie, l, cur, sc)
                if "even_b" not in skip:
                    self.phase_even_b(ie, sc, yT)
                if "outproj" not in skip:
                    prep_w1()
                    self.phase_outproj("ev_w_out", ie, 16, yT, cur, xa, extra=w1_th)
                ie += 1
            if "mlp" not in skip:
                self.phase_mlp(l, xa, self.xout if last else xb_, w1b=(w1b if "outproj" not in skip else None))
            lay_es.close()
            cur = xb_
        S.finish()
        es.close()
        return nc


def consts(T=None, even=False):
    i = np.arange(128)
    c = _consts_base(i)
    if even:
        f32 = np.float32
        inv = (f32(10000.0) ** (-(np.arange(64, dtype=f32)) / f32(64))).astype(f32)
        ang = (np.arange(T, dtype=f32)[None, :] * inv[:, None]).astype(f32).astype(np.float64)
        cos = np.cos(ang); sin = np.sin(ang)
        c["c_cos"] = np.concatenate([cos, cos], 0).astype(f32)
        c["c_sin"] = np.concatenate([-sin, sin], 0).astype(f32)
        c["c_ident"] = np.eye(128, dtype=f32)
        c["c_swap"] = (i[:, None] == ((i[None, :] + 64) % 128)).astype(f32)
        tri = (i[:, None] <= i[None, :]).astype(f32)
        c["c_tri"] = tri
        c["c_tri2"] = np.concatenate([tri, tri], 1)
        lg = np.log1p(-np.exp2(-5.0 - np.arange(4, dtype=np.float64)))
        rel = (i[None, :] - i[:, None]).astype(np.float64)
        dec = np.where(rel[:, None, :] >= 0, np.exp(lg[None, :, None] * np.maximum(rel, 0)[:, None, :]), 0.0)
        c["c_rdecay"] = dec.reshape(128, 512).astype(f32)
        qd = np.exp(lg[:, None] * (i[None, :] + 1.0))
        c["c_rqdec"] = np.broadcast_to(qd.reshape(1, 512), (128, 512)).astype(f32).copy()
        c["c_rkdec"] = np.exp(lg[None, :] * (127.0 - i[:, None])).astype(f32)
    return c


def _consts_base(i):
    return {
        "c_masklt": (i[:, None] < i[None, :]).astype(np.float32),
        "c_umat": (i[:, None] > i[None, :]).astype(np.float32),
        "c_bones": ((i[:, None] // 64) == (i[None, :] // 64)).astype(np.float32),
    }


def make_inputs(inputs, b, kinds):
    m = {"xT": np.ascontiguousarray(np.asarray(inputs["x"])[b].T)}
    L = len(kinds)
    for n in ("norm_mix", "norm_mlp", "mlp_w1", "mlp_w2"):
        m[n] = np.ascontiguousarray(np.asarray(inputs[n])[:L])
    if "e" in kinds:
        for n in ("ev_w_in", "ev_w_out", "ret_qn", "ret_kn", "ret_gn", "ssd_conv_w", "ssd_conv_b", "ssd_dt_bias", "ssd_a_log", "ssd_d", "ssd_norm"):
            m[n] = np.ascontiguousarray(np.asarray(inputs[n]))
    if "o" in kinds:
        for n in ("od_w_in", "od_w_out", "lru_conv_w", "lru_conv_b", "lru_wa", "lru_ba", "lru_wx", "lru_bx", "lru_lam", "sb_qn", "sb_kn"):
            m[n] = np.ascontiguousarray(np.asarray(inputs[n]))
    m.update(consts(m["xT"].shape[1], "e" in kinds))
    return m


KINDS = "eoeo"
SEQ = 8192
_NC_CACHE = {}


def kernel(**inputs):
    if "nc" not in _NC_CACHE:
        _NC_CACHE["nc"] = K(SEQ, list(KINDS)).build()
    nc = _NC_CACHE["nc"]
    nb = np.asarray(inputs["x"]).shape[0]
    in_maps = [make_inputs(inputs, b, KINDS) for b in range(nb)]
    res = run_bass_kernel_spmd(nc, in_maps, core_ids=list(range(nb)))
    out = np.stack([np.asarray(res.results[b]["outT"]).T for b in range(nb)])
    return np.ascontiguousarray(out.astype(np.float32))
```

```python
import contextlib
import math
import numpy as np
import concourse.bass as bass
import concourse.mybir as mybir
from concourse.bass_utils import run_bass_kernel_spmd

F32 = mybir.dt.float32
BF16 = mybir.dt.bfloat16
AF = mybir.ActivationFunctionType
ALU = mybir.AluOpType
AX = mybir.AxisListType

D = 1024
KC = 8
TB = 512
EPS = 1e-6


class Buf:
    __slots__ = ("name", "t", "lws", "rd")

    def __init__(self, name, t=None):
        self.name = name
        self.t = t
        self.lws = []
        self.rd = {}

    def __getitem__(self, k):
        return self.t[k]


class Rot:
    def __init__(self, bufs):
        self.bufs = list(bufs)
        self.i = 0

    def next(self):
        b = self.bufs[self.i % len(self.bufs)]
        self.i += 1
        return b


class Sched:
    ENG = ("pe", "act", "dve", "pool", "sp")

    def __init__(self, nc, es, n_dma_sems=32):
        self.nc = nc
        self.es = es
        self.q = {e: [] for e in self.ENG}
        self.cnt = {e: 0 for e in self.ENG}
        self.sem = {e: es.enter_context(nc.semaphore("c_" + e)) for e in ("pe", "act", "dve", "pool")}
        self.dsem = [es.enter_context(nc.semaphore("d%d" % i)) for i in range(n_dma_sems)]
        self.dcnt = [0] * n_dma_sems
        self.dnext = 0
        self.pnext = 0
        self.rec = None
        self.waited = {e: {} for e in self.ENG}
        self.ndma = 0

    def sb(self, name, shape, dt, es=None):
        self.uid = getattr(self, "uid", 0) + 1
        name = "%s_u%d" % (name, self.uid)
        return Buf(name, (es or self.es).enter_context(self.nc.sbuf_tensor(name, list(shape), dt)))

    def ps(self, name, shape, dt=F32):
        return Buf(name, self.es.enter_context(self.nc.psum_tensor(name, list(shape), dt)))

    def dram(self, name, shape, dt, kind="Internal"):
        return Buf(name, self.nc.dram_tensor(name, list(shape), dt, kind=kind).ap())

    def _need(self, eng, dep, waits):
        if dep is None:
            return
        key, val, semh = dep
        w = self.waited[eng]
        if w.get(key, 0) >= val:
            return
        w[key] = val
        waits.append((semh, val))

    def _deps(self, eng, reads, writes, same, par=False):
        waits = []
        for b in reads:
            for lw in b.lws:
                if same or lw[0] != eng:
                    self._need(eng, lw, waits)
        for b in writes:
            if not par:
                for lw in b.lws:
                    if same or lw[0] != eng:
                        self._need(eng, lw, waits)
            for k, d in b.rd.items():
                if same or k != eng:
                    self._need(eng, d, waits)
        return waits

    def hoist_begin(self, eng):
        self._hm = (eng, len(self.q[eng]))

    def hoist_end(self):
        eng, m = self._hm
        items = self.q[eng][m:]
        if len(items) < 2:
            return
        allw = []
        for waits, fn, inc in items:
            allw.extend(waits)
        best = {}
        for semh, val in allw:
            k = id(semh)
            if k not in best or best[k][1] < val:
                best[k] = (semh, val)
        self.q[eng][m] = (list(best.values()), items[0][1], items[0][2])
        for j in range(1, len(items)):
            self.q[eng][m + j] = ([], items[j][1], items[j][2])

    def record(self, body):
        old = self.rec
        self.rec = []
        body()
        r = self.rec
        self.rec = old
        return r

    def interleave(self, lists):
        its = [list(l) for l in lists]
        pos = [0] * len(its)
        left = sum(len(l) for l in its)
        while left:
            for k, l in enumerate(its):
                if pos[k] < len(l):
                    it = l[pos[k]]
                    pos[k] += 1
                    left -= 1
                    if it[0] == 0:
                        self.op(it[1], it[2], it[3], it[4])
                    else:
                        self.dma(it[1], it[2], it[3], it[4], it[5], it[6], **it[7])

    def op(self, eng, fn, reads=(), writes=()):
        if self.rec is not None:
            self.rec.append((0, eng, fn, list(reads), list(writes)))
            return None
        same = eng != "pe"
        waits = self._deps(eng, reads, writes, same)
        self.cnt[eng] += 1
        tok = (eng, self.cnt[eng], self.sem[eng])
        self.q[eng].append((waits, fn, (self.sem[eng], 1)))
        for b in reads:
            b.rd[eng] = tok
        for b in writes:
            b.lws = [tok]
            b.rd = {}
        return tok

    def dma(self, qeng, out_ap, in_ap, reads=(), writes=(), par=False, **kw):
        if self.rec is not None:
            self.rec.append((1, qeng, out_ap, in_ap, list(reads), list(writes), par, kw))
            return None
        waits = self._deps(qeng, reads, writes, True, par)
        if qeng == "pool":
            s = self.pnext
            self.pnext = (self.pnext + 1) % 4
        else:
            s = 4 + self.dnext
            self.dnext = (self.dnext + 1) % (len(self.dsem) - 4)
        if self.dcnt[s] > 0:
            self._need(qeng, ("d%d" % s, self.dcnt[s], self.dsem[s]), waits)
        self.dcnt[s] += 16
        tok = ("d%d" % s, self.dcnt[s], self.dsem[s])
        self.q[qeng].append((waits, lambda e: e.dma_start(out=out_ap, in_=in_ap, **kw), (self.dsem[s], 16)))
        for b in reads:
            b.rd["dma%d" % self.ndma] = tok
        for b in writes:
            if par:
                if b.rd:
                    b.lws = []
                    b.rd = {}
                b.lws.append(tok)
            else:
                b.lws = [tok]
                b.rd = {}
        self.ndma += 1
        return tok

    def barrier(self):
        for e in self.ENG:
            waits = []
            for e2 in ("pe", "act", "dve", "pool"):
                if e2 != e and self.cnt[e2] > 0:
                    self._need(e, (e2, self.cnt[e2], self.sem[e2]), waits)
            for s in range(len(self.dsem)):
                if self.dcnt[s] > 0:
                    self._need(e, ("d%d" % s, self.dcnt[s], self.dsem[s]), waits)
            if waits:
                self.q[e].append((waits, None, None))

    def finish(self):
        self.barrier()
        nc = self.nc
        q = self.q

        def replay(eh, items):
            for waits, fn, inc in items:
                for semh, val in waits:
                    eh.wait_ge(semh, val)
                if fn is not None:
                    ins = fn(eh)
                    if inc is not None:
                        ins.then_inc(inc[0], inc[1])

        with nc.Block() as block:
            @block.sync
            def _(e):
                replay(e, q["sp"])

            @block.tensor
            def _(e):
                replay(e, q["pe"])

            @block.scalar
            def _(e):
                replay(e, q["act"])

            @block.vector
            def _(e):
                replay(e, q["dve"])

            @block.gpsimd
            def _(e):
                replay(e, q["pool"])


class DT:
    def __init__(self, S, name, shape, dt, nblk, kind="Internal"):
        self.ap = S.nc.dram_tensor(name, list(shape), dt, kind=kind).ap()
        self.blk = [Buf("%s_b%d" % (name, i)) for i in range(nblk)]
        self.all = self.blk


class K:
    def __init__(self, T, layers):
        self.T = T
        self.NB = T // TB
        self.layers = layers
        self.nc = bass.Bass("TRN2", target_bir_lowering=False)
        self.es = contextlib.ExitStack()

    def mm(self, P, out_ap, lhsT, rhs, start, stop, reads):
        self.S.op("pe", lambda e: e.matmul(out_ap, lhsT=lhsT, rhs=rhs, start=start, stop=stop), reads=reads, writes=[P])

    def act(self, out_ap, in_ap, func, reads, writes, **kw):
        self.S.op("act", lambda e: e.activation(out=out_ap, in_=in_ap, func=func, **kw), reads=reads, writes=writes)

    def tt(self, eng, out_ap, a, b, op, reads, writes):
        self.S.op(eng, lambda e: e.tensor_tensor(out=out_ap, in0=a, in1=b, op=op), reads=reads, writes=writes)

    def ts(self, eng, out_ap, a, s1, op0, reads, writes, s2=None, op1=None):
        if op1 is None:
            self.S.op(eng, lambda e: e.tensor_scalar(out=out_ap, in0=a, scalar1=s1, scalar2=None, op0=op0), reads=reads, writes=writes)
        else:
            self.S.op(eng, lambda e: e.tensor_scalar(out=out_ap, in0=a, scalar1=s1, scalar2=s2, op0=op0, op1=op1), reads=reads, writes=writes)

    def stt(self, out_ap, a, s, b, op0, op1, reads, writes):
        self.S.op("dve", lambda e: e.scalar_tensor_tensor(out=out_ap, in0=a, scalar=s, in1=b, op0=op0, op1=op1), reads=reads, writes=writes)

    def copy(self, eng, out_ap, in_ap, reads, writes):
        if eng == "act":
            self.S.op("act", lambda e: e.copy(out=out_ap, in_=in_ap), reads=reads, writes=writes)
        else:
            self.S.op(eng, lambda e: e.tensor_copy(out=out_ap, in_=in_ap), reads=reads, writes=writes)

    def xview(self, xdt, b):
        return xdt.ap.rearrange("(kc p) t -> p kc t", p=128)[:, :, b * TB:(b + 1) * TB]

    def load_wcast(self, wbuf, w_ap, kc_n, ncols, grp=1, defer=False):
        S = self.S
        wv = w_ap.rearrange("(kc p) f -> p kc f", p=128)
        th = []
        for k0 in range(0, kc_n, grp):
            th.append(lambda k0=k0: S.dma("pool", wbuf[:, k0:k0 + grp, :], wv[:, k0:k0 + grp, :], reads=[], writes=[wbuf], par=True, max_dma_last_dim=4096))
        if defer:
            return th
        for t in th:
            t()

    def norm(self, xs, g_ap, hb, es_bufs):
        S = self.S
        sq, rst = es_bufs
        sqv = sq[:, 0:KC, :]
        P = self.PS.next()
        self.act(sqv, xs[:], AF.Square, [xs], [sq])
        for kc in range(KC):
            self.mm(P, P[:], self.ones[:], sqv[:, kc, :], kc == 0, kc == KC - 1, [self.ones, sq])
        self.act(rst[:], P[:], AF.Ln, [P], [rst], scale=1.0 / D, bias=self.epsb[:])
        self.act(rst[:], rst[:], AF.Exp, [rst], [rst], scale=-0.5)
        for kc in range(KC):
            self.stt(hb[:, kc, :], xs[:, kc, :], g_ap[:, kc:kc + 1], rst[:], ALU.mult, ALU.mult, [xs, rst, self.gains], [hb])

    def phase_mlp(self, l, xin, xout, w1b=None):
        S = self.S
        NB = self.NB
        with contextlib.ExitStack() as es:
            pre = w1b is not None
            if not pre:
                w1b = S.sb("w1b", [128, KC, 4096], BF16, es)
            w2b = S.sb("w2b", [128, 32, D], BF16, es)
            xs = S.sb("m_xs", [128, KC, TB], F32, es)
            hb = S.sb("m_hb", [128, KC, TB], BF16, es)
            ab = S.sb("m_ab", [128, 32, TB], BF16, es)
            rst = S.sb("m_rst", [128, TB], F32, es)
            tmps = Rot([S.sb("m_tmp%d" % i, [128, TB], F32, es) for i in range(2)])
            xos = Rot([S.sb("m_xo%d" % i, [128, TB], F32, es) for i in range(3)])
            PS8 = Rot(self.psb + self.psh)
            if not pre:
                self.load_wcast(w1b, self.w["mlp_w1"][l], KC, 4096)
            self.load_wcast(w2b, self.w["mlp_w2"][l], 32, D, grp=4)
            xinv = xin.ap.rearrange("(kc p) t -> p kc t", p=128)
            xoutv = xout.ap.rearrange("(kc p) t -> p kc t", p=128)

            def load_x(b):
                S.dma("sp", xs[:], self.xview(xin, b), reads=[xin.blk[b]], writes=[xs])

            def norm_next(b):
                self.norm(xs, self.g_mlp[l], hb, (hb, rst))

            load_x(0)
            norm_next(0)
            for b in range(NB):
                if b + 1 < NB:
                    load_x(b + 1)
                for fc in range(32):
                    P = PS8.next()
                    for kc in range(KC):
                        self.mm(P, P[:], w1b[:, kc, fc * 128:(fc + 1) * 128], hb[:, kc, :], kc == 0, kc == KC - 1, [w1b, hb])
                    tmp = tmps.next()
                    self.act(tmp[:], P[:], AF.Relu, [P], [tmp])
                    self.tt("dve" if fc % 2 == 0 else "pool", ab[:, fc, :], tmp[:], tmp[:], ALU.mult, [tmp], [ab])
                xo_pref = {}
                for oc in range(KC):
                    for o2 in (oc, oc + 1, oc + 2):
                        if o2 < KC and o2 not in xo_pref:
                            xo = xos.next()
                            S.dma("sp", xo[:], xinv[:, o2, b * TB:(b + 1) * TB], reads=[xin.blk[b]], writes=[xo])
                            xo_pref[o2] = xo
                    P = PS8.next()
                    for fc in range(32):
                        self.mm(P, P[:], w2b[:, fc, oc * 128:(oc + 1) * 128], ab[:, fc, :], fc == 0, fc == 31, [w2b, ab])
                    xo = xo_pref[oc]
                    self.tt("dve", xo[:], P[:], xo[:], ALU.add, [P, xo], [xo])
                    S.dma("pool", xoutv[:, oc, b * TB:(b + 1) * TB], xo[:], reads=[xo], writes=[xout.blk[b]], par=True)
                    if oc == 1 and b + 1 < NB:
                        norm_next(b + 1)
            S.barrier()

    def phase_outproj(self, wname, widx, nkc, yT, xin, xout, extra=()):
        S = self.S
        with contextlib.ExitStack() as es:
            wob = S.sb("wob", [128, nkc, D], BF16, es)
            xss = Rot([S.sb("o_xs%d" % i, [128, KC, TB], F32, es) for i in range(2)])
            ybs = Rot([S.sb("o_yb%d" % i, [128, nkc, TB], BF16, es) for i in range(2)])
            PS8 = Rot(self.psb + self.psh)
            self.load_wcast(wob, self.w[wname][widx], nkc, D, grp=4)
            yv = yT.ap.rearrange("(kc p) t -> p kc t", p=128)
            nxt = {}

            def loads(b):
                yb = ybs.next()
                xs = xss.next()
                S.dma("sp", yb[:], yv[:, 0:nkc, b * TB:(b + 1) * TB], reads=[yT.blk[b]], writes=[yb])
                S.dma("sp", xs[:], self.xview(xin, b), reads=[xin.blk[b]], writes=[xs])
                nxt[b] = (yb, xs)

            loads(0)
            for b in range(self.NB):
                if b + 1 < self.NB:
                    loads(b + 1)
                yb, xs = nxt.pop(b)
                for oc in range(KC):
                    P = PS8.next()
                    for kc in range(nkc):
                        self.mm(P, P[:], wob[:, kc, oc * 128:(oc + 1) * 128], yb[:, kc, :], kc == 0, kc == nkc - 1, [wob, yb])
                    self.tt("dve", xs[:, oc, :], P[:], xs[:, oc, :], ALU.add, [P, xs], [xs])
                S.dma("pool", self.xview(xout, b), xs[:], reads=[xs], writes=[xout.blk[b]])
                if b < len(extra):
                    extra[b]()
            for t in extra[self.NB:]:
                t()
            S.barrier()

    def phase_odd_a(self, o, l, xin, yT, qT, kT, vtm):
        S = self.S
        T = self.T
        with contextlib.ExitStack() as es:
            wib = S.sb("oa_wi", [128, KC, 3584], BF16, es)
            wab = S.sb("oa_wa", [128, 8, 128], BF16, es)
            wxb = S.sb("oa_wx", [128, 8, 128], BF16, es)
            xs = S.sb("oa_xs", [128, KC, TB], F32, es)
            hbs = [S.sb("oa_hb%d" % i, [128, KC, TB], BF16, es) for i in range(2)]
            sq = S.sb("oa_sq", [128, KC, TB], BF16, es)
            rst = S.sb("oa_rst", [128, TB], F32, es)
            xr = [[S.sb("oa_xr%d_%d" % (c, i), [128, TB + 3], F32, es) for i in range(2)] for c in range(8)]
            hst = [S.sb("oa_hst%d" % i, [128, 1], F32, es) for i in range(8)]
            f32t = Rot([S.sb("oa_f%d" % i, [128, TB], F32, es) for i in range(16)])
            b16t = Rot([S.sb("oa_b%d" % i, [128, TB], BF16, es) for i in range(10)])
            PS8 = Rot(self.psb + self.psh)
            vb = Rot([S.sb("oa_vb%d" % i, [128, 512], BF16, es) for i in range(2)])
            self.load_wcast(wib, self.w["od_w_in"][o], KC, 3584)
            S.dma("pool", wab[:], self.w["lru_wa"][o].rearrange("k i j -> i k j"), writes=[wab])
            S.dma("pool", wxb[:], self.w["lru_wx"][o].rearrange("k i j -> i k j"), writes=[wxb])
            for c in range(8):
                S.op("dve", lambda e, c=c: e.memset(hst[c][:], 0.0), writes=[hst[c]])
                S.op("pool", lambda e, c=c: e.memset(xr[c][1][:, TB:TB + 3], 0.0), writes=[xr[c][1]])
            cw = self.lru_cw[o]
            cb = self.lru_cb[o]
            ba = self.lru_ba[o]
            bx = self.lru_bx[o]
            cl = self.lru_cl[o]
            cst = self.cst
            def prep(b):
                S.dma("sp", xs[:], self.xview(xin, b), reads=[xin.blk[b]], writes=[xs])
                self.norm(xs, self.g_mix[l], hbs[b % 2], (sq, rst))

            prep(0)
            for b in range(self.NB):
                hb = hbs[b % 2]
                def lru_chunk(c):
                    cur = xr[c][b % 2]
                    prv = xr[c][(b + 1) % 2]
                    Pg = PS8.next()
                    for kc in range(KC):
                        self.mm(Pg, Pg[:], wib[:, kc, c * 128:(c + 1) * 128], hb[:, kc, :], kc == 0, kc == KC - 1, [wib, hb])
                    gl = f32t.next()
                    self.act(gl[:], Pg[:], AF.Gelu_apprx_tanh, [Pg], [gl])
                    Px = PS8.next()
                    for kc in range(KC):
                        self.mm(Px, Px[:], wib[:, kc, 1024 + c * 128:1024 + (c + 1) * 128], hb[:, kc, :], kc == 0, kc == KC - 1, [wib, hb])
                    self.copy("act", cur[:, 3:TB + 3], Px[:], [Px], [cur])
                    self.copy("pool", cur[:, 0:3], prv[:, TB:TB + 3], [prv], [cur])
                    xcv = f32t.next()
                    self.ts("dve", xcv[:], cur[:, 3:TB + 3], cw[:, c, 3:4], ALU.mult, [cur, cst], [xcv], s2=cb[:, c:c + 1], op1=ALU.add)
                    for k in range(3):
                        self.stt(xcv[:], cur[:, k:k + TB], cw[:, c, k:k + 1], xcv[:], ALU.mult, ALU.add, [cur, xcv, cst], [xcv])
                    xcb = b16t.next()
                    self.copy("pool", xcb[:], xcv[:], [xcv], [xcb])
                    Pr = PS8.next()
                    self.mm(Pr, Pr[:], wab[:, c, :], xcb[:], True, True, [wab, xcb])
                    Pi = PS8.next()
                    self.mm(Pi, Pi[:], wxb[:, c, :], xcb[:], True, True, [wxb, xcb])
                    rr = f32t.next()
                    self.act(rr[:], Pr[:], AF.Sigmoid, [Pr, cst], [rr], bias=ba[:, c:c + 1])
                    ii = f32t.next()
                    self.act(ii[:], Pi[:], AF.Sigmoid, [Pi, cst], [ii], bias=bx[:, c:c + 1])
                    aa = f32t.next()
                    self.act(aa[:], rr[:], AF.Exp, [rr, cst], [aa], scale=cl[:, c:c + 1])
                    a2 = f32t.next()
                    self.tt("pool", a2[:], aa[:], aa[:], ALU.mult, [aa], [a2])
                    self.act(a2[:], a2[:], AF.Sqrt, [a2], [a2], scale=-1.0, bias=self.oneb[:])
                    self.tt("pool", ii[:], ii[:], xcv[:], ALU.mult, [ii, xcv], [ii])
                    self.tt("dve", ii[:], ii[:], a2[:], ALU.mult, [ii, a2], [ii])
                    hh = f32t.next()
                    S.op("dve", lambda e, hh=hh, aa=aa, ii=ii, c=c: e.tensor_tensor_scan(out=hh[:], data0=aa[:], data1=ii[:], initial=hst[c][:, 0:1], op0=ALU.mult, op1=ALU.add),
                         reads=[aa, ii, hst[c]], writes=[hh])
                    self.copy("pool", hst[c][:, 0:1], hh[:, TB - 1:TB], [hh], [hst[c]])
                    yc = b16t.next()
                    self.tt("dve", yc[:], hh[:], gl[:], ALU.mult, [hh, gl], [yc])
                    S.dma("pool", yT.ap[c * 128:(c + 1) * 128, b * TB:(b + 1) * TB], yc[:], reads=[yc], writes=[yT.blk[b]], par=True)
                for c in range(0, 8, 2):
                    S.interleave([S.record(lambda c=c: lru_chunk(c)), S.record(lambda c=c: lru_chunk(c + 1))])
                    if c == 2 and b + 1 < self.NB:
                        prep(b + 1)
                def qk_tile(qi):
                    isq = qi < 4
                    col0 = 2048 + qi * 128
                    Pq = PS8.next()
                    for kc in range(KC):
                        self.mm(Pq, Pq[:], wib[:, kc, col0:col0 + 128], hb[:, kc, :], kc == 0, kc == KC - 1, [wib, hb])
                    s2 = b16t.next()
                    self.act(s2[:], Pq[:], AF.Square, [Pq], [s2])
                    Ps = PS8.next()
                    self.mm(Ps, Ps[:], self.bones[:], s2[:], True, True, [self.bones, s2])
                    r2 = f32t.next()
                    self.act(r2[:], Ps[:], AF.Ln, [Ps], [r2], scale=1.0 / 64, bias=self.epsb[:])
                    self.act(r2[:], r2[:], AF.Exp, [r2], [r2], scale=-0.5)
                    qo = b16t.next()
                    gn = (self.sb_qn if isq else self.sb_kn)[o]
                    self.stt(qo[:], Pq[:], gn[:, 0:1], r2[:], ALU.mult, ALU.mult, [Pq, r2, cst], [qo])
                    dst = qT if isq else kT
                    r0 = (qi % 4) * 128
                    S.dma("pool", dst.ap[r0:r0 + 128, b * TB:(b + 1) * TB], qo[:], reads=[qo], writes=[dst.blk[b]], par=True)
                for q0 in range(0, 8, 4):
                    S.interleave([S.record(lambda qi=q0 + j: qk_tile(qi)) for j in range(4)])
                for tt_ in range(4):
                    Pv = self.PS.next()
                    for kc in range(KC):
                        self.mm(Pv, Pv[:], hb[:, kc, tt_ * 128:(tt_ + 1) * 128], wib[:, kc, 3072:3584], kc == 0, kc == KC - 1, [wib, hb])
                    vv = vb.next()
                    self.copy("act", vv[:], Pv[:], [Pv], [vv])
                    t0 = b * TB + tt_ * 128
                    S.dma("pool", vtm.ap[t0:t0 + 128, 0:512], vv[:], reads=[vv], writes=[vtm.blk[b]], par=True)
            S.barrier()

    def phase_sb(self, qT, kT, vtm, yT):
        S = self.S
        T = self.T
        NKB = T // 128
        scale = 64 ** -0.5
        with contextlib.ExitStack() as es:
            def rot(name, shape, dt, n):
                return Rot([S.sb("%s%d" % (name, i), shape, dt, es) for i in range(n)])
            qh = [S.sb("sb_q%d" % i, [128, T], BF16, es) for i in range(2)]
            kh = [S.sb("sb_k%d" % i, [128, T], BF16, es) for i in range(2)]
            vh = [S.sb("sb_v%d" % i, [128, NKB, 128], BF16, es) for i in range(2)]
            for i in range(2):
                S.op("pool", lambda e, i=i: e.memset(qh[i][64:128, :], 0.0), writes=[qh[i]])
                S.op("pool", lambda e, i=i: e.memset(kh[i][64:128, :], 0.0), writes=[kh[i]])
                S.op("pool", lambda e, i=i: e.memset(vh[i][:, :, 64:128], 0.0), writes=[vh[i]])
            eeR = rot("sb_ee", [128, TB], F32, 2); spR = rot("sb_sp", [128, TB], F32, 7)
            l1R = rot("sb_l1", [128, TB], BF16, 7)
            argR = rot("sb_arg", [128, TB], F32, 3); wwR = rot("sb_ww", [128, TB], BF16, 4)
            ot = rot("sb_o", [64, TB], BF16, 2)
            PzR = Rot([self.psb[0], self.psb[1], self.psb[2], self.psb[3]]); PsfR = Rot([self.psh[0], self.psh[1]])
            PcR = Rot([self.psh[2]]); PaccR = Rot([self.psh[3]])

            def load_head(h):
                q, k, v = qh[h % 2], kh[h % 2], vh[h % 2]
                S.dma("sp", q[0:64, :], qT.ap[h * 64:(h + 1) * 64, :], reads=qT.all, writes=[q])
                S.dma("sp", k[0:64, :], kT.ap[h * 64:(h + 1) * 64, :], reads=kT.all, writes=[k])
                S.dma("sp", v[:, :, 0:64], vtm.ap[:, h * 64:(h + 1) * 64].rearrange("(n p) d -> p n d", p=128), reads=vtm.all, writes=[v])

            units = []
            for h in range(8):
                for Q in range(self.NB):
                    top = 4 * Q + 3
                    pc0 = None
                    for kb in range(top, -1, -1):
                        loc = kb - 4 * Q
                        c0 = max(loc, 0) * 128
                        units.append(dict(h=h, Q=Q, kb=kb, first=(kb == top), last=(kb == 0), newhead=(Q == 0 and kb == top),
                                          c0=c0, pc0=pc0, diag=(loc >= 0)))
                        pc0 = c0
            st = {}

            def pe_z(u):
                h, Q, kb, c0 = u["h"], u["Q"], u["kb"], u["c0"]
                if u["newhead"] and h == 0:
                    load_head(0)
                q, k = qh[h % 2], kh[h % 2]
                Pz = PzR.next()
                self.mm(Pz, Pz[:, c0:TB], k[:, kb * 128:(kb + 1) * 128], q[:, Q * TB + c0:(Q + 1) * TB], True, True, [k, q])
                u["Pz"] = Pz

            def act_sp(u):
                c0, Pz = u["c0"], u["Pz"]
                ee = eeR.next()
                self.act(ee[:, c0:TB], Pz[:, c0:TB], AF.Exp, [Pz], [ee], scale=-scale)
                sp_ = spR.next()
                self.act(sp_[:, c0:TB], ee[:, c0:TB], AF.Ln, [ee], [sp_], bias=self.oneb[:])
                u["sp"] = sp_

            def dve_l1(u):
                c0 = u["c0"]
                Pz, sp_ = u["Pz"], u["sp"]
                l1 = l1R.next()
                self.stt(l1[:, c0:TB], Pz[:, c0:TB], -scale, sp_[:, c0:TB], ALU.mult, ALU.subtract, [Pz, sp_], [l1])
                if u["diag"]:
                    self.tt("pool", l1[:, c0:c0 + 128], l1[:, c0:c0 + 128], self.mask_lt[:], ALU.mult, [l1, self.cst], [l1])
                    if c0 > 0:
                        S.op("pool", lambda e, l1=l1, c0=c0: e.memset(l1[:, 0:c0], 0.0), writes=[l1])
                u["l1"] = l1

            def pe_suffix(u):
                c0 = u["c0"]
                Psf = PsfR.next()
                self.mm(Psf, Psf[:, c0:TB], self.umat[:], u["l1"][:, c0:TB], True, True, [self.umat, u["l1"]])
                u["Psf"] = Psf

            def dve_arg(u):
                c0, pc0 = u["c0"], u["pc0"]
                sp_, Psf = u["sp"], u["Psf"]
                if u["first"]:
                    st["Pc_r"] = PcR.next()
                Pc = u["Pc"] = st["Pc_r"]
                arg = argR.next()
                self.tt("dve", arg[:, c0:TB], Psf[:, c0:TB], sp_[:, c0:TB], ALU.subtract, [Psf, sp_], [arg])
                if not u["first"]:
                    self.tt("dve", arg[:, c0:TB], Pc[:, c0:TB], arg[:, c0:TB], ALU.add, [Pc, arg], [arg])
                u["arg"] = arg

            def pe_colsum(u):
                c0 = u["c0"]
                if not u["last"]:
                    self.mm(u["Pc"], u["Pc"][:, 0:TB], self.ones[:], u["l1"][:, 0:TB], u["first"], False, [self.ones, u["l1"]])

            def act_w(u):
                c0 = u["c0"]
                ww = wwR.next()
                self.act(ww[:, c0:TB], u["arg"][:, c0:TB], AF.Exp, [u["arg"]], [ww])
                if u["diag"]:
                    self.tt("pool", ww[:, c0:c0 + 128], ww[:, c0:c0 + 128], self.mask_lt[:], ALU.mult, [ww, self.cst], [ww])
                    if c0 > 0:
                        S.op("pool", lambda e, ww=ww, c0=c0: e.memset(ww[:, 0:c0], 0.0), writes=[ww])
                u["ww"] = ww

            def pe_wv(u):
                h, Q, kb, c0 = u["h"], u["Q"], u["kb"], u["c0"]
                v = vh[h % 2]
                if u["newhead"] and h + 1 < 8:
                    load_head(h + 1)
                if u["first"]:
                    st["Pacc"] = PaccR.next()
                Pacc = st["Pacc"]
                self.mm(Pacc, Pacc[:, 0:TB], v[:, kb, :], u["ww"][:, 0:TB], u["first"], u["last"], [v, u["ww"]])
                if u["last"]:
                    oo = ot.next()
                    self.copy("act", oo[:], Pacc[0:64, 0:TB], [Pacc], [oo])
                    S.dma("pool", yT.ap[1024 + h * 64:1024 + (h + 1) * 64, Q * TB:(Q + 1) * TB], oo[:], reads=[oo], writes=[yT.blk[Q]], par=True)
                u.clear()

            n = len(units)
            import os
            fmap = dict(pe_z=pe_z, pe_suffix=pe_suffix, pe_wv=pe_wv, pe_colsum=pe_colsum, act_sp=act_sp, act_w=act_w, dve_arg=dve_arg, dve_l1=dve_l1)
            if os.environ.get("SB_SCHED"):
                sched = tuple((fmap[x.split(":")[0]], int(x.split(":")[1])) for x in os.environ["SB_SCHED"].split(","))
            else:
                sched = ((pe_z, 0), (pe_wv, 7), (pe_suffix, 4), (pe_colsum, 6), (act_sp, 1), (act_w, 6), (dve_arg, 5), (dve_l1, 2))
            for i in range(n + 8):
                for fn, off in sched:
                    j = i - off
                    if 0 <= j < n:
                        fn(units[j])
            S.barrier()

    def transpose_to(self, P, out_cols, src_buf, src_ap):
        pv = P[:].bitcast(BF16)
        self.S.op("pe", lambda e: e.transpose(out=pv[:, out_cols:out_cols + 128], in_=src_ap, identity=self.ident[:]),
                  reads=[src_buf, self.ident], writes=[P])

    def phase_even_a(self, ei, l, xin, sc):
        S = self.S
        T = self.T
        w = self.w
        with contextlib.ExitStack() as es:
            wib = S.sb("ea_wi", [128, KC, 5648], BF16, es)
            xs = S.sb("ea_xs", [128, KC, TB], F32, es)
            hbs = [S.sb("ea_hb%d" % i, [128, KC, TB], BF16, es) for i in range(2)]
            sq = S.sb("ea_sq", [128, KC, TB], BF16, es)
            rst = S.sb("ea_rst", [128, TB], F32, es)
            cosb = S.sb("ea_cos", [128, TB], F32, es)
            sinb = S.sb("ea_sin", [128, TB], F32, es)
            halo = [S.sb("ea_halo%d" % i, [128, 3], F32, es) for i in range(12)]
            PS8 = Rot(self.psb + self.psh)
            work = Rot([S.sb("ea_wk%d" % i, [128, TB + 3], F32, es) for i in range(4)])
            f32t = Rot([S.sb("ea_f%d" % i, [128, TB], F32, es) for i in range(12)])
            b16t = Rot([S.sb("ea_b%d" % i, [128, TB], BF16, es) for i in range(6)])
            xct = [S.sb("ea_xc%d" % i, [128, TB], BF16, es) for i in range(12)]
            tkb = Rot([S.sb("ea_tk%d" % i, [128, 1024], BF16, es) for i in range(3)])
            dts = Rot([S.sb("ea_dt%d" % i, [128, 32], F32, es) for i in range(2)])
            self.load_wcast(wib, w["ev_w_in"][ei], KC, 5648)
            for c in range(12):
                S.op("dve", lambda e, c=c: e.memset(halo[c][:], 0.0), writes=[halo[c]])
            cst = self.cst
            cw = self.ssd_cw[ei]; cb = self.ssd_cb[ei]
            qn = self.ret_qn[ei]; kn = self.ret_kn[ei]
            def prep(b):
                bs_ = slice(b * TB, (b + 1) * TB)
                S.dma("sp", xs[:], self.xview(xin, b), reads=[xin.blk[b]], writes=[xs])
                S.dma("sp", cosb[:], w["c_cos"][:, bs_], writes=[cosb])
                S.dma("sp", sinb[:], w["c_sin"][:, bs_], writes=[sinb])
                self.norm(xs, self.g_mix[l], hbs[b % 2], (sq, rst))

            prep(0)
            for b in range(self.NB):
                bs = slice(b * TB, (b + 1) * TB)
                hb = hbs[b % 2]
                def qk_tile(qi):
                    isq = qi < 4
                    col0 = qi * 128
                    Pq = PS8.next()
                    for kc in range(KC):
                        self.mm(Pq, Pq[:], wib[:, kc, col0:col0 + 128], hb[:, kc, :], kc == 0, kc == KC - 1, [wib, hb])
                    qraw = f32t.next()
                    self.copy("act", qraw[:], Pq[:], [Pq], [qraw])
                    s2 = b16t.next()
                    self.act(s2[:], Pq[:], AF.Square, [Pq], [s2])
                    Ps = PS8.next()
                    self.mm(Ps, Ps[:], self.ones[:], s2[:], True, True, [self.ones, s2])
                    r2 = f32t.next()
                    self.act(r2[:], Ps[:], AF.Ln, [Ps], [r2], scale=1.0 / 128, bias=self.epsb[:])
                    self.act(r2[:], r2[:], AF.Exp, [r2], [r2], scale=-0.5)
                    Pw = PS8.next()
                    self.mm(Pw, Pw[:], self.swapm[:], qraw[:], True, True, [self.swapm, qraw])
                    gn = qn if isq else kn
                    t1 = f32t.next()
                    self.stt(t1[:], qraw[:], gn[:, 0:1], cosb[:], ALU.mult, ALU.mult, [qraw, cosb, cst], [t1])
                    t2 = f32t.next()
                    self.stt(t2[:], Pw[:], gn[:, 1:2], sinb[:], ALU.mult, ALU.mult, [Pw, sinb, cst], [t2])
                    self.tt("pool", t1[:], t1[:], t2[:], ALU.add, [t1, t2], [t1])
                    qo = b16t.next()
                    self.stt(qo[:], t1[:], 1.0 if isq else 128 ** -0.5, r2[:], ALU.mult, ALU.mult, [t1, r2], [qo])
                    dst = sc["qr"] if isq else sc["kr"]
                    r0 = (qi % 4) * 128
                    S.dma("pool", dst.ap[r0:r0 + 128, bs], qo[:], reads=[qo], writes=[dst.blk[b]], par=True)
                for q0 in range(0, 8, 2):
                    S.interleave([S.record(lambda qi=q0 + j: qk_tile(qi)) for j in range(2)])
                if b + 1 < self.NB:
                    prep(b + 1)
                for (c0, dstn, fn) in ((1024, "v", None), (2048, "g", AF.Silu), (3072, "z", AF.Silu)):
                    for tt_ in range(4):
                        tk = tkb.next()
                        for hf in range(2):
                            Pv = PS8.next()
                            for kc in range(KC):
                                self.mm(Pv, Pv[:], hb[:, kc, tt_ * 128:(tt_ + 1) * 128], wib[:, kc, c0 + hf * 512:c0 + (hf + 1) * 512],
                                        kc == 0, kc == KC - 1, [wib, hb])
                            if fn is None:
                                self.copy("act", tk[:, hf * 512:(hf + 1) * 512], Pv[:], [Pv], [tk])
                            else:
                                self.act(tk[:, hf * 512:(hf + 1) * 512], Pv[:], fn, [Pv], [tk])
                        t0 = b * TB + tt_ * 128
                        S.dma("pool", sc[dstn].ap[t0:t0 + 128, :], tk[:], reads=[tk], writes=[sc[dstn].blk[b]], par=True)
                for tt_ in range(4):
                    Pd = PS8.next()
                    for kc in range(KC):
                        self.mm(Pd, Pd[:, 0:16], hb[:, kc, tt_ * 128:(tt_ + 1) * 128], wib[:, kc, 5632:5648], kc == 0, kc == KC - 1, [wib, hb])
                    dd = dts.next()
                    self.tt("dve", dd[:, 0:16], Pd[:, 0:16], self.ssd_dtb[ei], ALU.add, [Pd, cst], [dd])
                    self.act(dd[:, 0:16], dd[:, 0:16], AF.Exp, [dd], [dd])
                    self.act(dd[:, 0:16], dd[:, 0:16], AF.Ln, [dd], [dd], bias=self.oneb[:])
                    self.tt("dve", dd[:, 16:32], dd[:, 0:16], self.ssd_A[ei], ALU.mult, [dd, cst], [dd])
                    t0 = b * TB + tt_ * 128
                    S.dma("pool", sc["dtA"].ap[t0:t0 + 128, :], dd[:], reads=[dd], writes=[sc["dtA"].blk[b]], par=True)
                def xbc_chunk(c):
                    Px = PS8.next()
                    col0 = 4096 + c * 128
                    for kc in range(KC):
                        self.mm(Px, Px[:], wib[:, kc, col0:col0 + 128], hb[:, kc, :], kc == 0, kc == KC - 1, [wib, hb])
                    wk = work.next()
                    self.copy("act", wk[:, 3:TB + 3], Px[:], [Px], [wk])
                    self.copy("pool", wk[:, 0:3], halo[c][:], [halo[c]], [wk])
                    self.copy("pool", halo[c][:], wk[:, TB:TB + 3], [wk], [halo[c]])
                    xcv = f32t.next()
                    self.ts("dve", xcv[:], wk[:, 3:TB + 3], cw[:, c, 3:4], ALU.mult, [wk, cst], [xcv], s2=cb[:, c:c + 1], op1=ALU.add)
                    for k in range(3):
                        self.stt(xcv[:], wk[:, k:k + TB], cw[:, c, k:k + 1], xcv[:], ALU.mult, ALU.add, [wk, xcv, cst], [xcv])
                    self.act(xct[c][:], xcv[:], AF.Silu, [xcv], [xct[c]])
                    if c >= 8:
                        dstn = "BT" if c < 10 else "CT"
                        r0 = (c % 2) * 128
                        S.dma("pool", sc[dstn].ap[r0:r0 + 128, bs], xct[c][:], reads=[xct[c]], writes=[sc[dstn].blk[b]], par=True)
                for c0_ in range(0, 12, 4):
                    S.interleave([S.record(lambda c=c0_ + j: xbc_chunk(c)) for j in range(4)])
                for tt_ in range(4):
                    tk = tkb.next()
                    Pt = PS8.next()
                    for c in range(8):
                        self.transpose_to(Pt, c * 128, xct[c], xct[c][:, tt_ * 128:(tt_ + 1) * 128])
                    self.copy("act", tk[:], Pt[:].bitcast(BF16), [Pt], [tk])
                    t0 = b * TB + tt_ * 128
                    S.dma("pool", sc["xs"].ap[t0:t0 + 128, :], tk[:], reads=[tk], writes=[sc["xs"].blk[b]], par=True)
                    tk2 = tkb.next()
                    Pt2 = PS8.next()
                    for c in range(2):
                        self.transpose_to(Pt2, c * 128, xct[8 + c], xct[8 + c][:, tt_ * 128:(tt_ + 1) * 128])
                    self.copy("act", tk2[:, 0:256], Pt2[:].bitcast(BF16)[:, 0:256], [Pt2], [tk2])
                    S.dma("pool", sc["Btm"].ap[t0:t0 + 128, :], tk2[:, 0:256], reads=[tk2], writes=[sc["Btm"].blk[b]], par=True)
            S.barrier()

    def phase_even_b(self, ei, sc, yT):
        S = self.S
        T = self.T
        NCH = T // 128
        cst = self.cst
        gam = [1.0 - 2.0 ** (-5.0 - h) for h in range(4)]
        with contextlib.ExitStack() as es:
            def rot(name, shape, dt, n=2):
                return Rot([S.sb("%s%d" % (name, i), shape, dt, es) for i in range(n)])
            qrb = rot("eb_q", [128, 4, 128], BF16, 3); krb = rot("eb_k", [128, 4, 128], BF16, 3)
            vb = rot("eb_v", [128, 1024], BF16, 3); gb = rot("eb_g", [128, 1024], BF16, 3); zb = rot("eb_z", [128, 1024], BF16, 3)
            xsb = rot("eb_xs", [128, 16, 64], BF16, 3); btmb = rot("eb_bt", [128, 2, 128], BF16, 3)
            BTb = rot("eb_BT", [128, 2, 128], BF16, 3); CTb = rot("eb_CT", [128, 2, 128], BF16, 3)
            dtb = rot("eb_dt", [128, 32], F32, 3)
            Sst = S.sb("eb_S", [128, 4, 256], F32, es); Sbf = S.sb("eb_Sbf", [128, 4, 256], BF16, es)
            STs = S.sb("eb_ST", [128, 16, 64], F32, es); STbf = S.sb("eb_STbf", [128, 16, 64], BF16, es)
            smb = rot("eb_sm", [128, 4, 128], BF16); ktmb = rot("eb_ktm", [128, 4, 128], BF16); q2b = rot("eb_q2", [128, 4, 128], BF16)
            Dm = S.sb("eb_D", [128, 16, 128], F32, es)
            acol = rot("eb_acol", [128, 48], F32)
            segb = rot("eb_seg", [128, 4, 128], F32, 3); ltb = rot("eb_lt", [128, 4, 128], F32, 2); erb = rot("eb_er", [128, 4, 128], F32, 2)
            cbm = rot("eb_cbm", [128, 2, 128], F32)
            MTr = rot("eb_MT", [128, 16, 128], BF16, 2); CsTr = rot("eb_CsT", [128, 16, 128], BF16, 2)
            xdtb = rot("eb_xdt", [128, 16, 64], BF16); xdt2b = rot("eb_xdt2", [128, 16, 64], BF16)
            yf = rot("eb_yf", [128, 1024], F32, 2)
            yab = rot("eb_ya", [128, 1024], BF16, 2); ybb = rot("eb_yb", [128, 1024], BF16, 2)
            stat = rot("eb_stat", [128, 4, 6], F32); mv = rot("eb_mv", [128, 4, 2], F32); rs4 = rot("eb_rs4", [128, 8], F32)
            yTo = rot("eb_yTo", [128, 16, 128], BF16, 2)
            junk = S.sb("eb_junk", [128, 512], BF16, es)
            PSr = Rot(self.psb)
            PSs = Rot(self.psh)
            S.op("dve", lambda e: e.memset(Sst[:], 0.0), writes=[Sst])
            S.op("pool", lambda e: e.memset(Sbf[:], 0.0), writes=[Sbf])
            S.op("dve", lambda e: e.memset(STs[:], 0.0), writes=[STs])
            S.op("pool", lambda e: e.memset(STbf[:], 0.0), writes=[STbf])
            gnr_b = S.sb("eb_gnr", [128, 1024], F32, es); nrr_b = S.sb("eb_nrr", [128, 1024], F32, es)
            S.dma("sp", gnr_b[:], self.w["ret_gn"][ei].partition_broadcast(128), writes=[gnr_b])
            S.dma("sp", nrr_b[:], self.w["ssd_norm"][ei].partition_broadcast(128), writes=[nrr_b])
            gnrow = gnr_b[:]; nrow = nrr_b[:]; dsk = self.ssd_D[ei]
            PSrm = Rot([self.psb[0], self.psb[1]]); PSsm = Rot([self.psb[2], self.psb[3]])
            PSrp = Rot([self.psh[0], self.psh[1]]); PSsp = Rot([self.psh[2], self.psh[3]])
            PSo = PSrp

            def out_part(ya, yb, b, ts_):
                yo = yTo.next()
                for half, src in ((0, ya), (1, yb)):
                    Pt = PSo.next()
                    for cc in range(8):
                        self.transpose_to(Pt, cc * 128, src, src[:, cc * 128:(cc + 1) * 128])
                    self.copy("act", yo[:, half * 8:(half + 1) * 8, :].rearrange("p c i -> p (c i)"), Pt[:].bitcast(BF16), [Pt], [yo])
                S.dma("pool", yT.ap[:, ts_].rearrange("(c p) t -> p c t", p=128), yo[:], reads=[yo], writes=[yT.blk[b]], par=True)

            def loads(c):
                b = c // 4
                ts_ = slice(c * 128, (c + 1) * 128)
                L = dict(b=b, ts=ts_, qr=qrb.next(), kr=krb.next(), v=vb.next(), g=gb.next(), z=zb.next(), xs=xsb.next(),
                         btm=btmb.next(), BT=BTb.next(), CT=CTb.next(), dt=dtb.next())
                S.dma("sp", L["qr"][:], sc["qr"].ap[:, ts_].rearrange("(h d) t -> d h t", d=128), reads=[sc["qr"].blk[b]], writes=[L["qr"]])
                S.dma("sp", L["kr"][:], sc["kr"].ap[:, ts_].rearrange("(h d) t -> d h t", d=128), reads=[sc["kr"].blk[b]], writes=[L["kr"]])
                S.dma("sp", L["v"][:], sc["v"].ap[ts_, :], reads=[sc["v"].blk[b]], writes=[L["v"]])
                S.dma("sp", L["g"][:], sc["g"].ap[ts_, :], reads=[sc["g"].blk[b]], writes=[L["g"]])
                S.dma("sp", L["z"][:], sc["z"].ap[ts_, :], reads=[sc["z"].blk[b]], writes=[L["z"]])
                S.dma("sp", L["xs"][:], sc["xs"].ap[ts_, :].rearrange("t (h p) -> t h p", p=64), reads=[sc["xs"].blk[b]], writes=[L["xs"]])
                S.dma("sp", L["btm"][:], sc["Btm"].ap[ts_, :].rearrange("t (g s) -> t g s", s=128), reads=[sc["Btm"].blk[b]], writes=[L["btm"]])
                S.dma("sp", L["BT"][:], sc["BT"].ap[:, ts_].rearrange("(g s) t -> s g t", s=128), reads=[sc["BT"].blk[b]], writes=[L["BT"]])
                S.dma("sp", L["CT"][:], sc["CT"].ap[:, ts_].rearrange("(g s) t -> s g t", s=128), reads=[sc["CT"].blk[b]], writes=[L["CT"]])
                S.dma("sp", L["dt"][:], sc["dtA"].ap[ts_, :], reads=[sc["dtA"].blk[b]], writes=[L["dt"]])
                return L

            def ret_pre(L):
                qr, kr = L["qr"], L["kr"]
                Psc = PSrp.next()
                for h in range(4):
                    self.mm(Psc, Psc[:, h * 128:(h + 1) * 128], kr[:, h, :], qr[:, h, :], True, True, [kr, qr])
                sm = smb.next()
                self.tt("dve", sm[:].rearrange("p h i -> p (h i)"), Psc[:], self.ret_decay[:], ALU.mult, [Psc, cst], [sm])
                Pkt = PSrp.next()
                for h in range(4):
                    self.transpose_to(Pkt, h * 128, kr, kr[:, h, :])
                ktm = ktmb.next()
                self.tt("dve", ktm[:], Pkt[:].bitcast(BF16)[:, 0:512].rearrange("p (h d) -> p h d", d=128),
                        self.ret_kdec[:].unsqueeze(2).broadcast_to([128, 4, 128]), ALU.mult, [Pkt, cst], [ktm])
                q2 = q2b.next()
                self.tt("pool", q2[:].rearrange("p h i -> p (h i)"), qr[:].rearrange("p h i -> p (h i)"), self.ret_qdec[:], ALU.mult, [qr, cst], [q2])
                L.update(sm=sm, ktm=ktm, q2=q2)

            def ret_main(L):
                v, g, sm, ktm, q2 = L["v"], L["g"], L["sm"], L["ktm"], L["q2"]
                POw = (PSrm.next(), PSrm.next())
                for h in range(4):
                    PO = POw[h // 2]
                    hc = slice((h % 2) * 256, (h % 2 + 1) * 256)
                    self.mm(PO, PO[:, hc], sm[:, h, :], v[:, h * 256:(h + 1) * 256], True, False, [sm, v])
                    self.mm(PO, PO[:, hc], q2[:, h, :], Sbf[:, h, :], False, True, [q2, Sbf])
                st = stat.next(); m2 = mv.next(); r4 = rs4.next()
                for h in range(4):
                    PO = POw[h // 2]
                    S.op("dve", lambda e, h=h, st=st, PO=PO: e.bn_stats(out=st[:, h, :], in_=PO[:, (h % 2) * 256:(h % 2 + 1) * 256]), reads=[PO], writes=[st])
                for h in range(4):
                    S.op("dve", lambda e, h=h, st=st, m2=m2: e.bn_aggr(out=m2[:, h, :], in_=st[:, h, :]), reads=[st], writes=[m2])
                self.act(r4[:, 0:4], m2[:, :, 1], AF.Ln, [m2], [r4], bias=self.epsb[:])
                self.act(r4[:, 0:4], r4[:, 0:4], AF.Exp, [r4], [r4], scale=-0.5)
                y1 = yf.next()
                for h in range(4):
                    PO = POw[h // 2]
                    self.ts("dve", y1[:, h * 256:(h + 1) * 256], PO[:, (h % 2) * 256:(h % 2 + 1) * 256], m2[:, h, 0:1], ALU.subtract, [PO, m2, r4], [y1],
                            s2=r4[:, h:h + 1], op1=ALU.mult)
                Pkvw = (PSrm.next(), PSrm.next())
                for h in range(4):
                    Pkv = Pkvw[h // 2]
                    self.mm(Pkv, Pkv[:, (h % 2) * 256:(h % 2 + 1) * 256], ktm[:, h, :], v[:, h * 256:(h + 1) * 256], True, True, [ktm, v])
                for h in range(4):
                    Pkv = Pkvw[h // 2]
                    self.stt(Sst[:, h, :], Sst[:, h, :], gam[h] ** 128, Pkv[:, (h % 2) * 256:(h % 2 + 1) * 256], ALU.mult, ALU.add, [Sst, Pkv], [Sst])
                self.copy("act", Sbf[:], Sst[:], [Sst], [Sbf])
                self.tt("pool", y1[:], y1[:], gnrow, ALU.mult, [y1, gnr_b], [y1])
                ya = yab.next()
                self.tt("pool", ya[:], y1[:], g[:], ALU.mult, [y1, g], [ya])
                L["ya"] = ya

            def ssd_pre(L):
                dt, BT, CT, xs = L["dt"], L["BT"], L["CT"], L["xs"]
                PA = PSsp.next()
                self.mm(PA, PA[:, 0:16], self.tri_f[:], dt[:, 16:32], True, True, [self.tri_f, dt])
                self.mm(PA, PA[:, 16:32], self.ones_f[:], dt[:, 16:32], True, True, [self.ones_f, dt])
                ac = acol.next()
                self.copy("act", ac[:, 0:32], PA[:, 0:32], [PA], [ac])
                self.tt("dve", ac[:, 32:48], ac[:, 16:32], ac[:, 0:16], ALU.subtract, [ac], [ac])
                self.act(ac[:, 32:48], ac[:, 32:48], AF.Exp, [ac], [ac])
                self.act(ac[:, 16:32], ac[:, 16:32], AF.Exp, [ac], [ac])
                self.tt("dve", Dm[:], dt[:, 16:32].unsqueeze(2).broadcast_to([128, 16, 128]),
                        self.tri_f[:].unsqueeze(1).broadcast_to([128, 16, 128]), ALU.mult, [dt, self.tri_f], [Dm])
                PCB = PSsp.next()
                for gi in range(2):
                    self.mm(PCB, PCB[:, gi * 128:(gi + 1) * 128], BT[:, gi, :], CT[:, gi, :], True, True, [BT, CT])
                cb_ = cbm.next()
                self.tt("dve", cb_[:].rearrange("p g i -> p (g i)"), PCB[:, 0:256], self.tri2[:], ALU.mult, [PCB, cst], [cb_])
                MT = MTr.next(); CsT = CsTr.next()
                for q4 in range(4):
                    gi = q4 // 2
                    hs = slice(q4 * 4, q4 * 4 + 4)
                    Prb = PSsp.next()
                    self.mm(Prb, Prb[:], self.ones_f[:], Dm[:, hs, :].rearrange("p h i -> p (h i)"), True, True, [self.ones_f, Dm])
                    sg = segb.next()
                    self.tt("dve", sg[:], Prb[:].rearrange("p (h i) -> p h i", i=128), ac[:, q4 * 4:q4 * 4 + 4].unsqueeze(2).broadcast_to([128, 4, 128]),
                            ALU.subtract, [Prb, ac], [sg])
                    lt = ltb.next()
                    self.act(lt[:], sg[:], AF.Exp, [sg], [lt])
                    er = erb.next()
                    self.act(er[:].rearrange("p h i -> p (h i)"), Prb[:], AF.Exp, [Prb], [er])
                    self.stt(MT[:, hs, :], lt[:], 1.0, cb_[:, gi, :].unsqueeze(1).broadcast_to([128, 4, 128]), ALU.min, ALU.mult, [lt, cb_], [MT])
                    self.tt("pool", CsT[:, hs, :], er[:], CT[:, gi, :].unsqueeze(1).broadcast_to([128, 4, 128]), ALU.mult, [er, CT], [CsT])
                xdt = xdtb.next(); xdt2 = xdt2b.next()
                self.tt("dve", xdt[:], xs[:], dt[:, 0:16].unsqueeze(2).broadcast_to([128, 16, 64]), ALU.mult, [xs, dt], [xdt])
                self.tt("pool", xdt2[:], xdt[:], ac[:, 32:48].unsqueeze(2).broadcast_to([128, 16, 64]), ALU.mult, [xdt, ac], [xdt2])
                L.update(ac=ac, MT=MT, CsT=CsT, xdt=xdt, xdt2=xdt2)

            def ssd_main(L):
                xs, z, btm, ac, MT, CsT, xdt, xdt2 = L["xs"], L["z"], L["btm"], L["ac"], L["MT"], L["CsT"], L["xdt"], L["xdt2"]
                PYw = (PSsm.next(), PSsm.next())
                for h in range(16):
                    PY = PYw[h // 8]
                    hc = slice((h % 8) * 64, (h % 8 + 1) * 64)
                    self.mm(PY, PY[:, hc], MT[:, h, :], xdt[:, h, :], True, False, [MT, xdt])
                    self.mm(PY, PY[:, hc], CsT[:, h, :], STbf[:, h, :], False, True, [CsT, STbf])
                y2 = yf.next()
                self.tt("pool", y2[:].rearrange("p (h d) -> p h d", d=64), xs[:], dsk.unsqueeze(2).broadcast_to([128, 16, 64]), ALU.mult, [xs, cst], [y2])
                for gi in range(2):
                    self.tt("dve", y2[:, gi * 512:(gi + 1) * 512], PYw[gi][:], y2[:, gi * 512:(gi + 1) * 512], ALU.add, [PYw[gi], y2], [y2])
                PStw = (PSsm.next(), PSsm.next())
                for h in range(16):
                    PSt = PStw[h // 8]
                    self.mm(PSt, PSt[:, (h % 8) * 64:(h % 8 + 1) * 64], btm[:, h // 8, :], xdt2[:, h, :], True, True, [btm, xdt2])
                self.tt("dve", STs[:], STs[:], ac[:, 16:32].unsqueeze(2).broadcast_to([128, 16, 64]), ALU.mult, [STs, ac], [STs])
                STf = STs[:].rearrange("p h d -> p (h d)")
                for gi in range(2):
                    self.tt("dve", STf[:, gi * 512:(gi + 1) * 512], PStw[gi][:], STf[:, gi * 512:(gi + 1) * 512], ALU.add, [PStw[gi], STs], [STs])
                self.copy("act", STbf[:], STs[:], [STs], [STbf])
                self.tt("pool", y2[:], y2[:], z[:], ALU.mult, [y2, z], [y2])
                r8 = rs4.next()
                for gi in range(2):
                    S.op("act", lambda e, gi=gi, y2=y2, r8=r8: e.activation(out=junk[:], in_=y2[:, gi * 512:(gi + 1) * 512], func=AF.Square, accum_out=r8[:, gi:gi + 1]),
                         reads=[y2], writes=[junk, r8])
                self.act(r8[:, 0:2], r8[:, 0:2], AF.Ln, [r8], [r8], scale=1.0 / 512, bias=self.epsb[:])
                self.act(r8[:, 0:2], r8[:, 0:2], AF.Exp, [r8], [r8], scale=-0.5)
                yb = ybb.next()
                for gi in range(2):
                    self.stt(yb[:, gi * 512:(gi + 1) * 512], y2[:, gi * 512:(gi + 1) * 512], r8[:, gi:gi + 1], nrow[:, gi * 512:(gi + 1) * 512],
                             ALU.mult, ALU.mult, [y2, r8, nrr_b], [yb])
                L["yb"] = yb

            Ls = {0: loads(0)}
            if NCH > 1:
                Ls[1] = loads(1)
            S.interleave([S.record(lambda: ret_pre(Ls[0])), S.record(lambda: ssd_pre(Ls[0]))])
            for c in range(NCH):
                if c + 2 < NCH:
                    Ls[c + 2] = loads(c + 2)
                L = Ls[c]
                streams = [S.record(lambda: ret_main(L)), S.record(lambda: ssd_main(L))]
                if c + 1 < NCH:
                    Ln = Ls[c + 1]
                    streams += [S.record(lambda: ret_pre(Ln)), S.record(lambda: ssd_pre(Ln))]
                S.interleave(streams)
                out_part(L["ya"], L["yb"], L["b"], L["ts"])
                del Ls[c]
            S.barrier()

    def build(self):
        nc, es = self.nc, self.es
        T, NB = self.T, self.NB
        S = self.S = Sched(nc, es)
        ne = sum(1 for x in self.layers if x == "e")
        no = sum(1 for x in self.layers if x == "o")
        L = len(self.layers)
        w = self.w = {}

        def inp(name, shape):
            w[name] = nc.dram_tensor(name, list(shape), F32, kind="ExternalInput").ap()

        self.xin = DT(S, "xT", [D, T], F32, NB, kind="ExternalInput")
        self.xout = DT(S, "outT", [D, T], F32, NB, kind="ExternalOutput")
        inp("norm_mix", [L, D]); inp("norm_mlp", [L, D])
        inp("mlp_w1", [L, D, 4096]); inp("mlp_w2", [L, 4096, D])
        if no:
            inp("od_w_in", [no, D, 3584]); inp("od_w_out", [no, 1536, D])
            inp("lru_conv_w", [no, 4, D]); inp("lru_conv_b", [no, D])
            inp("lru_wa", [no, 8, 128, 128]); inp("lru_ba", [no, 8, 128])
            inp("lru_wx", [no, 8, 128, 128]); inp("lru_bx", [no, 8, 128])
            inp("lru_lam", [no, D]); inp("sb_qn", [no, 64]); inp("sb_kn", [no, 64])
        if ne:
            inp("ev_w_in", [ne, D, 5648]); inp("ev_w_out", [ne, 2048, D])
            inp("ret_qn", [ne, 128]); inp("ret_kn", [ne, 128]); inp("ret_gn", [ne, 1024])
            inp("ssd_conv_w", [ne, 4, 1536]); inp("ssd_conv_b", [ne, 1536])
            inp("ssd_dt_bias", [ne, 16]); inp("ssd_a_log", [ne, 16]); inp("ssd_d", [ne, 16]); inp("ssd_norm", [ne, 1024])
            inp("c_cos", [128, T]); inp("c_sin", [128, T])
            inp("c_ident", [128, 128]); inp("c_swap", [128, 128]); inp("c_tri", [128, 128]); inp("c_tri2", [128, 256])
            inp("c_rdecay", [128, 512]); inp("c_rqdec", [128, 512]); inp("c_rkdec", [128, 4])
        inp("c_masklt", [128, 128]); inp("c_umat", [128, 128]); inp("c_bones", [128, 128])

        allb = [S.ps("ps%d" % i, [128, 512]) for i in range(8)]
        self.psb = allb[0:4]
        self.psh = allb[4:8]
        self.PS = Rot(self.psb)
        self.PW = Rot([(allb[4], allb[5]), (allb[6], allb[7])])

        cst = self.cst = Buf("cst")

        def csb(name, shape, dt=F32):
            return es.enter_context(nc.sbuf_tensor(name, list(shape), dt))

        self.ones = S.sb("ones", [128, 128], BF16)
        S.op("dve", lambda e: e.memset(self.ones[:], 1.0), writes=[self.ones])
        self.epsb = csb("epsb", [128, 1]); self.oneb = csb("oneb", [128, 1])
        S.op("dve", lambda e: e.memset(self.epsb[:], EPS), writes=[cst])
        S.op("dve", lambda e: e.memset(self.oneb[:], 1.0), writes=[cst])
        self.mask_lt = csb("mask_lt", [128, 128], BF16)
        self.umat = S.sb("umat", [128, 128], BF16)
        self.bones = S.sb("bones", [128, 128], BF16)
        S.dma("pool", self.mask_lt[:], w["c_masklt"], writes=[cst])
        S.dma("pool", self.umat[:], w["c_umat"], writes=[self.umat])
        S.dma("pool", self.bones[:], w["c_bones"], writes=[self.bones])
        self.gains = Buf("gains")
        gmix = csb("gmix", [128, L, KC]); gmlp = csb("gmlp", [128, L, KC])
        S.dma("sp", gmix[:], w["norm_mix"].rearrange("l (kc p) -> p l kc", p=128), writes=[self.gains], allow_slow_non_contiguous=True)
        S.dma("sp", gmlp[:], w["norm_mlp"].rearrange("l (kc p) -> p l kc", p=128), writes=[self.gains], allow_slow_non_contiguous=True)
        self.g_mix = [gmix[:, l, :] for l in range(L)]
        self.g_mlp = [gmlp[:, l, :] for l in range(L)]
        if no:
            cw = csb("lru_cw", [128, no, 8, 4]); cb = csb("lru_cb", [128, no, 8])
            ba = csb("lru_ba_s", [128, no, 8]); bx = csb("lru_bx_s", [128, no, 8])
            cl = csb("lru_cl", [128, no, 8])
            qn = csb("sbqn", [128, no]); kn = csb("sbkn", [128, no])
            for o_ in range(no):
                for k_ in range(4):
                    S.dma("sp", cw[:, o_, :, k_], w["lru_conv_w"][o_, k_].rearrange("(c p) -> p c", p=128), writes=[cst], allow_slow_non_contiguous=True)
            S.dma("sp", cb[:], w["lru_conv_b"].rearrange("o (c p) -> p o c", p=128), writes=[cst], allow_slow_non_contiguous=True)
            S.dma("sp", ba[:], w["lru_ba"].rearrange("o c p -> p o c"), writes=[cst], allow_slow_non_contiguous=True)
            S.dma("sp", bx[:], w["lru_bx"].rearrange("o c p -> p o c"), writes=[cst], allow_slow_non_contiguous=True)
            S.dma("sp", cl[:], w["lru_lam"].rearrange("o (c p) -> p o c", p=128), writes=[cst], allow_slow_non_contiguous=True)
            for half in range(2):
                S.dma("sp", qn[half * 64:(half + 1) * 64, :], w["sb_qn"].rearrange("o d -> d o"), writes=[cst], allow_slow_non_contiguous=True)
                S.dma("sp", kn[half * 64:(half + 1) * 64, :], w["sb_kn"].rearrange("o d -> d o"), writes=[cst], allow_slow_non_contiguous=True)
            S.op("act", lambda e: e.activation(out=cl[:], in_=cl[:], func=AF.Exp, scale=-1.0), reads=[cst], writes=[cst])
            S.op("act", lambda e: e.activation(out=cl[:], in_=cl[:], func=AF.Ln, bias=self.oneb[:]), reads=[cst], writes=[cst])
            S.op("dve", lambda e: e.tensor_scalar(out=cl[:], in0=cl[:], scalar1=-8.0, scalar2=None, op0=ALU.mult), reads=[cst], writes=[cst])
            self.lru_cw = [cw[:, o] for o in range(no)]
            self.lru_cb = [cb[:, o] for o in range(no)]
            self.lru_ba = [ba[:, o] for o in range(no)]
            self.lru_bx = [bx[:, o] for o in range(no)]
            self.lru_cl = [cl[:, o] for o in range(no)]
            self.sb_qn = [qn[:, o:o + 1] for o in range(no)]
            self.sb_kn = [kn[:, o:o + 1] for o in range(no)]
        if ne:
            self.ident = S.sb("ident", [128, 128], BF16)
            S.dma("pool", self.ident[:], w["c_ident"], writes=[self.ident])
            self.swapm = S.sb("swapm", [128, 128], F32)
            S.dma("sp", self.swapm[:], w["c_swap"], writes=[self.swapm])
            self.tri_f = S.sb("tri_f", [128, 128], F32)
            S.dma("sp", self.tri_f[:], w["c_tri"], writes=[self.tri_f])
            self.ones_f = S.sb("ones_f", [128, 128], F32)
            S.op("dve", lambda e: e.memset(self.ones_f[:], 1.0), writes=[self.ones_f])
            self.tri2 = csb("tri2", [128, 256]); self.ret_decay = csb("rdecay", [128, 512]); self.ret_qdec = csb("rqdec", [128, 512])
            self.ret_kdec = csb("rkdec", [128, 4])
            S.dma("sp", self.tri2[:], w["c_tri2"], writes=[cst])
            S.dma("sp", self.ret_decay[:], w["c_rdecay"], writes=[cst])
            S.dma("sp", self.ret_qdec[:], w["c_rqdec"], writes=[cst])
            S.dma("sp", self.ret_kdec[:], w["c_rkdec"], writes=[cst])
            scw = csb("ssd_cw", [128, ne, 12, 4]); scb = csb("ssd_cb", [128, ne, 12])
            rqn = csb("ret_qn_s", [128, ne, 2]); rkn = csb("ret_kn_s", [128, ne, 2])
            dsk = csb("ssd_D_r", [128, ne, 16]); dtbr = csb("ssd_dtb_r", [128, ne, 16]); Ar = csb("ssd_A_r", [128, ne, 16])
            for e_ in range(ne):
                for k_ in range(4):
                    S.dma("sp", scw[:, e_, :, k_], w["ssd_conv_w"][e_, k_].rearrange("(c p) -> p c", p=128), writes=[cst], allow_slow_non_contiguous=True)
                S.dma("sp", scb[:, e_, :], w["ssd_conv_b"][e_].rearrange("(c p) -> p c", p=128), writes=[cst], allow_slow_non_contiguous=True)
                for nm, tl in (("ret_qn", rqn), ("ret_kn", rkn)):
                    col = w[nm][e_].rearrange("(d o) -> d o", o=1)
                    S.dma("sp", tl[:, e_, 0:1], col, writes=[cst], allow_slow_non_contiguous=True)
                    S.dma("sp", tl[0:64, e_, 1:2], col[64:128], writes=[cst], allow_slow_non_contiguous=True)
                    S.dma("sp", tl[64:128, e_, 1:2], col[0:64], writes=[cst], allow_slow_non_contiguous=True)
                S.dma("sp", dsk[:, e_, :], w["ssd_d"][e_].partition_broadcast(128), writes=[cst])
                S.dma("sp", dtbr[:, e_, :], w["ssd_dt_bias"][e_].partition_broadcast(128), writes=[cst])
                S.dma("sp", Ar[:, e_, :], w["ssd_a_log"][e_].partition_broadcast(128), writes=[cst])
            S.op("act", lambda e: e.activation(out=Ar[:], in_=Ar[:], func=AF.Exp), reads=[cst], writes=[cst])
            S.op("dve", lambda e: e.tensor_scalar(out=Ar[:], in0=Ar[:], scalar1=-1.0, scalar2=None, op0=ALU.mult), reads=[cst], writes=[cst])
            self.ssd_cw = [scw[:, e_] for e_ in range(ne)]; self.ssd_cb = [scb[:, e_] for e_ in range(ne)]
            self.ret_qn = [rqn[:, e_] for e_ in range(ne)]; self.ret_kn = [rkn[:, e_] for e_ in range(ne)]
            self.ssd_D = [dsk[:, e_] for e_ in range(ne)]; self.ssd_dtb = [dtbr[:, e_] for e_ in range(ne)]; self.ssd_A = [Ar[:, e_] for e_ in range(ne)]
        S.barrier()

        xa = DT(S, "xa", [D, T], F32, NB)
        xb_ = DT(S, "xb", [D, T], F32, NB)
        yT = DT(S, "yT", [2048, T], BF16, NB)
        qT = DT(S, "qT", [512, T], BF16, NB)
        kT = DT(S, "kT", [512, T], BF16, NB)
        vtm = DT(S, "vtm", [T, 1024], BF16, NB)
        sc = {"qr": qT, "kr": kT, "v": vtm}
        if ne:
            for nm in ("g", "z", "xs"):
                sc[nm] = DT(S, "sc_" + nm, [T, 1024], BF16, NB)
            sc["Btm"] = DT(S, "sc_Btm", [T, 256], BF16, NB)
            sc["BT"] = DT(S, "sc_BT", [256, T], BF16, NB)
            sc["CT"] = DT(S, "sc_CT", [256, T], BF16, NB)
            sc["dtA"] = DT(S, "sc_dtA", [T, 32], F32, NB)

        cur = self.xin
        ie = io = 0
        for l, kind in enumerate(self.layers):
            last = l == L - 1
            skip = getattr(self, "skip", ())
            lay_es = contextlib.ExitStack()
            w1b = None
            w1_th = ()

            def prep_w1():
                nonlocal w1b, w1_th
                if "mlp" in skip:
                    return
                w1b = S.sb("w1b", [128, KC, 4096], BF16, lay_es)
                w1_th = self.load_wcast(w1b, self.w["mlp_w1"][l], KC, 4096, defer=True)
            if kind == "o":
                if "odd_a" not in skip:
                    self.phase_odd_a(io, l, cur, yT, qT, kT, vtm)
                if "sb" not in skip:
                    self.phase_sb(qT, kT, vtm, yT)
                if "outproj" not in skip:
                    prep_w1()
                    self.phase_outproj("od_w_out", io, 12, yT, cur, xa, extra=w1_th)
                io += 1
            else:
                if "even_a" not in skip:
                    self.phase_even_a(ie, l, cur, sc)
                if "even_b" not in skip:
                    self.phase_even_b(ie, sc, yT)
                if "outproj" not in skip:
                    prep_w1()
                    self.phase_outproj("ev_w_out", ie, 16, yT, cur, xa, extra=w1_th)
                ie += 1
            if "mlp" not in skip:
                self.phase_mlp(l, xa, self.xout if last else xb_, w1b=(w1b if "outproj" not in skip else None))
            lay_es.close()
            cur = xb_
        S.finish()
        es.close()
        return nc


def consts(T=None, even=False):
    i = np.arange(128)
    c = _consts_base(i)
    if even:
        f32 = np.float32
        inv = (f32(10000.0) ** (-(np.arange(64, dtype=f32)) / f32(64))).astype(f32)
        ang = (np.arange(T, dtype=f32)[None, :] * inv[:, None]).astype(f32).astype(np.float64)
        cos = np.cos(ang); sin = np.sin(ang)
        c["c_cos"] = np.concatenate([cos, cos], 0).astype(f32)
        c["c_sin"] = np.concatenate([-sin, sin], 0).astype(f32)
        c["c_ident"] = np.eye(128, dtype=f32)
        c["c_swap"] = (i[:, None] == ((i[None, :] + 64) % 128)).astype(f32)
        tri = (i[:, None] <= i[None, :]).astype(f32)
        c["c_tri"] = tri
        c["c_tri2"] = np.concatenate([tri, tri], 1)
        lg = np.log1p(-np.exp2(-5.0 - np.arange(4, dtype=np.float64)))
        rel = (i[None, :] - i[:, None]).astype(np.float64)
        dec = np.where(rel[:, None, :] >= 0, np.exp(lg[None, :, None] * np.maximum(rel, 0)[:, None, :]), 0.0)
        c["c_rdecay"] = dec.reshape(128, 512).astype(f32)
        qd = np.exp(lg[:, None] * (i[None, :] + 1.0))
        c["c_rqdec"] = np.broadcast_to(qd.reshape(1, 512), (128, 512)).astype(f32).copy()
        c["c_rkdec"] = np.exp(lg[None, :] * (127.0 - i[:, None])).astype(f32)
    return c


def _consts_base(i):
    return {
        "c_masklt": (i[:, None] < i[None, :]).astype(np.float32),
        "c_umat": (i[:, None] > i[None, :]).astype(np.float32),
        "c_bones": ((i[:, None] // 64) == (i[None, :] // 64)).astype(np.float32),
    }


def make_inputs(inputs, b, kinds):
    m = {"xT": np.ascontiguousarray(np.asarray(inputs["x"])[b].T)}
    L = len(kinds)
    for n in ("norm_mix", "norm_mlp", "mlp_w1", "mlp_w2"):
        m[n] = np.ascontiguousarray(np.asarray(inputs[n])[:L])
    if "e" in kinds:
        for n in ("ev_w_in", "ev_w_out", "ret_qn", "ret_kn", "ret_gn", "ssd_conv_w", "ssd_conv_b", "ssd_dt_bias", "ssd_a_log", "ssd_d", "ssd_norm"):
            m[n] = np.ascontiguousarray(np.asarray(inputs[n]))
    if "o" in kinds:
        for n in ("od_w_in", "od_w_out", "lru_conv_w", "lru_conv_b", "lru_wa", "lru_ba", "lru_wx", "lru_bx", "lru_lam", "sb_qn", "sb_kn"):
            m[n] = np.ascontiguousarray(np.asarray(inputs[n]))
    m.update(consts(m["xT"].shape[1], "e" in kinds))
    return m


KINDS = "eoeo"
SEQ = 8192
_NC_CACHE = {}


def kernel(**inputs):
    if "nc" not in _NC_CACHE:
        _NC_CACHE["nc"] = K(SEQ, list(KINDS)).build()
    nc = _NC_CACHE["nc"]
    nb = np.asarray(inputs["x"]).shape[0]
    in_maps = [make_inputs(inputs, b, KINDS) for b in range(nb)]
    res = run_bass_kernel_spmd(nc, in_maps, core_ids=list(range(nb)))
    out = np.stack([np.asarray(res.results[b]["outT"]).T for b in range(nb)])
    return np.ascontiguousarray(out.astype(np.float32))
```
